# Optimizing a Trainium2 kernel written in Bass

```python
import math
import jax, jax.numpy as jnp
from jax import lax
import numpy as np

D_MODEL = 1024
BATCH = 2
SEQ = 8192
DEPTH = 2
DEC_BATCH = 128
DEC_SEQ = 8
PAST_LEN = 8192
PAGE_SIZE = 128

N_A = DEPTH // 2
N_B = DEPTH - N_A
POOL_WINDOWS = (2, 4, 8, 16)
POOL_GROUPS = 4
POOL_GC = D_MODEL // POOL_GROUPS
POOL_BUF = max(POOL_WINDOWS) - 1
N_HEADS = D_MODEL // 128
Q_RANK = 3 * D_MODEL // 8
KV_RANK = D_MODEL // 4
QK_NOPE = 128
QK_ROPE = 64
V_DIM = 128
SM_SCALE = 1.0 / math.sqrt(QK_NOPE + QK_ROPE)
ROPE_THETA = 10000.0
D_FF = 4 * D_MODEL
PLE_DIM = 256
Q_BLOCK = 128
NORM_EPS = 1e-6
NEG_INF = -1e30

kernel_name = 'yoco_pool_mla_decoder_step'


def _rmsnorm(x, g):
    xf = x.astype(jnp.float32)
    y = xf * lax.rsqrt(jnp.mean(xf * xf, axis=-1, keepdims=True) + NORM_EPS)
    return (y * g.astype(jnp.float32)).astype(x.dtype)


def _rope(x, pos):
    half = x.shape[-1] // 2
    inv_freq = jnp.power(ROPE_THETA, -jnp.arange(half, dtype=jnp.float32) / half)
    ang = pos.astype(jnp.float32)[:, None] * inv_freq[None, :]
    cos = jnp.cos(ang)[:, None, :]
    sin = jnp.sin(ang)[:, None, :]
    xf = x.astype(jnp.float32)
    x1, x2 = xf[..., :half], xf[..., half:]
    return jnp.concatenate([x1 * cos - x2 * sin, x2 * cos + x1 * sin], axis=-1).astype(x.dtype)


def _pool_mixer(u, buf, start_pos, w_grp, scale):
    T = u.shape[1]
    u_ext = jnp.concatenate([buf.astype(u.dtype), u], axis=1)
    cs = jnp.cumsum(u_ext.astype(jnp.float32), axis=1)
    cs = jnp.concatenate([jnp.zeros_like(cs[:, :1]), cs], axis=1)
    end = cs[:, POOL_BUF + 1:POOL_BUF + 1 + T]
    pos = start_pos + jnp.arange(T)
    uf = u.astype(jnp.float32)
    diffs = []
    for g, w in enumerate(POOL_WINDOWS):
        sl = slice(g * POOL_GC, (g + 1) * POOL_GC)
        win = end[..., sl] - cs[:, POOL_BUF + 1 - w:POOL_BUF + 1 - w + T, sl]
        cnt = jnp.minimum(pos + 1, w).astype(jnp.float32)[None, :, None]
        diffs.append(win / cnt - uf[..., sl])
    d = jnp.stack(diffs, axis=2).astype(u.dtype)
    mixed = jnp.einsum('btgc,gcd->btgd', d, w_grp).reshape(u.shape)
    return mixed * scale, u_ext[:, -POOL_BUF:]


def _shared_kv(h, pos, norm_kv, w_dkv, kv_norm):
    ckr = _rmsnorm(h, norm_kv) @ w_dkv
    c = _rmsnorm(ckr[..., :KV_RANK], kv_norm)
    kr = _rope(ckr[..., KV_RANK:][:, :, None, :], pos)[:, :, 0, :]
    return c, kr


def _mla_queries(u, pos, w_dq, q_norm, w_uq, w_uk):
    cq = _rmsnorm(u @ w_dq, q_norm)
    q = jnp.einsum('btr,rhe->bthe', cq, w_uq)
    q_lat = jnp.einsum('bthn,rhn->bthr', q[..., :QK_NOPE], w_uk)
    return q_lat, _rope(q[..., QK_NOPE:], pos)


def _latent_attend(q_lat, q_rope, parts):
    scores = []
    for c, kr, mask in parts:
        s = (jnp.einsum('bqhr,bkr->bhqk', q_lat, c)
             + jnp.einsum('bqhe,bke->bhqk', q_rope, kr)).astype(jnp.float32) * SM_SCALE
        if mask is not None:
            s = jnp.where(mask, s, NEG_INF)
        scores.append(s)
    probs = jax.nn.softmax(jnp.concatenate(scores, axis=-1), axis=-1)
    outs = []
    off = 0
    for c, _, _ in parts:
        n = c.shape[1]
        outs.append(jnp.einsum('bhqk,bkr->bqhr', probs[..., off:off + n].astype(c.dtype), c))
        off += n
    return sum(outs[1:], outs[0])


def _prompt_attention(q_lat, q_rope, c, kr):
    b, S = q_lat.shape[:2]
    nb = S // Q_BLOCK
    k_pos = jnp.arange(S)

    def blk(args):
        ql, qr, qpos = args
        mask = k_pos[None, :] <= qpos[:, None]
        return _latent_attend(ql, qr, [(c, kr, mask)])

    def split(a):
        return jnp.moveaxis(a.reshape((b, nb, Q_BLOCK) + a.shape[2:]), 1, 0)

    out = lax.map(blk, (split(q_lat), split(q_rope), k_pos.reshape(nb, Q_BLOCK)))
    return jnp.moveaxis(out, 0, 1).reshape((b, S) + out.shape[3:])


def _trunk(x, p, pool_state, past_c, past_kr, start_pos, w):
    T = x.shape[1]
    pos = start_pos + jnp.arange(T)
    h = x
    new_pool = []
    c = kr = None
    for i in range(DEPTH):
        u = _rmsnorm(h, w['norm_mix'][i])
        if i < N_A:
            mix, buf = _pool_mixer(u, pool_state[i], start_pos, w['pool_w'][i], w['pool_scale'][i])
            new_pool.append(buf)
        else:
            j = i - N_A
            q_lat, q_rope = _mla_queries(u, pos, w['w_dq'][j], w['q_norm'][j], w['w_uq'][j], w['w_uk'])
            if past_c is None:
                att = _prompt_attention(q_lat, q_rope, c, kr)
            else:
                causal = jnp.arange(T)[None, :] <= jnp.arange(T)[:, None]
                att = _latent_attend(q_lat, q_rope, [(past_c, past_kr, None), (c, kr, causal)])
            o = jnp.einsum('bqhr,rhv->bqhv', att, w['w_uv'])
            mix = jnp.einsum('bqhv,hvd->bqd', o, w['w_o'][j])
        h = h + mix
        a = jax.nn.relu(_rmsnorm(h, w['norm_mlp'][i]) @ w['w_up'][i])
        h = h + (a * a) @ w['w_down'][i]
        gate = jax.nn.sigmoid(_rmsnorm(h, w['norm_ple'][i]) @ w['w_ple_gate'][i])
        h = h + gate * (p[i] @ w['w_ple_proj'][i])
        if i == N_A - 1:
            c, kr = _shared_kv(h, pos, w['norm_kv'], w['w_dkv'], w['kv_norm'])
    return _rmsnorm(h, w['norm_final']), jnp.stack(new_pool), c, kr


def setup_inputs(seed: int = 0) -> dict:
    key = jax.random.key(seed)
    ks = iter(jax.random.split(key, 40))
    f32 = jnp.float32

    def nrm(shape, scale=1.0):
        return jax.random.normal(next(ks), shape, f32) * scale

    def gain(shape):
        return 1.0 + 0.1 * nrm(shape)

    n_pages = PAST_LEN // PAGE_SIZE
    n_pool = (DEC_BATCH * n_pages * 5) // 4
    page_table = jax.random.permutation(next(ks), n_pool)[:DEC_BATCH * n_pages]
    page_table = page_table.reshape(DEC_BATCH, n_pages).astype(jnp.int32)
    return {
        'x_prompt': nrm((BATCH, SEQ, D_MODEL)),
        'x_sample': nrm((DEC_BATCH, DEC_SEQ, D_MODEL)),
        'p_prompt': nrm((DEPTH, BATCH, SEQ, PLE_DIM)),
        'p_sample': nrm((DEPTH, DEC_BATCH, DEC_SEQ, PLE_DIM)),
        'state_pool': nrm((N_A, DEC_BATCH, POOL_BUF, D_MODEL)),
        'cache_latent': nrm((n_pool, PAGE_SIZE, KV_RANK)),
        'cache_krope': nrm((n_pool, PAGE_SIZE, QK_ROPE)),
        'page_table': page_table,
        'norm_mix': gain((DEPTH, D_MODEL)),
        'norm_mlp': gain((DEPTH, D_MODEL)),
        'norm_ple': gain((DEPTH, D_MODEL)),
        'pool_w': nrm((N_A, POOL_GROUPS, POOL_GC, POOL_GC), POOL_GC ** -0.5),
        'pool_scale': gain((N_A, D_MODEL)),
        'norm_kv': gain((D_MODEL,)),
        'w_dkv': nrm((D_MODEL, KV_RANK + QK_ROPE), D_MODEL ** -0.5),
        'kv_norm': gain((KV_RANK,)),
        'w_uk': nrm((KV_RANK, N_HEADS, QK_NOPE), KV_RANK ** -0.5),
        'w_uv': nrm((KV_RANK, N_HEADS, V_DIM), KV_RANK ** -0.5),
        'w_dq': nrm((N_B, D_MODEL, Q_RANK), D_MODEL ** -0.5),
        'q_norm': gain((N_B, Q_RANK)),
        'w_uq': nrm((N_B, Q_RANK, N_HEADS, QK_NOPE + QK_ROPE), Q_RANK ** -0.5),
        'w_o': nrm((N_B, N_HEADS, V_DIM, D_MODEL), (N_HEADS * V_DIM) ** -0.5),
        'w_up': nrm((DEPTH, D_MODEL, D_FF), D_MODEL ** -0.5),
        'w_down': nrm((DEPTH, D_FF, D_MODEL), D_FF ** -0.5),
        'w_ple_gate': nrm((DEPTH, D_MODEL, D_MODEL), D_MODEL ** -0.5),
        'w_ple_proj': nrm((DEPTH, PLE_DIM, D_MODEL), PLE_DIM ** -0.5),
        'norm_final': gain((D_MODEL,)),
    }


def reference(x_prompt, x_sample, p_prompt, p_sample, state_pool, cache_latent, cache_krope, page_table,
              norm_mix, norm_mlp, norm_ple, pool_w, pool_scale, norm_kv, w_dkv, kv_norm, w_uk, w_uv,
              w_dq, q_norm, w_uq, w_o, w_up, w_down, w_ple_gate, w_ple_proj, norm_final):
    w = dict(norm_mix=norm_mix, norm_mlp=norm_mlp, norm_ple=norm_ple, pool_w=pool_w,
             pool_scale=pool_scale, norm_kv=norm_kv, w_dkv=w_dkv, kv_norm=kv_norm, w_uk=w_uk,
             w_uv=w_uv, w_dq=w_dq, q_norm=q_norm, w_uq=w_uq, w_o=w_o, w_up=w_up, w_down=w_down,
             w_ple_gate=w_ple_gate, w_ple_proj=w_ple_proj, norm_final=norm_final)
    zero_pool = jnp.zeros((N_A, x_prompt.shape[0], POOL_BUF, D_MODEL), x_prompt.dtype)
    y_prompt, pool_prompt, latent_prompt, krope_prompt = _trunk(
        x_prompt, p_prompt, zero_pool, None, None, 0, w)
    b = page_table.shape[0]
    n_past = page_table.shape[1] * PAGE_SIZE
    past_c = cache_latent[page_table].reshape(b, n_past, KV_RANK)
    past_kr = cache_krope[page_table].reshape(b, n_past, QK_ROPE)
    y_sample, pool_sample, latent_sample, krope_sample = _trunk(
        x_sample, p_sample, state_pool, past_c, past_kr, PAST_LEN, w)
    return (y_prompt, y_sample, pool_prompt, pool_sample, latent_prompt, krope_prompt, latent_sample, krope_sample)
```

```python
import contextlib
import math
import numpy as np
import concourse.bass as bass
import concourse.mybir as mybir
from concourse.bass_utils import run_bass_kernel_spmd

F32 = mybir.dt.float32
BF16 = mybir.dt.bfloat16
I32 = mybir.dt.int32
AF = mybir.ActivationFunctionType
ALU = mybir.AluOpType

COMPUTE = ("pe", "act", "dve", "pool")
N_DMA_SLOTS = {"sp": 8, "pool": 2, "act": 2, "pe": 1, "dve": 1}


class Sched:
    def __init__(self, nc):
        self.nc = nc
        self.eng_names = ("pe", "act", "dve", "pool", "sp")
        self.prog = {e: [] for e in self.eng_names}
        self.n_comp = {e: 0 for e in self.eng_names}
        self.n_dma = {e: 0 for e in self.eng_names}
        self.slot_cnt = {}
        self.last_w = {}
        self.readers = {}
        self.known = {e: {} for e in self.eng_names}
        self.targets = {e: set() for e in self.eng_names}
        self.final_dma = []
        self.n_cc = 0

    def _add_dep(self, deps, tok, eng):
        if tok is None:
            return
        if tok[0] == "c":
            if tok[1] == eng and eng == "pe":
                return
            lane = ("c", tok[1])
            v = tok[2]
        else:
            lane = (tok[0], tok[1], tok[2])
            v = tok[3]
        if deps.get(lane, 0) < v:
            deps[lane] = v

    def op(self, eng, fn, reads=(), writes=(), dma=False, final=False, cc=False):
        deps = {}
        for k in reads:
            self._add_dep(deps, self.last_w.get(k), eng)
        for k in writes:
            self._add_dep(deps, self.last_w.get(k), eng)
            for t in self.readers.get(k, ()):
                self._add_dep(deps, t, eng)
        if cc:
            self.n_cc += 1
            tok = ("x", "cc", self.n_cc, 1)
        elif dma:
            n = self.n_dma[eng]
            self.n_dma[eng] = n + 1
            slot = n % N_DMA_SLOTS[eng]
            cnt = self.slot_cnt.get((eng, slot), 0) + 1
            self.slot_cnt[(eng, slot)] = cnt
            tok = ("d", eng, slot, cnt)
            if cnt > 1:
                lane = ("d", eng, slot)
                if deps.get(lane, 0) < cnt - 1:
                    deps[lane] = cnt - 1
        else:
            self.n_comp[eng] += 1
            tok = ("c", eng, self.n_comp[eng])
        waits = []
        kn = self.known[eng]
        for lane, v in deps.items():
            if kn.get(lane, 0) >= v:
                continue
            kn[lane] = v
            waits.append((lane, v))
            if lane[0] == "c":
                self.targets[lane[1]].add(v)
        self.prog[eng].append((waits, fn, tok))
        for k in reads:
            self.readers.setdefault(k, []).append(tok)
        for k in writes:
            self.last_w[k] = tok
            self.readers[k] = []
        if final:
            self.final_dma.append(tok)
        return tok

    def barrier(self):
        for eng in self.eng_names:
            deps = {}
            for F in COMPUTE:
                if self.n_comp[F] > 0 and not (F == eng and eng == "pe"):
                    deps[("c", F)] = self.n_comp[F]
            for (e, s_), c in self.slot_cnt.items():
                deps[("d", e, s_)] = c
            for i in range(1, self.n_cc + 1):
                deps[("x", "cc", i)] = 1
            waits = []
            kn = self.known[eng]
            for lane, v in deps.items():
                if kn.get(lane, 0) >= v:
                    continue
                kn[lane] = v
                waits.append((lane, v))
                if lane[0] == "c":
                    self.targets[lane[1]].add(v)
            self.prog[eng].append((waits, None, None))

    def emit(self):
        nc = self.nc
        with contextlib.ExitStack() as st:
            dsem = {}
            for i in range(1, self.n_cc + 1):
                dsem[("x", "cc", i)] = st.enter_context(nc.semaphore("cc_%d" % i))
            csem = {e: st.enter_context(nc.semaphore("c_" + e)) for e in COMPUTE}
            for (e, s) in self.slot_cnt:
                dsem[("d", e, s)] = st.enter_context(nc.semaphore("d_%s_%d" % (e, s)))
            cum = {}
            for e in COMPUTE:
                tg = sorted(self.targets[e])
                cum[e] = {idx: i + 1 for i, idx in enumerate(tg)}
            fin = {}
            for tok in self.final_dma:
                lane = ("d", tok[1], tok[2])
                fin[lane] = max(fin.get(lane, 0), tok[3])
            block = st.enter_context(nc.Block())
            engobj = {"pe": "tensor", "act": "scalar", "dve": "vector", "pool": "gpsimd", "sp": "sync"}

            def build(ename):
                def body(eng):
                    for waits, fn, tok in self.prog[ename]:
                        for lane, v in waits:
                            if lane[0] == "c":
                                eng.wait_ge(csem[lane[1]], cum[lane[1]][v])
                            elif lane[0] == "d":
                                eng.wait_ge(dsem[lane], 16 * v)
                            else:
                                eng.wait_ge(dsem[lane], v)
                        if fn is None:
                            continue
                        ins = fn(eng)
                        if tok[0] == "c":
                            if tok[2] in cum[tok[1]]:
                                ins.then_inc(csem[tok[1]], 1)
                        elif tok[0] == "d":
                            ins.then_inc(dsem[(tok[0], tok[1], tok[2])], 16)
                        else:
                            ins.then_inc(dsem[(tok[0], tok[1], tok[2])], 1)
                    if ename == "sp":
                        for lane, v in fin.items():
                            eng.wait_ge(dsem[lane], 16 * v)
                return body

            for ename in self.eng_names:
                if not self.prog[ename] and ename != "sp":
                    continue
                getattr(block, engobj[ename])(build(ename))


D = 1024
DC = 8
KVR = 256
QR = 384
NH = 8
NOPE = 128
ROPE = 64
PLE = 256
POOL_W = (2, 4, 8, 16)
EPS = 1e-6
SM_SCALE = 1.0 / math.sqrt(NOPE + ROPE)
THETA = 10000.0

FULL = dict(NCORE=8, BATCH=2, NBLK=64, DEC=128, DSEQ=8, NPG=64, NPOOL=10240, FF=4096, USE_CC=False, SAMPLE_ATT=True)


def core_blocks(cfg, k):
    nc_, nb = cfg["NCORE"], cfg["NBLK"]
    out = []
    for i in range(nb // (2 * nc_)):
        out.append(2 * nc_ * i + k)
        out.append(2 * nc_ * i + 2 * nc_ - 1 - k)
    return out


def owner_of(cfg, j):
    nc_ = cfg["NCORE"]
    r = j % (2 * nc_)
    k = r if r < nc_ else 2 * nc_ - 1 - r
    m = (j // (2 * nc_)) * 2 + (0 if r < nc_ else 1)
    return k, m


def build_program(cfg):
    NCORE, BATCH, NBLK, FF = cfg["NCORE"], cfg["BATCH"], cfg["NBLK"], cfg["FF"]
    VR = NCORE // BATCH
    NPASS = VR
    NTB = NBLK // VR
    NT = NTB
    T = (NT + 1) * 128
    TP = NT * 128
    NFB = FF // 512
    DPC = cfg["DEC"] // NCORE
    NPG = cfg["NPG"]
    assert DPC * cfg["DSEQ"] == 128
    groups = []
    l = 0
    while l < NT:
        n = min(4, NT - l)
        groups.append((l * 128, n * 128, list(range(l, l + n))))
        l += n
    groups.append((NT * 128, 128, [NT]))
    NG = len(groups)
    groups_all = groups

    nc = bass.Bass("TRN2", target_bir_lowering=False)

    def din(name, shape, dt=F32):
        return nc.dram_tensor(name, list(shape), dt, kind="ExternalInput").ap()

    def dout(name, shape, dt=F32):
        return nc.dram_tensor(name, list(shape), dt, kind="ExternalOutput").ap()

    xT_d = din("xT", [NPASS, D, T])
    xtok_d = din("xtok", [NPASS, T, D])
    xhalo_d = din("xhalo", [NPASS, NT, 16, D])
    uhalo_d = din("uhalo", [2, 120, D])
    spin_d = din("spin", [DPC, 15, D])
    pT0_d = din("pT0", [NPASS, PLE, T])
    pT1_d = din("pT1", [PLE, T])
    gvec_d = din("gvec", [128, 80])
    gmix0_d = din("gmix0", [1, D])
    cos_d = din("cosT", [NPASS, 64, T])
    sin_d = din("sinT", [NPASS, 64, T])
    amain_d = din("amain", [4, 128, 128])
    afirst_d = din("afirst", [NPASS, 4, 128, 128])
    ahalo_d = din("ahalo", [4, 16, 128])
    amain_s_d = din("amain_s", [4, 128, 128])
    ahalo_s_d = din("ahalo_s", [2, 4, 120, 128])
    bmask_d = din("bmask", [VR * 2, 128, 128])
    identf_d = din("identf", [128, 128])
    poolw_d = din("pool_w", [4, 256, 256])
    wup_d = din("w_up", [2, D, FF])
    wdn_d = din("w_down", [2, FF, D])
    wg_d = din("w_gate", [2, D, D])
    wp_d = din("w_proj", [2, PLE, D])
    wdkv_d = din("w_dkv4", [D, 384])
    wdq_d = din("w_dq", [D, QR])
    wuqn_d = din("w_uq_nope", [QR, NH * 128])
    wuqr_d = din("w_uq_rope", [QR, NH * 128])
    wukT_d = din("w_ukT", [128, NH * 256])
    wuv_d = din("w_uv", [KVR, NH * 128])
    wo_d = din("w_o", [NH * 128, D])
    if cfg["SAMPLE_ATT"]:
        clat_d = din("cache_latent", [cfg["NPOOL"], 128, KVR])
        ckr_d = din("cache_krope", [cfg["NPOOL"], 128, ROPE])
        ptab_d = din("ptab", [1, DPC * NPG], I32)
        smask_d = din("smask", [128, 128])
        pidx_d = din("pidx", [128, 1])

    yT_o = dout("yT", [D, T])
    latT_o = dout("latT", [KVR, T])
    krT_o = dout("krT", [ROPE, T])
    poolp_o = dout("poolp", [15, D])
    pools_o = dout("pools", [DPC, 15, D])

    cx_all = nc.dram_tensor("cx_all", [VR * 576, TP], BF16)

    def c_tok_view(t, r):
        return t.ap()[r * 576 + 320:(r + 1) * 576, :].rearrange("a (t c) -> (a t) c", c=256)

    S = Sched(nc)
    st = contextlib.ExitStack()
    with st:
        ARENA = 206 * 1024
        arena = st.enter_context(nc.sbuf_tensor("arena", [128, ARENA], mybir.dt.uint8))
        top = [0]

        def sb(name, shape, dt):
            nb = 2 if dt == BF16 else 4
            per = int(np.prod(shape[1:])) * nb
            off = (top[0] + 63) // 64 * 64
            assert off + per <= ARENA, (name, off, per, ARENA)
            top[0] = off + per
            v = arena[0:shape[0], off:off + per].bitcast(dt)
            if len(shape) == 3:
                v = v.rearrange("p (a b) -> p a b", a=shape[1])
            elif len(shape) == 4:
                v = v.rearrange("p (a b c) -> p a b c", a=shape[1], b=shape[2])
            return v

        ps = [st.enter_context(nc.psum_tensor("ps%d" % i, [128, 512], F32)) for i in range(7)]
        psb = st.enter_context(nc.psum_tensor("psb", [128, 1024], BF16))
        PSK = ["ps%d" % i for i in range(7)]

        hT = sb("hT", [128, DC, T], F32)
        gvec = sb("gvec", [128, 80], F32)
        ones_bf = sb("ones_bf", [128, 128], BF16)
        ident_bf = sb("ident_bf", [128, 128], BF16)
        mask_bf = sb("mask_bf", [128, VR * 2, 128], BF16)
        sqb = [sb("sqb%d" % i, [128, 512], BF16) for i in range(2)]
        rstd = [sb("rstd%d" % i, [128, 512], F32) for i in range(2)]
        tmpf = [sb("tmpf%d" % i, [128, 512], F32) for i in range(2)]
        tmpb = [sb("tmpb%d" % i, [128, 512], BF16) for i in range(3)]
        sKT = sb("sKT", [128, 3, 128], BF16)
        sV = sb("sV", [128, 258], BF16)
        smask_bf = sb("smask_bf", [128, 16, 8], BF16)
        idx_all = sb("idx_all", [128, DPC * NPG], I32)
        pidx = sb("pidx", [128, 1], F32)
        cqT = sb("cqT", [128, 3, T], BF16)
        M_U = top[0]
        uT = sb("uT", [128, DC, T], BF16)
        wA = [sb("wA%d" % i, [128, 8, 512], BF16) for i in range(2)]
        wB = [sb("wB%d" % i, [128, 4, 1024], BF16) for i in range(2)]
        aT = [sb("aT%d" % i, [128, 4, 512], BF16) for i in range(2)]
        M_1 = top[0]

        G_MIX1, G_MLP0, G_MLP1, G_PLE0, G_PLE1, G_PSC, G_KV, G_FIN, G_Q, G_KVN = 0, 8, 16, 24, 32, 40, 48, 56, 64, 67

        cnt = {"ps": 0, "sq": 0, "rs": 0, "tf": 0, "tb": 0}

        def rot(kind, n):
            v = cnt[kind]
            cnt[kind] = (v + 1) % n
            return v

        def dma(eng, out, in_, reads=(), writes=(), final=False, **kw):
            return S.op(eng, lambda e: e.dma_start(out=out, in_=in_, **kw), reads=reads, writes=writes, dma=True, final=final)

        def mm(out, lhsT, rhs, start, stop, reads, writes):
            S.op("pe", lambda e: e.matmul(out, lhsT=lhsT, rhs=rhs, start=start, stop=stop), reads=reads, writes=writes)

        dma("sp", gvec[:], gvec_d[:, :], writes=["gvec"])
        dma("pool", ident_bf[:], identf_d[:, :], writes=["ident_bf"])
        dma("pool", mask_bf[:], bmask_d.rearrange("m p t -> p m t"), writes=["mask_bf"])
        S.op("dve", lambda e: e.memset(ones_bf[:], 1.0), writes=["ones_bf"])
        if cfg["SAMPLE_ATT"]:
            dma("pool", smask_bf[:], smask_d.rearrange("p (s i) -> p s i", i=8), writes=["smask_bf"])
            dma("sp", idx_all[:], ptab_d[0:1, :].broadcast_to([128, DPC * NPG]), writes=["idx_all"])
            dma("sp", pidx[:], pidx_d[:, :], writes=["pidx"])
            S.op("dve", lambda e: e.tensor_scalar(out=idx_all[:], in0=idx_all[:], scalar1=128.0, scalar2=pidx[:, 0:1], op0=ALU.mult, op1=ALU.add),
                 reads=["idx_all", "pidx"], writes=["idx_all"])
            S.op("dve", lambda e: e.memset(sV[:, 256:258], 1.0), writes=["sV"])
        for pj in range(NPASS):
            own = (pj == NPASS - 1)
            groups = groups_all if own else groups_all[:-1]
            NG = len(groups)
            S.barrier()
            top[0] = M_1
            for (c0, ncol, tiles) in groups:
                gi = c0
                dma("sp", hT[:, :, c0:c0 + ncol], xT_d[pj].rearrange("(c p) t -> p c t", p=128)[:, :, c0:c0 + ncol],
                    writes=[("hT", ch, gi) for ch in range(DC)])

            def norm_T(src_fn, src_keys_fn, nch, dim, gcol, out_fn, out_keys_fn, g_id, ncol, eps=EPS, out2_fn=None, out2_keys_fn=None):
                b = PSK[rot("ps", 2)]
                bank = ps[int(b[2:])]
                for ch in range(nch):
                    q = rot("sq", 2)
                    S.op("act", lambda e, ch=ch, q=q: e.activation(out=sqb[q][:, 0:ncol], in_=src_fn(ch), func=AF.Square),
                         reads=src_keys_fn(ch), writes=["sqb%d" % q])
                    mm(bank[:, 0:ncol], ones_bf[:], sqb[q][:, 0:ncol], ch == 0, ch == nch - 1, ["ones_bf", "sqb%d" % q], [b])
                r = rot("rs", 2)
                S.op("act", lambda e: e.activation(out=rstd[r][:, 0:ncol], in_=bank[:, 0:ncol], func=AF.Ln, scale=1.0 / dim, bias=eps),
                     reads=[b], writes=["rstd%d" % r])
                S.op("act", lambda e: e.activation(out=rstd[r][:, 0:ncol], in_=rstd[r][:, 0:ncol], func=AF.Exp, scale=-0.5),
                     reads=["rstd%d" % r], writes=["rstd%d" % r])
                for ch in range(nch):
                    S.op("dve", lambda e, ch=ch: e.scalar_tensor_tensor(out=out_fn(ch), in0=src_fn(ch), scalar=gvec[:, gcol + ch:gcol + ch + 1],
                                                                        in1=rstd[r][:, 0:ncol], op0=ALU.mult, op1=ALU.mult),
                         reads=list(src_keys_fn(ch)) + ["gvec", "rstd%d" % r], writes=out_keys_fn(ch))
                    if out2_fn is not None:
                        S.op("dve", lambda e, ch=ch: e.scalar_tensor_tensor(out=out2_fn(ch), in0=src_fn(ch), scalar=gvec[:, gcol + ch:gcol + ch + 1],
                                                                            in1=rstd[r][:, 0:ncol], op0=ALU.mult, op1=ALU.mult),
                             reads=list(src_keys_fn(ch)) + ["gvec", "rstd%d" % r], writes=out2_keys_fn(ch))

            def norm_h(gcol):
                for (c0, ncol, tiles) in groups:
                    norm_T(lambda ch, c0=c0, ncol=ncol: hT[:, ch, c0:c0 + ncol], lambda ch, c0=c0: [("hT", ch, c0)], DC, D, gcol,
                           lambda ch, c0=c0, ncol=ncol: uT[:, ch, c0:c0 + ncol], lambda ch, c0=c0: [("uT", ch, c0)], c0, ncol)

            def load_w(buf, bufkey, src_ap):
                dma("pool", buf, src_ap, writes=[bufkey])

            if True:
                top[0] = M_U
                sb0 = sb
                xt = [sb0("xt%d" % i, [128, D], F32) for i in range(2)]
                xh = [sb0("xh%d" % i, [16, D], F32) for i in range(2)]
                gbc = sb0("gbc", [128, D], F32)
                ub = [sb0("ub%d" % i, [128, D], BF16) for i in range(2)]
                uhb = [sb0("uhb%d" % i, [16, D], BF16) for i in range(2)]
                uf = sb0("uf", [128, D], F32)
                junk = sb0("junk", [128, D], F32)
                ss = [sb0("ss%d" % i, [128, 2], F32) for i in range(2)]
                am = sb0("am", [128, 4, 128], BF16)
                af = sb0("af", [128, 4, 128], BF16)
                ah = sb0("ah", [16, 4, 128], BF16)
                ams = sb0("ams", [128, 4, 128], BF16)
                ahs = sb0("ahs", [120, 2, 4, 128], BF16)
                uhs = sb0("uhs", [120, 2, D], BF16)
                dT = sb0("dT", [128, DC, 512], BF16)
                pw = sb0("pw", [128, 4, 2, 256], BF16)

                dma("sp", gbc[:], gmix0_d[0:1, :].broadcast_to([128, D]), writes=["gbc"])
                dma("pool", am[:], amain_d.rearrange("g p t -> p g t"), writes=["am"])
                dma("pool", af[:], afirst_d[pj].rearrange("g p t -> p g t"), writes=["af"])
                dma("pool", ah[:], ahalo_d.rearrange("g p t -> p g t"), writes=["ah"])
                dma("pool", ams[:], amain_s_d.rearrange("g p t -> p g t"), writes=["ams"])
                dma("pool", ahs[:], ahalo_s_d.rearrange("a g p t -> p a g t"), writes=["ahs"])
                dma("pool", uhs[:], uhalo_d.rearrange("a p d -> p a d"), writes=["uhs"])
                dma("pool", pw[:], poolw_d.rearrange("g (k p) d -> p g k d", p=128), writes=["pw"])
                if own:
                    dma("sp", pools_o[:, 0:7, :], spin_d[:, 8:15, :], final=True)

                for (c0, ncol, tiles) in groups:
                    for ti, l in enumerate(tiles):
                        i2 = l % 2
                        is_s = (l == NT)
                        dma("sp", xt[i2][:], xtok_d[pj][l * 128:(l + 1) * 128, :], writes=["xt%d" % i2])
                        S.op("act", lambda e, i2=i2: e.activation(out=junk[:], in_=xt[i2][:], func=AF.Square, accum_out=ss[i2][:, 0:1]),
                             reads=["xt%d" % i2], writes=["junk", "ss%d" % i2])
                        S.op("act", lambda e, i2=i2: e.activation(out=ss[i2][:, 0:1], in_=ss[i2][:, 0:1], func=AF.Ln, scale=1.0 / D, bias=EPS),
                             reads=["ss%d" % i2], writes=["ss%d" % i2])
                        S.op("act", lambda e, i2=i2: e.activation(out=ss[i2][:, 0:1], in_=ss[i2][:, 0:1], func=AF.Exp, scale=-0.5),
                             reads=["ss%d" % i2], writes=["ss%d" % i2])
                        S.op("dve", lambda e, i2=i2: e.scalar_tensor_tensor(out=ub[i2][:], in0=xt[i2][:], scalar=ss[i2][:, 0:1], in1=gbc[:],
                                                                            op0=ALU.mult, op1=ALU.mult),
                             reads=["xt%d" % i2, "ss%d" % i2, "gbc"], writes=["ub%d" % i2])
                        last_of_batch = own and (not is_s) and (l == NT - 1)
                        if is_s or last_of_batch:
                            S.op("dve", lambda e, i2=i2: e.scalar_tensor_tensor(out=uf[:], in0=xt[i2][:], scalar=ss[i2][:, 0:1], in1=gbc[:],
                                                                                op0=ALU.mult, op1=ALU.mult),
                                 reads=["xt%d" % i2, "ss%d" % i2, "gbc"], writes=["uf"])
                            if is_s:
                                dma("sp", pools_o[:, 7:15, :].rearrange("s i d -> (s i) d") if False else pools_o[:, 7:15, :],
                                    uf[:].rearrange("(s i) d -> s i d", i=8) if False else uf[:], reads=["uf"], final=True) if False else None
                                for s_ in range(DPC):
                                    dma("sp", pools_o[s_, 7:15, :], uf[s_ * 8:(s_ + 1) * 8, :], reads=["uf"], final=True)
                            else:
                                dma("sp", poolp_o[:, :], uf[113:128, :], reads=["uf"], final=True)
                        if not is_s:
                            dma("sp", xh[i2][:], xhalo_d[pj][l], writes=["xh%d" % i2])
                            S.op("act", lambda e, i2=i2: e.activation(out=junk[0:16, :], in_=xh[i2][:], func=AF.Square, accum_out=ss[i2][0:16, 1:2]),
                                 reads=["xh%d" % i2], writes=["junk", "ssh%d" % i2])
                            S.op("act", lambda e, i2=i2: e.activation(out=ss[i2][0:16, 1:2], in_=ss[i2][0:16, 1:2], func=AF.Ln, scale=1.0 / D, bias=EPS),
                                 reads=["ssh%d" % i2], writes=["ssh%d" % i2])
                            S.op("act", lambda e, i2=i2: e.activation(out=ss[i2][0:16, 1:2], in_=ss[i2][0:16, 1:2], func=AF.Exp, scale=-0.5),
                                 reads=["ssh%d" % i2], writes=["ssh%d" % i2])
                            S.op("dve", lambda e, i2=i2: e.scalar_tensor_tensor(out=uhb[i2][:], in0=xh[i2][:], scalar=ss[i2][0:16, 1:2], in1=gbc[0:16, :],
                                                                                op0=ALU.mult, op1=ALU.mult),
                                 reads=["xh%d" % i2, "ssh%d" % i2, "gbc"], writes=["uhb%d" % i2])
                        first = (not is_s) and (l == 0)
                        for half in range(2):
                            b = PSK[rot("ps", 2)]
                            bank = ps[int(b[2:])]
                            for cc in range(4):
                                ch = half * 4 + cc
                                g = ch // 2
                                o = bank[:, cc * 128:(cc + 1) * 128]
                                if is_s:
                                    mm(o, ub[i2][:, ch * 128:(ch + 1) * 128], ams[:, g, :], True, False, ["ub%d" % i2, "ams"], [b])
                                    mm(o, uhs[:, 0, ch * 128:(ch + 1) * 128], ahs[:, 0, g, :], False, False, ["uhs", "ahs"], [b])
                                    mm(o, uhs[:, 1, ch * 128:(ch + 1) * 128], ahs[:, 1, g, :], False, True, ["uhs", "ahs"], [b])
                                else:
                                    amat = af if first else am
                                    mm(o, ub[i2][:, ch * 128:(ch + 1) * 128], amat[:, g, :], True, False, ["ub%d" % i2, "af", "am"], [b])
                                    mm(o, uhb[i2][:, ch * 128:(ch + 1) * 128], ah[:, g, :], False, True, ["uhb%d" % i2, "ah"], [b])
                            S.op("act", lambda e, half=half, ti=ti, bank=bank: e.activation(
                                out=dT[:, half * 4:half * 4 + 4, ti * 128:(ti + 1) * 128], in_=bank[:, :].rearrange("p (c t) -> p c t", c=4), func=AF.Copy),
                                reads=[b], writes=[("dT", half, ti)])
                    for j in range(DC):
                        g = j // 2
                        b = PSK[2 + rot("ps", 2)] if False else PSK[rot("ps", 2)]
                        bank = ps[int(b[2:])]
                        for kc in range(2):
                            mm(bank[:, 0:ncol], pw[:, g, kc, (j % 2) * 128:(j % 2) * 128 + 128], dT[:, 2 * g + kc, 0:ncol], kc == 0, kc == 1,
                               ["pw"] + [("dT", (2 * g + kc) // 4, ti) for ti in range(len(tiles))], [b])
                        S.op("dve", lambda e, j=j, bank=bank, c0=c0, ncol=ncol: e.scalar_tensor_tensor(
                            out=hT[:, j, c0:c0 + ncol], in0=bank[:, 0:ncol], scalar=gvec[:, G_PSC + j:G_PSC + j + 1], in1=hT[:, j, c0:c0 + ncol],
                            op0=ALU.mult, op1=ALU.add), reads=[b, "gvec", ("hT", j, c0)], writes=[("hT", j, c0)])

            def mlp(layer, gcol):
                norm_h(gcol)
                for fb in range(NFB):
                    i2 = fb % 2
                    load_w(wA[i2][:], "wA%d" % i2, wup_d[layer].rearrange("(k p) f -> p k f", p=128)[:, :, fb * 512:(fb + 1) * 512])
                    load_w(wB[i2][:], "wB%d" % i2, wdn_d[layer][fb * 512:(fb + 1) * 512, :].rearrange("(k p) d -> p k d", p=128))
                    for gi, (c0, ncol, tiles) in enumerate(groups):
                        a2 = gi % 2
                        for fc in range(4):
                            b = PSK[rot("ps", 2)]
                            bank = ps[int(b[2:])]
                            for k in range(DC):
                                mm(bank[:, 0:ncol], wA[i2][:, k, fc * 128:(fc + 1) * 128], uT[:, k, c0:c0 + ncol], k == 0, k == DC - 1,
                                   ["wA%d" % i2, ("uT", k, c0)], [b])
                            tb = rot("tb", 3)
                            S.op("act", lambda e, bank=bank, tb=tb, ncol=ncol: e.activation(out=tmpb[tb][:, 0:ncol], in_=bank[:, 0:ncol], func=AF.Relu),
                                 reads=[b], writes=["tmpb%d" % tb])
                            S.op("dve", lambda e, tb=tb, a2=a2, fc=fc, ncol=ncol: e.tensor_tensor(out=aT[a2][:, fc, 0:ncol], in0=tmpb[tb][:, 0:ncol],
                                                                                             in1=tmpb[tb][:, 0:ncol], op=ALU.mult),
                                 reads=["tmpb%d" % tb], writes=[("aT", a2, fc)])
                        for j in range(DC):
                            b = PSK[2 + rot("rs", 2)]
                            bank = ps[int(b[2:])]
                            for fc in range(4):
                                mm(bank[:, 0:ncol], wB[i2][:, fc, j * 128:(j + 1) * 128], aT[a2][:, fc, 0:ncol], fc == 0, fc == 3,
                                   ["wB%d" % i2, ("aT", a2, fc)], [b])
                            S.op("dve", lambda e, j=j, bank=bank, c0=c0, ncol=ncol: e.tensor_tensor(
                                out=hT[:, j, c0:c0 + ncol], in0=bank[:, 0:ncol], in1=hT[:, j, c0:c0 + ncol], op=ALU.add),
                                reads=[b, ("hT", j, c0)], writes=[("hT", j, c0)])

            def ple(layer, gcol):
                norm_h(gcol)
                for hlf in range(2):
                    load_w(wA[hlf][:], "wA%d" % hlf, wg_d[layer].rearrange("(k p) f -> p k f", p=128)[:, :, hlf * 512:(hlf + 1) * 512])
                load_w(wB[0][:, 0:2, :], "wB0", wp_d[layer].rearrange("(k p) d -> p k d", p=128))
                for gi, (c0, ncol, tiles) in enumerate(groups):
                    a2 = gi % 2
                    dma("pool", aT[a2][:, 0:2, 0:ncol], (pT0_d[pj] if layer == 0 else pT1_d).rearrange("(k p) t -> p k t", p=128)[:, :, c0:c0 + ncol],
                        writes=[("aT", a2, 0), ("aT", a2, 1)])
                    for j in range(DC):
                        b = PSK[rot("ps", 2)]
                        bank = ps[int(b[2:])]
                        for k in range(DC):
                            mm(bank[:, 0:ncol], wA[j // 4][:, k, (j % 4) * 128:(j % 4) * 128 + 128], uT[:, k, c0:c0 + ncol], k == 0, k == DC - 1,
                               ["wA%d" % (j // 4), ("uT", k, c0)], [b])
                        tf = rot("tf", 2)
                        S.op("act", lambda e, bank=bank, tf=tf, ncol=ncol: e.activation(out=tmpf[tf][:, 0:ncol], in_=bank[:, 0:ncol], func=AF.Sigmoid),
                             reads=[b], writes=["tmpf%d" % tf])
                        b2 = PSK[2 + rot("rs", 2)]
                        bank2 = ps[int(b2[2:])]
                        for k in range(2):
                            mm(bank2[:, 0:ncol], wB[0][:, k, j * 128:(j + 1) * 128], aT[a2][:, k, 0:ncol], k == 0, k == 1,
                               ["wB0", ("aT", a2, k)], [b2])
                        S.op("dve", lambda e, bank2=bank2, tf=tf, ncol=ncol: e.tensor_tensor(out=tmpf[tf][:, 0:ncol], in0=bank2[:, 0:ncol],
                                                                                         in1=tmpf[tf][:, 0:ncol], op=ALU.mult),
                             reads=[b2, "tmpf%d" % tf], writes=["tmpf%d" % tf])
                        S.op("dve", lambda e, j=j, tf=tf, c0=c0, ncol=ncol: e.tensor_tensor(out=hT[:, j, c0:c0 + ncol], in0=tmpf[tf][:, 0:ncol],
                                                                                         in1=hT[:, j, c0:c0 + ncol], op=ALU.add),
                             reads=["tmpf%d" % tf, ("hT", j, c0)], writes=[("hT", j, c0)])

            S.barrier()
            top[0] = M_1
            mlp(0, G_MLP0)
            ple(0, G_PLE0)

            if True:
                sb1 = sb
                cf = sb1("cf", [128, 2, 512], F32)
                co = sb1("co", [128, 2, 512], F32)
                cb = sb1("cb", [128, 2, 512], BF16)
                ctok = sb1("ctok", [128, 4, 256], BF16)
                krf = sb1("krf", [64, 2, 512], F32)
                kro = sb1("kro", [64, 512], F32)
                krb = sb1("krb", [64, 512], BF16)
                cs = sb1("cs", [64, 512], F32)
                sn = sb1("sn", [64, 512], F32)
                norm_h(G_KV)
                wkv = wA[0]
                load_w(wkv[:, :, 0:384], "wA0", wdkv_d.rearrange("(k p) f -> p k f", p=128))
                for gi, (c0, ncol, tiles) in enumerate(groups):
                    for mchunk in range(4):
                        msz = 128 if mchunk < 2 else 64
                        m0 = mchunk * 128 if mchunk < 2 else 256 + (mchunk - 2) * 64
                        b = PSK[rot("ps", 2)]
                        bank = ps[int(b[2:])]
                        for k in range(DC):
                            mm(bank[0:msz, 0:ncol], wkv[:, k, m0:m0 + msz], uT[:, k, c0:c0 + ncol], k == 0, k == DC - 1, ["wA0", ("uT", k, c0)], [b])
                        if mchunk < 2:
                            S.op("act", lambda e, bank=bank, mchunk=mchunk, ncol=ncol: e.activation(out=cf[:, mchunk, 0:ncol], in_=bank[:, 0:ncol], func=AF.Copy),
                                 reads=[b], writes=[("cf", mchunk)])
                        else:
                            S.op("act", lambda e, bank=bank, mchunk=mchunk, ncol=ncol: e.activation(out=krf[:, mchunk - 2, 0:ncol], in_=bank[0:64, 0:ncol], func=AF.Copy),
                                 reads=[b], writes=[("krf", mchunk - 2)])
                    norm_T(lambda ch, ncol=ncol: cf[:, ch, 0:ncol], lambda ch: [("cf", ch)], 2, KVR, G_KVN,
                           lambda ch, ncol=ncol: co[:, ch, 0:ncol], lambda ch: [("co", ch)], c0, ncol,
                           out2_fn=lambda ch, ncol=ncol: cb[:, ch, 0:ncol], out2_keys_fn=lambda ch: [("cb", ch)])
                    if own:
                        dma("sp", latT_o.rearrange("(k p) t -> p k t", p=128)[:, :, c0:c0 + ncol], co[:, :, 0:ncol], reads=[("co", 0), ("co", 1)], final=True)
                    dma("sp", cs[:, 0:ncol], cos_d[pj][:, c0:c0 + ncol], writes=["cs"])
                    dma("sp", sn[:, 0:ncol], sin_d[pj][:, c0:c0 + ncol], writes=["sn"])
                    S.op("dve", lambda e, ncol=ncol: e.tensor_tensor(out=krf[:, 0, 0:ncol], in0=krf[:, 0, 0:ncol], in1=cs[:, 0:ncol], op=ALU.mult),
                         reads=[("krf", 0), "cs"], writes=[("krf", 0)])
                    S.op("dve", lambda e, ncol=ncol: e.tensor_tensor(out=krf[:, 1, 0:ncol], in0=krf[:, 1, 0:ncol], in1=sn[:, 0:ncol], op=ALU.mult),
                         reads=[("krf", 1), "sn"], writes=[("krf", 1)])
                    S.op("dve", lambda e, ncol=ncol: e.tensor_tensor(out=kro[:, 0:ncol], in0=krf[:, 0, 0:ncol], in1=krf[:, 1, 0:ncol], op=ALU.add),
                         reads=[("krf", 0), ("krf", 1)], writes=["kro"])
                    S.op("dve", lambda e, ncol=ncol: e.tensor_copy(out=krb[:, 0:ncol], in_=kro[:, 0:ncol]), reads=["kro"], writes=["krb"])
                    if own:
                        dma("sp", krT_o[:, c0:c0 + ncol], kro[:, 0:ncol], reads=["kro"], final=True)
                    if not (own and gi == NG - 1):
                        dma("sp", cx_all.ap()[pj * 576:pj * 576 + 256, c0:c0 + ncol].rearrange("(k p) t -> p k t", p=128), cb[:, :, 0:ncol],
                            reads=[("cb", 0), ("cb", 1)], writes=["cx_all"])
                        dma("sp", cx_all.ap()[pj * 576 + 256:pj * 576 + 320, c0:c0 + ncol], krb[:, 0:ncol], reads=["krb"], writes=["cx_all"])
                        for ti, l in enumerate(tiles):
                            for ch in range(2):
                                S.op("pe", lambda e, ti=ti, ch=ch: e.transpose(out=psb[:, (ti * 2 + ch) * 128:(ti * 2 + ch + 1) * 128],
                                                                              in_=cb[:, ch, ti * 128:(ti + 1) * 128], identity=ident_bf[:]),
                                     reads=[("cb", ch), "ident_bf"], writes=["psb"])
                        nt_ = len(tiles)
                        S.op("act", lambda e, nt_=nt_: e.activation(out=ctok[:, 0:nt_, :], in_=psb[:, 0:nt_ * 256].rearrange("p (t c) -> p t c", c=256), func=AF.Copy),
                             reads=["psb"], writes=["ctok"])
                        dma("sp", c_tok_view(cx_all, pj)[c0:c0 + ncol, :].rearrange("(t p) c -> p t c", p=128), ctok[:, 0:nt_, :], reads=["ctok"], writes=["cx_all"])
                    else:
                        S.op("dve", lambda e: e.tensor_copy(out=sKT[:, 0:2, :], in_=cb[:, :, 0:128]), reads=[("cb", 0), ("cb", 1)], writes=["sKT"])
                        S.op("dve", lambda e: e.tensor_copy(out=sKT[0:64, 2, :], in_=krb[:, 0:128]), reads=["krb"], writes=["sKT"])
                        for ch in range(2):
                            S.op("pe", lambda e, ch=ch: e.transpose(out=psb[:, ch * 128:(ch + 1) * 128], in_=cb[:, ch, 0:128], identity=ident_bf[:]),
                                 reads=[("cb", ch), "ident_bf"], writes=["psb"])
                        S.op("act", lambda e: e.activation(out=sV[:, 0:256], in_=psb[:, 0:256], func=AF.Copy), reads=["psb"], writes=["sV"])

        groups = groups_all
        NG = len(groups)
        if True:
            sb2 = sb
            S.barrier()
            top[0] = M_1
            cqf = sb2("cqf", [128, 3, 512], F32)
            norm_h(G_MIX1)
            wq = wA[1]
            load_w(wq[:, :, 0:QR], "wA1", wdq_d.rearrange("(k p) f -> p k f", p=128))
            for gi, (c0, ncol, tiles) in enumerate(groups):
                for mc in range(3):
                    b = PSK[rot("ps", 2)]
                    bank = ps[int(b[2:])]
                    for k in range(DC):
                        mm(bank[:, 0:ncol], wq[:, k, mc * 128:(mc + 1) * 128], uT[:, k, c0:c0 + ncol], k == 0, k == DC - 1, ["wA1", ("uT", k, c0)], [b])
                    S.op("act", lambda e, bank=bank, mc=mc, ncol=ncol: e.activation(out=cqf[:, mc, 0:ncol], in_=bank[:, 0:ncol], func=AF.Copy),
                         reads=[b], writes=[("cqf", mc)])
                norm_T(lambda ch, ncol=ncol: cqf[:, ch, 0:ncol], lambda ch: [("cqf", ch)], 3, QR, G_Q,
                       lambda ch, c0=c0, ncol=ncol: cqT[:, ch, c0:c0 + ncol], lambda ch, c0=c0: [("cqT", ch, c0)], c0, ncol)
            S.barrier()
            top[0] = M_U
            AGW = 256
            w_uqn = sb2("w_uqn", [128, 3, NH * 128], BF16)
            w_uqr = sb2("w_uqr", [128, 3, NH * 128], BF16)
            w_ukT = sb2("w_ukT", [128, NH * 256], BF16)
            w_uv = sb2("w_uv", [128, 2, NH * 128], BF16)
            w_o = sb2("w_o", [128, NH, D], BF16)
            qT = sb2("qT", [128, 3, NH, AGW], BF16)
            qn = [sb2("qn%d" % i, [128, AGW], BF16) for i in range(2)]
            qrf = sb2("qrf", [64, 2, AGW], F32)
            cs2 = sb2("cs2", [64, AGW], F32)
            sn2 = sb2("sn2", [64, AGW], F32)
            KT = [sb2("KT%d" % i, [128, 3, 8 * 128], BF16) for i in range(2)]
            V = [sb2("V%d" % i, [128, 8, 258], BF16) for i in range(2)]
            PT = tmpb
            acc = sb2("acc", [128, NH, 257], F32)
            rden = sb2("rden", [128, NH], F32)
            Ob = sb2("Ob", [128, NH, 256], BF16)
            OT = sb2("OT", [128, 2, NH, AGW], BF16)
            oT = sb2("oT", [128, NH, AGW], BF16)
            pgV = [sb2("pgV%d" % i, [128, 258], BF16) for i in range(2)]
            pgK = [sb2("pgK%d" % i, [128, 64], BF16) for i in range(2)]
            pKT = [sb2("pKT%d" % i, [128, 3, 128], BF16) for i in range(2)]
            Obs = sb2("Obs", [64, 256], BF16)
            rdens = sb2("rdens", [64, 1], F32)
            for i in range(2):
                S.op("dve", lambda e, i=i: e.memset(pgV[i][:, 256:258], 1.0), writes=["pgV%d" % i])
            print("arena top (attention phase):", top[0], ARENA)
            dma("pool", w_uqn[:], wuqn_d.rearrange("(k p) f -> p k f", p=128), writes=["w_uqn"])
            dma("pool", w_uqr[:], wuqr_d.rearrange("(k p) f -> p k f", p=128), writes=["w_uqr"])
            dma("pool", w_ukT[:], wukT_d[:, :], writes=["w_ukT"])
            dma("pool", w_uv[:], wuv_d.rearrange("(k p) f -> p k f", p=128), writes=["w_uv"])
            dma("pool", w_o[:], wo_d.rearrange("(h p) d -> p h d", p=128), writes=["w_o"])
            for i in range(2):
                S.op("dve", lambda e, i=i: e.memset(V[i][:, :, 256:258], 1.0), writes=["V%d" % i])
            agroups = []
            l = 0
            while l < NT:
                n = min(2, NT - l)
                agroups.append((l * 128, n * 128, list(range(l, l + n))))
                l += n
            agroups.append((NT * 128, 128, [NT]))
            kv_cnt = [0]
            for gi, (c0, ncol, tiles) in enumerate(agroups):
                is_sg = (gi == len(agroups) - 1)
                if is_sg and not cfg["SAMPLE_ATT"]:
                    continue
                dma("sp", cs2[:, 0:ncol], cos_d[NPASS - 1][:, c0:c0 + ncol], writes=["cs2"])
                dma("sp", sn2[:, 0:ncol], sin_d[NPASS - 1][:, c0:c0 + ncol], writes=["sn2"])
                for h in range(NH):
                    b = PSK[6]
                    bank = ps[6]
                    for k in range(3):
                        mm(bank[:, 0:ncol], w_uqn[:, k, h * 128:(h + 1) * 128], cqT[:, k, c0:c0 + ncol], k == 0, k == 2, ["w_uqn", ("cqT", k, c0)], [b])
                    q2 = h % 2
                    S.op("act", lambda e, q2=q2, ncol=ncol: e.activation(out=qn[q2][:, 0:ncol], in_=ps[6][:, 0:ncol], func=AF.Copy),
                         reads=[b], writes=["qn%d" % q2])
                    for rc in range(2):
                        mm(bank[:, 0:ncol], w_ukT[:, h * 256 + rc * 128:h * 256 + (rc + 1) * 128], qn[q2][:, 0:ncol], True, True, ["w_ukT", "qn%d" % q2], [b])
                        S.op("act", lambda e, rc=rc, h=h, ncol=ncol: e.activation(out=qT[:, rc, h, 0:ncol], in_=ps[6][:, 0:ncol], func=AF.Copy),
                             reads=[b], writes=[("qT", h)])
                    for half in range(2):
                        for k in range(3):
                            mm(bank[0:64, 0:ncol], w_uqr[:, k, h * 128 + half * 64:h * 128 + (half + 1) * 64], cqT[:, k, c0:c0 + ncol], k == 0, k == 2,
                               ["w_uqr", ("cqT", k, c0)], [b])
                        tab = cs2 if half == 0 else sn2
                        S.op("dve", lambda e, half=half, tab=tab, ncol=ncol: e.tensor_tensor(out=qrf[:, half, 0:ncol], in0=ps[6][0:64, 0:ncol], in1=tab[:, 0:ncol], op=ALU.mult),
                             reads=[b, "cs2", "sn2"], writes=[("qrf", half)])
                    S.op("dve", lambda e, h=h, ncol=ncol: e.tensor_tensor(out=qT[0:64, 2, h, 0:ncol], in0=qrf[:, 0, 0:ncol], in1=qrf[:, 1, 0:ncol], op=ALU.add),
                         reads=[("qrf", 0), ("qrf", 1)], writes=[("qT", h)])
                if not is_sg:
                    for ti, l in enumerate(tiles):
                        m_ = l
                        first_chunk = True
                        for r in range(VR):
                          for s0 in range(0, m_ + 1, 8):
                            nsl = min(8, m_ + 1 - s0)
                            dg = m_ - s0
                            kb = kv_cnt[0] % 2
                            kv_cnt[0] += 1
                            dma("sp", KT[kb][:, 0:2, 0:nsl * 128],
                                cx_all.ap()[r * 576:r * 576 + 256, s0 * 128:(s0 + nsl) * 128].rearrange("(k p) t -> p k t", p=128),
                                reads=["cx_all"], writes=["KT%d" % kb])
                            dma("sp", KT[kb][0:64, 2, 0:nsl * 128],
                                cx_all.ap()[r * 576 + 256:r * 576 + 320, s0 * 128:(s0 + nsl) * 128],
                                reads=["cx_all"], writes=["KT%d" % kb])
                            dma("sp", V[kb][:, 0:nsl, 0:256],
                                c_tok_view(cx_all, r)[s0 * 128:(s0 + nsl) * 128, :].rearrange("(s p) c -> p s c", p=128),
                                reads=["cx_all"], writes=["V%d" % kb])
                            for hg in range(2):
                                for sl in range(nsl):
                                    sb_ = PSK[4 + rot("sq", 2)]
                                    sbank = ps[int(sb_[2:])]
                                    for k in range(3):
                                        kk = 128 if k < 2 else 64
                                        mm(sbank[:, :].rearrange("p (h t) -> p h t", h=4), KT[kb][0:kk, k, sl * 128:(sl + 1) * 128],
                                           qT[0:kk, k, hg * 4:hg * 4 + 4, ti * 128:(ti + 1) * 128], k == 0, k == 2,
                                           ["KT%d" % kb] + [("qT", hg * 4 + hh) for hh in range(4)], [sb_])
                                    p3 = rot("tb", 3)
                                    S.op("act", lambda e, sbank=sbank, p3=p3: e.activation(out=PT[p3][:], in_=sbank[:, :], func=AF.Exp, scale=SM_SCALE),
                                         reads=[sb_], writes=["PT%d" % p3])
                                    if dg == sl:
                                        mi = r * 2 + (m_ % 2)
                                        S.op("dve", lambda e, p3=p3, mi=mi: e.tensor_tensor(
                                            out=PT[p3][:, :].rearrange("p (h t) -> p h t", h=4), in0=PT[p3][:, :].rearrange("p (h t) -> p h t", h=4),
                                            in1=mask_bf[:, mi:mi + 1, :].broadcast_to([128, 4, 128]), op=ALU.mult),
                                             reads=["PT%d" % p3, "mask_bf"], writes=["PT%d" % p3])
                                    for hh in range(4):
                                        mm(ps[hh][:, 0:257], PT[p3][:, hh * 128:(hh + 1) * 128], V[kb][:, sl, 0:257], sl == 0, sl == nsl - 1,
                                           ["PT%d" % p3, "V%d" % kb], [PSK[hh]])
                                for hh in range(4):
                                    h = hg * 4 + hh
                                    if first_chunk:
                                        S.op("dve", lambda e, hh=hh, h=h: e.tensor_copy(out=acc[:, h, :], in_=ps[hh][:, 0:257]), reads=[PSK[hh]], writes=[("acc", h)])
                                    else:
                                        S.op("dve", lambda e, hh=hh, h=h: e.tensor_tensor(out=acc[:, h, :], in0=ps[hh][:, 0:257], in1=acc[:, h, :], op=ALU.add),
                                             reads=[PSK[hh], ("acc", h)], writes=[("acc", h)])
                            first_chunk = False
                        S.op("dve", lambda e: e.reciprocal(out=rden[:], in_=acc[:, :, 256]), reads=[("acc", h) for h in range(NH)], writes=["rden"])
                        for h in range(NH):
                            S.op("dve", lambda e, h=h: e.tensor_scalar(out=Ob[:, h, :], in0=acc[:, h, 0:256], scalar1=rden[:, h:h + 1], scalar2=None, op0=ALU.mult),
                                 reads=[("acc", h), "rden"], writes=[("Ob", h)])
                        for h4 in range(2):
                            for hh in range(4):
                                h = h4 * 4 + hh
                                for rc in range(2):
                                    S.op("pe", lambda e, h=h, hh=hh, rc=rc: e.transpose(out=psb[:, (hh * 2 + rc) * 128:(hh * 2 + rc + 1) * 128],
                                                                                       in_=Ob[:, h, rc * 128:(rc + 1) * 128], identity=ident_bf[:]),
                                         reads=[("Ob", h), "ident_bf"], writes=["psb"])
                            for rc in range(2):
                                S.op("act", lambda e, h4=h4, rc=rc, ti=ti: e.activation(
                                    out=OT[:, rc, h4 * 4:h4 * 4 + 4, ti * 128:(ti + 1) * 128],
                                    in_=psb[:, :].rearrange("p (h r t) -> p r h t", h=4, r=2)[:, rc], func=AF.Copy),
                                    reads=["psb"], writes=[("OT", ti)])
                else:
                    NPOOL = cfg["NPOOL"]
                    pcount = [0]
                    for s_ in range(DPC):
                        accb = PSK[s_ % 2]
                        accbank = ps[s_ % 2]
                        for j in range(NPG + 1):
                            own_pg = (j == NPG)
                            if not own_pg:
                                pb = pcount[0] % 2
                                p3g = pcount[0] % 3
                                pcount[0] += 1
                                idx = s_ * NPG + j

                                S.op("pool", lambda e, idx=idx, pb=pb: e.indirect_dma_start(
                                    out=pgV[pb][:, 0:256], out_offset=None, in_=clat_d.rearrange("n p c -> (n p) c"),
                                    in_offset=bass.IndirectOffsetOnAxis(ap=idx_all[:, idx:idx + 1], axis=0)),
                                    reads=["idx_all"], writes=["pgV%d" % pb], dma=True)
                                S.op("pool", lambda e, idx=idx, pb=pb: e.indirect_dma_start(
                                    out=pgK[pb][:, :], out_offset=None, in_=ckr_d.rearrange("n p c -> (n p) c"),
                                    in_offset=bass.IndirectOffsetOnAxis(ap=idx_all[:, idx:idx + 1], axis=0)),
                                    reads=["idx_all"], writes=["pgK%d" % pb], dma=True)
                                for ch in range(2):
                                    S.op("pe", lambda e, pb=pb, ch=ch: e.transpose(out=psb[:, ch * 128:(ch + 1) * 128], in_=pgV[pb][:, ch * 128:(ch + 1) * 128], identity=ident_bf[:]),
                                         reads=["pgV%d" % pb, "ident_bf"], writes=["psb"])
                                S.op("pe", lambda e, pb=pb: e.transpose(out=psb[0:64, 256:384], in_=pgK[pb][:, 0:64], identity=ident_bf[:]),
                                     reads=["pgK%d" % pb, "ident_bf"], writes=["psb"])
                                S.op("act", lambda e, pb=pb: e.activation(out=pKT[pb][:, 0:2, :], in_=psb[:, 0:256].rearrange("p (c t) -> p c t", c=2), func=AF.Copy),
                                     reads=["psb"], writes=["pKT%d" % pb])
                                S.op("act", lambda e, pb=pb: e.activation(out=pKT[pb][0:64, 2, :], in_=psb[0:64, 256:384], func=AF.Copy),
                                     reads=["psb"], writes=["pKT%d" % pb])
                                kt_ap, kt_key, v_ap, v_key = pKT[pb], "pKT%d" % pb, pgV[pb], "pgV%d" % pb
                            else:
                                kt_ap, kt_key, v_ap, v_key = sKT, "sKT", sV, "sV"
                            sb_ = PSK[4 + rot("sq", 2)]
                            sbank = ps[int(sb_[2:])]
                            for k in range(3):
                                kk = 128 if k < 2 else 64
                                mm(sbank[:, 0:64].rearrange("p (h t) -> p h t", h=NH), kt_ap[0:kk, k, :],
                                   qT[0:kk, k, :, s_ * 8:(s_ + 1) * 8], k == 0, k == 2, [kt_key] + [("qT", hh) for hh in range(NH)], [sb_])
                            p3 = rot("tb", 3)
                            S.op("act", lambda e, sbank=sbank, p3=p3: e.activation(out=PT[p3][:, 0:64], in_=sbank[:, 0:64], func=AF.Exp, scale=SM_SCALE),
                                 reads=[sb_], writes=["PT%d" % p3])
                            if own_pg:
                                S.op("dve", lambda e, p3=p3, s_=s_: e.tensor_tensor(
                                    out=PT[p3][:, 0:64].rearrange("p (h t) -> p h t", h=NH), in0=PT[p3][:, 0:64].rearrange("p (h t) -> p h t", h=NH),
                                    in1=smask_bf[:, s_:s_ + 1, :].broadcast_to([128, NH, 8]), op=ALU.mult),
                                    reads=["PT%d" % p3, "smask_bf"], writes=["PT%d" % p3])
                            mm(accbank[0:64, 0:257], PT[p3][:, 0:64], v_ap[:, 0:257], j == 0, own_pg, ["PT%d" % p3, v_key], [accb])
                        S.op("dve", lambda e, accbank=accbank: e.reciprocal(out=rdens[:], in_=accbank[0:64, 256:257]), reads=[accb], writes=["rdens"])
                        S.op("dve", lambda e, accbank=accbank: e.tensor_scalar(out=Obs[:], in0=accbank[0:64, 0:256], scalar1=rdens[:, 0:1], scalar2=None, op0=ALU.mult),
                             reads=[accb, "rdens"], writes=["Obs"])
                        for rc in range(2):
                            S.op("pe", lambda e, rc=rc: e.transpose(out=psb[:, rc * 64:(rc + 1) * 64], in_=Obs[:, rc * 128:(rc + 1) * 128], identity=ident_bf[0:64, 0:64]),
                                 reads=["Obs", "ident_bf"], writes=["psb"])
                        for rc in range(2):
                            S.op("act", lambda e, rc=rc, s_=s_: e.activation(out=OT[:, rc, :, s_ * 8:(s_ + 1) * 8],
                                                                            in_=psb[:, rc * 64:(rc + 1) * 64].rearrange("p (h t) -> p h t", h=NH), func=AF.Copy),
                                 reads=["psb"], writes=[("OT", 0)])
                for h in range(NH):
                    b = PSK[6]
                    for rc in range(2):
                        mm(ps[6][:, 0:ncol], w_uv[:, rc, h * 128:(h + 1) * 128], OT[:, rc, h, 0:ncol], rc == 0, rc == 1,
                           ["w_uv"] + [("OT", ti) for ti in range(len(tiles))], [b])
                    S.op("act", lambda e, h=h, ncol=ncol: e.activation(out=oT[:, h, 0:ncol], in_=ps[6][:, 0:ncol], func=AF.Copy), reads=[b], writes=[("oT", h)])
                for j in range(DC):
                    b = PSK[6]
                    for h in range(NH):
                        mm(ps[6][:, 0:ncol], w_o[:, h, j * 128:(j + 1) * 128], oT[:, h, 0:ncol], h == 0, h == NH - 1, ["w_o", ("oT", h)], [b])
                    S.op("dve", lambda e, j=j, c0=c0, ncol=ncol: e.tensor_tensor(out=hT[:, j, c0:c0 + ncol], in0=ps[6][:, 0:ncol], in1=hT[:, j, c0:c0 + ncol], op=ALU.add),
                         reads=[b, ("hT", j, c0)], writes=[("hT", j, c0)])

        S.barrier()
        top[0] = M_1
        mlp(1, G_MLP1)
        ple(1, G_PLE1)
        for (c0, ncol, tiles) in groups:
            yst = tmpf
            norm_T(lambda ch, c0=c0, ncol=ncol: hT[:, ch, c0:c0 + ncol], lambda ch, c0=c0: [("hT", ch, c0)], DC, D, G_FIN,
                   lambda ch, c0=c0, ncol=ncol: hT[:, ch, c0:c0 + ncol], lambda ch, c0=c0: [("hT", ch, c0)], c0, ncol)
            dma("sp", yT_o.rearrange("(c p) t -> p c t", p=128)[:, :, c0:c0 + ncol], hT[:, :, c0:c0 + ncol],
                reads=[("hT", ch, c0) for ch in range(DC)], final=True)
        S.emit()
    return nc


def vr_blocks(cfg, v):
    VR = cfg["NCORE"] // cfg["BATCH"]
    out = []
    for i in range(cfg["NBLK"] // (2 * VR)):
        out.append(2 * VR * i + v)
        out.append(2 * VR * i + 2 * VR - 1 - v)
    return out


def _pass_vranks(cfg, v):
    VR = cfg["NCORE"] // cfg["BATCH"]
    return [(v + 1 + j) % VR for j in range(VR)]


def _tables(cfg, k):
    NCORE, BATCH, NBLK = cfg["NCORE"], cfg["BATCH"], cfg["NBLK"]
    VR = NCORE // BATCH
    v = k % VR
    DPC = cfg["DEC"] // NCORE
    past = cfg["NPG"] * 128
    half = ROPE // 2
    inv_freq = np.power(np.float32(THETA), -np.arange(half, dtype=np.float32) / np.float32(half)).astype(np.float32)
    t = np.arange(128)
    amain = np.zeros((4, 128, 128), np.float32)
    afirst0 = np.zeros((4, 128, 128), np.float32)
    ahalo = np.zeros((4, 16, 128), np.float32)
    amain_s = np.zeros((4, 128, 128), np.float32)
    ahalo_s = np.zeros((2, 4, 120, 128), np.float32)
    for g, w in enumerate(POOL_W):
        diff = t[None, :] - t[:, None]
        inw = (diff >= 0) & (diff < w)
        amain[g] = inw / np.float32(w) - np.eye(128, dtype=np.float32)
        cntv = np.minimum(t + 1, w).astype(np.float32)
        afirst0[g] = inw / cntv[None, :] - np.eye(128, dtype=np.float32)
        i = np.arange(16)
        ahalo[g] = (i[:, None] >= t[None, :] + 17 - w) / np.float32(w)
        for s_ in range(16):
            for ii in range(8):
                for j in range(8):
                    amain_s[g, s_ * 8 + ii, s_ * 8 + j] = (1.0 / w if 0 <= j - ii < w else 0.0) - (1.0 if ii == j else 0.0)
        for a in range(2):
            for sl in range(8):
                for r in range(15):
                    for j in range(8):
                        if r > j + 15 - w:
                            ahalo_s[a, g, sl * 15 + r, (a * 8 + sl) * 8 + j] = 1.0 / w
    cosT, sinT, afirst = [], [], []
    for vv in _pass_vranks(cfg, v):
        blocks = vr_blocks(cfg, vv)
        pos = [np.arange(blk * 128, (blk + 1) * 128) for blk in blocks]
        pos.append(np.tile(past + np.arange(cfg["DSEQ"]), DPC))
        pos = np.concatenate(pos)
        ang = (pos.astype(np.float32)[:, None] * inv_freq[None, :]).astype(np.float32)
        cos = np.cos(ang).astype(np.float32)
        sin = np.sin(ang).astype(np.float32)
        cosT.append(np.concatenate([cos, cos], 1).T)
        sinT.append(np.concatenate([-sin, sin], 1).T)
        afirst.append(afirst0 if blocks[0] == 0 else amain)
    tri = (t[:, None] <= t[None, :]).astype(np.float32)
    bmask = np.zeros((VR * 2, 128, 128), np.float32)
    for rel, vv in enumerate(_pass_vranks(cfg, v)):
        bmask[rel * 2 + 0] = 1.0 if vv < v else (tri if vv == v else 0.0)
        bmask[rel * 2 + 1] = 1.0 if vv > v else (tri if vv == v else 0.0)
    smask = np.zeros((128, 16, 8), np.float32)
    for key in range(128):
        for i in range(8):
            if key % 8 <= i:
                smask[key, key // 8, i] = 1.0
    f = lambda a: np.ascontiguousarray(np.asarray(a, dtype=np.float32))
    return dict(pidx=np.arange(128, dtype=np.float32).reshape(128, 1), smask=smask.reshape(128, 128), cosT=f(np.stack(cosT)), sinT=f(np.stack(sinT)), amain=amain, afirst=f(np.stack(afirst)), ahalo=ahalo, amain_s=amain_s,
                ahalo_s=ahalo_s, bmask=bmask, identf=np.eye(128, dtype=np.float32))


def _prep(cfg, inp):
    NCORE, BATCH, NBLK = cfg["NCORE"], cfg["BATCH"], cfg["NBLK"]
    VR = NCORE // BATCH
    DPC = cfg["DEC"] // NCORE
    f = lambda a: np.ascontiguousarray(np.asarray(a, dtype=np.float32))
    xp, xs = f(inp["x_prompt"]), f(inp["x_sample"])
    pp, psm = f(inp["p_prompt"]), f(inp["p_sample"])
    stp = f(inp["state_pool"])

    def chunks(vv):
        return f(vv).reshape(-1, 128).T
    gvec = np.zeros((128, 80), np.float32)
    for col, vv in ((0, inp["norm_mix"][1]), (8, inp["norm_mlp"][0]), (16, inp["norm_mlp"][1]), (24, inp["norm_ple"][0]), (32, inp["norm_ple"][1]),
                    (40, inp["pool_scale"][0]), (48, inp["norm_kv"]), (56, inp["norm_final"]), (64, inp["q_norm"][0]), (67, inp["kv_norm"])):
        c = chunks(vv)
        gvec[:, col:col + c.shape[1]] = c
    wdkv = f(inp["w_dkv"])
    w_dkv4 = np.concatenate([wdkv[:, :KVR + ROPE], wdkv[:, KVR + 32:KVR + 64], wdkv[:, KVR:KVR + 32]], 1)
    wuq = f(inp["w_uq"][0])
    w_uq_nope = wuq[:, :, :NOPE].reshape(QR, NH * 128)
    rp = wuq[:, :, NOPE:]
    w_uq_rope = np.concatenate([rp, rp[:, :, 32:], rp[:, :, :32]], 2).reshape(QR, NH * 128)
    w_ukT = f(np.transpose(f(inp["w_uk"]), (2, 1, 0))).reshape(128, NH * 256)
    shared = dict(gvec=gvec, gmix0=f(inp["norm_mix"][0])[None, :], pool_w=f(inp["pool_w"][0]), w_up=f(inp["w_up"]), w_down=f(inp["w_down"]),
                  w_gate=f(inp["w_ple_gate"]), w_proj=f(inp["w_ple_proj"]), w_dkv4=f(w_dkv4), w_dq=f(inp["w_dq"][0]),
                  w_uq_nope=f(w_uq_nope), w_uq_rope=f(w_uq_rope), w_ukT=w_ukT, w_uv=f(inp["w_uv"]).reshape(KVR, NH * 128),
                  w_o=f(inp["w_o"][0]).reshape(NH * 128, D))
    if cfg["SAMPLE_ATT"]:
        shared["cache_latent"] = f(inp["cache_latent"])
        shared["cache_krope"] = f(inp["cache_krope"])
    maps = []
    for k in range(NCORE):
        b, v = k // VR, k % VR
        xs_k = xs[k * DPC:(k + 1) * DPC].reshape(128, D)
        xtok_l, xT_l, halo_l, pT0_l = [], [], [], []
        for vv in _pass_vranks(cfg, v):
            blocks = vr_blocks(cfg, vv)
            xt_ = np.concatenate([xp[b, blk * 128:(blk + 1) * 128] for blk in blocks] + [xs_k], 0)
            p0 = np.concatenate([pp[0, b, blk * 128:(blk + 1) * 128] for blk in blocks] + [psm[0, k * DPC:(k + 1) * DPC].reshape(128, PLE)], 0)
            hal = np.zeros((len(blocks), 16, D), np.float32)
            for i_, blk in enumerate(blocks):
                if blk > 0:
                    hal[i_] = xp[b, blk * 128 - 16:blk * 128]
            xtok_l.append(xt_)
            xT_l.append(xt_.T)
            halo_l.append(hal)
            pT0_l.append(p0.T)
        blocks = vr_blocks(cfg, v)
        p1 = np.concatenate([pp[1, b, blk * 128:(blk + 1) * 128] for blk in blocks] + [psm[1, k * DPC:(k + 1) * DPC].reshape(128, PLE)], 0)
        m = dict(shared)
        m.update(_tables(cfg, k))
        m["xtok"] = f(np.stack(xtok_l))
        m["xT"] = f(np.stack(xT_l))
        m["xhalo"] = f(np.stack(halo_l))
        m["pT0"] = f(np.stack(pT0_l))
        m["pT1"] = f(p1.T)
        sp = stp[0, k * DPC:(k + 1) * DPC]
        m["spin"] = f(sp)
        m["uhalo"] = f(sp.reshape(2, 120, D))
        if cfg["SAMPLE_ATT"]:
            m["ptab"] = np.ascontiguousarray(np.asarray(inp["page_table"])[k * DPC:(k + 1) * DPC].astype(np.int32).reshape(1, -1))
        maps.append(m)
    return maps


def _assemble(cfg, res):
    NCORE, BATCH, NBLK = cfg["NCORE"], cfg["BATCH"], cfg["NBLK"]
    VR = NCORE // BATCH
    DPC = cfg["DEC"] // NCORE
    SEQ = NBLK * 128
    y_p = np.zeros((BATCH, SEQ, D), np.float32)
    lat_p = np.zeros((BATCH, SEQ, KVR), np.float32)
    kr_p = np.zeros((BATCH, SEQ, ROPE), np.float32)
    y_s = np.zeros((cfg["DEC"], cfg["DSEQ"], D), np.float32)
    lat_s = np.zeros((cfg["DEC"], cfg["DSEQ"], KVR), np.float32)
    kr_s = np.zeros((cfg["DEC"], cfg["DSEQ"], ROPE), np.float32)
    pool_s = np.zeros((1, cfg["DEC"], 15, D), np.float32)
    pool_p = np.zeros((1, BATCH, 15, D), np.float32)
    for k in range(NCORE):
        r = res[k]
        b, v = k // VR, k % VR
        yT, latT, krT = np.asarray(r["yT"]), np.asarray(r["latT"]), np.asarray(r["krT"])
        l = 0
        for blk in vr_blocks(cfg, v):
            sl = slice(l * 128, (l + 1) * 128)
            y_p[b, blk * 128:(blk + 1) * 128] = yT[:, sl].T
            lat_p[b, blk * 128:(blk + 1) * 128] = latT[:, sl].T
            kr_p[b, blk * 128:(blk + 1) * 128] = krT[:, sl].T
            l += 1
        sl = slice(l * 128, (l + 1) * 128)
        y_s[k * DPC:(k + 1) * DPC] = yT[:, sl].T.reshape(DPC, cfg["DSEQ"], D)
        lat_s[k * DPC:(k + 1) * DPC] = latT[:, sl].T.reshape(DPC, cfg["DSEQ"], KVR)
        kr_s[k * DPC:(k + 1) * DPC] = krT[:, sl].T.reshape(DPC, cfg["DSEQ"], ROPE)
        pool_s[0, k * DPC:(k + 1) * DPC] = np.asarray(r["pools"])
        if v == 0:
            pool_p[0, b] = np.asarray(r["poolp"])
    return (y_p, y_s, pool_p, pool_s, lat_p, kr_p, lat_s, kr_s)


_NC_CACHE = {}


def kernel(**inputs):
    cfg = dict(FULL)
    key = "full"
    if key not in _NC_CACHE:
        _NC_CACHE[key] = build_program(cfg)
    nc = _NC_CACHE[key]
    maps = _prep(cfg, inputs)
    res = run_bass_kernel_spmd(nc, maps, core_ids=list(range(cfg["NCORE"])))
    return _assemble(cfg, res.results)
```

```python
import contextlib
import math
import numpy as np
import concourse.bass as bass
import concourse.mybir as mybir
from concourse.bass_utils import run_bass_kernel_spmd

F32 = mybir.dt.float32
BF16 = mybir.dt.bfloat16
I32 = mybir.dt.int32
AF = mybir.ActivationFunctionType
ALU = mybir.AluOpType

COMPUTE = ("pe", "act", "dve", "pool")
N_DMA_SLOTS = {"sp": 8, "pool": 2, "act": 2, "pe": 1, "dve": 1}


class Sched:
    def __init__(self, nc):
        self.nc = nc
        self.eng_names = ("pe", "act", "dve", "pool", "sp")
        self.prog = {e: [] for e in self.eng_names}
        self.n_comp = {e: 0 for e in self.eng_names}
        self.n_dma = {e: 0 for e in self.eng_names}
        self.slot_cnt = {}
        self.last_w = {}
        self.readers = {}
        self.known = {e: {} for e in self.eng_names}
        self.targets = {e: set() for e in self.eng_names}
        self.final_dma = []
        self.n_cc = 0

    def _add_dep(self, deps, tok, eng):
        if tok is None:
            return
        if tok[0] == "c":
            if tok[1] == eng and eng == "pe":
                return
            lane = ("c", tok[1])
            v = tok[2]
        else:
            lane = (tok[0], tok[1], tok[2])
            v = tok[3]
        if deps.get(lane, 0) < v:
            deps[lane] = v

    def op(self, eng, fn, reads=(), writes=(), dma=False, final=False, cc=False):
        deps = {}
        for k in reads:
            self._add_dep(deps, self.last_w.get(k), eng)
        for k in writes:
            self._add_dep(deps, self.last_w.get(k), eng)
            for t in self.readers.get(k, ()):
                self._add_dep(deps, t, eng)
        if cc:
            self.n_cc += 1
            tok = ("x", "cc", self.n_cc, 1)
        elif dma:
            n = self.n_dma[eng]
            self.n_dma[eng] = n + 1
            slot = n % N_DMA_SLOTS[eng]
            cnt = self.slot_cnt.get((eng, slot), 0) + 1
            self.slot_cnt[(eng, slot)] = cnt
            tok = ("d", eng, slot, cnt)
            if cnt > 1:
                lane = ("d", eng, slot)
                if deps.get(lane, 0) < cnt - 1:
                    deps[lane] = cnt - 1
        else:
            self.n_comp[eng] += 1
            tok = ("c", eng, self.n_comp[eng])
        waits = []
        kn = self.known[eng]
        for lane, v in deps.items():
            if kn.get(lane, 0) >= v:
                continue
            kn[lane] = v
            waits.append((lane, v))
            if lane[0] == "c":
                self.targets[lane[1]].add(v)
        self.prog[eng].append((waits, fn, tok))
        for k in reads:
            self.readers.setdefault(k, []).append(tok)
        for k in writes:
            self.last_w[k] = tok
            self.readers[k] = []
        if final:
            self.final_dma.append(tok)
        return tok

    def barrier(self):
        for eng in self.eng_names:
            deps = {}
            for F in COMPUTE:
                if self.n_comp[F] > 0 and not (F == eng and eng == "pe"):
                    deps[("c", F)] = self.n_comp[F]
            for (e, s_), c in self.slot_cnt.items():
                deps[("d", e, s_)] = c
            for i in range(1, self.n_cc + 1):
                deps[("x", "cc", i)] = 1
            waits = []
            kn = self.known[eng]
            for lane, v in deps.items():
                if kn.get(lane, 0) >= v:
                    continue
                kn[lane] = v
                waits.append((lane, v))
                if lane[0] == "c":
                    self.targets[lane[1]].add(v)
            self.prog[eng].append((waits, None, None))

    def emit(self):
        nc = self.nc
        with contextlib.ExitStack() as st:
            dsem = {}
            for i in range(1, self.n_cc + 1):
                dsem[("x", "cc", i)] = st.enter_context(nc.semaphore("cc_%d" % i))
            csem = {e: st.enter_context(nc.semaphore("c_" + e)) for e in COMPUTE}
            for (e, s) in self.slot_cnt:
                dsem[("d", e, s)] = st.enter_context(nc.semaphore("d_%s_%d" % (e, s)))
            cum = {}
            for e in COMPUTE:
                tg = sorted(self.targets[e])
                cum[e] = {idx: i + 1 for i, idx in enumerate(tg)}
            fin = {}
            for tok in self.final_dma:
                lane = ("d", tok[1], tok[2])
                fin[lane] = max(fin.get(lane, 0), tok[3])
            block = st.enter_context(nc.Block())
            engobj = {"pe": "tensor", "act": "scalar", "dve": "vector", "pool": "gpsimd", "sp": "sync"}

            def build(ename):
                def body(eng):
                    for waits, fn, tok in self.prog[ename]:
                        for lane, v in waits:
                            if lane[0] == "c":
                                eng.wait_ge(csem[lane[1]], cum[lane[1]][v])
                            elif lane[0] == "d":
                                eng.wait_ge(dsem[lane], 16 * v)
                            else:
                                eng.wait_ge(dsem[lane], v)
                        if fn is None:
                            continue
                        ins = fn(eng)
                        if tok[0] == "c":
                            if tok[2] in cum[tok[1]]:
                                ins.then_inc(csem[tok[1]], 1)
                        elif tok[0] == "d":
                            ins.then_inc(dsem[(tok[0], tok[1], tok[2])], 16)
                        else:
                            ins.then_inc(dsem[(tok[0], tok[1], tok[2])], 1)
                    if ename == "sp":
                        for lane, v in fin.items():
                            eng.wait_ge(dsem[lane], 16 * v)
                return body

            for ename in self.eng_names:
                if not self.prog[ename] and ename != "sp":
                    continue
                getattr(block, engobj[ename])(build(ename))


D = 1024
DC = 8
KVR = 256
QR = 384
NH = 8
NOPE = 128
ROPE = 64
PLE = 256
POOL_W = (2, 4, 8, 16)
EPS = 1e-6
SM_SCALE = 1.0 / math.sqrt(NOPE + ROPE)
THETA = 10000.0

FULL = dict(NCORE=8, BATCH=2, NBLK=64, DEC=128, DSEQ=8, NPG=64, NPOOL=10240, FF=4096, USE_CC=False, SAMPLE_ATT=True)


def core_blocks(cfg, k):
    nc_, nb = cfg["NCORE"], cfg["NBLK"]
    out = []
    for i in range(nb // (2 * nc_)):
        out.append(2 * nc_ * i + k)
        out.append(2 * nc_ * i + 2 * nc_ - 1 - k)
    return out


def owner_of(cfg, j):
    nc_ = cfg["NCORE"]
    r = j % (2 * nc_)
    k = r if r < nc_ else 2 * nc_ - 1 - r
    m = (j // (2 * nc_)) * 2 + (0 if r < nc_ else 1)
    return k, m


def build_program(cfg):
    NCORE, BATCH, NBLK, FF = cfg["NCORE"], cfg["BATCH"], cfg["NBLK"], cfg["FF"]
    VR = NCORE // BATCH
    NPASS = VR
    NTB = NBLK // VR
    NT = NTB
    T = (NT + 1) * 128
    TP = NT * 128
    NFB = FF // 512
    DPC = cfg["DEC"] // NCORE
    NPG = cfg["NPG"]
    assert DPC * cfg["DSEQ"] == 128
    groups = []
    l = 0
    while l < NT:
        n = min(4, NT - l)
        groups.append((l * 128, n * 128, list(range(l, l + n))))
        l += n
    groups.append((NT * 128, 128, [NT]))
    NG = len(groups)
    groups_all = groups

    nc = bass.Bass("TRN2", target_bir_lowering=False)

    def din(name, shape, dt=F32):
        return nc.dram_tensor(name, list(shape), dt, kind="ExternalInput").ap()

    def dout(name, shape, dt=F32):
        return nc.dram_tensor(name, list(shape), dt, kind="ExternalOutput").ap()

    xT_d = din("xT", [NPASS, D, T])
    xtok_d = din("xtok", [NPASS, T, D])
    xhalo_d = din("xhalo", [NPASS, NT, 16, D])
    uhalo_d = din("uhalo", [2, 120, D])
    spin_d = din("spin", [DPC, 15, D])
    pT0_d = din("pT0", [NPASS, PLE, T])
    pT1_d = din("pT1", [PLE, T])
    gvec_d = din("gvec", [128, 80])
    gmix0_d = din("gmix0", [1, D])
    cos_d = din("cosT", [NPASS, 64, T])
    sin_d = din("sinT", [NPASS, 64, T])
    amain_d = din("amain", [4, 128, 128])
    afirst_d = din("afirst", [NPASS, 4, 128, 128])
    ahalo_d = din("ahalo", [4, 16, 128])
    amain_s_d = din("amain_s", [4, 128, 128])
    ahalo_s_d = din("ahalo_s", [2, 4, 120, 128])
    bmask_d = din("bmask", [VR * 2, 128, 128])
    identf_d = din("identf", [128, 128])
    poolw_d = din("pool_w", [4, 256, 256])
    wup_d = din("w_up", [2, D, FF])
    wdn_d = din("w_down", [2, FF, D])
    wg_d = din("w_gate", [2, D, D])
    wp_d = din("w_proj", [2, PLE, D])
    wdkv_d = din("w_dkv4", [D, 384])
    wdq_d = din("w_dq", [D, QR])
    wuqn_d = din("w_uq_nope", [QR, NH * 128])
    wuqr_d = din("w_uq_rope", [QR, NH * 128])
    wukT_d = din("w_ukT", [128, NH * 256])
    wuv_d = din("w_uv", [KVR, NH * 128])
    wo_d = din("w_o", [NH * 128, D])
    if cfg["SAMPLE_ATT"]:
        clat_d = din("cache_latent", [cfg["NPOOL"], 128, KVR])
        ckr_d = din("cache_krope", [cfg["NPOOL"], 128, ROPE])
        ptab_d = din("ptab", [1, DPC * NPG], I32)
        smask_d = din("smask", [128, 128])
        pidx_d = din("pidx", [128, 1])

    yT_o = dout("yT", [D, T])
    latT_o = dout("latT", [KVR, T])
    krT_o = dout("krT", [ROPE, T])
    poolp_o = dout("poolp", [15, D])
    pools_o = dout("pools", [DPC, 15, D])

    cx_all = nc.dram_tensor("cx_all", [VR * 576, TP], BF16)

    def c_tok_view(t, r):
        return t.ap()[r * 576 + 320:(r + 1) * 576, :].rearrange("a (t c) -> (a t) c", c=256)

    S = Sched(nc)
    st = contextlib.ExitStack()
    with st:
        ARENA = 206 * 1024
        arena = st.enter_context(nc.sbuf_tensor("arena", [128, ARENA], mybir.dt.uint8))
        top = [0]

        def sb(name, shape, dt):
            nb = 2 if dt == BF16 else 4
            per = int(np.prod(shape[1:])) * nb
            off = (top[0] + 63) // 64 * 64
            assert off + per <= ARENA, (name, off, per, ARENA)
            top[0] = off + per
            v = arena[0:shape[0], off:off + per].bitcast(dt)
            if len(shape) == 3:
                v = v.rearrange("p (a b) -> p a b", a=shape[1])
            elif len(shape) == 4:
                v = v.rearrange("p (a b c) -> p a b c", a=shape[1], b=shape[2])
            return v

        ps = [st.enter_context(nc.psum_tensor("ps%d" % i, [128, 512], F32)) for i in range(7)]
        psb = st.enter_context(nc.psum_tensor("psb", [128, 1024], BF16))
        PSK = ["ps%d" % i for i in range(7)]

        hT = sb("hT", [128, DC, T], F32)
        gvec = sb("gvec", [128, 80], F32)
        ones_bf = sb("ones_bf", [128, 128], BF16)
        ident_bf = sb("ident_bf", [128, 128], BF16)
        mask_bf = sb("mask_bf", [128, VR * 2, 128], BF16)
        sqb = [sb("sqb%d" % i, [128, 512], BF16) for i in range(2)]
        rstd = [sb("rstd%d" % i, [128, 512], F32) for i in range(2)]
        tmpf = [sb("tmpf%d" % i, [128, 512], F32) for i in range(2)]
        tmpb = [sb("tmpb%d" % i, [128, 512], BF16) for i in range(3)]
        sKT = sb("sKT", [128, 3, 128], BF16)
        sV = sb("sV", [128, 258], BF16)
        smask_bf = sb("smask_bf", [128, 16, 8], BF16)
        idx_all = sb("idx_all", [128, DPC * NPG], I32)
        pidx = sb("pidx", [128, 1], F32)
        cqT = sb("cqT", [128, 3, T], BF16)
        M_U = top[0]
        uT = sb("uT", [128, DC, T], BF16)
        wA = [sb("wA%d" % i, [128, 8, 512], BF16) for i in range(2)]
        wB = [sb("wB%d" % i, [128, 4, 1024], BF16) for i in range(2)]
        aT = [sb("aT%d" % i, [128, 4, 512], BF16) for i in range(2)]
        M_1 = top[0]

        G_MIX1, G_MLP0, G_MLP1, G_PLE0, G_PLE1, G_PSC, G_KV, G_FIN, G_Q, G_KVN = 0, 8, 16, 24, 32, 40, 48, 56, 64, 67

        cnt = {"ps": 0, "sq": 0, "rs": 0, "tf": 0, "tb": 0}

        def rot(kind, n):
            v = cnt[kind]
            cnt[kind] = (v + 1) % n
            return v

        def dma(eng, out, in_, reads=(), writes=(), final=False, **kw):
            return S.op(eng, lambda e: e.dma_start(out=out, in_=in_, **kw), reads=reads, writes=writes, dma=True, final=final)

        def mm(out, lhsT, rhs, start, stop, reads, writes):
            S.op("pe", lambda e: e.matmul(out, lhsT=lhsT, rhs=rhs, start=start, stop=stop), reads=reads, writes=writes)

        dma("sp", gvec[:], gvec_d[:, :], writes=["gvec"])
        dma("pool", ident_bf[:], identf_d[:, :], writes=["ident_bf"])
        dma("pool", mask_bf[:], bmask_d.rearrange("m p t -> p m t"), writes=["mask_bf"])
        S.op("dve", lambda e: e.memset(ones_bf[:], 1.0), writes=["ones_bf"])
        if cfg["SAMPLE_ATT"]:
            dma("pool", smask_bf[:], smask_d.rearrange("p (s i) -> p s i", i=8), writes=["smask_bf"])
            dma("sp", idx_all[:], ptab_d[0:1, :].broadcast_to([128, DPC * NPG]), writes=["idx_all"])
            dma("sp", pidx[:], pidx_d[:, :], writes=["pidx"])
            S.op("dve", lambda e: e.tensor_scalar(out=idx_all[:], in0=idx_all[:], scalar1=128.0, scalar2=pidx[:, 0:1], op0=ALU.mult, op1=ALU.add),
                 reads=["idx_all", "pidx"], writes=["idx_all"])
            S.op("dve", lambda e: e.memset(sV[:, 256:258], 1.0), writes=["sV"])
        for pj in range(NPASS):
            own = (pj == NPASS - 1)
            groups = groups_all if own else groups_all[:-1]
            NG = len(groups)
            S.barrier()
            top[0] = M_1
            for (c0, ncol, tiles) in groups:
                gi = c0
                dma("sp", hT[:, :, c0:c0 + ncol], xT_d[pj].rearrange("(c p) t -> p c t", p=128)[:, :, c0:c0 + ncol],
                    writes=[("hT", ch, gi) for ch in range(DC)])

            def norm_T(src_fn, src_keys_fn, nch, dim, gcol, out_fn, out_keys_fn, g_id, ncol, eps=EPS, out2_fn=None, out2_keys_fn=None):
                b = PSK[rot("ps", 2)]
                bank = ps[int(b[2:])]
                for ch in range(nch):
                    q = rot("sq", 2)
                    S.op("act", lambda e, ch=ch, q=q: e.activation(out=sqb[q][:, 0:ncol], in_=src_fn(ch), func=AF.Square),
                         reads=src_keys_fn(ch), writes=["sqb%d" % q])
                    mm(bank[:, 0:ncol], ones_bf[:], sqb[q][:, 0:ncol], ch == 0, ch == nch - 1, ["ones_bf", "sqb%d" % q], [b])
                r = rot("rs", 2)
                S.op("act", lambda e: e.activation(out=rstd[r][:, 0:ncol], in_=bank[:, 0:ncol], func=AF.Ln, scale=1.0 / dim, bias=eps),
                     reads=[b], writes=["rstd%d" % r])
                S.op("act", lambda e: e.activation(out=rstd[r][:, 0:ncol], in_=rstd[r][:, 0:ncol], func=AF.Exp, scale=-0.5),
                     reads=["rstd%d" % r], writes=["rstd%d" % r])
                for ch in range(nch):
                    S.op("dve", lambda e, ch=ch: e.scalar_tensor_tensor(out=out_fn(ch), in0=src_fn(ch), scalar=gvec[:, gcol + ch:gcol + ch + 1],
                                                                        in1=rstd[r][:, 0:ncol], op0=ALU.mult, op1=ALU.mult),
                         reads=list(src_keys_fn(ch)) + ["gvec", "rstd%d" % r], writes=out_keys_fn(ch))
                    if out2_fn is not None:
                        S.op("dve", lambda e, ch=ch: e.scalar_tensor_tensor(out=out2_fn(ch), in0=src_fn(ch), scalar=gvec[:, gcol + ch:gcol + ch + 1],
                                                                            in1=rstd[r][:, 0:ncol], op0=ALU.mult, op1=ALU.mult),
                             reads=list(src_keys_fn(ch)) + ["gvec", "rstd%d" % r], writes=out2_keys_fn(ch))

            def norm_h(gcol):
                for (c0, ncol, tiles) in groups:
                    norm_T(lambda ch, c0=c0, ncol=ncol: hT[:, ch, c0:c0 + ncol], lambda ch, c0=c0: [("hT", ch, c0)], DC, D, gcol,
                           lambda ch, c0=c0, ncol=ncol: uT[:, ch, c0:c0 + ncol], lambda ch, c0=c0: [("uT", ch, c0)], c0, ncol)

            def load_w(buf, bufkey, src_ap):
                dma("pool", buf, src_ap, writes=[bufkey])

            if True:
                top[0] = M_U
                sb0 = sb
                xt = [sb0("xt%d" % i, [128, D], F32) for i in range(2)]
                xh = [sb0("xh%d" % i, [16, D], F32) for i in range(2)]
                gbc = sb0("gbc", [128, D], F32)
                ub = [sb0("ub%d" % i, [128, D], BF16) for i in range(2)]
                uhb = [sb0("uhb%d" % i, [16, D], BF16) for i in range(2)]
                uf = sb0("uf", [128, D], F32)
                junk = sb0("junk", [128, D], F32)
                ss = [sb0("ss%d" % i, [128, 2], F32) for i in range(2)]
                am = sb0("am", [128, 4, 128], BF16)
                af = sb0("af", [128, 4, 128], BF16)
                ah = sb0("ah", [16, 4, 128], BF16)
                ams = sb0("ams", [128, 4, 128], BF16)
                ahs = sb0("ahs", [120, 2, 4, 128], BF16)
                uhs = sb0("uhs", [120, 2, D], BF16)
                dT = sb0("dT", [128, DC, 512], BF16)
                pw = sb0("pw", [128, 4, 2, 256], BF16)

                dma("sp", gbc[:], gmix0_d[0:1, :].broadcast_to([128, D]), writes=["gbc"])
                dma("pool", am[:], amain_d.rearrange("g p t -> p g t"), writes=["am"])
                dma("pool", af[:], afirst_d[pj].rearrange("g p t -> p g t"), writes=["af"])
                dma("pool", ah[:], ahalo_d.rearrange("g p t -> p g t"), writes=["ah"])
                dma("pool", ams[:], amain_s_d.rearrange("g p t -> p g t"), writes=["ams"])
                dma("pool", ahs[:], ahalo_s_d.rearrange("a g p t -> p a g t"), writes=["ahs"])
                dma("pool", uhs[:], uhalo_d.rearrange("a p d -> p a d"), writes=["uhs"])
                dma("pool", pw[:], poolw_d.rearrange("g (k p) d -> p g k d", p=128), writes=["pw"])
                if own:
                    dma("sp", pools_o[:, 0:7, :], spin_d[:, 8:15, :], final=True)

                for (c0, ncol, tiles) in groups:
                    for ti, l in enumerate(tiles):
                        i2 = l % 2
                        is_s = (l == NT)
                        dma("sp", xt[i2][:], xtok_d[pj][l * 128:(l + 1) * 128, :], writes=["xt%d" % i2])
                        S.op("act", lambda e, i2=i2: e.activation(out=junk[:], in_=xt[i2][:], func=AF.Square, accum_out=ss[i2][:, 0:1]),
                             reads=["xt%d" % i2], writes=["junk", "ss%d" % i2])
                        S.op("act", lambda e, i2=i2: e.activation(out=ss[i2][:, 0:1], in_=ss[i2][:, 0:1], func=AF.Ln, scale=1.0 / D, bias=EPS),
                             reads=["ss%d" % i2], writes=["ss%d" % i2])
                        S.op("act", lambda e, i2=i2: e.activation(out=ss[i2][:, 0:1], in_=ss[i2][:, 0:1], func=AF.Exp, scale=-0.5),
                             reads=["ss%d" % i2], writes=["ss%d" % i2])
                        S.op("dve", lambda e, i2=i2: e.scalar_tensor_tensor(out=ub[i2][:], in0=xt[i2][:], scalar=ss[i2][:, 0:1], in1=gbc[:],
                                                                            op0=ALU.mult, op1=ALU.mult),
                             reads=["xt%d" % i2, "ss%d" % i2, "gbc"], writes=["ub%d" % i2])
                        last_of_batch = own and (not is_s) and (l == NT - 1)
                        if is_s or last_of_batch:
                            S.op("dve", lambda e, i2=i2: e.scalar_tensor_tensor(out=uf[:], in0=xt[i2][:], scalar=ss[i2][:, 0:1], in1=gbc[:],
                                                                                op0=ALU.mult, op1=ALU.mult),
                                 reads=["xt%d" % i2, "ss%d" % i2, "gbc"], writes=["uf"])
                            if is_s:
                                dma("sp", pools_o[:, 7:15, :].rearrange("s i d -> (s i) d") if False else pools_o[:, 7:15, :],
                                    uf[:].rearrange("(s i) d -> s i d", i=8) if False else uf[:], reads=["uf"], final=True) if False else None
                                for s_ in range(DPC):
                                    dma("sp", pools_o[s_, 7:15, :], uf[s_ * 8:(s_ + 1) * 8, :], reads=["uf"], final=True)
                            else:
                                dma("sp", poolp_o[:, :], uf[113:128, :], reads=["uf"], final=True)
                        if not is_s:
                            dma("sp", xh[i2][:], xhalo_d[pj][l], writes=["xh%d" % i2])
                            S.op("act", lambda e, i2=i2: e.activation(out=junk[0:16, :], in_=xh[i2][:], func=AF.Square, accum_out=ss[i2][0:16, 1:2]),
                                 reads=["xh%d" % i2], writes=["junk", "ssh%d" % i2])
                            S.op("act", lambda e, i2=i2: e.activation(out=ss[i2][0:16, 1:2], in_=ss[i2][0:16, 1:2], func=AF.Ln, scale=1.0 / D, bias=EPS),
                                 reads=["ssh%d" % i2], writes=["ssh%d" % i2])
                            S.op("act", lambda e, i2=i2: e.activation(out=ss[i2][0:16, 1:2], in_=ss[i2][0:16, 1:2], func=AF.Exp, scale=-0.5),
                                 reads=["ssh%d" % i2], writes=["ssh%d" % i2])
                            S.op("dve", lambda e, i2=i2: e.scalar_tensor_tensor(out=uhb[i2][:], in0=xh[i2][:], scalar=ss[i2][0:16, 1:2], in1=gbc[0:16, :],
                                                                                op0=ALU.mult, op1=ALU.mult),
                                 reads=["xh%d" % i2, "ssh%d" % i2, "gbc"], writes=["uhb%d" % i2])
                        first = (not is_s) and (l == 0)
                        for half in range(2):
                            b = PSK[rot("ps", 2)]
                            bank = ps[int(b[2:])]
                            for cc in range(4):
                                ch = half * 4 + cc
                                g = ch // 2
                                o = bank[:, cc * 128:(cc + 1) * 128]
                                if is_s:
                                    mm(o, ub[i2][:, ch * 128:(ch + 1) * 128], ams[:, g, :], True, False, ["ub%d" % i2, "ams"], [b])
                                    mm(o, uhs[:, 0, ch * 128:(ch + 1) * 128], ahs[:, 0, g, :], False, False, ["uhs", "ahs"], [b])
                                    mm(o, uhs[:, 1, ch * 128:(ch + 1) * 128], ahs[:, 1, g, :], False, True, ["uhs", "ahs"], [b])
                                else:
                                    amat = af if first else am
                                    mm(o, ub[i2][:, ch * 128:(ch + 1) * 128], amat[:, g, :], True, False, ["ub%d" % i2, "af", "am"], [b])
                                    mm(o, uhb[i2][:, ch * 128:(ch + 1) * 128], ah[:, g, :], False, True, ["uhb%d" % i2, "ah"], [b])
                            S.op("act", lambda e, half=half, ti=ti, bank=bank: e.activation(
                                out=dT[:, half * 4:half * 4 + 4, ti * 128:(ti + 1) * 128], in_=bank[:, :].rearrange("p (c t) -> p c t", c=4), func=AF.Copy),
                                reads=[b], writes=[("dT", half, ti)])
                    for j in range(DC):
                        g = j // 2
                        b = PSK[2 + rot("ps", 2)] if False else PSK[rot("ps", 2)]
                        bank = ps[int(b[2:])]
                        for kc in range(2):
                            mm(bank[:, 0:ncol], pw[:, g, kc, (j % 2) * 128:(j % 2) * 128 + 128], dT[:, 2 * g + kc, 0:ncol], kc == 0, kc == 1,
                               ["pw"] + [("dT", (2 * g + kc) // 4, ti) for ti in range(len(tiles))], [b])
                        S.op("dve", lambda e, j=j, bank=bank, c0=c0, ncol=ncol: e.scalar_tensor_tensor(
                            out=hT[:, j, c0:c0 + ncol], in0=bank[:, 0:ncol], scalar=gvec[:, G_PSC + j:G_PSC + j + 1], in1=hT[:, j, c0:c0 + ncol],
                            op0=ALU.mult, op1=ALU.add), reads=[b, "gvec", ("hT", j, c0)], writes=[("hT", j, c0)])

            def mlp(layer, gcol):
                norm_h(gcol)
                for fb in range(NFB):
                    i2 = fb % 2
                    load_w(wA[i2][:], "wA%d" % i2, wup_d[layer].rearrange("(k p) f -> p k f", p=128)[:, :, fb * 512:(fb + 1) * 512])
                    load_w(wB[i2][:], "wB%d" % i2, wdn_d[layer][fb * 512:(fb + 1) * 512, :].rearrange("(k p) d -> p k d", p=128))
                    for gi, (c0, ncol, tiles) in enumerate(groups):
                        a2 = gi % 2
                        for fc in range(4):
                            b = PSK[rot("ps", 2)]
                            bank = ps[int(b[2:])]
                            for k in range(DC):
                                mm(bank[:, 0:ncol], wA[i2][:, k, fc * 128:(fc + 1) * 128], uT[:, k, c0:c0 + ncol], k == 0, k == DC - 1,
                                   ["wA%d" % i2, ("uT", k, c0)], [b])
                            tb = rot("tb", 3)
                            S.op("act", lambda e, bank=bank, tb=tb, ncol=ncol: e.activation(out=tmpb[tb][:, 0:ncol], in_=bank[:, 0:ncol], func=AF.Relu),
                                 reads=[b], writes=["tmpb%d" % tb])
                            S.op("dve", lambda e, tb=tb, a2=a2, fc=fc, ncol=ncol: e.tensor_tensor(out=aT[a2][:, fc, 0:ncol], in0=tmpb[tb][:, 0:ncol],
                                                                                             in1=tmpb[tb][:, 0:ncol], op=ALU.mult),
                                 reads=["tmpb%d" % tb], writes=[("aT", a2, fc)])
                        for j in range(DC):
                            b = PSK[2 + rot("rs", 2)]
                            bank = ps[int(b[2:])]
                            for fc in range(4):
                                mm(bank[:, 0:ncol], wB[i2][:, fc, j * 128:(j + 1) * 128], aT[a2][:, fc, 0:ncol], fc == 0, fc == 3,
                                   ["wB%d" % i2, ("aT", a2, fc)], [b])
                            S.op("dve", lambda e, j=j, bank=bank, c0=c0, ncol=ncol: e.tensor_tensor(
                                out=hT[:, j, c0:c0 + ncol], in0=bank[:, 0:ncol], in1=hT[:, j, c0:c0 + ncol], op=ALU.add),
                                reads=[b, ("hT", j, c0)], writes=[("hT", j, c0)])

            def ple(layer, gcol):
                norm_h(gcol)
                for hlf in range(2):
                    load_w(wA[hlf][:], "wA%d" % hlf, wg_d[layer].rearrange("(k p) f -> p k f", p=128)[:, :, hlf * 512:(hlf + 1) * 512])
                load_w(wB[0][:, 0:2, :], "wB0", wp_d[layer].rearrange("(k p) d -> p k d", p=128))
                for gi, (c0, ncol, tiles) in enumerate(groups):
                    a2 = gi % 2
                    dma("pool", aT[a2][:, 0:2, 0:ncol], (pT0_d[pj] if layer == 0 else pT1_d).rearrange("(k p) t -> p k t", p=128)[:, :, c0:c0 + ncol],
                        writes=[("aT", a2, 0), ("aT", a2, 1)])
                    for j in range(DC):
                        b = PSK[rot("ps", 2)]
                        bank = ps[int(b[2:])]
                        for k in range(DC):
                            mm(bank[:, 0:ncol], wA[j // 4][:, k, (j % 4) * 128:(j % 4) * 128 + 128], uT[:, k, c0:c0 + ncol], k == 0, k == DC - 1,
                               ["wA%d" % (j // 4), ("uT", k, c0)], [b])
                        tf = rot("tf", 2)
                        S.op("act", lambda e, bank=bank, tf=tf, ncol=ncol: e.activation(out=tmpf[tf][:, 0:ncol], in_=bank[:, 0:ncol], func=AF.Sigmoid),
                             reads=[b], writes=["tmpf%d" % tf])
                        b2 = PSK[2 + rot("rs", 2)]
                        bank2 = ps[int(b2[2:])]
                        for k in range(2):
                            mm(bank2[:, 0:ncol], wB[0][:, k, j * 128:(j + 1) * 128], aT[a2][:, k, 0:ncol], k == 0, k == 1,
                               ["wB0", ("aT", a2, k)], [b2])
                        S.op("dve", lambda e, bank2=bank2, tf=tf, ncol=ncol: e.tensor_tensor(out=tmpf[tf][:, 0:ncol], in0=bank2[:, 0:ncol],
                                                                                         in1=tmpf[tf][:, 0:ncol], op=ALU.mult),
                             reads=[b2, "tmpf%d" % tf], writes=["tmpf%d" % tf])
                        S.op("dve", lambda e, j=j, tf=tf, c0=c0, ncol=ncol: e.tensor_tensor(out=hT[:, j, c0:c0 + ncol], in0=tmpf[tf][:, 0:ncol],
                                                                                         in1=hT[:, j, c0:c0 + ncol], op=ALU.add),
                             reads=["tmpf%d" % tf, ("hT", j, c0)], writes=[("hT", j, c0)])

            S.barrier()
            top[0] = M_1
            mlp(0, G_MLP0)
            ple(0, G_PLE0)

            if True:
                sb1 = sb
                cf = sb1("cf", [128, 2, 512], F32)
                co = sb1("co", [128, 2, 512], F32)
                cb = sb1("cb", [128, 2, 512], BF16)
                ctok = sb1("ctok", [128, 4, 256], BF16)
                krf = sb1("krf", [64, 2, 512], F32)
                kro = sb1("kro", [64, 512], F32)
                krb = sb1("krb", [64, 512], BF16)
                cs = sb1("cs", [64, 512], F32)
                sn = sb1("sn", [64, 512], F32)
                norm_h(G_KV)
                wkv = wA[0]
                load_w(wkv[:, :, 0:384], "wA0", wdkv_d.rearrange("(k p) f -> p k f", p=128))
                for gi, (c0, ncol, tiles) in enumerate(groups):
                    for mchunk in range(4):
                        msz = 128 if mchunk < 2 else 64
                        m0 = mchunk * 128 if mchunk < 2 else 256 + (mchunk - 2) * 64
                        b = PSK[rot("ps", 2)]
                        bank = ps[int(b[2:])]
                        for k in range(DC):
                            mm(bank[0:msz, 0:ncol], wkv[:, k, m0:m0 + msz], uT[:, k, c0:c0 + ncol], k == 0, k == DC - 1, ["wA0", ("uT", k, c0)], [b])
                        if mchunk < 2:
                            S.op("act", lambda e, bank=bank, mchunk=mchunk, ncol=ncol: e.activation(out=cf[:, mchunk, 0:ncol], in_=bank[:, 0:ncol], func=AF.Copy),
                                 reads=[b], writes=[("cf", mchunk)])
                        else:
                            S.op("act", lambda e, bank=bank, mchunk=mchunk, ncol=ncol: e.activation(out=krf[:, mchunk - 2, 0:ncol], in_=bank[0:64, 0:ncol], func=AF.Copy),
                                 reads=[b], writes=[("krf", mchunk - 2)])
                    norm_T(lambda ch, ncol=ncol: cf[:, ch, 0:ncol], lambda ch: [("cf", ch)], 2, KVR, G_KVN,
                           lambda ch, ncol=ncol: co[:, ch, 0:ncol], lambda ch: [("co", ch)], c0, ncol,
                           out2_fn=lambda ch, ncol=ncol: cb[:, ch, 0:ncol], out2_keys_fn=lambda ch: [("cb", ch)])
                    if own:
                        dma("sp", latT_o.rearrange("(k p) t -> p k t", p=128)[:, :, c0:c0 + ncol], co[:, :, 0:ncol], reads=[("co", 0), ("co", 1)], final=True)
                    dma("sp", cs[:, 0:ncol], cos_d[pj][:, c0:c0 + ncol], writes=["cs"])
                    dma("sp", sn[:, 0:ncol], sin_d[pj][:, c0:c0 + ncol], writes=["sn"])
                    S.op("dve", lambda e, ncol=ncol: e.tensor_tensor(out=krf[:, 0, 0:ncol], in0=krf[:, 0, 0:ncol], in1=cs[:, 0:ncol], op=ALU.mult),
                         reads=[("krf", 0), "cs"], writes=[("krf", 0)])
                    S.op("dve", lambda e, ncol=ncol: e.tensor_tensor(out=krf[:, 1, 0:ncol], in0=krf[:, 1, 0:ncol], in1=sn[:, 0:ncol], op=ALU.mult),
                         reads=[("krf", 1), "sn"], writes=[("krf", 1)])
                    S.op("dve", lambda e, ncol=ncol: e.tensor_tensor(out=kro[:, 0:ncol], in0=krf[:, 0, 0:ncol], in1=krf[:, 1, 0:ncol], op=ALU.add),
                         reads=[("krf", 0), ("krf", 1)], writes=["kro"])
                    S.op("dve", lambda e, ncol=ncol: e.tensor_copy(out=krb[:, 0:ncol], in_=kro[:, 0:ncol]), reads=["kro"], writes=["krb"])
                    if own:
                        dma("sp", krT_o[:, c0:c0 + ncol], kro[:, 0:ncol], reads=["kro"], final=True)
                    if not (own and gi == NG - 1):
                        dma("sp", cx_all.ap()[pj * 576:pj * 576 + 256, c0:c0 + ncol].rearrange("(k p) t -> p k t", p=128), cb[:, :, 0:ncol],
                            reads=[("cb", 0), ("cb", 1)], writes=["cx_all"])
                        dma("sp", cx_all.ap()[pj * 576 + 256:pj * 576 + 320, c0:c0 + ncol], krb[:, 0:ncol], reads=["krb"], writes=["cx_all"])
                        for ti, l in enumerate(tiles):
                            for ch in range(2):
                                S.op("pe", lambda e, ti=ti, ch=ch: e.transpose(out=psb[:, (ti * 2 + ch) * 128:(ti * 2 + ch + 1) * 128],
                                                                              in_=cb[:, ch, ti * 128:(ti + 1) * 128], identity=ident_bf[:]),
                                     reads=[("cb", ch), "ident_bf"], writes=["psb"])
                        nt_ = len(tiles)
                        S.op("act", lambda e, nt_=nt_: e.activation(out=ctok[:, 0:nt_, :], in_=psb[:, 0:nt_ * 256].rearrange("p (t c) -> p t c", c=256), func=AF.Copy),
                             reads=["psb"], writes=["ctok"])
                        dma("sp", c_tok_view(cx_all, pj)[c0:c0 + ncol, :].rearrange("(t p) c -> p t c", p=128), ctok[:, 0:nt_, :], reads=["ctok"], writes=["cx_all"])
                    else:
                        S.op("dve", lambda e: e.tensor_copy(out=sKT[:, 0:2, :], in_=cb[:, :, 0:128]), reads=[("cb", 0), ("cb", 1)], writes=["sKT"])
                        S.op("dve", lambda e: e.tensor_copy(out=sKT[0:64, 2, :], in_=krb[:, 0:128]), reads=["krb"], writes=["sKT"])
                        for ch in range(2):
                            S.op("pe", lambda e, ch=ch: e.transpose(out=psb[:, ch * 128:(ch + 1) * 128], in_=cb[:, ch, 0:128], identity=ident_bf[:]),
                                 reads=[("cb", ch), "ident_bf"], writes=["psb"])
                        S.op("act", lambda e: e.activation(out=sV[:, 0:256], in_=psb[:, 0:256], func=AF.Copy), reads=["psb"], writes=["sV"])

        groups = groups_all
        NG = len(groups)
        if True:
            sb2 = sb
            S.barrier()
            top[0] = M_1
            cqf = sb2("cqf", [128, 3, 512], F32)
            norm_h(G_MIX1)
            wq = wA[1]
            load_w(wq[:, :, 0:QR], "wA1", wdq_d.rearrange("(k p) f -> p k f", p=128))
            for gi, (c0, ncol, tiles) in enumerate(groups):
                for mc in range(3):
                    b = PSK[rot("ps", 2)]
                    bank = ps[int(b[2:])]
                    for k in range(DC):
                        mm(bank[:, 0:ncol], wq[:, k, mc * 128:(mc + 1) * 128], uT[:, k, c0:c0 + ncol], k == 0, k == DC - 1, ["wA1", ("uT", k, c0)], [b])
                    S.op("act", lambda e, bank=bank, mc=mc, ncol=ncol: e.activation(out=cqf[:, mc, 0:ncol], in_=bank[:, 0:ncol], func=AF.Copy),
                         reads=[b], writes=[("cqf", mc)])
                norm_T(lambda ch, ncol=ncol: cqf[:, ch, 0:ncol], lambda ch: [("cqf", ch)], 3, QR, G_Q,
                       lambda ch, c0=c0, ncol=ncol: cqT[:, ch, c0:c0 + ncol], lambda ch, c0=c0: [("cqT", ch, c0)], c0, ncol)
            S.barrier()
            top[0] = M_U
            AGW = 128
            w_uqn = sb2("w_uqn", [128, 3, NH * 128], BF16)
            w_uqr = sb2("w_uqr", [128, 3, NH * 128], BF16)
            w_ukT = sb2("w_ukT", [128, NH * 256], BF16)
            w_uv = sb2("w_uv", [128, 2, NH * 128], BF16)
            w_o = sb2("w_o", [128, NH, D], BF16)
            qT = sb2("qT", [128, 3, NH, AGW], BF16)
            qn = [sb2("qn%d" % i, [128, AGW], BF16) for i in range(2)]
            qrf = sb2("qrf", [64, 2, AGW], F32)
            cs2 = sb2("cs2", [64, AGW], F32)
            sn2 = sb2("sn2", [64, AGW], F32)
            KT = [sb2("KT%d" % i, [128, 3, 8 * 128], BF16) for i in range(2)]
            V = [sb2("V%d" % i, [128, 8, 258], BF16) for i in range(2)]
            PT = tmpb
            acc = sb2("acc", [128, NH, 257], F32)
            rden = sb2("rden", [128, NH], F32)
            Ob = sb2("Ob", [128, NH, 256], BF16)
            OT = sb2("OT", [128, 2, NH, AGW], BF16)
            oT = sb2("oT", [128, NH, AGW], BF16)
            NPB_ = 8 if cfg["SAMPLE_ATT"] else 2
            pgV = [sb2("pgV%d" % i, [128, 258], BF16) for i in range(NPB_)]
            pgK = [sb2("pgK%d" % i, [128, 64], BF16) for i in range(NPB_)]
            qTs = sb2("qTs", [128, 3, NH, 128], BF16)
            OTs = sb2("OTs", [128, 2, NH, 128], BF16)
            acc_s = sb2("acc_s", [64, 257], F32)
            pKT = [sb2("pKT%d" % i, [128, 3, 128], BF16) for i in range(2)]
            Obs = sb2("Obs", [64, 256], BF16)
            rdens = sb2("rdens", [64, 1], F32)
            for i in range(NPB_):
                S.op("dve", lambda e, i=i: e.memset(pgV[i][:, 256:258], 1.0), writes=["pgV%d" % i])
            print("arena top (attention phase):", top[0], ARENA)
            dma("pool", w_uqn[:], wuqn_d.rearrange("(k p) f -> p k f", p=128), writes=["w_uqn"])
            dma("pool", w_uqr[:], wuqr_d.rearrange("(k p) f -> p k f", p=128), writes=["w_uqr"])
            dma("pool", w_ukT[:], wukT_d[:, :], writes=["w_ukT"])
            dma("pool", w_uv[:], wuv_d.rearrange("(k p) f -> p k f", p=128), writes=["w_uv"])
            dma("pool", w_o[:], wo_d.rearrange("(h p) d -> p h d", p=128), writes=["w_o"])
            for i in range(2):
                S.op("dve", lambda e, i=i: e.memset(V[i][:, :, 256:258], 1.0), writes=["V%d" % i])
            agroups = []
            l = 0
            while l < NT:
                n = min(1, NT - l)
                agroups.append((l * 128, n * 128, list(range(l, l + n))))
                l += n
            agroups.append((NT * 128, 128, [NT]))
            kv_cnt = [0]
            NPOOL = cfg["NPOOL"]
            NPB = NPB_
            pcount = [0]

            def qproj(c0, ncol, qT, qk):
                dma("sp", cs2[:, 0:ncol], cos_d[NPASS - 1][:, c0:c0 + ncol], writes=["cs2"])
                dma("sp", sn2[:, 0:ncol], sin_d[NPASS - 1][:, c0:c0 + ncol], writes=["sn2"])
                for h in range(NH):
                    b = PSK[6]
                    bank = ps[6]
                    for k in range(3):
                        mm(bank[:, 0:ncol], w_uqn[:, k, h * 128:(h + 1) * 128], cqT[:, k, c0:c0 + ncol], k == 0, k == 2, ["w_uqn", ("cqT", k, c0)], [b])
                    q2 = h % 2
                    S.op("act", lambda e, q2=q2, ncol=ncol: e.activation(out=qn[q2][:, 0:ncol], in_=ps[6][:, 0:ncol], func=AF.Copy),
                         reads=[b], writes=["qn%d" % q2])
                    for rc in range(2):
                        mm(bank[:, 0:ncol], w_ukT[:, h * 256 + rc * 128:h * 256 + (rc + 1) * 128], qn[q2][:, 0:ncol], True, True, ["w_ukT", "qn%d" % q2], [b])
                        S.op("act", lambda e, rc=rc, h=h, ncol=ncol: e.activation(out=qT[:, rc, h, 0:ncol], in_=ps[6][:, 0:ncol], func=AF.Copy),
                             reads=[b], writes=[(qk, h)])
                    for half in range(2):
                        for k in range(3):
                            mm(bank[0:64, 0:ncol], w_uqr[:, k, h * 128 + half * 64:h * 128 + (half + 1) * 64], cqT[:, k, c0:c0 + ncol], k == 0, k == 2,
                               ["w_uqr", ("cqT", k, c0)], [b])
                        tab = cs2 if half == 0 else sn2
                        S.op("dve", lambda e, half=half, tab=tab, ncol=ncol: e.tensor_tensor(out=qrf[:, half, 0:ncol], in0=ps[6][0:64, 0:ncol], in1=tab[:, 0:ncol], op=ALU.mult),
                             reads=[b, "cs2", "sn2"], writes=[("qrf", half)])
                    S.op("dve", lambda e, h=h, ncol=ncol: e.tensor_tensor(out=qT[0:64, 2, h, 0:ncol], in0=qrf[:, 0, 0:ncol], in1=qrf[:, 1, 0:ncol], op=ALU.add),
                         reads=[("qrf", 0), ("qrf", 1)], writes=[(qk, h)])

            def sample_unit(s_, j0, j1):
                accb = PSK[0]
                accbank = ps[0]
                for j in range(j0, j1):
                    own_pg = (j == NPG)
                    if not own_pg:
                        pb = pcount[0] % NPB
                        pcount[0] += 1
                        idx = s_ * NPG + j

                        S.op("pool", lambda e, idx=idx, pb=pb: e.indirect_dma_start(
                            out=pgV[pb][:, 0:256], out_offset=None, in_=clat_d.rearrange("n p c -> (n p) c"),
                            in_offset=bass.IndirectOffsetOnAxis(ap=idx_all[:, idx:idx + 1], axis=0)),
                            reads=["idx_all"], writes=["pgV%d" % pb], dma=True)
                        S.op("pool", lambda e, idx=idx, pb=pb: e.indirect_dma_start(
                            out=pgK[pb][:, :], out_offset=None, in_=ckr_d.rearrange("n p c -> (n p) c"),
                            in_offset=bass.IndirectOffsetOnAxis(ap=idx_all[:, idx:idx + 1], axis=0)),
                            reads=["idx_all"], writes=["pgK%d" % pb], dma=True)
                        for ch in range(2):
                            S.op("pe", lambda e, pb=pb, ch=ch: e.transpose(out=psb[:, ch * 128:(ch + 1) * 128], in_=pgV[pb][:, ch * 128:(ch + 1) * 128], identity=ident_bf[:]),
                                 reads=["pgV%d" % pb, "ident_bf"], writes=["psb"])
                        S.op("pe", lambda e, pb=pb: e.transpose(out=psb[0:64, 256:384], in_=pgK[pb][:, 0:64], identity=ident_bf[:]),
                             reads=["pgK%d" % pb, "ident_bf"], writes=["psb"])
                        S.op("act", lambda e, pb=pb: e.activation(out=pKT[pb % 2][:, 0:2, :], in_=psb[:, 0:256].rearrange("p (c t) -> p c t", c=2), func=AF.Copy),
                             reads=["psb"], writes=["pKT%d" % (pb % 2)])
                        S.op("act", lambda e, pb=pb: e.activation(out=pKT[pb % 2][0:64, 2, :], in_=psb[0:64, 256:384], func=AF.Copy),
                             reads=["psb"], writes=["pKT%d" % (pb % 2)])
                        kt_ap, kt_key, v_ap, v_key = pKT[pb % 2], "pKT%d" % (pb % 2), pgV[pb], "pgV%d" % pb
                    else:
                        kt_ap, kt_key, v_ap, v_key = sKT, "sKT", sV, "sV"
                    sb_ = PSK[4 + rot("sq", 2)]
                    sbank = ps[int(sb_[2:])]
                    for k in range(3):
                        kk = 128 if k < 2 else 64
                        mm(sbank[:, 0:64].rearrange("p (h t) -> p h t", h=NH), kt_ap[0:kk, k, :],
                           qTs[0:kk, k, :, s_ * 8:(s_ + 1) * 8], k == 0, k == 2, [kt_key] + [("qTs", hh) for hh in range(NH)], [sb_])
                    p3 = rot("tb", 3)
                    S.op("act", lambda e, sbank=sbank, p3=p3: e.activation(out=PT[p3][:, 0:64], in_=sbank[:, 0:64], func=AF.Exp, scale=SM_SCALE),
                         reads=[sb_], writes=["PT%d" % p3])
                    if own_pg:
                        S.op("dve", lambda e, p3=p3, s_=s_: e.tensor_tensor(
                            out=PT[p3][:, 0:64].rearrange("p (h t) -> p h t", h=NH), in0=PT[p3][:, 0:64].rearrange("p (h t) -> p h t", h=NH),
                            in1=smask_bf[:, s_:s_ + 1, :].broadcast_to([128, NH, 8]), op=ALU.mult),
                            reads=["PT%d" % p3, "smask_bf"], writes=["PT%d" % p3])
                    mm(accbank[0:64, 0:257], PT[p3][:, 0:64], v_ap[:, 0:257], j == j0, j == j1 - 1, ["PT%d" % p3, v_key], [accb])
                if j0 == 0:
                    S.op("dve", lambda e, accbank=accbank: e.tensor_copy(out=acc_s[:], in_=accbank[0:64, 0:257]), reads=[accb], writes=["acc_s"])
                else:
                    S.op("dve", lambda e, accbank=accbank: e.tensor_tensor(out=acc_s[:], in0=accbank[0:64, 0:257], in1=acc_s[:], op=ALU.add),
                         reads=[accb, "acc_s"], writes=["acc_s"])
                if j1 != NPG + 1:
                    return
                S.op("dve", lambda e: e.reciprocal(out=rdens[:], in_=acc_s[:, 256:257]), reads=["acc_s"], writes=["rdens"])
                S.op("dve", lambda e: e.tensor_scalar(out=Obs[:], in0=acc_s[:, 0:256], scalar1=rdens[:, 0:1], scalar2=None, op0=ALU.mult),
                     reads=["acc_s", "rdens"], writes=["Obs"])
                for rc in range(2):
                    S.op("pe", lambda e, rc=rc: e.transpose(out=psb[:, rc * 64:(rc + 1) * 64], in_=Obs[:, rc * 128:(rc + 1) * 128], identity=ident_bf[0:64, 0:64]),
                         reads=["Obs", "ident_bf"], writes=["psb"])
                for rc in range(2):
                    S.op("act", lambda e, rc=rc, s_=s_: e.activation(out=OTs[:, rc, :, s_ * 8:(s_ + 1) * 8],
                                                                    in_=psb[:, rc * 64:(rc + 1) * 64].rearrange("p (h t) -> p h t", h=NH), func=AF.Copy),
                         reads=["psb"], writes=[("OTs", 0)])

            def outproj(c0, ncol, OT, otkeys):
                for h in range(NH):
                    b = PSK[6]
                    for rc in range(2):
                        mm(ps[6][:, 0:ncol], w_uv[:, rc, h * 128:(h + 1) * 128], OT[:, rc, h, 0:ncol], rc == 0, rc == 1,
                           ["w_uv"] + otkeys, [b])
                    S.op("act", lambda e, h=h, ncol=ncol: e.activation(out=oT[:, h, 0:ncol], in_=ps[6][:, 0:ncol], func=AF.Copy), reads=[b], writes=[("oT", h)])
                for j in range(DC):
                    b = PSK[6]
                    for h in range(NH):
                        mm(ps[6][:, 0:ncol], w_o[:, h, j * 128:(j + 1) * 128], oT[:, h, 0:ncol], h == 0, h == NH - 1, ["w_o", ("oT", h)], [b])
                    S.op("dve", lambda e, j=j, c0=c0, ncol=ncol: e.tensor_tensor(out=hT[:, j, c0:c0 + ncol], in0=ps[6][:, 0:ncol], in1=hT[:, j, c0:c0 + ncol], op=ALU.add),
                         reads=[b, ("hT", j, c0)], writes=[("hT", j, c0)])


            sc0 = NT * 128
            if cfg["SAMPLE_ATT"]:
                qproj(sc0, 128, qTs, "qTs")
            units = []
            UP = 8
            for s__ in range(DPC):
                for j0 in range(0, NPG, UP):
                    j1 = min(NPG, j0 + UP)
                    units.append((s__, j0, j1 + 1 if j1 == NPG else j1))
            n_slots = sum(VR * ((l_ + 8) // 8) for l_ in range(NT))
            per_slot = -(-len(units) // max(1, n_slots))
            upos = [0]

            def pop_units(n):
                if not cfg["SAMPLE_ATT"]:
                    return
                for _ in range(n):
                    if upos[0] < len(units):
                        sample_unit(*units[upos[0]])
                        upos[0] += 1
            for gi, (c0, ncol, tiles) in enumerate(agroups[:-1]):
                qproj(c0, ncol, qT, "qT")
                for ti, l in enumerate(tiles):
                    m_ = l
                    first_chunk = True
                    for r in range(VR):
                      for s0 in range(0, m_ + 1, 8):
                        nsl = min(8, m_ + 1 - s0)
                        dg = m_ - s0
                        kb = kv_cnt[0] % 2
                        kv_cnt[0] += 1
                        dma("sp", KT[kb][:, 0:2, 0:nsl * 128],
                            cx_all.ap()[r * 576:r * 576 + 256, s0 * 128:(s0 + nsl) * 128].rearrange("(k p) t -> p k t", p=128),
                            reads=["cx_all"], writes=["KT%d" % kb])
                        dma("sp", KT[kb][0:64, 2, 0:nsl * 128],
                            cx_all.ap()[r * 576 + 256:r * 576 + 320, s0 * 128:(s0 + nsl) * 128],
                            reads=["cx_all"], writes=["KT%d" % kb])
                        dma("sp", V[kb][:, 0:nsl, 0:256],
                            c_tok_view(cx_all, r)[s0 * 128:(s0 + nsl) * 128, :].rearrange("(s p) c -> p s c", p=128),
                            reads=["cx_all"], writes=["V%d" % kb])
                        for hg in range(2):
                            for sl in range(nsl):
                                sb_ = PSK[4 + rot("sq", 2)]
                                sbank = ps[int(sb_[2:])]
                                for k in range(3):
                                    kk = 128 if k < 2 else 64
                                    mm(sbank[:, :].rearrange("p (h t) -> p h t", h=4), KT[kb][0:kk, k, sl * 128:(sl + 1) * 128],
                                       qT[0:kk, k, hg * 4:hg * 4 + 4, ti * 128:(ti + 1) * 128], k == 0, k == 2,
                                       ["KT%d" % kb] + [("qT", hg * 4 + hh) for hh in range(4)], [sb_])
                                p3 = rot("tb", 3)
                                S.op("act", lambda e, sbank=sbank, p3=p3: e.activation(out=PT[p3][:], in_=sbank[:, :], func=AF.Exp, scale=SM_SCALE),
                                     reads=[sb_], writes=["PT%d" % p3])
                                if dg == sl:
                                    mi = r * 2 + (m_ % 2)
                                    S.op("dve", lambda e, p3=p3, mi=mi: e.tensor_tensor(
                                        out=PT[p3][:, :].rearrange("p (h t) -> p h t", h=4), in0=PT[p3][:, :].rearrange("p (h t) -> p h t", h=4),
                                        in1=mask_bf[:, mi:mi + 1, :].broadcast_to([128, 4, 128]), op=ALU.mult),
                                         reads=["PT%d" % p3, "mask_bf"], writes=["PT%d" % p3])
                                for hh in range(4):
                                    mm(ps[hh][:, 0:257], PT[p3][:, hh * 128:(hh + 1) * 128], V[kb][:, sl, 0:257], sl == 0, sl == nsl - 1,
                                       ["PT%d" % p3, "V%d" % kb], [PSK[hh]])
                            for hh in range(4):
                                h = hg * 4 + hh
                                if first_chunk:
                                    S.op("dve", lambda e, hh=hh, h=h: e.tensor_copy(out=acc[:, h, :], in_=ps[hh][:, 0:257]), reads=[PSK[hh]], writes=[("acc", h)])
                                else:
                                    S.op("dve", lambda e, hh=hh, h=h: e.tensor_tensor(out=acc[:, h, :], in0=ps[hh][:, 0:257], in1=acc[:, h, :], op=ALU.add),
                                         reads=[PSK[hh], ("acc", h)], writes=[("acc", h)])
                        first_chunk = False
                        pop_units(per_slot)
                    S.op("dve", lambda e: e.reciprocal(out=rden[:], in_=acc[:, :, 256]), reads=[("acc", h) for h in range(NH)], writes=["rden"])
                    for h in range(NH):
                        S.op("dve", lambda e, h=h: e.tensor_scalar(out=Ob[:, h, :], in0=acc[:, h, 0:256], scalar1=rden[:, h:h + 1], scalar2=None, op0=ALU.mult),
                             reads=[("acc", h), "rden"], writes=[("Ob", h)])
                    for h4 in range(2):
                        for hh in range(4):
                            h = h4 * 4 + hh
                            for rc in range(2):
                                S.op("pe", lambda e, h=h, hh=hh, rc=rc: e.transpose(out=psb[:, (hh * 2 + rc) * 128:(hh * 2 + rc + 1) * 128],
                                                                                   in_=Ob[:, h, rc * 128:(rc + 1) * 128], identity=ident_bf[:]),
                                     reads=[("Ob", h), "ident_bf"], writes=["psb"])
                        for rc in range(2):
                            S.op("act", lambda e, h4=h4, rc=rc, ti=ti: e.activation(
                                out=OT[:, rc, h4 * 4:h4 * 4 + 4, ti * 128:(ti + 1) * 128],
                                in_=psb[:, :].rearrange("p (h r t) -> p r h t", h=4, r=2)[:, rc], func=AF.Copy),
                                reads=["psb"], writes=[("OT", ti)])
                outproj(c0, ncol, OT, [("OT", ti) for ti in range(len(tiles))])
            if cfg["SAMPLE_ATT"]:
                pop_units(len(units))
                outproj(sc0, 128, OTs, [("OTs", 0)])

        S.barrier()
        top[0] = M_1
        mlp(1, G_MLP1)
        ple(1, G_PLE1)
        for (c0, ncol, tiles) in groups:
            yst = tmpf
            norm_T(lambda ch, c0=c0, ncol=ncol: hT[:, ch, c0:c0 + ncol], lambda ch, c0=c0: [("hT", ch, c0)], DC, D, G_FIN,
                   lambda ch, c0=c0, ncol=ncol: hT[:, ch, c0:c0 + ncol], lambda ch, c0=c0: [("hT", ch, c0)], c0, ncol)
            dma("sp", yT_o.rearrange("(c p) t -> p c t", p=128)[:, :, c0:c0 + ncol], hT[:, :, c0:c0 + ncol],
                reads=[("hT", ch, c0) for ch in range(DC)], final=True)
        S.emit()
    return nc


def vr_blocks(cfg, v):
    VR = cfg["NCORE"] // cfg["BATCH"]
    out = []
    for i in range(cfg["NBLK"] // (2 * VR)):
        out.append(2 * VR * i + v)
        out.append(2 * VR * i + 2 * VR - 1 - v)
    return out


def _pass_vranks(cfg, v):
    VR = cfg["NCORE"] // cfg["BATCH"]
    return [(v + 1 + j) % VR for j in range(VR)]


def _tables(cfg, k):
    NCORE, BATCH, NBLK = cfg["NCORE"], cfg["BATCH"], cfg["NBLK"]
    VR = NCORE // BATCH
    v = k % VR
    DPC = cfg["DEC"] // NCORE
    past = cfg["NPG"] * 128
    half = ROPE // 2
    inv_freq = np.power(np.float32(THETA), -np.arange(half, dtype=np.float32) / np.float32(half)).astype(np.float32)
    t = np.arange(128)
    amain = np.zeros((4, 128, 128), np.float32)
    afirst0 = np.zeros((4, 128, 128), np.float32)
    ahalo = np.zeros((4, 16, 128), np.float32)
    amain_s = np.zeros((4, 128, 128), np.float32)
    ahalo_s = np.zeros((2, 4, 120, 128), np.float32)
    for g, w in enumerate(POOL_W):
        diff = t[None, :] - t[:, None]
        inw = (diff >= 0) & (diff < w)
        amain[g] = inw / np.float32(w) - np.eye(128, dtype=np.float32)
        cntv = np.minimum(t + 1, w).astype(np.float32)
        afirst0[g] = inw / cntv[None, :] - np.eye(128, dtype=np.float32)
        i = np.arange(16)
        ahalo[g] = (i[:, None] >= t[None, :] + 17 - w) / np.float32(w)
        for s_ in range(16):
            for ii in range(8):
                for j in range(8):
                    amain_s[g, s_ * 8 + ii, s_ * 8 + j] = (1.0 / w if 0 <= j - ii < w else 0.0) - (1.0 if ii == j else 0.0)
        for a in range(2):
            for sl in range(8):
                for r in range(15):
                    for j in range(8):
                        if r > j + 15 - w:
                            ahalo_s[a, g, sl * 15 + r, (a * 8 + sl) * 8 + j] = 1.0 / w
    cosT, sinT, afirst = [], [], []
    for vv in _pass_vranks(cfg, v):
        blocks = vr_blocks(cfg, vv)
        pos = [np.arange(blk * 128, (blk + 1) * 128) for blk in blocks]
        pos.append(np.tile(past + np.arange(cfg["DSEQ"]), DPC))
        pos = np.concatenate(pos)
        ang = (pos.astype(np.float32)[:, None] * inv_freq[None, :]).astype(np.float32)
        cos = np.cos(ang).astype(np.float32)
        sin = np.sin(ang).astype(np.float32)
        cosT.append(np.concatenate([cos, cos], 1).T)
        sinT.append(np.concatenate([-sin, sin], 1).T)
        afirst.append(afirst0 if blocks[0] == 0 else amain)
    tri = (t[:, None] <= t[None, :]).astype(np.float32)
    bmask = np.zeros((VR * 2, 128, 128), np.float32)
    for rel, vv in enumerate(_pass_vranks(cfg, v)):
        bmask[rel * 2 + 0] = 1.0 if vv < v else (tri if vv == v else 0.0)
        bmask[rel * 2 + 1] = 1.0 if vv > v else (tri if vv == v else 0.0)
    smask = np.zeros((128, 16, 8), np.float32)
    for key in range(128):
        for i in range(8):
            if key % 8 <= i:
                smask[key, key // 8, i] = 1.0
    f = lambda a: np.ascontiguousarray(np.asarray(a, dtype=np.float32))
    return dict(pidx=np.arange(128, dtype=np.float32).reshape(128, 1), smask=smask.reshape(128, 128), cosT=f(np.stack(cosT)), sinT=f(np.stack(sinT)), amain=amain, afirst=f(np.stack(afirst)), ahalo=ahalo, amain_s=amain_s,
                ahalo_s=ahalo_s, bmask=bmask, identf=np.eye(128, dtype=np.float32))


def _prep(cfg, inp):
    NCORE, BATCH, NBLK = cfg["NCORE"], cfg["BATCH"], cfg["NBLK"]
    VR = NCORE // BATCH
    DPC = cfg["DEC"] // NCORE
    f = lambda a: np.ascontiguousarray(np.asarray(a, dtype=np.float32))
    xp, xs = f(inp["x_prompt"]), f(inp["x_sample"])
    pp, psm = f(inp["p_prompt"]), f(inp["p_sample"])
    stp = f(inp["state_pool"])

    def chunks(vv):
        return f(vv).reshape(-1, 128).T
    gvec = np.zeros((128, 80), np.float32)
    for col, vv in ((0, inp["norm_mix"][1]), (8, inp["norm_mlp"][0]), (16, inp["norm_mlp"][1]), (24, inp["norm_ple"][0]), (32, inp["norm_ple"][1]),
                    (40, inp["pool_scale"][0]), (48, inp["norm_kv"]), (56, inp["norm_final"]), (64, inp["q_norm"][0]), (67, inp["kv_norm"])):
        c = chunks(vv)
        gvec[:, col:col + c.shape[1]] = c
    wdkv = f(inp["w_dkv"])
    w_dkv4 = np.concatenate([wdkv[:, :KVR + ROPE], wdkv[:, KVR + 32:KVR + 64], wdkv[:, KVR:KVR + 32]], 1)
    wuq = f(inp["w_uq"][0])
    w_uq_nope = wuq[:, :, :NOPE].reshape(QR, NH * 128)
    rp = wuq[:, :, NOPE:]
    w_uq_rope = np.concatenate([rp, rp[:, :, 32:], rp[:, :, :32]], 2).reshape(QR, NH * 128)
    w_ukT = f(np.transpose(f(inp["w_uk"]), (2, 1, 0))).reshape(128, NH * 256)
    shared = dict(gvec=gvec, gmix0=f(inp["norm_mix"][0])[None, :], pool_w=f(inp["pool_w"][0]), w_up=f(inp["w_up"]), w_down=f(inp["w_down"]),
                  w_gate=f(inp["w_ple_gate"]), w_proj=f(inp["w_ple_proj"]), w_dkv4=f(w_dkv4), w_dq=f(inp["w_dq"][0]),
                  w_uq_nope=f(w_uq_nope), w_uq_rope=f(w_uq_rope), w_ukT=w_ukT, w_uv=f(inp["w_uv"]).reshape(KVR, NH * 128),
                  w_o=f(inp["w_o"][0]).reshape(NH * 128, D))
    if cfg["SAMPLE_ATT"]:
        shared["cache_latent"] = f(inp["cache_latent"])
        shared["cache_krope"] = f(inp["cache_krope"])
    maps = []
    for k in range(NCORE):
        b, v = k // VR, k % VR
        xs_k = xs[k * DPC:(k + 1) * DPC].reshape(128, D)
        xtok_l, xT_l, halo_l, pT0_l = [], [], [], []
        for vv in _pass_vranks(cfg, v):
            blocks = vr_blocks(cfg, vv)
            xt_ = np.concatenate([xp[b, blk * 128:(blk + 1) * 128] for blk in blocks] + [xs_k], 0)
            p0 = np.concatenate([pp[0, b, blk * 128:(blk + 1) * 128] for blk in blocks] + [psm[0, k * DPC:(k + 1) * DPC].reshape(128, PLE)], 0)
            hal = np.zeros((len(blocks), 16, D), np.float32)
            for i_, blk in enumerate(blocks):
                if blk > 0:
                    hal[i_] = xp[b, blk * 128 - 16:blk * 128]
            xtok_l.append(xt_)
            xT_l.append(xt_.T)
            halo_l.append(hal)
            pT0_l.append(p0.T)
        blocks = vr_blocks(cfg, v)
        p1 = np.concatenate([pp[1, b, blk * 128:(blk + 1) * 128] for blk in blocks] + [psm[1, k * DPC:(k + 1) * DPC].reshape(128, PLE)], 0)
        m = dict(shared)
        m.update(_tables(cfg, k))
        m["xtok"] = f(np.stack(xtok_l))
        m["xT"] = f(np.stack(xT_l))
        m["xhalo"] = f(np.stack(halo_l))
        m["pT0"] = f(np.stack(pT0_l))
        m["pT1"] = f(p1.T)
        sp = stp[0, k * DPC:(k + 1) * DPC]
        m["spin"] = f(sp)
        m["uhalo"] = f(sp.reshape(2, 120, D))
        if cfg["SAMPLE_ATT"]:
            m["ptab"] = np.ascontiguousarray(np.asarray(inp["page_table"])[k * DPC:(k + 1) * DPC].astype(np.int32).reshape(1, -1))
        maps.append(m)
    return maps


def _assemble(cfg, res):
    NCORE, BATCH, NBLK = cfg["NCORE"], cfg["BATCH"], cfg["NBLK"]
    VR = NCORE // BATCH
    DPC = cfg["DEC"] // NCORE
    SEQ = NBLK * 128
    y_p = np.zeros((BATCH, SEQ, D), np.float32)
    lat_p = np.zeros((BATCH, SEQ, KVR), np.float32)
    kr_p = np.zeros((BATCH, SEQ, ROPE), np.float32)
    y_s = np.zeros((cfg["DEC"], cfg["DSEQ"], D), np.float32)
    lat_s = np.zeros((cfg["DEC"], cfg["DSEQ"], KVR), np.float32)
    kr_s = np.zeros((cfg["DEC"], cfg["DSEQ"], ROPE), np.float32)
    pool_s = np.zeros((1, cfg["DEC"], 15, D), np.float32)
    pool_p = np.zeros((1, BATCH, 15, D), np.float32)
    for k in range(NCORE):
        r = res[k]
        b, v = k // VR, k % VR
        yT, latT, krT = np.asarray(r["yT"]), np.asarray(r["latT"]), np.asarray(r["krT"])
        l = 0
        for blk in vr_blocks(cfg, v):
            sl = slice(l * 128, (l + 1) * 128)
            y_p[b, blk * 128:(blk + 1) * 128] = yT[:, sl].T
            lat_p[b, blk * 128:(blk + 1) * 128] = latT[:, sl].T
            kr_p[b, blk * 128:(blk + 1) * 128] = krT[:, sl].T
            l += 1
        sl = slice(l * 128, (l + 1) * 128)
        y_s[k * DPC:(k + 1) * DPC] = yT[:, sl].T.reshape(DPC, cfg["DSEQ"], D)
        lat_s[k * DPC:(k + 1) * DPC] = latT[:, sl].T.reshape(DPC, cfg["DSEQ"], KVR)
        kr_s[k * DPC:(k + 1) * DPC] = krT[:, sl].T.reshape(DPC, cfg["DSEQ"], ROPE)
        pool_s[0, k * DPC:(k + 1) * DPC] = np.asarray(r["pools"])
        if v == 0:
            pool_p[0, b] = np.asarray(r["poolp"])
    return (y_p, y_s, pool_p, pool_s, lat_p, kr_p, lat_s, kr_s)


_NC_CACHE = {}


def kernel(**inputs):
    cfg = dict(FULL)
    key = "full"
    if key not in _NC_CACHE:
        _NC_CACHE[key] = build_program(cfg)
    nc = _NC_CACHE[key]
    maps = _prep(cfg, inputs)
    res = run_bass_kernel_spmd(nc, maps, core_ids=list(range(cfg["NCORE"])))
    return _assemble(cfg, res.results)
```

```python
import contextlib
import math
import numpy as np
import concourse.bass as bass
import concourse.mybir as mybir
from concourse.bass_utils import run_bass_kernel_spmd

F32 = mybir.dt.float32
BF16 = mybir.dt.bfloat16
I32 = mybir.dt.int32
AF = mybir.ActivationFunctionType
ALU = mybir.AluOpType

COMPUTE = ("pe", "act", "dve", "pool")
N_DMA_SLOTS = {"sp": 8, "pool": 2, "act": 2, "pe": 1, "dve": 1}


class Sched:
    def __init__(self, nc):
        self.nc = nc
        self.eng_names = ("pe", "act", "dve", "pool", "sp")
        self.prog = {e: [] for e in self.eng_names}
        self.n_comp = {e: 0 for e in self.eng_names}
        self.n_dma = {e: 0 for e in self.eng_names}
        self.slot_cnt = {}
        self.last_w = {}
        self.readers = {}
        self.known = {e: {} for e in self.eng_names}
        self.targets = {e: set() for e in self.eng_names}
        self.final_dma = []
        self.n_cc = 0

    def _add_dep(self, deps, tok, eng):
        if tok is None:
            return
        if tok[0] == "c":
            if tok[1] == eng and eng == "pe":
                return
            lane = ("c", tok[1])
            v = tok[2]
        else:
            lane = (tok[0], tok[1], tok[2])
            v = tok[3]
        if deps.get(lane, 0) < v:
            deps[lane] = v

    def op(self, eng, fn, reads=(), writes=(), dma=False, final=False, cc=False):
        deps = {}
        for k in reads:
            self._add_dep(deps, self.last_w.get(k), eng)
        for k in writes:
            self._add_dep(deps, self.last_w.get(k), eng)
            for t in self.readers.get(k, ()):
                self._add_dep(deps, t, eng)
        if cc:
            self.n_cc += 1
            tok = ("x", "cc", self.n_cc, 1)
        elif dma:
            n = self.n_dma[eng]
            self.n_dma[eng] = n + 1
            slot = n % N_DMA_SLOTS[eng]
            cnt = self.slot_cnt.get((eng, slot), 0) + 1
            self.slot_cnt[(eng, slot)] = cnt
            tok = ("d", eng, slot, cnt)
            if cnt > 1:
                lane = ("d", eng, slot)
                if deps.get(lane, 0) < cnt - 1:
                    deps[lane] = cnt - 1
        else:
            self.n_comp[eng] += 1
            tok = ("c", eng, self.n_comp[eng])
        waits = []
        kn = self.known[eng]
        for lane, v in deps.items():
            if kn.get(lane, 0) >= v:
                continue
            kn[lane] = v
            waits.append((lane, v))
            if lane[0] == "c":
                self.targets[lane[1]].add(v)
        self.prog[eng].append((waits, fn, tok))
        for k in reads:
            self.readers.setdefault(k, []).append(tok)
        for k in writes:
            self.last_w[k] = tok
            self.readers[k] = []
        if final:
            self.final_dma.append(tok)
        return tok

    def barrier(self):
        for eng in self.eng_names:
            deps = {}
            for F in COMPUTE:
                if self.n_comp[F] > 0 and not (F == eng and eng == "pe"):
                    deps[("c", F)] = self.n_comp[F]
            for (e, s_), c in self.slot_cnt.items():
                deps[("d", e, s_)] = c
            for i in range(1, self.n_cc + 1):
                deps[("x", "cc", i)] = 1
            waits = []
            kn = self.known[eng]
            for lane, v in deps.items():
                if kn.get(lane, 0) >= v:
                    continue
                kn[lane] = v
                waits.append((lane, v))
                if lane[0] == "c":
                    self.targets[lane[1]].add(v)
            self.prog[eng].append((waits, None, None))

    def emit(self):
        nc = self.nc
        with contextlib.ExitStack() as st:
            dsem = {}
            for i in range(1, self.n_cc + 1):
                dsem[("x", "cc", i)] = st.enter_context(nc.semaphore("cc_%d" % i))
            csem = {e: st.enter_context(nc.semaphore("c_" + e)) for e in COMPUTE}
            for (e, s) in self.slot_cnt:
                dsem[("d", e, s)] = st.enter_context(nc.semaphore("d_%s_%d" % (e, s)))
            cum = {}
            for e in COMPUTE:
                tg = sorted(self.targets[e])
                cum[e] = {idx: i + 1 for i, idx in enumerate(tg)}
            fin = {}
            for tok in self.final_dma:
                lane = ("d", tok[1], tok[2])
                fin[lane] = max(fin.get(lane, 0), tok[3])
            block = st.enter_context(nc.Block())
            engobj = {"pe": "tensor", "act": "scalar", "dve": "vector", "pool": "gpsimd", "sp": "sync"}

            def build(ename):
                def body(eng):
                    for waits, fn, tok in self.prog[ename]:
                        for lane, v in waits:
                            if lane[0] == "c":
                                eng.wait_ge(csem[lane[1]], cum[lane[1]][v])
                            elif lane[0] == "d":
                                eng.wait_ge(dsem[lane], 16 * v)
                            else:
                                eng.wait_ge(dsem[lane], v)
                        if fn is None:
                            continue
                        ins = fn(eng)
                        if tok[0] == "c":
                            if tok[2] in cum[tok[1]]:
                                ins.then_inc(csem[tok[1]], 1)
                        elif tok[0] == "d":
                            ins.then_inc(dsem[(tok[0], tok[1], tok[2])], 16)
                        else:
                            ins.then_inc(dsem[(tok[0], tok[1], tok[2])], 1)
                    if ename == "sp":
                        for lane, v in fin.items():
                            eng.wait_ge(dsem[lane], 16 * v)
                return body

            for ename in self.eng_names:
                if not self.prog[ename] and ename != "sp":
                    continue
                getattr(block, engobj[ename])(build(ename))


D = 1024
DC = 8
KVR = 256
QR = 384
NH = 8
NOPE = 128
ROPE = 64
PLE = 256
POOL_W = (2, 4, 8, 16)
EPS = 1e-6
SM_SCALE = 1.0 / math.sqrt(NOPE + ROPE)
THETA = 10000.0

FULL = dict(NCORE=8, BATCH=2, NBLK=64, DEC=128, DSEQ=8, NPG=64, NPOOL=10240, FF=4096, USE_CC=False, SAMPLE_ATT=True)


def core_blocks(cfg, k):
    nc_, nb = cfg["NCORE"], cfg["NBLK"]
    out = []
    for i in range(nb // (2 * nc_)):
        out.append(2 * nc_ * i + k)
        out.append(2 * nc_ * i + 2 * nc_ - 1 - k)
    return out


def owner_of(cfg, j):
    nc_ = cfg["NCORE"]
    r = j % (2 * nc_)
    k = r if r < nc_ else 2 * nc_ - 1 - r
    m = (j // (2 * nc_)) * 2 + (0 if r < nc_ else 1)
    return k, m


def build_program(cfg):
    NCORE, BATCH, NBLK, FF = cfg["NCORE"], cfg["BATCH"], cfg["NBLK"], cfg["FF"]
    VR = NCORE // BATCH
    NPASS = VR
    NTB = NBLK // VR
    NT = NTB
    T = (NT + 1) * 128
    TP = NT * 128
    NFB = FF // 512
    DPC = cfg["DEC"] // NCORE
    NPG = cfg["NPG"]
    assert DPC * cfg["DSEQ"] == 128
    groups = []
    l = 0
    while l < NT:
        n = min(4, NT - l)
        groups.append((l * 128, n * 128, list(range(l, l + n))))
        l += n
    groups.append((NT * 128, 128, [NT]))
    NG = len(groups)
    groups_all = groups

    nc = bass.Bass("TRN2", target_bir_lowering=False)

    def din(name, shape, dt=F32):
        return nc.dram_tensor(name, list(shape), dt, kind="ExternalInput").ap()

    def dout(name, shape, dt=F32):
        return nc.dram_tensor(name, list(shape), dt, kind="ExternalOutput").ap()

    xT_d = din("xT", [NPASS, D, T])
    xtok_d = din("xtok", [NPASS, T, D])
    xhalo_d = din("xhalo", [NPASS, NT, 16, D])
    uhalo_d = din("uhalo", [2, 120, D])
    spin_d = din("spin", [DPC, 15, D])
    pT0_d = din("pT0", [NPASS, PLE, T])
    pT1_d = din("pT1", [PLE, T])
    gvec_d = din("gvec", [128, 80])
    gmix0_d = din("gmix0", [1, D])
    cos_d = din("cosT", [NPASS, 64, T])
    sin_d = din("sinT", [NPASS, 64, T])
    amain_d = din("amain", [4, 128, 128])
    afirst_d = din("afirst", [NPASS, 4, 128, 128])
    ahalo_d = din("ahalo", [4, 16, 128])
    amain_s_d = din("amain_s", [4, 128, 128])
    ahalo_s_d = din("ahalo_s", [2, 4, 120, 128])
    bmask_d = din("bmask", [VR * 2, 128, 128])
    identf_d = din("identf", [128, 128])
    poolw_d = din("pool_w", [4, 256, 256])
    wup_d = din("w_up", [2, D, FF])
    wdn_d = din("w_down", [2, FF, D])
    wg_d = din("w_gate", [2, D, D])
    wp_d = din("w_proj", [2, PLE, D])
    wdkv_d = din("w_dkv4", [D, 384])
    wdq_d = din("w_dq", [D, QR])
    wuqn_d = din("w_uq_nope", [QR, NH * 128])
    wuqr_d = din("w_uq_rope", [QR, NH * 128])
    wukT_d = din("w_ukT", [128, NH * 256])
    wuv_d = din("w_uv", [KVR, NH * 128])
    wo_d = din("w_o", [NH * 128, D])
    if cfg["SAMPLE_ATT"]:
        clat_d = din("cache_latent", [cfg["NPOOL"], 128, KVR])
        ckr_d = din("cache_krope", [cfg["NPOOL"], 128, ROPE])
        ptab_d = din("ptab", [1, DPC * NPG], I32)
        smask_d = din("smask", [128, 128])
        pidx_d = din("pidx", [128, 1])

    yT_o = dout("yT", [D, T])
    latT_o = dout("latT", [KVR, T])
    krT_o = dout("krT", [ROPE, T])
    poolp_o = dout("poolp", [15, D])
    pools_o = dout("pools", [DPC, 15, D])

    cx_all = nc.dram_tensor("cx_all", [VR * 576, TP], BF16)

    def c_tok_view(t, r):
        return t.ap()[r * 576 + 320:(r + 1) * 576, :].rearrange("a (t c) -> (a t) c", c=256)

    S = Sched(nc)
    st = contextlib.ExitStack()
    with st:
        ARENA = 206 * 1024
        arena = st.enter_context(nc.sbuf_tensor("arena", [128, ARENA], mybir.dt.uint8))
        top = [0]

        def sb(name, shape, dt):
            nb = 2 if dt == BF16 else 4
            per = int(np.prod(shape[1:])) * nb
            off = (top[0] + 63) // 64 * 64
            assert off + per <= ARENA, (name, off, per, ARENA)
            top[0] = off + per
            v = arena[0:shape[0], off:off + per].bitcast(dt)
            if len(shape) == 3:
                v = v.rearrange("p (a b) -> p a b", a=shape[1])
            elif len(shape) == 4:
                v = v.rearrange("p (a b c) -> p a b c", a=shape[1], b=shape[2])
            return v

        ps = [st.enter_context(nc.psum_tensor("ps%d" % i, [128, 512], F32)) for i in range(7)]
        psb = st.enter_context(nc.psum_tensor("psb", [128, 1024], BF16))
        PSK = ["ps%d" % i for i in range(7)]

        hT = sb("hT", [128, DC, T], F32)
        gvec = sb("gvec", [128, 80], F32)
        ones_bf = sb("ones_bf", [128, 128], BF16)
        ident_bf = sb("ident_bf", [128, 128], BF16)
        mask_bf = sb("mask_bf", [128, VR * 2, 128], BF16)
        sqb = [sb("sqb%d" % i, [128, 512], BF16) for i in range(2)]
        rstd = [sb("rstd%d" % i, [128, 512], F32) for i in range(2)]
        tmpf = [sb("tmpf%d" % i, [128, 512], F32) for i in range(2)]
        tmpb = [sb("tmpb%d" % i, [128, 512], BF16) for i in range(3)]
        sKT = sb("sKT", [128, 3, 128], BF16)
        sV = sb("sV", [128, 258], BF16)
        smask_bf = sb("smask_bf", [128, 16, 8], BF16)
        idx_all = sb("idx_all", [128, DPC * NPG], I32)
        pidx = sb("pidx", [128, 1], F32)
        cqT = sb("cqT", [128, 3, T], BF16)
        M_U = top[0]
        uT = sb("uT", [128, DC, T], BF16)
        wA = [sb("wA%d" % i, [128, 8, 512], BF16) for i in range(2)]
        wB = [sb("wB%d" % i, [128, 4, 1024], BF16) for i in range(2)]
        aT = [sb("aT%d" % i, [128, 4, 512], BF16) for i in range(2)]
        M_1 = top[0]

        G_MIX1, G_MLP0, G_MLP1, G_PLE0, G_PLE1, G_PSC, G_KV, G_FIN, G_Q, G_KVN = 0, 8, 16, 24, 32, 40, 48, 56, 64, 67

        cnt = {"ps": 0, "sq": 0, "rs": 0, "tf": 0, "tb": 0}

        def rot(kind, n):
            v = cnt[kind]
            cnt[kind] = (v + 1) % n
            return v

        def dma(eng, out, in_, reads=(), writes=(), final=False, **kw):
            return S.op(eng, lambda e: e.dma_start(out=out, in_=in_, **kw), reads=reads, writes=writes, dma=True, final=final)

        def mm(out, lhsT, rhs, start, stop, reads, writes):
            S.op("pe", lambda e: e.matmul(out, lhsT=lhsT, rhs=rhs, start=start, stop=stop), reads=reads, writes=writes)

        dma("sp", gvec[:], gvec_d[:, :], writes=["gvec"])
        dma("pool", ident_bf[:], identf_d[:, :], writes=["ident_bf"])
        dma("pool", mask_bf[:], bmask_d.rearrange("m p t -> p m t"), writes=["mask_bf"])
        S.op("dve", lambda e: e.memset(ones_bf[:], 1.0), writes=["ones_bf"])
        if cfg["SAMPLE_ATT"]:
            dma("pool", smask_bf[:], smask_d.rearrange("p (s i) -> p s i", i=8), writes=["smask_bf"])
            dma("sp", idx_all[:], ptab_d[0:1, :].broadcast_to([128, DPC * NPG]), writes=["idx_all"])
            dma("sp", pidx[:], pidx_d[:, :], writes=["pidx"])
            S.op("dve", lambda e: e.tensor_scalar(out=idx_all[:], in0=idx_all[:], scalar1=128.0, scalar2=pidx[:, 0:1], op0=ALU.mult, op1=ALU.add),
                 reads=["idx_all", "pidx"], writes=["idx_all"])
            S.op("dve", lambda e: e.memset(sV[:, 256:258], 1.0), writes=["sV"])
        for pj in range(NPASS):
            own = (pj == NPASS - 1)
            groups = groups_all if own else groups_all[:-1]
            NG = len(groups)
            S.barrier()
            top[0] = M_1
            for (c0, ncol, tiles) in groups:
                gi = c0
                dma("sp", hT[:, :, c0:c0 + ncol], xT_d[pj].rearrange("(c p) t -> p c t", p=128)[:, :, c0:c0 + ncol],
                    writes=[("hT", ch, gi) for ch in range(DC)])

            def norm_T(src_fn, src_keys_fn, nch, dim, gcol, out_fn, out_keys_fn, g_id, ncol, eps=EPS, out2_fn=None, out2_keys_fn=None):
                b = PSK[rot("ps", 2)]
                bank = ps[int(b[2:])]
                for ch in range(nch):
                    q = rot("sq", 2)
                    S.op("act", lambda e, ch=ch, q=q: e.activation(out=sqb[q][:, 0:ncol], in_=src_fn(ch), func=AF.Square),
                         reads=src_keys_fn(ch), writes=["sqb%d" % q])
                    mm(bank[:, 0:ncol], ones_bf[:], sqb[q][:, 0:ncol], ch == 0, ch == nch - 1, ["ones_bf", "sqb%d" % q], [b])
                r = rot("rs", 2)
                S.op("act", lambda e: e.activation(out=rstd[r][:, 0:ncol], in_=bank[:, 0:ncol], func=AF.Ln, scale=1.0 / dim, bias=eps),
                     reads=[b], writes=["rstd%d" % r])
                S.op("act", lambda e: e.activation(out=rstd[r][:, 0:ncol], in_=rstd[r][:, 0:ncol], func=AF.Exp, scale=-0.5),
                     reads=["rstd%d" % r], writes=["rstd%d" % r])
                for ch in range(nch):
                    S.op("dve", lambda e, ch=ch: e.scalar_tensor_tensor(out=out_fn(ch), in0=src_fn(ch), scalar=gvec[:, gcol + ch:gcol + ch + 1],
                                                                        in1=rstd[r][:, 0:ncol], op0=ALU.mult, op1=ALU.mult),
                         reads=list(src_keys_fn(ch)) + ["gvec", "rstd%d" % r], writes=out_keys_fn(ch))
                    if out2_fn is not None:
                        S.op("dve", lambda e, ch=ch: e.scalar_tensor_tensor(out=out2_fn(ch), in0=src_fn(ch), scalar=gvec[:, gcol + ch:gcol + ch + 1],
                                                                            in1=rstd[r][:, 0:ncol], op0=ALU.mult, op1=ALU.mult),
                             reads=list(src_keys_fn(ch)) + ["gvec", "rstd%d" % r], writes=out2_keys_fn(ch))

            def norm_h(gcol):
                for (c0, ncol, tiles) in groups:
                    norm_T(lambda ch, c0=c0, ncol=ncol: hT[:, ch, c0:c0 + ncol], lambda ch, c0=c0: [("hT", ch, c0)], DC, D, gcol,
                           lambda ch, c0=c0, ncol=ncol: uT[:, ch, c0:c0 + ncol], lambda ch, c0=c0: [("uT", ch, c0)], c0, ncol)

            def load_w(buf, bufkey, src_ap):
                dma("pool", buf, src_ap, writes=[bufkey])

            if True:
                top[0] = M_U
                sb0 = sb
                xt = [sb0("xt%d" % i, [128, D], F32) for i in range(2)]
                xh = [sb0("xh%d" % i, [16, D], F32) for i in range(2)]
                gbc = sb0("gbc", [128, D], F32)
                ub = [sb0("ub%d" % i, [128, D], BF16) for i in range(2)]
                uhb = [sb0("uhb%d" % i, [16, D], BF16) for i in range(2)]
                uf = sb0("uf", [128, D], F32)
                junk = sb0("junk", [128, D], F32)
                ss = [sb0("ss%d" % i, [128, 2], F32) for i in range(2)]
                am = sb0("am", [128, 4, 128], BF16)
                af = sb0("af", [128, 4, 128], BF16)
                ah = sb0("ah", [16, 4, 128], BF16)
                ams = sb0("ams", [128, 4, 128], BF16)
                ahs = sb0("ahs", [120, 2, 4, 128], BF16)
                uhs = sb0("uhs", [120, 2, D], BF16)
                dT = sb0("dT", [128, DC, 512], BF16)
                pw = sb0("pw", [128, 4, 2, 256], BF16)

                dma("sp", gbc[:], gmix0_d[0:1, :].broadcast_to([128, D]), writes=["gbc"])
                dma("pool", am[:], amain_d.rearrange("g p t -> p g t"), writes=["am"])
                dma("pool", af[:], afirst_d[pj].rearrange("g p t -> p g t"), writes=["af"])
                dma("pool", ah[:], ahalo_d.rearrange("g p t -> p g t"), writes=["ah"])
                dma("pool", ams[:], amain_s_d.rearrange("g p t -> p g t"), writes=["ams"])
                dma("pool", ahs[:], ahalo_s_d.rearrange("a g p t -> p a g t"), writes=["ahs"])
                dma("pool", uhs[:], uhalo_d.rearrange("a p d -> p a d"), writes=["uhs"])
                dma("pool", pw[:], poolw_d.rearrange("g (k p) d -> p g k d", p=128), writes=["pw"])
                if own:
                    dma("sp", pools_o[:, 0:7, :], spin_d[:, 8:15, :], final=True)

                for (c0, ncol, tiles) in groups:
                    for ti, l in enumerate(tiles):
                        i2 = l % 2
                        is_s = (l == NT)
                        dma("sp", xt[i2][:], xtok_d[pj][l * 128:(l + 1) * 128, :], writes=["xt%d" % i2])
                        S.op("act", lambda e, i2=i2: e.activation(out=junk[:], in_=xt[i2][:], func=AF.Square, accum_out=ss[i2][:, 0:1]),
                             reads=["xt%d" % i2], writes=["junk", "ss%d" % i2])
                        S.op("act", lambda e, i2=i2: e.activation(out=ss[i2][:, 0:1], in_=ss[i2][:, 0:1], func=AF.Ln, scale=1.0 / D, bias=EPS),
                             reads=["ss%d" % i2], writes=["ss%d" % i2])
                        S.op("act", lambda e, i2=i2: e.activation(out=ss[i2][:, 0:1], in_=ss[i2][:, 0:1], func=AF.Exp, scale=-0.5),
                             reads=["ss%d" % i2], writes=["ss%d" % i2])
                        S.op("dve", lambda e, i2=i2: e.scalar_tensor_tensor(out=ub[i2][:], in0=xt[i2][:], scalar=ss[i2][:, 0:1], in1=gbc[:],
                                                                            op0=ALU.mult, op1=ALU.mult),
                             reads=["xt%d" % i2, "ss%d" % i2, "gbc"], writes=["ub%d" % i2])
                        last_of_batch = own and (not is_s) and (l == NT - 1)
                        if is_s or last_of_batch:
                            S.op("dve", lambda e, i2=i2: e.scalar_tensor_tensor(out=uf[:], in0=xt[i2][:], scalar=ss[i2][:, 0:1], in1=gbc[:],
                                                                                op0=ALU.mult, op1=ALU.mult),
                                 reads=["xt%d" % i2, "ss%d" % i2, "gbc"], writes=["uf"])
                            if is_s:
                                dma("sp", pools_o[:, 7:15, :].rearrange("s i d -> (s i) d") if False else pools_o[:, 7:15, :],
                                    uf[:].rearrange("(s i) d -> s i d", i=8) if False else uf[:], reads=["uf"], final=True) if False else None
                                for s_ in range(DPC):
                                    dma("sp", pools_o[s_, 7:15, :], uf[s_ * 8:(s_ + 1) * 8, :], reads=["uf"], final=True)
                            else:
                                dma("sp", poolp_o[:, :], uf[113:128, :], reads=["uf"], final=True)
                        if not is_s:
                            dma("sp", xh[i2][:], xhalo_d[pj][l], writes=["xh%d" % i2])
                            S.op("act", lambda e, i2=i2: e.activation(out=junk[0:16, :], in_=xh[i2][:], func=AF.Square, accum_out=ss[i2][0:16, 1:2]),
                                 reads=["xh%d" % i2], writes=["junk", "ssh%d" % i2])
                            S.op("act", lambda e, i2=i2: e.activation(out=ss[i2][0:16, 1:2], in_=ss[i2][0:16, 1:2], func=AF.Ln, scale=1.0 / D, bias=EPS),
                                 reads=["ssh%d" % i2], writes=["ssh%d" % i2])
                            S.op("act", lambda e, i2=i2: e.activation(out=ss[i2][0:16, 1:2], in_=ss[i2][0:16, 1:2], func=AF.Exp, scale=-0.5),
                                 reads=["ssh%d" % i2], writes=["ssh%d" % i2])
                            S.op("dve", lambda e, i2=i2: e.scalar_tensor_tensor(out=uhb[i2][:], in0=xh[i2][:], scalar=ss[i2][0:16, 1:2], in1=gbc[0:16, :],
                                                                                op0=ALU.mult, op1=ALU.mult),
                                 reads=["xh%d" % i2, "ssh%d" % i2, "gbc"], writes=["uhb%d" % i2])
                        first = (not is_s) and (l == 0)
                        for half in range(2):
                            b = PSK[rot("ps", 2)]
                            bank = ps[int(b[2:])]
                            for cc in range(4):
                                ch = half * 4 + cc
                                g = ch // 2
                                o = bank[:, cc * 128:(cc + 1) * 128]
                                if is_s:
                                    mm(o, ub[i2][:, ch * 128:(ch + 1) * 128], ams[:, g, :], True, False, ["ub%d" % i2, "ams"], [b])
                                    mm(o, uhs[:, 0, ch * 128:(ch + 1) * 128], ahs[:, 0, g, :], False, False, ["uhs", "ahs"], [b])
                                    mm(o, uhs[:, 1, ch * 128:(ch + 1) * 128], ahs[:, 1, g, :], False, True, ["uhs", "ahs"], [b])
                                else:
                                    amat = af if first else am
                                    mm(o, ub[i2][:, ch * 128:(ch + 1) * 128], amat[:, g, :], True, False, ["ub%d" % i2, "af", "am"], [b])
                                    mm(o, uhb[i2][:, ch * 128:(ch + 1) * 128], ah[:, g, :], False, True, ["uhb%d" % i2, "ah"], [b])
                            S.op("act", lambda e, half=half, ti=ti, bank=bank: e.activation(
                                out=dT[:, half * 4:half * 4 + 4, ti * 128:(ti + 1) * 128], in_=bank[:, :].rearrange("p (c t) -> p c t", c=4), func=AF.Copy),
                                reads=[b], writes=[("dT", half, ti)])
                    for j in range(DC):
                        g = j // 2
                        b = PSK[2 + rot("ps", 2)] if False else PSK[rot("ps", 2)]
                        bank = ps[int(b[2:])]
                        for kc in range(2):
                            mm(bank[:, 0:ncol], pw[:, g, kc, (j % 2) * 128:(j % 2) * 128 + 128], dT[:, 2 * g + kc, 0:ncol], kc == 0, kc == 1,
                               ["pw"] + [("dT", (2 * g + kc) // 4, ti) for ti in range(len(tiles))], [b])
                        S.op("dve", lambda e, j=j, bank=bank, c0=c0, ncol=ncol: e.scalar_tensor_tensor(
                            out=hT[:, j, c0:c0 + ncol], in0=bank[:, 0:ncol], scalar=gvec[:, G_PSC + j:G_PSC + j + 1], in1=hT[:, j, c0:c0 + ncol],
                            op0=ALU.mult, op1=ALU.add), reads=[b, "gvec", ("hT", j, c0)], writes=[("hT", j, c0)])

            def mlp(layer, gcol):
                norm_h(gcol)
                for fb in range(NFB):
                    i2 = fb % 2
                    load_w(wA[i2][:], "wA%d" % i2, wup_d[layer].rearrange("(k p) f -> p k f", p=128)[:, :, fb * 512:(fb + 1) * 512])
                    load_w(wB[i2][:], "wB%d" % i2, wdn_d[layer][fb * 512:(fb + 1) * 512, :].rearrange("(k p) d -> p k d", p=128))
                    for gi, (c0, ncol, tiles) in enumerate(groups):
                        a2 = gi % 2
                        for fc in range(4):
                            b = PSK[rot("ps", 2)]
                            bank = ps[int(b[2:])]
                            for k in range(DC):
                                mm(bank[:, 0:ncol], wA[i2][:, k, fc * 128:(fc + 1) * 128], uT[:, k, c0:c0 + ncol], k == 0, k == DC - 1,
                                   ["wA%d" % i2, ("uT", k, c0)], [b])
                            tb = rot("tb", 3)
                            S.op("act", lambda e, bank=bank, tb=tb, ncol=ncol: e.activation(out=tmpb[tb][:, 0:ncol], in_=bank[:, 0:ncol], func=AF.Relu),
                                 reads=[b], writes=["tmpb%d" % tb])
                            S.op("dve", lambda e, tb=tb, a2=a2, fc=fc, ncol=ncol: e.tensor_tensor(out=aT[a2][:, fc, 0:ncol], in0=tmpb[tb][:, 0:ncol],
                                                                                             in1=tmpb[tb][:, 0:ncol], op=ALU.mult),
                                 reads=["tmpb%d" % tb], writes=[("aT", a2, fc)])
                        for j in range(DC):
                            b = PSK[2 + rot("rs", 2)]
                            bank = ps[int(b[2:])]
                            for fc in range(4):
                                mm(bank[:, 0:ncol], wB[i2][:, fc, j * 128:(j + 1) * 128], aT[a2][:, fc, 0:ncol], fc == 0, fc == 3,
                                   ["wB%d" % i2, ("aT", a2, fc)], [b])
                            S.op("dve", lambda e, j=j, bank=bank, c0=c0, ncol=ncol: e.tensor_tensor(
                                out=hT[:, j, c0:c0 + ncol], in0=bank[:, 0:ncol], in1=hT[:, j, c0:c0 + ncol], op=ALU.add),
                                reads=[b, ("hT", j, c0)], writes=[("hT", j, c0)])

            def ple(layer, gcol):
                norm_h(gcol)
                for hlf in range(2):
                    load_w(wA[hlf][:], "wA%d" % hlf, wg_d[layer].rearrange("(k p) f -> p k f", p=128)[:, :, hlf * 512:(hlf + 1) * 512])
                load_w(wB[0][:, 0:2, :], "wB0", wp_d[layer].rearrange("(k p) d -> p k d", p=128))
                for gi, (c0, ncol, tiles) in enumerate(groups):
                    a2 = gi % 2
                    dma("pool", aT[a2][:, 0:2, 0:ncol], (pT0_d[pj] if layer == 0 else pT1_d).rearrange("(k p) t -> p k t", p=128)[:, :, c0:c0 + ncol],
                        writes=[("aT", a2, 0), ("aT", a2, 1)])
                    for j in range(DC):
                        b = PSK[rot("ps", 2)]
                        bank = ps[int(b[2:])]
                        for k in range(DC):
                            mm(bank[:, 0:ncol], wA[j // 4][:, k, (j % 4) * 128:(j % 4) * 128 + 128], uT[:, k, c0:c0 + ncol], k == 0, k == DC - 1,
                               ["wA%d" % (j // 4), ("uT", k, c0)], [b])
                        tf = rot("tf", 2)
                        S.op("act", lambda e, bank=bank, tf=tf, ncol=ncol: e.activation(out=tmpf[tf][:, 0:ncol], in_=bank[:, 0:ncol], func=AF.Sigmoid),
                             reads=[b], writes=["tmpf%d" % tf])
                        b2 = PSK[2 + rot("rs", 2)]
                        bank2 = ps[int(b2[2:])]
                        for k in range(2):
                            mm(bank2[:, 0:ncol], wB[0][:, k, j * 128:(j + 1) * 128], aT[a2][:, k, 0:ncol], k == 0, k == 1,
                               ["wB0", ("aT", a2, k)], [b2])
                        S.op("dve", lambda e, bank2=bank2, tf=tf, ncol=ncol: e.tensor_tensor(out=tmpf[tf][:, 0:ncol], in0=bank2[:, 0:ncol],
                                                                                         in1=tmpf[tf][:, 0:ncol], op=ALU.mult),
                             reads=[b2, "tmpf%d" % tf], writes=["tmpf%d" % tf])
                        S.op("dve", lambda e, j=j, tf=tf, c0=c0, ncol=ncol: e.tensor_tensor(out=hT[:, j, c0:c0 + ncol], in0=tmpf[tf][:, 0:ncol],
                                                                                         in1=hT[:, j, c0:c0 + ncol], op=ALU.add),
                             reads=["tmpf%d" % tf, ("hT", j, c0)], writes=[("hT", j, c0)])

            S.barrier()
            top[0] = M_1
            mlp(0, G_MLP0)
            ple(0, G_PLE0)

            if True:
                sb1 = sb
                cf = sb1("cf", [128, 2, 512], F32)
                co = sb1("co", [128, 2, 512], F32)
                cb = sb1("cb", [128, 2, 512], BF16)
                ctok = sb1("ctok", [128, 4, 256], BF16)
                krf = sb1("krf", [64, 2, 512], F32)
                kro = sb1("kro", [64, 512], F32)
                krb = sb1("krb", [64, 512], BF16)
                cs = sb1("cs", [64, 512], F32)
                sn = sb1("sn", [64, 512], F32)
                norm_h(G_KV)
                wkv = wA[0]
                load_w(wkv[:, :, 0:384], "wA0", wdkv_d.rearrange("(k p) f -> p k f", p=128))
                for gi, (c0, ncol, tiles) in enumerate(groups):
                    for mchunk in range(4):
                        msz = 128 if mchunk < 2 else 64
                        m0 = mchunk * 128 if mchunk < 2 else 256 + (mchunk - 2) * 64
                        b = PSK[rot("ps", 2)]
                        bank = ps[int(b[2:])]
                        for k in range(DC):
                            mm(bank[0:msz, 0:ncol], wkv[:, k, m0:m0 + msz], uT[:, k, c0:c0 + ncol], k == 0, k == DC - 1, ["wA0", ("uT", k, c0)], [b])
                        if mchunk < 2:
                            S.op("act", lambda e, bank=bank, mchunk=mchunk, ncol=ncol: e.activation(out=cf[:, mchunk, 0:ncol], in_=bank[:, 0:ncol], func=AF.Copy),
                                 reads=[b], writes=[("cf", mchunk)])
                        else:
                            S.op("act", lambda e, bank=bank, mchunk=mchunk, ncol=ncol: e.activation(out=krf[:, mchunk - 2, 0:ncol], in_=bank[0:64, 0:ncol], func=AF.Copy),
                                 reads=[b], writes=[("krf", mchunk - 2)])
                    norm_T(lambda ch, ncol=ncol: cf[:, ch, 0:ncol], lambda ch: [("cf", ch)], 2, KVR, G_KVN,
                           lambda ch, ncol=ncol: co[:, ch, 0:ncol], lambda ch: [("co", ch)], c0, ncol,
                           out2_fn=lambda ch, ncol=ncol: cb[:, ch, 0:ncol], out2_keys_fn=lambda ch: [("cb", ch)])
                    if own:
                        dma("sp", latT_o.rearrange("(k p) t -> p k t", p=128)[:, :, c0:c0 + ncol], co[:, :, 0:ncol], reads=[("co", 0), ("co", 1)], final=True)
                    dma("sp", cs[:, 0:ncol], cos_d[pj][:, c0:c0 + ncol], writes=["cs"])
                    dma("sp", sn[:, 0:ncol], sin_d[pj][:, c0:c0 + ncol], writes=["sn"])
                    S.op("dve", lambda e, ncol=ncol: e.tensor_tensor(out=krf[:, 0, 0:ncol], in0=krf[:, 0, 0:ncol], in1=cs[:, 0:ncol], op=ALU.mult),
                         reads=[("krf", 0), "cs"], writes=[("krf", 0)])
                    S.op("dve", lambda e, ncol=ncol: e.tensor_tensor(out=krf[:, 1, 0:ncol], in0=krf[:, 1, 0:ncol], in1=sn[:, 0:ncol], op=ALU.mult),
                         reads=[("krf", 1), "sn"], writes=[("krf", 1)])
                    S.op("dve", lambda e, ncol=ncol: e.tensor_tensor(out=kro[:, 0:ncol], in0=krf[:, 0, 0:ncol], in1=krf[:, 1, 0:ncol], op=ALU.add),
                         reads=[("krf", 0), ("krf", 1)], writes=["kro"])
                    S.op("dve", lambda e, ncol=ncol: e.tensor_copy(out=krb[:, 0:ncol], in_=kro[:, 0:ncol]), reads=["kro"], writes=["krb"])
                    if own:
                        dma("sp", krT_o[:, c0:c0 + ncol], kro[:, 0:ncol], reads=["kro"], final=True)
                    if not (own and gi == NG - 1):
                        dma("sp", cx_all.ap()[pj * 576:pj * 576 + 256, c0:c0 + ncol].rearrange("(k p) t -> p k t", p=128), cb[:, :, 0:ncol],
                            reads=[("cb", 0), ("cb", 1)], writes=["cx_all"])
                        dma("sp", cx_all.ap()[pj * 576 + 256:pj * 576 + 320, c0:c0 + ncol], krb[:, 0:ncol], reads=["krb"], writes=["cx_all"])
                        for ti, l in enumerate(tiles):
                            for ch in range(2):
                                S.op("pe", lambda e, ti=ti, ch=ch: e.transpose(out=psb[:, (ti * 2 + ch) * 128:(ti * 2 + ch + 1) * 128],
                                                                              in_=cb[:, ch, ti * 128:(ti + 1) * 128], identity=ident_bf[:]),
                                     reads=[("cb", ch), "ident_bf"], writes=["psb"])
                        nt_ = len(tiles)
                        S.op("act", lambda e, nt_=nt_: e.activation(out=ctok[:, 0:nt_, :], in_=psb[:, 0:nt_ * 256].rearrange("p (t c) -> p t c", c=256), func=AF.Copy),
                             reads=["psb"], writes=["ctok"])
                        dma("sp", c_tok_view(cx_all, pj)[c0:c0 + ncol, :].rearrange("(t p) c -> p t c", p=128), ctok[:, 0:nt_, :], reads=["ctok"], writes=["cx_all"])
                    else:
                        S.op("dve", lambda e: e.tensor_copy(out=sKT[:, 0:2, :], in_=cb[:, :, 0:128]), reads=[("cb", 0), ("cb", 1)], writes=["sKT"])
                        S.op("dve", lambda e: e.tensor_copy(out=sKT[0:64, 2, :], in_=krb[:, 0:128]), reads=["krb"], writes=["sKT"])
                        for ch in range(2):
                            S.op("pe", lambda e, ch=ch: e.transpose(out=psb[:, ch * 128:(ch + 1) * 128], in_=cb[:, ch, 0:128], identity=ident_bf[:]),
                                 reads=[("cb", ch), "ident_bf"], writes=["psb"])
                        S.op("act", lambda e: e.activation(out=sV[:, 0:256], in_=psb[:, 0:256], func=AF.Copy), reads=["psb"], writes=["sV"])

        groups = groups_all
        NG = len(groups)
        if True:
            sb2 = sb
            S.barrier()
            top[0] = M_1
            cqf = sb2("cqf", [128, 3, 512], F32)
            norm_h(G_MIX1)
            wq = wA[1]
            load_w(wq[:, :, 0:QR], "wA1", wdq_d.rearrange("(k p) f -> p k f", p=128))
            for gi, (c0, ncol, tiles) in enumerate(groups):
                for mc in range(3):
                    b = PSK[rot("ps", 2)]
                    bank = ps[int(b[2:])]
                    for k in range(DC):
                        mm(bank[:, 0:ncol], wq[:, k, mc * 128:(mc + 1) * 128], uT[:, k, c0:c0 + ncol], k == 0, k == DC - 1, ["wA1", ("uT", k, c0)], [b])
                    S.op("act", lambda e, bank=bank, mc=mc, ncol=ncol: e.activation(out=cqf[:, mc, 0:ncol], in_=bank[:, 0:ncol], func=AF.Copy),
                         reads=[b], writes=[("cqf", mc)])
                norm_T(lambda ch, ncol=ncol: cqf[:, ch, 0:ncol], lambda ch: [("cqf", ch)], 3, QR, G_Q,
                       lambda ch, c0=c0, ncol=ncol: cqT[:, ch, c0:c0 + ncol], lambda ch, c0=c0: [("cqT", ch, c0)], c0, ncol)
            S.barrier()
            top[0] = M_U
            AGW = 128
            w_uqn = sb2("w_uqn", [128, 3, NH * 128], BF16)
            w_uqr = sb2("w_uqr", [128, 3, NH * 128], BF16)
            w_ukT = sb2("w_ukT", [128, NH * 256], BF16)
            w_uv = sb2("w_uv", [128, 2, NH * 128], BF16)
            w_o = sb2("w_o", [128, NH, D], BF16)
            qT = sb2("qT", [128, 3, NH, AGW], BF16)
            qn = [sb2("qn%d" % i, [128, AGW], BF16) for i in range(2)]
            qrf = sb2("qrf", [64, 2, AGW], F32)
            cs2 = sb2("cs2", [64, AGW], F32)
            sn2 = sb2("sn2", [64, AGW], F32)
            KT = [sb2("KT%d" % i, [128, 3, 8 * 128], BF16) for i in range(2)]
            V = [sb2("V%d" % i, [128, 8, 258], BF16) for i in range(2)]
            PT = tmpb
            acc = sb2("acc", [128, NH, 257], F32)
            rden = sb2("rden", [128, NH], F32)
            Ob = sb2("Ob", [128, NH, 256], BF16)
            OT = sb2("OT", [128, 2, NH, AGW], BF16)
            oT = sb2("oT", [128, NH, AGW], BF16)
            NPB_ = 8 if cfg["SAMPLE_ATT"] else 2
            pgV = [sb2("pgV%d" % i, [128, 258], BF16) for i in range(NPB_)]
            pgK = [sb2("pgK%d" % i, [128, 64], BF16) for i in range(NPB_)]
            qTs = sb2("qTs", [128, 3, NH, 128], BF16)
            OTs = sb2("OTs", [128, 2, NH, 128], BF16)
            acc_s = sb2("acc_s", [64, 257], F32)
            pKT2 = [sb2("pKT2_%d" % i, [128, 2, 3, 128], BF16) for i in range(2)]
            ps6b = ps[6][:, :].bitcast(BF16)
            tcount = [0]
            Obs = sb2("Obs", [64, 256], BF16)
            rdens = sb2("rdens", [64, 1], F32)
            for i in range(NPB_):
                S.op("dve", lambda e, i=i: e.memset(pgV[i][:, 256:258], 1.0), writes=["pgV%d" % i])
            print("arena top (attention phase):", top[0], ARENA)
            dma("pool", w_uqn[:], wuqn_d.rearrange("(k p) f -> p k f", p=128), writes=["w_uqn"])
            dma("pool", w_uqr[:], wuqr_d.rearrange("(k p) f -> p k f", p=128), writes=["w_uqr"])
            dma("pool", w_ukT[:], wukT_d[:, :], writes=["w_ukT"])
            dma("pool", w_uv[:], wuv_d.rearrange("(k p) f -> p k f", p=128), writes=["w_uv"])
            dma("pool", w_o[:], wo_d.rearrange("(h p) d -> p h d", p=128), writes=["w_o"])
            for i in range(2):
                S.op("dve", lambda e, i=i: e.memset(V[i][:, :, 256:258], 1.0), writes=["V%d" % i])
            agroups = []
            l = 0
            while l < NT:
                n = min(1, NT - l)
                agroups.append((l * 128, n * 128, list(range(l, l + n))))
                l += n
            agroups.append((NT * 128, 128, [NT]))
            kv_cnt = [0]
            NPOOL = cfg["NPOOL"]
            NPB = NPB_
            pcount = [0]

            def qproj(c0, ncol, qT, qk):
                dma("sp", cs2[:, 0:ncol], cos_d[NPASS - 1][:, c0:c0 + ncol], writes=["cs2"])
                dma("sp", sn2[:, 0:ncol], sin_d[NPASS - 1][:, c0:c0 + ncol], writes=["sn2"])
                for h in range(NH):
                    b = PSK[6]
                    bank = ps[6]
                    for k in range(3):
                        mm(bank[:, 0:ncol], w_uqn[:, k, h * 128:(h + 1) * 128], cqT[:, k, c0:c0 + ncol], k == 0, k == 2, ["w_uqn", ("cqT", k, c0)], [b])
                    q2 = h % 2
                    S.op("act", lambda e, q2=q2, ncol=ncol: e.activation(out=qn[q2][:, 0:ncol], in_=ps[6][:, 0:ncol], func=AF.Copy),
                         reads=[b], writes=["qn%d" % q2])
                    for rc in range(2):
                        mm(bank[:, 0:ncol], w_ukT[:, h * 256 + rc * 128:h * 256 + (rc + 1) * 128], qn[q2][:, 0:ncol], True, True, ["w_ukT", "qn%d" % q2], [b])
                        S.op("act", lambda e, rc=rc, h=h, ncol=ncol: e.activation(out=qT[:, rc, h, 0:ncol], in_=ps[6][:, 0:ncol], func=AF.Copy),
                             reads=[b], writes=[(qk, h)])
                    for half in range(2):
                        for k in range(3):
                            mm(bank[0:64, 0:ncol], w_uqr[:, k, h * 128 + half * 64:h * 128 + (half + 1) * 64], cqT[:, k, c0:c0 + ncol], k == 0, k == 2,
                               ["w_uqr", ("cqT", k, c0)], [b])
                        tab = cs2 if half == 0 else sn2
                        S.op("dve", lambda e, half=half, tab=tab, ncol=ncol: e.tensor_tensor(out=qrf[:, half, 0:ncol], in0=ps[6][0:64, 0:ncol], in1=tab[:, 0:ncol], op=ALU.mult),
                             reads=[b, "cs2", "sn2"], writes=[("qrf", half)])
                    S.op("dve", lambda e, h=h, ncol=ncol: e.tensor_tensor(out=qT[0:64, 2, h, 0:ncol], in0=qrf[:, 0, 0:ncol], in1=qrf[:, 1, 0:ncol], op=ALU.add),
                         reads=[("qrf", 0), ("qrf", 1)], writes=[(qk, h)])

            def sample_unit(s_, j0, j1):
                accb = PSK[0]
                accbank = ps[0]
                jl = list(range(j0, j1))
                groups_ = []
                while jl:
                    if len(jl) >= 2 and jl[0] != NPG and jl[1] != NPG:
                        groups_.append(jl[:2]); jl = jl[2:]
                    else:
                        groups_.append(jl[:1]); jl = jl[1:]
                for grp in groups_:
                    own_pg = (grp[0] == NPG)
                    ng = len(grp)
                    sb_ = PSK[4 + rot("sq", 2)]
                    sbank = ps[int(sb_[2:])]
                    if not own_pg:
                        tsel = tcount[0] % 2
                        tcount[0] += 1
                        tb_key = "psb" if tsel == 0 else PSK[6]
                        tbank = psb if tsel == 0 else ps6b
                        kbuf = tsel
                        pbs = []
                        for pp, j in enumerate(grp):
                            pb = pcount[0] % NPB
                            pcount[0] += 1
                            pbs.append(pb)
                            idx = s_ * NPG + j
                            S.op("pool", lambda e, idx=idx, pb=pb: e.indirect_dma_start(
                                out=pgV[pb][:, 0:256], out_offset=None, in_=clat_d.rearrange("n p c -> (n p) c"),
                                in_offset=bass.IndirectOffsetOnAxis(ap=idx_all[:, idx:idx + 1], axis=0)),
                                reads=["idx_all"], writes=["pgV%d" % pb], dma=True)
                            S.op("pool", lambda e, idx=idx, pb=pb: e.indirect_dma_start(
                                out=pgK[pb][:, :], out_offset=None, in_=ckr_d.rearrange("n p c -> (n p) c"),
                                in_offset=bass.IndirectOffsetOnAxis(ap=idx_all[:, idx:idx + 1], axis=0)),
                                reads=["idx_all"], writes=["pgK%d" % pb], dma=True)
                            for ch in range(2):
                                S.op("pe", lambda e, pb=pb, ch=ch, pp=pp, tbank=tbank: e.transpose(
                                    out=tbank[:, pp * 384 + ch * 128:pp * 384 + (ch + 1) * 128], in_=pgV[pb][:, ch * 128:(ch + 1) * 128], identity=ident_bf[:]),
                                    reads=["pgV%d" % pb, "ident_bf"], writes=[tb_key])
                            S.op("pe", lambda e, pb=pb, pp=pp, tbank=tbank: e.transpose(out=tbank[0:64, pp * 384 + 256:pp * 384 + 384], in_=pgK[pb][:, 0:64], identity=ident_bf[:]),
                                 reads=["pgK%d" % pb, "ident_bf"], writes=[tb_key])
                        S.op("act", lambda e, kbuf=kbuf, ng=ng, tbank=tbank: e.activation(
                            out=pKT2[kbuf][:, 0:ng, 0:2, :], in_=tbank[:, 0:ng * 384].rearrange("p (g c t) -> p g c t", g=ng, c=3)[:, :, 0:2, :], func=AF.Copy),
                            reads=[tb_key], writes=["pKT%d" % kbuf])
                        S.op("act", lambda e, kbuf=kbuf, ng=ng, tbank=tbank: e.activation(
                            out=pKT2[kbuf][0:64, 0:ng, 2, :], in_=tbank[0:64, 0:ng * 384].rearrange("p (g c t) -> p g c t", g=ng, c=3)[:, :, 2, :], func=AF.Copy),
                            reads=[tb_key], writes=["pKT%d" % kbuf])
                        srcs = [(pKT2[kbuf][:, pp], "pKT%d" % kbuf, pgV[pbs[pp]], "pgV%d" % pbs[pp]) for pp in range(ng)]
                    else:
                        srcs = [(sKT, "sKT", sV, "sV")]
                    for pp, (kt_ap, kt_key, v_ap, v_key) in enumerate(srcs):
                        for k in range(3):
                            kk = 128 if k < 2 else 64
                            mm(sbank[:, pp * 64:(pp + 1) * 64].rearrange("p (h t) -> p h t", h=NH), kt_ap[0:kk, k, :],
                               qTs[0:kk, k, :, s_ * 8:(s_ + 1) * 8], k == 0, k == 2, [kt_key] + [("qTs", hh) for hh in range(NH)], [sb_])
                    p3 = rot("tb", 3)
                    S.op("act", lambda e, sbank=sbank, p3=p3, ng=ng: e.activation(out=PT[p3][:, 0:ng * 64], in_=sbank[:, 0:ng * 64], func=AF.Exp, scale=SM_SCALE),
                         reads=[sb_], writes=["PT%d" % p3])
                    if own_pg:
                        S.op("dve", lambda e, p3=p3, s_=s_: e.tensor_tensor(
                            out=PT[p3][:, 0:64].rearrange("p (h t) -> p h t", h=NH), in0=PT[p3][:, 0:64].rearrange("p (h t) -> p h t", h=NH),
                            in1=smask_bf[:, s_:s_ + 1, :].broadcast_to([128, NH, 8]), op=ALU.mult),
                            reads=["PT%d" % p3, "smask_bf"], writes=["PT%d" % p3])
                    for pp, (kt_ap, kt_key, v_ap, v_key) in enumerate(srcs):
                        jj = grp[pp]
                        mm(accbank[0:64, 0:257], PT[p3][:, pp * 64:(pp + 1) * 64], v_ap[:, 0:257], jj == j0, jj == j1 - 1, ["PT%d" % p3, v_key], [accb])
                if j0 == 0:
                    S.op("dve", lambda e, accbank=accbank: e.tensor_copy(out=acc_s[:], in_=accbank[0:64, 0:257]), reads=[accb], writes=["acc_s"])
                else:
                    S.op("dve", lambda e, accbank=accbank: e.tensor_tensor(out=acc_s[:], in0=accbank[0:64, 0:257], in1=acc_s[:], op=ALU.add),
                         reads=[accb, "acc_s"], writes=["acc_s"])
                if j1 != NPG + 1:
                    return
                S.op("dve", lambda e: e.reciprocal(out=rdens[:], in_=acc_s[:, 256:257]), reads=["acc_s"], writes=["rdens"])
                S.op("dve", lambda e: e.tensor_scalar(out=Obs[:], in0=acc_s[:, 0:256], scalar1=rdens[:, 0:1], scalar2=None, op0=ALU.mult),
                     reads=["acc_s", "rdens"], writes=["Obs"])
                for rc in range(2):
                    S.op("pe", lambda e, rc=rc: e.transpose(out=psb[:, rc * 64:(rc + 1) * 64], in_=Obs[:, rc * 128:(rc + 1) * 128], identity=ident_bf[0:64, 0:64]),
                         reads=["Obs", "ident_bf"], writes=["psb"])
                for rc in range(2):
                    S.op("act", lambda e, rc=rc, s_=s_: e.activation(out=OTs[:, rc, :, s_ * 8:(s_ + 1) * 8],
                                                                    in_=psb[:, rc * 64:(rc + 1) * 64].rearrange("p (h t) -> p h t", h=NH), func=AF.Copy),
                         reads=["psb"], writes=[("OTs", 0)])

            def outproj(c0, ncol, OT, otkeys):
                for h in range(NH):
                    b = PSK[6]
                    for rc in range(2):
                        mm(ps[6][:, 0:ncol], w_uv[:, rc, h * 128:(h + 1) * 128], OT[:, rc, h, 0:ncol], rc == 0, rc == 1,
                           ["w_uv"] + otkeys, [b])
                    S.op("act", lambda e, h=h, ncol=ncol: e.activation(out=oT[:, h, 0:ncol], in_=ps[6][:, 0:ncol], func=AF.Copy), reads=[b], writes=[("oT", h)])
                for j in range(DC):
                    b = PSK[6]
                    for h in range(NH):
                        mm(ps[6][:, 0:ncol], w_o[:, h, j * 128:(j + 1) * 128], oT[:, h, 0:ncol], h == 0, h == NH - 1, ["w_o", ("oT", h)], [b])
                    S.op("dve", lambda e, j=j, c0=c0, ncol=ncol: e.tensor_tensor(out=hT[:, j, c0:c0 + ncol], in0=ps[6][:, 0:ncol], in1=hT[:, j, c0:c0 + ncol], op=ALU.add),
                         reads=[b, ("hT", j, c0)], writes=[("hT", j, c0)])


            sc0 = NT * 128
            if cfg["SAMPLE_ATT"]:
                qproj(sc0, 128, qTs, "qTs")
            units = []
            UP = 8
            for s__ in range(DPC):
                for j0 in range(0, NPG, UP):
                    j1 = min(NPG, j0 + UP)
                    units.append((s__, j0, j1 + 1 if j1 == NPG else j1))
            n_slots = sum(VR * ((l_ + 8) // 8) for l_ in range(NT))
            per_slot = -(-len(units) // max(1, n_slots))
            upos = [0]

            def pop_units(n):
                if not cfg["SAMPLE_ATT"]:
                    return
                for _ in range(n):
                    if upos[0] < len(units):
                        sample_unit(*units[upos[0]])
                        upos[0] += 1
            for gi, (c0, ncol, tiles) in enumerate(agroups[:-1]):
                qproj(c0, ncol, qT, "qT")
                for ti, l in enumerate(tiles):
                    m_ = l
                    first_chunk = True
                    for r in range(VR):
                      for s0 in range(0, m_ + 1, 8):
                        nsl = min(8, m_ + 1 - s0)
                        dg = m_ - s0
                        kb = kv_cnt[0] % 2
                        kv_cnt[0] += 1
                        dma("sp", KT[kb][:, 0:2, 0:nsl * 128],
                            cx_all.ap()[r * 576:r * 576 + 256, s0 * 128:(s0 + nsl) * 128].rearrange("(k p) t -> p k t", p=128),
                            reads=["cx_all"], writes=["KT%d" % kb])
                        dma("sp", KT[kb][0:64, 2, 0:nsl * 128],
                            cx_all.ap()[r * 576 + 256:r * 576 + 320, s0 * 128:(s0 + nsl) * 128],
                            reads=["cx_all"], writes=["KT%d" % kb])
                        dma("sp", V[kb][:, 0:nsl, 0:256],
                            c_tok_view(cx_all, r)[s0 * 128:(s0 + nsl) * 128, :].rearrange("(s p) c -> p s c", p=128),
                            reads=["cx_all"], writes=["V%d" % kb])
                        for hg in range(2):
                            for sl in range(nsl):
                                sb_ = PSK[4 + rot("sq", 2)]
                                sbank = ps[int(sb_[2:])]
                                for k in range(3):
                                    kk = 128 if k < 2 else 64
                                    mm(sbank[:, :].rearrange("p (h t) -> p h t", h=4), KT[kb][0:kk, k, sl * 128:(sl + 1) * 128],
                                       qT[0:kk, k, hg * 4:hg * 4 + 4, ti * 128:(ti + 1) * 128], k == 0, k == 2,
                                       ["KT%d" % kb] + [("qT", hg * 4 + hh) for hh in range(4)], [sb_])
                                p3 = rot("tb", 3)
                                S.op("act", lambda e, sbank=sbank, p3=p3: e.activation(out=PT[p3][:], in_=sbank[:, :], func=AF.Exp, scale=SM_SCALE),
                                     reads=[sb_], writes=["PT%d" % p3])
                                if dg == sl:
                                    mi = r * 2 + (m_ % 2)
                                    S.op("dve", lambda e, p3=p3, mi=mi: e.tensor_tensor(
                                        out=PT[p3][:, :].rearrange("p (h t) -> p h t", h=4), in0=PT[p3][:, :].rearrange("p (h t) -> p h t", h=4),
                                        in1=mask_bf[:, mi:mi + 1, :].broadcast_to([128, 4, 128]), op=ALU.mult),
                                         reads=["PT%d" % p3, "mask_bf"], writes=["PT%d" % p3])
                                for hh in range(4):
                                    mm(ps[hh][:, 0:257], PT[p3][:, hh * 128:(hh + 1) * 128], V[kb][:, sl, 0:257], sl == 0, sl == nsl - 1,
                                       ["PT%d" % p3, "V%d" % kb], [PSK[hh]])
                            for hh in range(4):
                                h = hg * 4 + hh
                                if first_chunk:
                                    S.op("dve", lambda e, hh=hh, h=h: e.tensor_copy(out=acc[:, h, :], in_=ps[hh][:, 0:257]), reads=[PSK[hh]], writes=[("acc", h)])
                                else:
                                    S.op("dve", lambda e, hh=hh, h=h: e.tensor_tensor(out=acc[:, h, :], in0=ps[hh][:, 0:257], in1=acc[:, h, :], op=ALU.add),
                                         reads=[PSK[hh], ("acc", h)], writes=[("acc", h)])
                        first_chunk = False
                        pop_units(per_slot)
                    S.op("dve", lambda e: e.reciprocal(out=rden[:], in_=acc[:, :, 256]), reads=[("acc", h) for h in range(NH)], writes=["rden"])
                    for h in range(NH):
                        S.op("dve", lambda e, h=h: e.tensor_scalar(out=Ob[:, h, :], in0=acc[:, h, 0:256], scalar1=rden[:, h:h + 1], scalar2=None, op0=ALU.mult),
                             reads=[("acc", h), "rden"], writes=[("Ob", h)])
                    for h4 in range(2):
                        for hh in range(4):
                            h = h4 * 4 + hh
                            for rc in range(2):
                                S.op("pe", lambda e, h=h, hh=hh, rc=rc: e.transpose(out=psb[:, (hh * 2 + rc) * 128:(hh * 2 + rc + 1) * 128],
                                                                                   in_=Ob[:, h, rc * 128:(rc + 1) * 128], identity=ident_bf[:]),
                                     reads=[("Ob", h), "ident_bf"], writes=["psb"])
                        for rc in range(2):
                            S.op("act", lambda e, h4=h4, rc=rc, ti=ti: e.activation(
                                out=OT[:, rc, h4 * 4:h4 * 4 + 4, ti * 128:(ti + 1) * 128],
                                in_=psb[:, :].rearrange("p (h r t) -> p r h t", h=4, r=2)[:, rc], func=AF.Copy),
                                reads=["psb"], writes=[("OT", ti)])
                outproj(c0, ncol, OT, [("OT", ti) for ti in range(len(tiles))])
            if cfg["SAMPLE_ATT"]:
                pop_units(len(units))
                outproj(sc0, 128, OTs, [("OTs", 0)])

        S.barrier()
        top[0] = M_1
        mlp(1, G_MLP1)
        ple(1, G_PLE1)
        for (c0, ncol, tiles) in groups:
            yst = tmpf
            norm_T(lambda ch, c0=c0, ncol=ncol: hT[:, ch, c0:c0 + ncol], lambda ch, c0=c0: [("hT", ch, c0)], DC, D, G_FIN,
                   lambda ch, c0=c0, ncol=ncol: hT[:, ch, c0:c0 + ncol], lambda ch, c0=c0: [("hT", ch, c0)], c0, ncol)
            dma("sp", yT_o.rearrange("(c p) t -> p c t", p=128)[:, :, c0:c0 + ncol], hT[:, :, c0:c0 + ncol],
                reads=[("hT", ch, c0) for ch in range(DC)], final=True)
        S.emit()
    return nc


def vr_blocks(cfg, v):
    VR = cfg["NCORE"] // cfg["BATCH"]
    out = []
    for i in range(cfg["NBLK"] // (2 * VR)):
        out.append(2 * VR * i + v)
        out.append(2 * VR * i + 2 * VR - 1 - v)
    return out


def _pass_vranks(cfg, v):
    VR = cfg["NCORE"] // cfg["BATCH"]
    return [(v + 1 + j) % VR for j in range(VR)]


def _tables(cfg, k):
    NCORE, BATCH, NBLK = cfg["NCORE"], cfg["BATCH"], cfg["NBLK"]
    VR = NCORE // BATCH
    v = k % VR
    DPC = cfg["DEC"] // NCORE
    past = cfg["NPG"] * 128
    half = ROPE // 2
    inv_freq = np.power(np.float32(THETA), -np.arange(half, dtype=np.float32) / np.float32(half)).astype(np.float32)
    t = np.arange(128)
    amain = np.zeros((4, 128, 128), np.float32)
    afirst0 = np.zeros((4, 128, 128), np.float32)
    ahalo = np.zeros((4, 16, 128), np.float32)
    amain_s = np.zeros((4, 128, 128), np.float32)
    ahalo_s = np.zeros((2, 4, 120, 128), np.float32)
    for g, w in enumerate(POOL_W):
        diff = t[None, :] - t[:, None]
        inw = (diff >= 0) & (diff < w)
        amain[g] = inw / np.float32(w) - np.eye(128, dtype=np.float32)
        cntv = np.minimum(t + 1, w).astype(np.float32)
        afirst0[g] = inw / cntv[None, :] - np.eye(128, dtype=np.float32)
        i = np.arange(16)
        ahalo[g] = (i[:, None] >= t[None, :] + 17 - w) / np.float32(w)
        for s_ in range(16):
            for ii in range(8):
                for j in range(8):
                    amain_s[g, s_ * 8 + ii, s_ * 8 + j] = (1.0 / w if 0 <= j - ii < w else 0.0) - (1.0 if ii == j else 0.0)
        for a in range(2):
            for sl in range(8):
                for r in range(15):
                    for j in range(8):
                        if r > j + 15 - w:
                            ahalo_s[a, g, sl * 15 + r, (a * 8 + sl) * 8 + j] = 1.0 / w
    cosT, sinT, afirst = [], [], []
    for vv in _pass_vranks(cfg, v):
        blocks = vr_blocks(cfg, vv)
        pos = [np.arange(blk * 128, (blk + 1) * 128) for blk in blocks]
        pos.append(np.tile(past + np.arange(cfg["DSEQ"]), DPC))
        pos = np.concatenate(pos)
        ang = (pos.astype(np.float32)[:, None] * inv_freq[None, :]).astype(np.float32)
        cos = np.cos(ang).astype(np.float32)
        sin = np.sin(ang).astype(np.float32)
        cosT.append(np.concatenate([cos, cos], 1).T)
        sinT.append(np.concatenate([-sin, sin], 1).T)
        afirst.append(afirst0 if blocks[0] == 0 else amain)
    tri = (t[:, None] <= t[None, :]).astype(np.float32)
    bmask = np.zeros((VR * 2, 128, 128), np.float32)
    for rel, vv in enumerate(_pass_vranks(cfg, v)):
        bmask[rel * 2 + 0] = 1.0 if vv < v else (tri if vv == v else 0.0)
        bmask[rel * 2 + 1] = 1.0 if vv > v else (tri if vv == v else 0.0)
    smask = np.zeros((128, 16, 8), np.float32)
    for key in range(128):
        for i in range(8):
            if key % 8 <= i:
                smask[key, key // 8, i] = 1.0
    f = lambda a: np.ascontiguousarray(np.asarray(a, dtype=np.float32))
    return dict(pidx=np.arange(128, dtype=np.float32).reshape(128, 1), smask=smask.reshape(128, 128), cosT=f(np.stack(cosT)), sinT=f(np.stack(sinT)), amain=amain, afirst=f(np.stack(afirst)), ahalo=ahalo, amain_s=amain_s,
                ahalo_s=ahalo_s, bmask=bmask, identf=np.eye(128, dtype=np.float32))


def _prep(cfg, inp):
    NCORE, BATCH, NBLK = cfg["NCORE"], cfg["BATCH"], cfg["NBLK"]
    VR = NCORE // BATCH
    DPC = cfg["DEC"] // NCORE
    f = lambda a: np.ascontiguousarray(np.asarray(a, dtype=np.float32))
    xp, xs = f(inp["x_prompt"]), f(inp["x_sample"])
    pp, psm = f(inp["p_prompt"]), f(inp["p_sample"])
    stp = f(inp["state_pool"])

    def chunks(vv):
        return f(vv).reshape(-1, 128).T
    gvec = np.zeros((128, 80), np.float32)
    for col, vv in ((0, inp["norm_mix"][1]), (8, inp["norm_mlp"][0]), (16, inp["norm_mlp"][1]), (24, inp["norm_ple"][0]), (32, inp["norm_ple"][1]),
                    (40, inp["pool_scale"][0]), (48, inp["norm_kv"]), (56, inp["norm_final"]), (64, inp["q_norm"][0]), (67, inp["kv_norm"])):
        c = chunks(vv)
        gvec[:, col:col + c.shape[1]] = c
    wdkv = f(inp["w_dkv"])
    w_dkv4 = np.concatenate([wdkv[:, :KVR + ROPE], wdkv[:, KVR + 32:KVR + 64], wdkv[:, KVR:KVR + 32]], 1)
    wuq = f(inp["w_uq"][0])
    w_uq_nope = wuq[:, :, :NOPE].reshape(QR, NH * 128)
    rp = wuq[:, :, NOPE:]
    w_uq_rope = np.concatenate([rp, rp[:, :, 32:], rp[:, :, :32]], 2).reshape(QR, NH * 128)
    w_ukT = f(np.transpose(f(inp["w_uk"]), (2, 1, 0))).reshape(128, NH * 256)
    shared = dict(gvec=gvec, gmix0=f(inp["norm_mix"][0])[None, :], pool_w=f(inp["pool_w"][0]), w_up=f(inp["w_up"]), w_down=f(inp["w_down"]),
                  w_gate=f(inp["w_ple_gate"]), w_proj=f(inp["w_ple_proj"]), w_dkv4=f(w_dkv4), w_dq=f(inp["w_dq"][0]),
                  w_uq_nope=f(w_uq_nope), w_uq_rope=f(w_uq_rope), w_ukT=w_ukT, w_uv=f(inp["w_uv"]).reshape(KVR, NH * 128),
                  w_o=f(inp["w_o"][0]).reshape(NH * 128, D))
    if cfg["SAMPLE_ATT"]:
        shared["cache_latent"] = f(inp["cache_latent"])
        shared["cache_krope"] = f(inp["cache_krope"])
    maps = []
    for k in range(NCORE):
        b, v = k // VR, k % VR
        xs_k = xs[k * DPC:(k + 1) * DPC].reshape(128, D)
        xtok_l, xT_l, halo_l, pT0_l = [], [], [], []
        for vv in _pass_vranks(cfg, v):
            blocks = vr_blocks(cfg, vv)
            xt_ = np.concatenate([xp[b, blk * 128:(blk + 1) * 128] for blk in blocks] + [xs_k], 0)
            p0 = np.concatenate([pp[0, b, blk * 128:(blk + 1) * 128] for blk in blocks] + [psm[0, k * DPC:(k + 1) * DPC].reshape(128, PLE)], 0)
            hal = np.zeros((len(blocks), 16, D), np.float32)
            for i_, blk in enumerate(blocks):
                if blk > 0:
                    hal[i_] = xp[b, blk * 128 - 16:blk * 128]
            xtok_l.append(xt_)
            xT_l.append(xt_.T)
            halo_l.append(hal)
            pT0_l.append(p0.T)
        blocks = vr_blocks(cfg, v)
        p1 = np.concatenate([pp[1, b, blk * 128:(blk + 1) * 128] for blk in blocks] + [psm[1, k * DPC:(k + 1) * DPC].reshape(128, PLE)], 0)
        m = dict(shared)
        m.update(_tables(cfg, k))
        m["xtok"] = f(np.stack(xtok_l))
        m["xT"] = f(np.stack(xT_l))
        m["xhalo"] = f(np.stack(halo_l))
        m["pT0"] = f(np.stack(pT0_l))
        m["pT1"] = f(p1.T)
        sp = stp[0, k * DPC:(k + 1) * DPC]
        m["spin"] = f(sp)
        m["uhalo"] = f(sp.reshape(2, 120, D))
        if cfg["SAMPLE_ATT"]:
            m["ptab"] = np.ascontiguousarray(np.asarray(inp["page_table"])[k * DPC:(k + 1) * DPC].astype(np.int32).reshape(1, -1))
        maps.append(m)
    return maps


def _assemble(cfg, res):
    NCORE, BATCH, NBLK = cfg["NCORE"], cfg["BATCH"], cfg["NBLK"]
    VR = NCORE // BATCH
    DPC = cfg["DEC"] // NCORE
    SEQ = NBLK * 128
    y_p = np.zeros((BATCH, SEQ, D), np.float32)
    lat_p = np.zeros((BATCH, SEQ, KVR), np.float32)
    kr_p = np.zeros((BATCH, SEQ, ROPE), np.float32)
    y_s = np.zeros((cfg["DEC"], cfg["DSEQ"], D), np.float32)
    lat_s = np.zeros((cfg["DEC"], cfg["DSEQ"], KVR), np.float32)
    kr_s = np.zeros((cfg["DEC"], cfg["DSEQ"], ROPE), np.float32)
    pool_s = np.zeros((1, cfg["DEC"], 15, D), np.float32)
    pool_p = np.zeros((1, BATCH, 15, D), np.float32)
    for k in range(NCORE):
        r = res[k]
        b, v = k // VR, k % VR
        yT, latT, krT = np.asarray(r["yT"]), np.asarray(r["latT"]), np.asarray(r["krT"])
        l = 0
        for blk in vr_blocks(cfg, v):
            sl = slice(l * 128, (l + 1) * 128)
            y_p[b, blk * 128:(blk + 1) * 128] = yT[:, sl].T
            lat_p[b, blk * 128:(blk + 1) * 128] = latT[:, sl].T
            kr_p[b, blk * 128:(blk + 1) * 128] = krT[:, sl].T
            l += 1
        sl = slice(l * 128, (l + 1) * 128)
        y_s[k * DPC:(k + 1) * DPC] = yT[:, sl].T.reshape(DPC, cfg["DSEQ"], D)
        lat_s[k * DPC:(k + 1) * DPC] = latT[:, sl].T.reshape(DPC, cfg["DSEQ"], KVR)
        kr_s[k * DPC:(k + 1) * DPC] = krT[:, sl].T.reshape(DPC, cfg["DSEQ"], ROPE)
        pool_s[0, k * DPC:(k + 1) * DPC] = np.asarray(r["pools"])
        if v == 0:
            pool_p[0, b] = np.asarray(r["poolp"])
    return (y_p, y_s, pool_p, pool_s, lat_p, kr_p, lat_s, kr_s)


_NC_CACHE = {}


def kernel(**inputs):
    cfg = dict(FULL)
    key = "full"
    if key not in _NC_CACHE:
        _NC_CACHE[key] = build_program(cfg)
    nc = _NC_CACHE[key]
    maps = _prep(cfg, inputs)
    res = run_bass_kernel_spmd(nc, maps, core_ids=list(range(cfg["NCORE"])))
    return _assemble(cfg, res.results)
```

```python
import contextlib
import math
import numpy as np
import concourse.bass as bass
import concourse.mybir as mybir
from concourse.bass_utils import run_bass_kernel_spmd

F32 = mybir.dt.float32
BF16 = mybir.dt.bfloat16
I32 = mybir.dt.int32
AF = mybir.ActivationFunctionType
ALU = mybir.AluOpType

COMPUTE = ("pe", "act", "dve", "pool")
N_DMA_SLOTS = {"sp": 8, "pool": 4, "act": 2, "pe": 1, "dve": 1}


class Sched:
    def __init__(self, nc):
        self.nc = nc
        self.eng_names = ("pe", "act", "dve", "pool", "sp")
        self.prog = {e: [] for e in self.eng_names}
        self.n_comp = {e: 0 for e in self.eng_names}
        self.n_dma = {e: 0 for e in self.eng_names}
        self.slot_cnt = {}
        self.last_w = {}
        self.readers = {}
        self.known = {e: {} for e in self.eng_names}
        self.targets = {e: set() for e in self.eng_names}
        self.final_dma = []
        self.n_cc = 0

    def _add_dep(self, deps, tok, eng):
        if tok is None:
            return
        if tok[0] == "c":
            if tok[1] == eng and eng == "pe":
                return
            lane = ("c", tok[1])
            v = tok[2]
        else:
            lane = (tok[0], tok[1], tok[2])
            v = tok[3]
        if deps.get(lane, 0) < v:
            deps[lane] = v

    def op(self, eng, fn, reads=(), writes=(), dma=False, final=False, cc=False):
        deps = {}
        for k in reads:
            self._add_dep(deps, self.last_w.get(k), eng)
        for k in writes:
            self._add_dep(deps, self.last_w.get(k), eng)
            for t in self.readers.get(k, ()):
                self._add_dep(deps, t, eng)
        if cc:
            self.n_cc += 1
            tok = ("x", "cc", self.n_cc, 1)
        elif dma:
            n = self.n_dma[eng]
            self.n_dma[eng] = n + 1
            slot = n % N_DMA_SLOTS[eng]
            cnt = self.slot_cnt.get((eng, slot), 0) + 1
            self.slot_cnt[(eng, slot)] = cnt
            tok = ("d", eng, slot, cnt)
            if cnt > 1:
                lane = ("d", eng, slot)
                if deps.get(lane, 0) < cnt - 1:
                    deps[lane] = cnt - 1
        else:
            self.n_comp[eng] += 1
            tok = ("c", eng, self.n_comp[eng])
        waits = []
        kn = self.known[eng]
        for lane, v in deps.items():
            if kn.get(lane, 0) >= v:
                continue
            kn[lane] = v
            waits.append((lane, v))
            if lane[0] == "c":
                self.targets[lane[1]].add(v)
        self.prog[eng].append((waits, fn, tok))
        for k in reads:
            self.readers.setdefault(k, []).append(tok)
        for k in writes:
            self.last_w[k] = tok
            self.readers[k] = []
        if final:
            self.final_dma.append(tok)
        return tok

    def barrier(self):
        for eng in self.eng_names:
            deps = {}
            for F in COMPUTE:
                if self.n_comp[F] > 0 and not (F == eng and eng == "pe"):
                    deps[("c", F)] = self.n_comp[F]
            for (e, s_), c in self.slot_cnt.items():
                deps[("d", e, s_)] = c
            for i in range(1, self.n_cc + 1):
                deps[("x", "cc", i)] = 1
            waits = []
            kn = self.known[eng]
            for lane, v in deps.items():
                if kn.get(lane, 0) >= v:
                    continue
                kn[lane] = v
                waits.append((lane, v))
                if lane[0] == "c":
                    self.targets[lane[1]].add(v)
            self.prog[eng].append((waits, None, None))

    def emit(self):
        nc = self.nc
        with contextlib.ExitStack() as st:
            dsem = {}
            for i in range(1, self.n_cc + 1):
                dsem[("x", "cc", i)] = st.enter_context(nc.semaphore("cc_%d" % i))
            csem = {e: st.enter_context(nc.semaphore("c_" + e)) for e in COMPUTE}
            for (e, s) in self.slot_cnt:
                dsem[("d", e, s)] = st.enter_context(nc.semaphore("d_%s_%d" % (e, s)))
            cum = {}
            for e in COMPUTE:
                tg = sorted(self.targets[e])
                cum[e] = {idx: i + 1 for i, idx in enumerate(tg)}
            fin = {}
            for tok in self.final_dma:
                lane = ("d", tok[1], tok[2])
                fin[lane] = max(fin.get(lane, 0), tok[3])
            block = st.enter_context(nc.Block())
            engobj = {"pe": "tensor", "act": "scalar", "dve": "vector", "pool": "gpsimd", "sp": "sync"}

            def build(ename):
                def body(eng):
                    for waits, fn, tok in self.prog[ename]:
                        for lane, v in waits:
                            if lane[0] == "c":
                                eng.wait_ge(csem[lane[1]], cum[lane[1]][v])
                            elif lane[0] == "d":
                                eng.wait_ge(dsem[lane], 16 * v)
                            else:
                                eng.wait_ge(dsem[lane], v)
                        if fn is None:
                            continue
                        ins = fn(eng)
                        if tok[0] == "c":
                            if tok[2] in cum[tok[1]]:
                                ins.then_inc(csem[tok[1]], 1)
                        elif tok[0] == "d":
                            ins.then_inc(dsem[(tok[0], tok[1], tok[2])], 16)
                        else:
                            ins.then_inc(dsem[(tok[0], tok[1], tok[2])], 1)
                    if ename == "sp":
                        for lane, v in fin.items():
                            eng.wait_ge(dsem[lane], 16 * v)
                return body

            for ename in self.eng_names:
                if not self.prog[ename] and ename != "sp":
                    continue
                getattr(block, engobj[ename])(build(ename))


D = 1024
DC = 8
KVR = 256
QR = 384
NH = 8
NOPE = 128
ROPE = 64
PLE = 256
POOL_W = (2, 4, 8, 16)
EPS = 1e-6
SM_SCALE = 1.0 / math.sqrt(NOPE + ROPE)
THETA = 10000.0

FULL = dict(NCORE=8, BATCH=2, NBLK=64, DEC=128, DSEQ=8, NPG=64, NPOOL=10240, FF=4096, USE_CC=False, SAMPLE_ATT=True)


def core_blocks(cfg, k):
    nc_, nb = cfg["NCORE"], cfg["NBLK"]
    out = []
    for i in range(nb // (2 * nc_)):
        out.append(2 * nc_ * i + k)
        out.append(2 * nc_ * i + 2 * nc_ - 1 - k)
    return out


def owner_of(cfg, j):
    nc_ = cfg["NCORE"]
    r = j % (2 * nc_)
    k = r if r < nc_ else 2 * nc_ - 1 - r
    m = (j // (2 * nc_)) * 2 + (0 if r < nc_ else 1)
    return k, m


def build_program(cfg):
    NCORE, BATCH, NBLK, FF = cfg["NCORE"], cfg["BATCH"], cfg["NBLK"], cfg["FF"]
    VR = NCORE // BATCH
    NPASS = VR
    NTB = NBLK // VR
    NT = NTB
    T = (NT + 1) * 128
    TP = NT * 128
    NFB = FF // 512
    DPC = cfg["DEC"] // NCORE
    NPG = cfg["NPG"]
    assert DPC * cfg["DSEQ"] == 128
    groups = []
    l = 0
    while l < NT:
        n = min(4, NT - l)
        groups.append((l * 128, n * 128, list(range(l, l + n))))
        l += n
    groups.append((NT * 128, 128, [NT]))
    NG = len(groups)
    groups_all = groups

    nc = bass.Bass("TRN2", target_bir_lowering=False)

    def din(name, shape, dt=F32):
        return nc.dram_tensor(name, list(shape), dt, kind="ExternalInput").ap()

    def dout(name, shape, dt=F32):
        return nc.dram_tensor(name, list(shape), dt, kind="ExternalOutput").ap()

    xT_d = din("xT", [NPASS, D, T])
    xtok_d = din("xtok", [NPASS, T, D])
    xhalo_d = din("xhalo", [NPASS, NT, 16, D])
    uhalo_d = din("uhalo", [2, 120, D])
    spin_d = din("spin", [DPC, 15, D])
    pT0_d = din("pT0", [NPASS, PLE, T])
    pT1_d = din("pT1", [PLE, T])
    gvec_d = din("gvec", [128, 80])
    gmix0_d = din("gmix0", [1, D])
    cos_d = din("cosT", [NPASS, 64, T])
    sin_d = din("sinT", [NPASS, 64, T])
    amain_d = din("amain", [4, 128, 128])
    afirst_d = din("afirst", [NPASS, 4, 128, 128])
    ahalo_d = din("ahalo", [4, 16, 128])
    amain_s_d = din("amain_s", [4, 128, 128])
    ahalo_s_d = din("ahalo_s", [2, 4, 120, 128])
    bmask_d = din("bmask", [VR * 2, 128, 128])
    identf_d = din("identf", [128, 128])
    poolw_d = din("pool_w", [4, 256, 256])
    wup_d = din("w_up", [2, D, FF])
    wdn_d = din("w_down", [2, FF, D])
    wg_d = din("w_gate", [2, D, D])
    wp_d = din("w_proj", [2, PLE, D])
    wdkv_d = din("w_dkv4", [D, 384])
    wdq_d = din("w_dq", [D, QR])
    wuqn_d = din("w_uq_nope", [QR, NH * 128])
    wuqr_d = din("w_uq_rope", [QR, NH * 128])
    wukT_d = din("w_ukT", [128, NH * 256])
    wuv_d = din("w_uv", [KVR, NH * 128])
    wo_d = din("w_o", [NH * 128, D])
    if cfg["SAMPLE_ATT"]:
        clat_d = din("cache_latent", [cfg["NPOOL"], 128, KVR])
        ckr_d = din("cache_krope", [cfg["NPOOL"], 128, ROPE])
        ptab_d = din("ptab", [1, DPC * NPG], I32)
        smask_d = din("smask", [128, 128])
        pidx_d = din("pidx", [128, 1])

    yT_o = dout("yT", [D, T])
    latT_o = dout("latT", [KVR, T])
    krT_o = dout("krT", [ROPE, T])
    poolp_o = dout("poolp", [15, D])
    pools_o = dout("pools", [DPC, 15, D])

    cx_all = nc.dram_tensor("cx_all", [VR * 576, TP], BF16)

    def c_tok_view(t, r):
        return t.ap()[r * 576 + 320:(r + 1) * 576, :].rearrange("a (t c) -> (a t) c", c=256)

    S = Sched(nc)
    st = contextlib.ExitStack()
    with st:
        ARENA = 206 * 1024
        arena = st.enter_context(nc.sbuf_tensor("arena", [128, ARENA], mybir.dt.uint8))
        top = [0]

        def sb(name, shape, dt):
            nb = 2 if dt == BF16 else 4
            per = int(np.prod(shape[1:])) * nb
            off = (top[0] + 63) // 64 * 64
            assert off + per <= ARENA, (name, off, per, ARENA)
            top[0] = off + per
            v = arena[0:shape[0], off:off + per].bitcast(dt)
            if len(shape) == 3:
                v = v.rearrange("p (a b) -> p a b", a=shape[1])
            elif len(shape) == 4:
                v = v.rearrange("p (a b c) -> p a b c", a=shape[1], b=shape[2])
            return v

        ps = [st.enter_context(nc.psum_tensor("ps%d" % i, [128, 512], F32)) for i in range(7)]
        psb = st.enter_context(nc.psum_tensor("psb", [128, 1024], BF16))
        PSK = ["ps%d" % i for i in range(7)]

        hT = sb("hT", [128, DC, T], F32)
        gvec = sb("gvec", [128, 80], F32)
        ones_bf = sb("ones_bf", [128, 128], BF16)
        ident_bf = sb("ident_bf", [128, 128], BF16)
        mask_bf = sb("mask_bf", [128, VR * 2, 128], BF16)
        sqb = [sb("sqb%d" % i, [128, 512], BF16) for i in range(2)]
        rstd = [sb("rstd%d" % i, [128, 512], F32) for i in range(2)]
        tmpf = [sb("tmpf%d" % i, [128, 512], F32) for i in range(2)]
        tmpb = [sb("tmpb%d" % i, [128, 512], BF16) for i in range(3)]
        sKT = sb("sKT", [128, 3, 128], BF16)
        sV = sb("sV", [128, 258], BF16)
        smask_bf = sb("smask_bf", [128, 16, 8], BF16)
        idx_all = sb("idx_all", [128, DPC * NPG], I32)
        pidx = sb("pidx", [128, 1], F32)
        cqT = sb("cqT", [128, 3, T], BF16)
        M_U = top[0]
        uT = sb("uT", [128, DC, T], BF16)
        wA = [sb("wA%d" % i, [128, 8, 512], BF16) for i in range(2)]
        wB = [sb("wB%d" % i, [128, 4, 1024], BF16) for i in range(2)]
        aT = [sb("aT%d" % i, [128, 4, 512], BF16) for i in range(2)]
        M_1 = top[0]

        G_MIX1, G_MLP0, G_MLP1, G_PLE0, G_PLE1, G_PSC, G_KV, G_FIN, G_Q, G_KVN = 0, 8, 16, 24, 32, 40, 48, 56, 64, 67

        cnt = {"ps": 0, "sq": 0, "rs": 0, "tf": 0, "tb": 0, "up4": 0, "dn3": 0}

        def rot(kind, n):
            v = cnt[kind]
            cnt[kind] = (v + 1) % n
            return v

        def dma(eng, out, in_, reads=(), writes=(), final=False, **kw):
            return S.op(eng, lambda e: e.dma_start(out=out, in_=in_, **kw), reads=reads, writes=writes, dma=True, final=final)

        def mm(out, lhsT, rhs, start, stop, reads, writes):
            S.op("pe", lambda e: e.matmul(out, lhsT=lhsT, rhs=rhs, start=start, stop=stop), reads=reads, writes=writes)

        dma("sp", gvec[:], gvec_d[:, :], writes=["gvec"])
        dma("pool", ident_bf[:], identf_d[:, :], writes=["ident_bf"])
        dma("pool", mask_bf[:], bmask_d.rearrange("m p t -> p m t"), writes=["mask_bf"])
        S.op("dve", lambda e: e.memset(ones_bf[:], 1.0), writes=["ones_bf"])
        if cfg["SAMPLE_ATT"]:
            dma("pool", smask_bf[:], smask_d.rearrange("p (s i) -> p s i", i=8), writes=["smask_bf"])
            dma("sp", idx_all[:], ptab_d[0:1, :].broadcast_to([128, DPC * NPG]), writes=["idx_all"])
            dma("sp", pidx[:], pidx_d[:, :], writes=["pidx"])
            S.op("dve", lambda e: e.tensor_scalar(out=idx_all[:], in0=idx_all[:], scalar1=128.0, scalar2=pidx[:, 0:1], op0=ALU.mult, op1=ALU.add),
                 reads=["idx_all", "pidx"], writes=["idx_all"])
            S.op("dve", lambda e: e.memset(sV[:, 256:258], 1.0), writes=["sV"])
        for pj in range(NPASS):
            own = (pj == NPASS - 1)
            groups = groups_all if own else groups_all[:-1]
            NG = len(groups)
            S.barrier()
            top[0] = M_1
            for (c0, ncol, tiles) in groups:
                gi = c0
                dma("sp", hT[:, :, c0:c0 + ncol], xT_d[pj].rearrange("(c p) t -> p c t", p=128)[:, :, c0:c0 + ncol],
                    writes=[("hT", ch, gi) for ch in range(DC)])

            def norm_T(src_fn, src_keys_fn, nch, dim, gcol, out_fn, out_keys_fn, g_id, ncol, eps=EPS, out2_fn=None, out2_keys_fn=None):
                b = PSK[rot("ps", 2)]
                bank = ps[int(b[2:])]
                for ch in range(nch):
                    q = rot("sq", 2)
                    S.op("act", lambda e, ch=ch, q=q: e.activation(out=sqb[q][:, 0:ncol], in_=src_fn(ch), func=AF.Square),
                         reads=src_keys_fn(ch), writes=["sqb%d" % q])
                    mm(bank[:, 0:ncol], ones_bf[:], sqb[q][:, 0:ncol], ch == 0, ch == nch - 1, ["ones_bf", "sqb%d" % q], [b])
                r = rot("rs", 2)
                S.op("act", lambda e: e.activation(out=rstd[r][:, 0:ncol], in_=bank[:, 0:ncol], func=AF.Ln, scale=1.0 / dim, bias=eps),
                     reads=[b], writes=["rstd%d" % r])
                S.op("act", lambda e: e.activation(out=rstd[r][:, 0:ncol], in_=rstd[r][:, 0:ncol], func=AF.Exp, scale=-0.5),
                     reads=["rstd%d" % r], writes=["rstd%d" % r])
                for ch in range(nch):
                    S.op("dve", lambda e, ch=ch: e.scalar_tensor_tensor(out=out_fn(ch), in0=src_fn(ch), scalar=gvec[:, gcol + ch:gcol + ch + 1],
                                                                        in1=rstd[r][:, 0:ncol], op0=ALU.mult, op1=ALU.mult),
                         reads=list(src_keys_fn(ch)) + ["gvec", "rstd%d" % r], writes=out_keys_fn(ch))
                    if out2_fn is not None:
                        S.op("dve", lambda e, ch=ch: e.scalar_tensor_tensor(out=out2_fn(ch), in0=src_fn(ch), scalar=gvec[:, gcol + ch:gcol + ch + 1],
                                                                            in1=rstd[r][:, 0:ncol], op0=ALU.mult, op1=ALU.mult),
                             reads=list(src_keys_fn(ch)) + ["gvec", "rstd%d" % r], writes=out2_keys_fn(ch))

            def norm_h(gcol):
                for (c0, ncol, tiles) in groups:
                    norm_T(lambda ch, c0=c0, ncol=ncol: hT[:, ch, c0:c0 + ncol], lambda ch, c0=c0: [("hT", ch, c0)], DC, D, gcol,
                           lambda ch, c0=c0, ncol=ncol: uT[:, ch, c0:c0 + ncol], lambda ch, c0=c0: [("uT", ch, c0)], c0, ncol)

            def load_w(buf, bufkey, src_ap):
                dma("pool", buf, src_ap, writes=[bufkey])

            if True:
                top[0] = M_U
                sb0 = sb
                xt = [sb0("xt%d" % i, [128, D], F32) for i in range(2)]
                xh = [sb0("xh%d" % i, [16, D], F32) for i in range(2)]
                gbc = sb0("gbc", [128, D], F32)
                ub = [sb0("ub%d" % i, [128, D], BF16) for i in range(2)]
                uhb = [sb0("uhb%d" % i, [16, D], BF16) for i in range(2)]
                uf = sb0("uf", [128, D], F32)
                junk = sb0("junk", [128, D], F32)
                ss = [sb0("ss%d" % i, [128, 2], F32) for i in range(2)]
                am = sb0("am", [128, 4, 128], BF16)
                af = sb0("af", [128, 4, 128], BF16)
                ah = sb0("ah", [16, 4, 128], BF16)
                ams = sb0("ams", [128, 4, 128], BF16)
                ahs = sb0("ahs", [120, 2, 4, 128], BF16)
                uhs = sb0("uhs", [120, 2, D], BF16)
                dT = sb0("dT", [128, DC, 512], BF16)
                pw = sb0("pw", [128, 4, 2, 256], BF16)

                dma("sp", gbc[:], gmix0_d[0:1, :].broadcast_to([128, D]), writes=["gbc"])
                dma("pool", am[:], amain_d.rearrange("g p t -> p g t"), writes=["am"])
                dma("pool", af[:], afirst_d[pj].rearrange("g p t -> p g t"), writes=["af"])
                dma("pool", ah[:], ahalo_d.rearrange("g p t -> p g t"), writes=["ah"])
                dma("pool", ams[:], amain_s_d.rearrange("g p t -> p g t"), writes=["ams"])
                dma("pool", ahs[:], ahalo_s_d.rearrange("a g p t -> p a g t"), writes=["ahs"])
                dma("pool", uhs[:], uhalo_d.rearrange("a p d -> p a d"), writes=["uhs"])
                dma("pool", pw[:], poolw_d.rearrange("g (k p) d -> p g k d", p=128), writes=["pw"])
                if own:
                    dma("sp", pools_o[:, 0:7, :], spin_d[:, 8:15, :], final=True)

                for (c0, ncol, tiles) in groups:
                    for ti, l in enumerate(tiles):
                        i2 = l % 2
                        is_s = (l == NT)
                        dma("sp", xt[i2][:], xtok_d[pj][l * 128:(l + 1) * 128, :], writes=["xt%d" % i2])
                        S.op("act", lambda e, i2=i2: e.activation(out=junk[:], in_=xt[i2][:], func=AF.Square, accum_out=ss[i2][:, 0:1]),
                             reads=["xt%d" % i2], writes=["junk", "ss%d" % i2])
                        S.op("act", lambda e, i2=i2: e.activation(out=ss[i2][:, 0:1], in_=ss[i2][:, 0:1], func=AF.Ln, scale=1.0 / D, bias=EPS),
                             reads=["ss%d" % i2], writes=["ss%d" % i2])
                        S.op("act", lambda e, i2=i2: e.activation(out=ss[i2][:, 0:1], in_=ss[i2][:, 0:1], func=AF.Exp, scale=-0.5),
                             reads=["ss%d" % i2], writes=["ss%d" % i2])
                        S.op("dve", lambda e, i2=i2: e.scalar_tensor_tensor(out=ub[i2][:], in0=xt[i2][:], scalar=ss[i2][:, 0:1], in1=gbc[:],
                                                                            op0=ALU.mult, op1=ALU.mult),
                             reads=["xt%d" % i2, "ss%d" % i2, "gbc"], writes=["ub%d" % i2])
                        last_of_batch = own and (not is_s) and (l == NT - 1)
                        if is_s or last_of_batch:
                            S.op("dve", lambda e, i2=i2: e.scalar_tensor_tensor(out=uf[:], in0=xt[i2][:], scalar=ss[i2][:, 0:1], in1=gbc[:],
                                                                                op0=ALU.mult, op1=ALU.mult),
                                 reads=["xt%d" % i2, "ss%d" % i2, "gbc"], writes=["uf"])
                            if is_s:
                                dma("sp", pools_o[:, 7:15, :].rearrange("s i d -> (s i) d") if False else pools_o[:, 7:15, :],
                                    uf[:].rearrange("(s i) d -> s i d", i=8) if False else uf[:], reads=["uf"], final=True) if False else None
                                for s_ in range(DPC):
                                    dma("sp", pools_o[s_, 7:15, :], uf[s_ * 8:(s_ + 1) * 8, :], reads=["uf"], final=True)
                            else:
                                dma("sp", poolp_o[:, :], uf[113:128, :], reads=["uf"], final=True)
                        if not is_s:
                            dma("sp", xh[i2][:], xhalo_d[pj][l], writes=["xh%d" % i2])
                            S.op("act", lambda e, i2=i2: e.activation(out=junk[0:16, :], in_=xh[i2][:], func=AF.Square, accum_out=ss[i2][0:16, 1:2]),
                                 reads=["xh%d" % i2], writes=["junk", "ssh%d" % i2])
                            S.op("act", lambda e, i2=i2: e.activation(out=ss[i2][0:16, 1:2], in_=ss[i2][0:16, 1:2], func=AF.Ln, scale=1.0 / D, bias=EPS),
                                 reads=["ssh%d" % i2], writes=["ssh%d" % i2])
                            S.op("act", lambda e, i2=i2: e.activation(out=ss[i2][0:16, 1:2], in_=ss[i2][0:16, 1:2], func=AF.Exp, scale=-0.5),
                                 reads=["ssh%d" % i2], writes=["ssh%d" % i2])
                            S.op("dve", lambda e, i2=i2: e.scalar_tensor_tensor(out=uhb[i2][:], in0=xh[i2][:], scalar=ss[i2][0:16, 1:2], in1=gbc[0:16, :],
                                                                                op0=ALU.mult, op1=ALU.mult),
                                 reads=["xh%d" % i2, "ssh%d" % i2, "gbc"], writes=["uhb%d" % i2])
                        first = (not is_s) and (l == 0)
                        for half in range(2):
                            b = PSK[rot("ps", 2)]
                            bank = ps[int(b[2:])]
                            for cc in range(4):
                                ch = half * 4 + cc
                                g = ch // 2
                                o = bank[:, cc * 128:(cc + 1) * 128]
                                if is_s:
                                    mm(o, ub[i2][:, ch * 128:(ch + 1) * 128], ams[:, g, :], True, False, ["ub%d" % i2, "ams"], [b])
                                    mm(o, uhs[:, 0, ch * 128:(ch + 1) * 128], ahs[:, 0, g, :], False, False, ["uhs", "ahs"], [b])
                                    mm(o, uhs[:, 1, ch * 128:(ch + 1) * 128], ahs[:, 1, g, :], False, True, ["uhs", "ahs"], [b])
                                else:
                                    amat = af if first else am
                                    mm(o, ub[i2][:, ch * 128:(ch + 1) * 128], amat[:, g, :], True, False, ["ub%d" % i2, "af", "am"], [b])
                                    mm(o, uhb[i2][:, ch * 128:(ch + 1) * 128], ah[:, g, :], False, True, ["uhb%d" % i2, "ah"], [b])
                            S.op("act", lambda e, half=half, ti=ti, bank=bank: e.activation(
                                out=dT[:, half * 4:half * 4 + 4, ti * 128:(ti + 1) * 128], in_=bank[:, :].rearrange("p (c t) -> p c t", c=4), func=AF.Copy),
                                reads=[b], writes=[("dT", half, ti)])
                    for j in range(DC):
                        g = j // 2
                        b = PSK[2 + rot("ps", 2)] if False else PSK[rot("ps", 2)]
                        bank = ps[int(b[2:])]
                        for kc in range(2):
                            mm(bank[:, 0:ncol], pw[:, g, kc, (j % 2) * 128:(j % 2) * 128 + 128], dT[:, 2 * g + kc, 0:ncol], kc == 0, kc == 1,
                               ["pw"] + [("dT", (2 * g + kc) // 4, ti) for ti in range(len(tiles))], [b])
                        S.op("dve", lambda e, j=j, bank=bank, c0=c0, ncol=ncol: e.scalar_tensor_tensor(
                            out=hT[:, j, c0:c0 + ncol], in0=bank[:, 0:ncol], scalar=gvec[:, G_PSC + j:G_PSC + j + 1], in1=hT[:, j, c0:c0 + ncol],
                            op0=ALU.mult, op1=ALU.add), reads=[b, "gvec", ("hT", j, c0)], writes=[("hT", j, c0)])

            def mlp(layer, gcol):
                norm_h(gcol)
                for fb in range(NFB):
                    i2 = fb % 2
                    load_w(wA[i2][:], "wA%d" % i2, wup_d[layer].rearrange("(k p) f -> p k f", p=128)[:, :, fb * 512:(fb + 1) * 512])
                    load_w(wB[i2][:], "wB%d" % i2, wdn_d[layer][fb * 512:(fb + 1) * 512, :].rearrange("(k p) d -> p k d", p=128))
                    for gi, (c0, ncol, tiles) in enumerate(groups):
                        a2 = gi % 2
                        for fc in range(4):
                            b = PSK[(0, 1, 4, 5)[rot("up4", 4)]]
                            bank = ps[int(b[2:])]
                            for k in range(DC):
                                mm(bank[:, 0:ncol], wA[i2][:, k, fc * 128:(fc + 1) * 128], uT[:, k, c0:c0 + ncol], k == 0, k == DC - 1,
                                   ["wA%d" % i2, ("uT", k, c0)], [b])
                            tb = rot("tb", 3)
                            S.op("act", lambda e, bank=bank, tb=tb, ncol=ncol: e.activation(out=tmpb[tb][:, 0:ncol], in_=bank[:, 0:ncol], func=AF.Relu),
                                 reads=[b], writes=["tmpb%d" % tb])
                            S.op("dve", lambda e, tb=tb, a2=a2, fc=fc, ncol=ncol: e.tensor_tensor(out=aT[a2][:, fc, 0:ncol], in0=tmpb[tb][:, 0:ncol],
                                                                                             in1=tmpb[tb][:, 0:ncol], op=ALU.mult),
                                 reads=["tmpb%d" % tb], writes=[("aT", a2, fc)])
                        for j in range(DC):
                            b = PSK[(2, 3, 6)[rot("dn3", 3)]]
                            bank = ps[int(b[2:])]
                            for fc in range(4):
                                mm(bank[:, 0:ncol], wB[i2][:, fc, j * 128:(j + 1) * 128], aT[a2][:, fc, 0:ncol], fc == 0, fc == 3,
                                   ["wB%d" % i2, ("aT", a2, fc)], [b])
                            S.op("dve", lambda e, j=j, bank=bank, c0=c0, ncol=ncol: e.tensor_tensor(
                                out=hT[:, j, c0:c0 + ncol], in0=bank[:, 0:ncol], in1=hT[:, j, c0:c0 + ncol], op=ALU.add),
                                reads=[b, ("hT", j, c0)], writes=[("hT", j, c0)])

            def ple(layer, gcol):
                norm_h(gcol)
                for hlf in range(2):
                    load_w(wA[hlf][:], "wA%d" % hlf, wg_d[layer].rearrange("(k p) f -> p k f", p=128)[:, :, hlf * 512:(hlf + 1) * 512])
                load_w(wB[0][:, 0:2, :], "wB0", wp_d[layer].rearrange("(k p) d -> p k d", p=128))
                for gi, (c0, ncol, tiles) in enumerate(groups):
                    a2 = gi % 2
                    dma("pool", aT[a2][:, 0:2, 0:ncol], (pT0_d[pj] if layer == 0 else pT1_d).rearrange("(k p) t -> p k t", p=128)[:, :, c0:c0 + ncol],
                        writes=[("aT", a2, 0), ("aT", a2, 1)])
                    for j in range(DC):
                        b = PSK[rot("ps", 2)]
                        bank = ps[int(b[2:])]
                        for k in range(DC):
                            mm(bank[:, 0:ncol], wA[j // 4][:, k, (j % 4) * 128:(j % 4) * 128 + 128], uT[:, k, c0:c0 + ncol], k == 0, k == DC - 1,
                               ["wA%d" % (j // 4), ("uT", k, c0)], [b])
                        tf = rot("tf", 2)
                        S.op("act", lambda e, bank=bank, tf=tf, ncol=ncol: e.activation(out=tmpf[tf][:, 0:ncol], in_=bank[:, 0:ncol], func=AF.Sigmoid),
                             reads=[b], writes=["tmpf%d" % tf])
                        b2 = PSK[2 + rot("rs", 2)]
                        bank2 = ps[int(b2[2:])]
                        for k in range(2):
                            mm(bank2[:, 0:ncol], wB[0][:, k, j * 128:(j + 1) * 128], aT[a2][:, k, 0:ncol], k == 0, k == 1,
                               ["wB0", ("aT", a2, k)], [b2])
                        S.op("dve", lambda e, bank2=bank2, tf=tf, ncol=ncol: e.tensor_tensor(out=tmpf[tf][:, 0:ncol], in0=bank2[:, 0:ncol],
                                                                                         in1=tmpf[tf][:, 0:ncol], op=ALU.mult),
                             reads=[b2, "tmpf%d" % tf], writes=["tmpf%d" % tf])
                        S.op("dve", lambda e, j=j, tf=tf, c0=c0, ncol=ncol: e.tensor_tensor(out=hT[:, j, c0:c0 + ncol], in0=tmpf[tf][:, 0:ncol],
                                                                                         in1=hT[:, j, c0:c0 + ncol], op=ALU.add),
                             reads=["tmpf%d" % tf, ("hT", j, c0)], writes=[("hT", j, c0)])

            S.barrier()
            top[0] = M_1
            mlp(0, G_MLP0)
            ple(0, G_PLE0)

            if True:
                sb1 = sb
                cf = sb1("cf", [128, 2, 512], F32)
                co = sb1("co", [128, 2, 512], F32)
                cb = sb1("cb", [128, 2, 512], BF16)
                ctok = sb1("ctok", [128, 4, 256], BF16)
                krf = sb1("krf", [64, 2, 512], F32)
                kro = sb1("kro", [64, 512], F32)
                krb = sb1("krb", [64, 512], BF16)
                cs = sb1("cs", [64, 512], F32)
                sn = sb1("sn", [64, 512], F32)
                norm_h(G_KV)
                wkv = wA[0]
                load_w(wkv[:, :, 0:384], "wA0", wdkv_d.rearrange("(k p) f -> p k f", p=128))
                for gi, (c0, ncol, tiles) in enumerate(groups):
                    for mchunk in range(4):
                        msz = 128 if mchunk < 2 else 64
                        m0 = mchunk * 128 if mchunk < 2 else 256 + (mchunk - 2) * 64
                        b = PSK[rot("ps", 2)]
                        bank = ps[int(b[2:])]
                        for k in range(DC):
                            mm(bank[0:msz, 0:ncol], wkv[:, k, m0:m0 + msz], uT[:, k, c0:c0 + ncol], k == 0, k == DC - 1, ["wA0", ("uT", k, c0)], [b])
                        if mchunk < 2:
                            S.op("act", lambda e, bank=bank, mchunk=mchunk, ncol=ncol: e.activation(out=cf[:, mchunk, 0:ncol], in_=bank[:, 0:ncol], func=AF.Copy),
                                 reads=[b], writes=[("cf", mchunk)])
                        else:
                            S.op("act", lambda e, bank=bank, mchunk=mchunk, ncol=ncol: e.activation(out=krf[:, mchunk - 2, 0:ncol], in_=bank[0:64, 0:ncol], func=AF.Copy),
                                 reads=[b], writes=[("krf", mchunk - 2)])
                    norm_T(lambda ch, ncol=ncol: cf[:, ch, 0:ncol], lambda ch: [("cf", ch)], 2, KVR, G_KVN,
                           lambda ch, ncol=ncol: co[:, ch, 0:ncol], lambda ch: [("co", ch)], c0, ncol,
                           out2_fn=lambda ch, ncol=ncol: cb[:, ch, 0:ncol], out2_keys_fn=lambda ch: [("cb", ch)])
                    if own:
                        dma("sp", latT_o.rearrange("(k p) t -> p k t", p=128)[:, :, c0:c0 + ncol], co[:, :, 0:ncol], reads=[("co", 0), ("co", 1)], final=True)
                    dma("sp", cs[:, 0:ncol], cos_d[pj][:, c0:c0 + ncol], writes=["cs"])
                    dma("sp", sn[:, 0:ncol], sin_d[pj][:, c0:c0 + ncol], writes=["sn"])
                    S.op("dve", lambda e, ncol=ncol: e.tensor_tensor(out=krf[:, 0, 0:ncol], in0=krf[:, 0, 0:ncol], in1=cs[:, 0:ncol], op=ALU.mult),
                         reads=[("krf", 0), "cs"], writes=[("krf", 0)])
                    S.op("dve", lambda e, ncol=ncol: e.tensor_tensor(out=krf[:, 1, 0:ncol], in0=krf[:, 1, 0:ncol], in1=sn[:, 0:ncol], op=ALU.mult),
                         reads=[("krf", 1), "sn"], writes=[("krf", 1)])
                    S.op("dve", lambda e, ncol=ncol: e.tensor_tensor(out=kro[:, 0:ncol], in0=krf[:, 0, 0:ncol], in1=krf[:, 1, 0:ncol], op=ALU.add),
                         reads=[("krf", 0), ("krf", 1)], writes=["kro"])
                    S.op("dve", lambda e, ncol=ncol: e.tensor_copy(out=krb[:, 0:ncol], in_=kro[:, 0:ncol]), reads=["kro"], writes=["krb"])
                    if own:
                        dma("sp", krT_o[:, c0:c0 + ncol], kro[:, 0:ncol], reads=["kro"], final=True)
                    if not (own and gi == NG - 1):
                        dma("sp", cx_all.ap()[pj * 576:pj * 576 + 256, c0:c0 + ncol].rearrange("(k p) t -> p k t", p=128), cb[:, :, 0:ncol],
                            reads=[("cb", 0), ("cb", 1)], writes=["cx_all"])
                        dma("sp", cx_all.ap()[pj * 576 + 256:pj * 576 + 320, c0:c0 + ncol], krb[:, 0:ncol], reads=["krb"], writes=["cx_all"])
                        for ti, l in enumerate(tiles):
                            for ch in range(2):
                                S.op("pe", lambda e, ti=ti, ch=ch: e.transpose(out=psb[:, (ti * 2 + ch) * 128:(ti * 2 + ch + 1) * 128],
                                                                              in_=cb[:, ch, ti * 128:(ti + 1) * 128], identity=ident_bf[:]),
                                     reads=[("cb", ch), "ident_bf"], writes=["psb"])
                        nt_ = len(tiles)
                        S.op("act", lambda e, nt_=nt_: e.activation(out=ctok[:, 0:nt_, :], in_=psb[:, 0:nt_ * 256].rearrange("p (t c) -> p t c", c=256), func=AF.Copy),
                             reads=["psb"], writes=["ctok"])
                        dma("sp", c_tok_view(cx_all, pj)[c0:c0 + ncol, :].rearrange("(t p) c -> p t c", p=128), ctok[:, 0:nt_, :], reads=["ctok"], writes=["cx_all"])
                    else:
                        S.op("dve", lambda e: e.tensor_copy(out=sKT[:, 0:2, :], in_=cb[:, :, 0:128]), reads=[("cb", 0), ("cb", 1)], writes=["sKT"])
                        S.op("dve", lambda e: e.tensor_copy(out=sKT[0:64, 2, :], in_=krb[:, 0:128]), reads=["krb"], writes=["sKT"])
                        for ch in range(2):
                            S.op("pe", lambda e, ch=ch: e.transpose(out=psb[:, ch * 128:(ch + 1) * 128], in_=cb[:, ch, 0:128], identity=ident_bf[:]),
                                 reads=[("cb", ch), "ident_bf"], writes=["psb"])
                        S.op("act", lambda e: e.activation(out=sV[:, 0:256], in_=psb[:, 0:256], func=AF.Copy), reads=["psb"], writes=["sV"])

        groups = groups_all
        NG = len(groups)
        if True:
            sb2 = sb
            S.barrier()
            top[0] = M_1
            cqf = sb2("cqf", [128, 3, 512], F32)
            norm_h(G_MIX1)
            wq = wA[1]
            load_w(wq[:, :, 0:QR], "wA1", wdq_d.rearrange("(k p) f -> p k f", p=128))
            for gi, (c0, ncol, tiles) in enumerate(groups):
                for mc in range(3):
                    b = PSK[rot("ps", 2)]
                    bank = ps[int(b[2:])]
                    for k in range(DC):
                        mm(bank[:, 0:ncol], wq[:, k, mc * 128:(mc + 1) * 128], uT[:, k, c0:c0 + ncol], k == 0, k == DC - 1, ["wA1", ("uT", k, c0)], [b])
                    S.op("act", lambda e, bank=bank, mc=mc, ncol=ncol: e.activation(out=cqf[:, mc, 0:ncol], in_=bank[:, 0:ncol], func=AF.Copy),
                         reads=[b], writes=[("cqf", mc)])
                norm_T(lambda ch, ncol=ncol: cqf[:, ch, 0:ncol], lambda ch: [("cqf", ch)], 3, QR, G_Q,
                       lambda ch, c0=c0, ncol=ncol: cqT[:, ch, c0:c0 + ncol], lambda ch, c0=c0: [("cqT", ch, c0)], c0, ncol)
            S.barrier()
            top[0] = M_U
            AGW = 128
            w_uqn = sb2("w_uqn", [128, 3, NH * 128], BF16)
            w_uqr = sb2("w_uqr", [128, 3, NH * 128], BF16)
            w_ukT = sb2("w_ukT", [128, NH * 256], BF16)
            w_uv = sb2("w_uv", [128, 2, NH * 128], BF16)
            w_o = sb2("w_o", [128, NH, D], BF16)
            qT = sb2("qT", [128, 3, NH, AGW], BF16)
            qn = [sb2("qn%d" % i, [128, AGW], BF16) for i in range(2)]
            qrf = sb2("qrf", [64, 2, AGW], F32)
            cs2 = sb2("cs2", [64, AGW], F32)
            sn2 = sb2("sn2", [64, AGW], F32)
            KT = [sb2("KT%d" % i, [128, 3, 8 * 128], BF16) for i in range(2)]
            V = [sb2("V%d" % i, [128, 8, 258], BF16) for i in range(2)]
            PT = tmpb
            acc = sb2("acc", [128, NH, 257], F32)
            rden = sb2("rden", [128, NH], F32)
            Ob = sb2("Ob", [128, NH, 256], BF16)
            OT = sb2("OT", [128, 2, NH, AGW], BF16)
            oT = sb2("oT", [128, NH, AGW], BF16)
            NPB_ = 8 if cfg["SAMPLE_ATT"] else 2
            pgV = [sb2("pgV%d" % i, [128, 258], BF16) for i in range(NPB_)]
            pgK = [sb2("pgK%d" % i, [128, 64], BF16) for i in range(NPB_)]
            qTs = sb2("qTs", [128, 3, NH, 128], BF16)
            OTs = sb2("OTs", [128, 2, NH, 128], BF16)
            acc_s = sb2("acc_s", [64, 257], F32)
            pKT2 = [sb2("pKT2_%d" % i, [128, 2, 3, 128], BF16) for i in range(2)]
            ps6b = ps[6][:, :].bitcast(BF16)
            tcount = [0]
            Obs = sb2("Obs", [64, 256], BF16)
            rdens = sb2("rdens", [64, 1], F32)
            for i in range(NPB_):
                S.op("dve", lambda e, i=i: e.memset(pgV[i][:, 256:258], 1.0), writes=["pgV%d" % i])
            print("arena top (attention phase):", top[0], ARENA)
            dma("pool", w_uqn[:], wuqn_d.rearrange("(k p) f -> p k f", p=128), writes=["w_uqn"])
            dma("pool", w_uqr[:], wuqr_d.rearrange("(k p) f -> p k f", p=128), writes=["w_uqr"])
            dma("pool", w_ukT[:], wukT_d[:, :], writes=["w_ukT"])
            dma("pool", w_uv[:], wuv_d.rearrange("(k p) f -> p k f", p=128), writes=["w_uv"])
            dma("pool", w_o[:], wo_d.rearrange("(h p) d -> p h d", p=128), writes=["w_o"])
            for i in range(2):
                S.op("dve", lambda e, i=i: e.memset(V[i][:, :, 256:258], 1.0), writes=["V%d" % i])
            agroups = []
            l = 0
            while l < NT:
                n = min(1, NT - l)
                agroups.append((l * 128, n * 128, list(range(l, l + n))))
                l += n
            agroups.append((NT * 128, 128, [NT]))
            kv_cnt = [0]
            NPOOL = cfg["NPOOL"]
            NPB = NPB_
            pcount = [0]

            def qproj(c0, ncol, qT, qk):
                dma("sp", cs2[:, 0:ncol], cos_d[NPASS - 1][:, c0:c0 + ncol], writes=["cs2"])
                dma("sp", sn2[:, 0:ncol], sin_d[NPASS - 1][:, c0:c0 + ncol], writes=["sn2"])
                for h in range(NH):
                    b = PSK[6]
                    bank = ps[6]
                    for k in range(3):
                        mm(bank[:, 0:ncol], w_uqn[:, k, h * 128:(h + 1) * 128], cqT[:, k, c0:c0 + ncol], k == 0, k == 2, ["w_uqn", ("cqT", k, c0)], [b])
                    q2 = h % 2
                    S.op("act", lambda e, q2=q2, ncol=ncol: e.activation(out=qn[q2][:, 0:ncol], in_=ps[6][:, 0:ncol], func=AF.Copy),
                         reads=[b], writes=["qn%d" % q2])
                    for rc in range(2):
                        mm(bank[:, 0:ncol], w_ukT[:, h * 256 + rc * 128:h * 256 + (rc + 1) * 128], qn[q2][:, 0:ncol], True, True, ["w_ukT", "qn%d" % q2], [b])
                        S.op("act", lambda e, rc=rc, h=h, ncol=ncol: e.activation(out=qT[:, rc, h, 0:ncol], in_=ps[6][:, 0:ncol], func=AF.Copy),
                             reads=[b], writes=[(qk, h)])
                    for half in range(2):
                        for k in range(3):
                            mm(bank[0:64, 0:ncol], w_uqr[:, k, h * 128 + half * 64:h * 128 + (half + 1) * 64], cqT[:, k, c0:c0 + ncol], k == 0, k == 2,
                               ["w_uqr", ("cqT", k, c0)], [b])
                        tab = cs2 if half == 0 else sn2
                        S.op("dve", lambda e, half=half, tab=tab, ncol=ncol: e.tensor_tensor(out=qrf[:, half, 0:ncol], in0=ps[6][0:64, 0:ncol], in1=tab[:, 0:ncol], op=ALU.mult),
                             reads=[b, "cs2", "sn2"], writes=[("qrf", half)])
                    S.op("dve", lambda e, h=h, ncol=ncol: e.tensor_tensor(out=qT[0:64, 2, h, 0:ncol], in0=qrf[:, 0, 0:ncol], in1=qrf[:, 1, 0:ncol], op=ALU.add),
                         reads=[("qrf", 0), ("qrf", 1)], writes=[(qk, h)])

            def sample_unit(s_, j0, j1):
                accb = PSK[0]
                accbank = ps[0]
                jl = list(range(j0, j1))
                groups_ = []
                while jl:
                    if len(jl) >= 2 and jl[0] != NPG and jl[1] != NPG:
                        groups_.append(jl[:2]); jl = jl[2:]
                    else:
                        groups_.append(jl[:1]); jl = jl[1:]
                for grp in groups_:
                    own_pg = (grp[0] == NPG)
                    ng = len(grp)
                    sb_ = PSK[4 + rot("sq", 2)]
                    sbank = ps[int(sb_[2:])]
                    if not own_pg:
                        tsel = tcount[0] % 2
                        tcount[0] += 1
                        tb_key = "psb" if tsel == 0 else PSK[6]
                        tbank = psb if tsel == 0 else ps6b
                        kbuf = tsel
                        pbs = []
                        for pp, j in enumerate(grp):
                            pb = pcount[0] % NPB
                            pcount[0] += 1
                            pbs.append(pb)
                            idx = s_ * NPG + j
                            S.op("pool", lambda e, idx=idx, pb=pb: e.indirect_dma_start(
                                out=pgV[pb][:, 0:256], out_offset=None, in_=clat_d.rearrange("n p c -> (n p) c"),
                                in_offset=bass.IndirectOffsetOnAxis(ap=idx_all[:, idx:idx + 1], axis=0)),
                                reads=["idx_all"], writes=["pgV%d" % pb], dma=True)
                            S.op("pool", lambda e, idx=idx, pb=pb: e.indirect_dma_start(
                                out=pgK[pb][:, :], out_offset=None, in_=ckr_d.rearrange("n p c -> (n p) c"),
                                in_offset=bass.IndirectOffsetOnAxis(ap=idx_all[:, idx:idx + 1], axis=0)),
                                reads=["idx_all"], writes=["pgK%d" % pb], dma=True)
                            for ch in range(2):
                                S.op("pe", lambda e, pb=pb, ch=ch, pp=pp, tbank=tbank: e.transpose(
                                    out=tbank[:, pp * 384 + ch * 128:pp * 384 + (ch + 1) * 128], in_=pgV[pb][:, ch * 128:(ch + 1) * 128], identity=ident_bf[:]),
                                    reads=["pgV%d" % pb, "ident_bf"], writes=[tb_key])
                            S.op("pe", lambda e, pb=pb, pp=pp, tbank=tbank: e.transpose(out=tbank[0:64, pp * 384 + 256:pp * 384 + 384], in_=pgK[pb][:, 0:64], identity=ident_bf[:]),
                                 reads=["pgK%d" % pb, "ident_bf"], writes=[tb_key])
                        S.op("act", lambda e, kbuf=kbuf, ng=ng, tbank=tbank: e.activation(
                            out=pKT2[kbuf][:, 0:ng, 0:2, :], in_=tbank[:, 0:ng * 384].rearrange("p (g c t) -> p g c t", g=ng, c=3)[:, :, 0:2, :], func=AF.Copy),
                            reads=[tb_key], writes=["pKT%d" % kbuf])
                        S.op("act", lambda e, kbuf=kbuf, ng=ng, tbank=tbank: e.activation(
                            out=pKT2[kbuf][0:64, 0:ng, 2, :], in_=tbank[0:64, 0:ng * 384].rearrange("p (g c t) -> p g c t", g=ng, c=3)[:, :, 2, :], func=AF.Copy),
                            reads=[tb_key], writes=["pKT%d" % kbuf])
                        srcs = [(pKT2[kbuf][:, pp], "pKT%d" % kbuf, pgV[pbs[pp]], "pgV%d" % pbs[pp]) for pp in range(ng)]
                    else:
                        srcs = [(sKT, "sKT", sV, "sV")]
                    for pp, (kt_ap, kt_key, v_ap, v_key) in enumerate(srcs):
                        for k in range(3):
                            kk = 128 if k < 2 else 64
                            mm(sbank[:, pp * 64:(pp + 1) * 64].rearrange("p (h t) -> p h t", h=NH), kt_ap[0:kk, k, :],
                               qTs[0:kk, k, :, s_ * 8:(s_ + 1) * 8], k == 0, k == 2, [kt_key] + [("qTs", hh) for hh in range(NH)], [sb_])
                    p3 = rot("tb", 3)
                    S.op("act", lambda e, sbank=sbank, p3=p3, ng=ng: e.activation(out=PT[p3][:, 0:ng * 64], in_=sbank[:, 0:ng * 64], func=AF.Exp, scale=SM_SCALE),
                         reads=[sb_], writes=["PT%d" % p3])
                    if own_pg:
                        S.op("dve", lambda e, p3=p3, s_=s_: e.tensor_tensor(
                            out=PT[p3][:, 0:64].rearrange("p (h t) -> p h t", h=NH), in0=PT[p3][:, 0:64].rearrange("p (h t) -> p h t", h=NH),
                            in1=smask_bf[:, s_:s_ + 1, :].broadcast_to([128, NH, 8]), op=ALU.mult),
                            reads=["PT%d" % p3, "smask_bf"], writes=["PT%d" % p3])
                    for pp, (kt_ap, kt_key, v_ap, v_key) in enumerate(srcs):
                        jj = grp[pp]
                        mm(accbank[0:64, 0:257], PT[p3][:, pp * 64:(pp + 1) * 64], v_ap[:, 0:257], jj == j0, jj == j1 - 1, ["PT%d" % p3, v_key], [accb])
                if j0 == 0:
                    S.op("dve", lambda e, accbank=accbank: e.tensor_copy(out=acc_s[:], in_=accbank[0:64, 0:257]), reads=[accb], writes=["acc_s"])
                else:
                    S.op("dve", lambda e, accbank=accbank: e.tensor_tensor(out=acc_s[:], in0=accbank[0:64, 0:257], in1=acc_s[:], op=ALU.add),
                         reads=[accb, "acc_s"], writes=["acc_s"])
                if j1 != NPG + 1:
                    return
                S.op("dve", lambda e: e.reciprocal(out=rdens[:], in_=acc_s[:, 256:257]), reads=["acc_s"], writes=["rdens"])
                S.op("dve", lambda e: e.tensor_scalar(out=Obs[:], in0=acc_s[:, 0:256], scalar1=rdens[:, 0:1], scalar2=None, op0=ALU.mult),
                     reads=["acc_s", "rdens"], writes=["Obs"])
                for rc in range(2):
                    S.op("pe", lambda e, rc=rc: e.transpose(out=psb[:, rc * 64:(rc + 1) * 64], in_=Obs[:, rc * 128:(rc + 1) * 128], identity=ident_bf[0:64, 0:64]),
                         reads=["Obs", "ident_bf"], writes=["psb"])
                for rc in range(2):
                    S.op("act", lambda e, rc=rc, s_=s_: e.activation(out=OTs[:, rc, :, s_ * 8:(s_ + 1) * 8],
                                                                    in_=psb[:, rc * 64:(rc + 1) * 64].rearrange("p (h t) -> p h t", h=NH), func=AF.Copy),
                         reads=["psb"], writes=[("OTs", 0)])

            def outproj(c0, ncol, OT, otkeys):
                for h in range(NH):
                    b = PSK[6]
                    for rc in range(2):
                        mm(ps[6][:, 0:ncol], w_uv[:, rc, h * 128:(h + 1) * 128], OT[:, rc, h, 0:ncol], rc == 0, rc == 1,
                           ["w_uv"] + otkeys, [b])
                    S.op("act", lambda e, h=h, ncol=ncol: e.activation(out=oT[:, h, 0:ncol], in_=ps[6][:, 0:ncol], func=AF.Copy), reads=[b], writes=[("oT", h)])
                for j in range(DC):
                    b = PSK[6]
                    for h in range(NH):
                        mm(ps[6][:, 0:ncol], w_o[:, h, j * 128:(j + 1) * 128], oT[:, h, 0:ncol], h == 0, h == NH - 1, ["w_o", ("oT", h)], [b])
                    S.op("dve", lambda e, j=j, c0=c0, ncol=ncol: e.tensor_tensor(out=hT[:, j, c0:c0 + ncol], in0=ps[6][:, 0:ncol], in1=hT[:, j, c0:c0 + ncol], op=ALU.add),
                         reads=[b, ("hT", j, c0)], writes=[("hT", j, c0)])


            sc0 = NT * 128
            if cfg["SAMPLE_ATT"]:
                qproj(sc0, 128, qTs, "qTs")
            units = []
            UP = 8
            for s__ in range(DPC):
                for j0 in range(0, NPG, UP):
                    j1 = min(NPG, j0 + UP)
                    units.append((s__, j0, j1 + 1 if j1 == NPG else j1))
            n_slots = sum(VR * ((l_ + 8) // 8) for l_ in range(NT))
            per_slot = -(-len(units) // max(1, n_slots))
            upos = [0]

            def pop_units(n):
                if not cfg["SAMPLE_ATT"]:
                    return
                for _ in range(n):
                    if upos[0] < len(units):
                        sample_unit(*units[upos[0]])
                        upos[0] += 1
            for gi, (c0, ncol, tiles) in enumerate(agroups[:-1]):
                qproj(c0, ncol, qT, "qT")
                for ti, l in enumerate(tiles):
                    m_ = l
                    first_chunk = True
                    for r in range(VR):
                      for s0 in range(0, m_ + 1, 8):
                        nsl = min(8, m_ + 1 - s0)
                        dg = m_ - s0
                        kb = kv_cnt[0] % 2
                        kv_cnt[0] += 1
                        dma("sp", KT[kb][:, 0:2, 0:nsl * 128],
                            cx_all.ap()[r * 576:r * 576 + 256, s0 * 128:(s0 + nsl) * 128].rearrange("(k p) t -> p k t", p=128),
                            reads=["cx_all"], writes=["KT%d" % kb])
                        dma("sp", KT[kb][0:64, 2, 0:nsl * 128],
                            cx_all.ap()[r * 576 + 256:r * 576 + 320, s0 * 128:(s0 + nsl) * 128],
                            reads=["cx_all"], writes=["KT%d" % kb])
                        dma("sp", V[kb][:, 0:nsl, 0:256],
                            c_tok_view(cx_all, r)[s0 * 128:(s0 + nsl) * 128, :].rearrange("(s p) c -> p s c", p=128),
                            reads=["cx_all"], writes=["V%d" % kb])
                        for hg in range(2):
                            for sl in range(nsl):
                                sb_ = PSK[4 + rot("sq", 2)]
                                sbank = ps[int(sb_[2:])]
                                for k in range(3):
                                    kk = 128 if k < 2 else 64
                                    mm(sbank[:, :].rearrange("p (h t) -> p h t", h=4), KT[kb][0:kk, k, sl * 128:(sl + 1) * 128],
                                       qT[0:kk, k, hg * 4:hg * 4 + 4, ti * 128:(ti + 1) * 128], k == 0, k == 2,
                                       ["KT%d" % kb] + [("qT", hg * 4 + hh) for hh in range(4)], [sb_])
                                p3 = rot("tb", 3)
                                S.op("act", lambda e, sbank=sbank, p3=p3: e.activation(out=PT[p3][:], in_=sbank[:, :], func=AF.Exp, scale=SM_SCALE),
                                     reads=[sb_], writes=["PT%d" % p3])
                                if dg == sl:
                                    mi = r * 2 + (m_ % 2)
                                    S.op("dve", lambda e, p3=p3, mi=mi: e.tensor_tensor(
                                        out=PT[p3][:, :].rearrange("p (h t) -> p h t", h=4), in0=PT[p3][:, :].rearrange("p (h t) -> p h t", h=4),
                                        in1=mask_bf[:, mi:mi + 1, :].broadcast_to([128, 4, 128]), op=ALU.mult),
                                         reads=["PT%d" % p3, "mask_bf"], writes=["PT%d" % p3])
                                for hh in range(4):
                                    mm(ps[hh][:, 0:257], PT[p3][:, hh * 128:(hh + 1) * 128], V[kb][:, sl, 0:257], sl == 0, sl == nsl - 1,
                                       ["PT%d" % p3, "V%d" % kb], [PSK[hh]])
                            for hh in range(4):
                                h = hg * 4 + hh
                                if first_chunk:
                                    S.op("dve", lambda e, hh=hh, h=h: e.tensor_copy(out=acc[:, h, :], in_=ps[hh][:, 0:257]), reads=[PSK[hh]], writes=[("acc", h)])
                                else:
                                    S.op("dve", lambda e, hh=hh, h=h: e.tensor_tensor(out=acc[:, h, :], in0=ps[hh][:, 0:257], in1=acc[:, h, :], op=ALU.add),
                                         reads=[PSK[hh], ("acc", h)], writes=[("acc", h)])
                        first_chunk = False
                        pop_units(per_slot)
                    S.op("dve", lambda e: e.reciprocal(out=rden[:], in_=acc[:, :, 256]), reads=[("acc", h) for h in range(NH)], writes=["rden"])
                    for h in range(NH):
                        S.op("dve", lambda e, h=h: e.tensor_scalar(out=Ob[:, h, :], in0=acc[:, h, 0:256], scalar1=rden[:, h:h + 1], scalar2=None, op0=ALU.mult),
                             reads=[("acc", h), "rden"], writes=[("Ob", h)])
                    for h4 in range(2):
                        for hh in range(4):
                            h = h4 * 4 + hh
                            for rc in range(2):
                                S.op("pe", lambda e, h=h, hh=hh, rc=rc: e.transpose(out=psb[:, (hh * 2 + rc) * 128:(hh * 2 + rc + 1) * 128],
                                                                                   in_=Ob[:, h, rc * 128:(rc + 1) * 128], identity=ident_bf[:]),
                                     reads=[("Ob", h), "ident_bf"], writes=["psb"])
                        for rc in range(2):
                            S.op("act", lambda e, h4=h4, rc=rc, ti=ti: e.activation(
                                out=OT[:, rc, h4 * 4:h4 * 4 + 4, ti * 128:(ti + 1) * 128],
                                in_=psb[:, :].rearrange("p (h r t) -> p r h t", h=4, r=2)[:, rc], func=AF.Copy),
                                reads=["psb"], writes=[("OT", ti)])
                outproj(c0, ncol, OT, [("OT", ti) for ti in range(len(tiles))])
            if cfg["SAMPLE_ATT"]:
                pop_units(len(units))
                outproj(sc0, 128, OTs, [("OTs", 0)])

        S.barrier()
        top[0] = M_1
        mlp(1, G_MLP1)
        ple(1, G_PLE1)
        for (c0, ncol, tiles) in groups:
            yst = tmpf
            norm_T(lambda ch, c0=c0, ncol=ncol: hT[:, ch, c0:c0 + ncol], lambda ch, c0=c0: [("hT", ch, c0)], DC, D, G_FIN,
                   lambda ch, c0=c0, ncol=ncol: hT[:, ch, c0:c0 + ncol], lambda ch, c0=c0: [("hT", ch, c0)], c0, ncol)
            dma("sp", yT_o.rearrange("(c p) t -> p c t", p=128)[:, :, c0:c0 + ncol], hT[:, :, c0:c0 + ncol],
                reads=[("hT", ch, c0) for ch in range(DC)], final=True)
        S.emit()
    return nc


def vr_blocks(cfg, v):
    VR = cfg["NCORE"] // cfg["BATCH"]
    out = []
    for i in range(cfg["NBLK"] // (2 * VR)):
        out.append(2 * VR * i + v)
        out.append(2 * VR * i + 2 * VR - 1 - v)
    return out


def _pass_vranks(cfg, v):
    VR = cfg["NCORE"] // cfg["BATCH"]
    return [(v + 1 + j) % VR for j in range(VR)]


def _tables(cfg, k):
    NCORE, BATCH, NBLK = cfg["NCORE"], cfg["BATCH"], cfg["NBLK"]
    VR = NCORE // BATCH
    v = k % VR
    DPC = cfg["DEC"] // NCORE
    past = cfg["NPG"] * 128
    half = ROPE // 2
    inv_freq = np.power(np.float32(THETA), -np.arange(half, dtype=np.float32) / np.float32(half)).astype(np.float32)
    t = np.arange(128)
    amain = np.zeros((4, 128, 128), np.float32)
    afirst0 = np.zeros((4, 128, 128), np.float32)
    ahalo = np.zeros((4, 16, 128), np.float32)
    amain_s = np.zeros((4, 128, 128), np.float32)
    ahalo_s = np.zeros((2, 4, 120, 128), np.float32)
    for g, w in enumerate(POOL_W):
        diff = t[None, :] - t[:, None]
        inw = (diff >= 0) & (diff < w)
        amain[g] = inw / np.float32(w) - np.eye(128, dtype=np.float32)
        cntv = np.minimum(t + 1, w).astype(np.float32)
        afirst0[g] = inw / cntv[None, :] - np.eye(128, dtype=np.float32)
        i = np.arange(16)
        ahalo[g] = (i[:, None] >= t[None, :] + 17 - w) / np.float32(w)
        for s_ in range(16):
            for ii in range(8):
                for j in range(8):
                    amain_s[g, s_ * 8 + ii, s_ * 8 + j] = (1.0 / w if 0 <= j - ii < w else 0.0) - (1.0 if ii == j else 0.0)
        for a in range(2):
            for sl in range(8):
                for r in range(15):
                    for j in range(8):
                        if r > j + 15 - w:
                            ahalo_s[a, g, sl * 15 + r, (a * 8 + sl) * 8 + j] = 1.0 / w
    cosT, sinT, afirst = [], [], []
    for vv in _pass_vranks(cfg, v):
        blocks = vr_blocks(cfg, vv)
        pos = [np.arange(blk * 128, (blk + 1) * 128) for blk in blocks]
        pos.append(np.tile(past + np.arange(cfg["DSEQ"]), DPC))
        pos = np.concatenate(pos)
        ang = (pos.astype(np.float32)[:, None] * inv_freq[None, :]).astype(np.float32)
        cos = np.cos(ang).astype(np.float32)
        sin = np.sin(ang).astype(np.float32)
        cosT.append(np.concatenate([cos, cos], 1).T)
        sinT.append(np.concatenate([-sin, sin], 1).T)
        afirst.append(afirst0 if blocks[0] == 0 else amain)
    tri = (t[:, None] <= t[None, :]).astype(np.float32)
    bmask = np.zeros((VR * 2, 128, 128), np.float32)
    for rel, vv in enumerate(_pass_vranks(cfg, v)):
        bmask[rel * 2 + 0] = 1.0 if vv < v else (tri if vv == v else 0.0)
        bmask[rel * 2 + 1] = 1.0 if vv > v else (tri if vv == v else 0.0)
    smask = np.zeros((128, 16, 8), np.float32)
    for key in range(128):
        for i in range(8):
            if key % 8 <= i:
                smask[key, key // 8, i] = 1.0
    f = lambda a: np.ascontiguousarray(np.asarray(a, dtype=np.float32))
    return dict(pidx=np.arange(128, dtype=np.float32).reshape(128, 1), smask=smask.reshape(128, 128), cosT=f(np.stack(cosT)), sinT=f(np.stack(sinT)), amain=amain, afirst=f(np.stack(afirst)), ahalo=ahalo, amain_s=amain_s,
                ahalo_s=ahalo_s, bmask=bmask, identf=np.eye(128, dtype=np.float32))


def _prep(cfg, inp):
    NCORE, BATCH, NBLK = cfg["NCORE"], cfg["BATCH"], cfg["NBLK"]
    VR = NCORE // BATCH
    DPC = cfg["DEC"] // NCORE
    f = lambda a: np.ascontiguousarray(np.asarray(a, dtype=np.float32))
    xp, xs = f(inp["x_prompt"]), f(inp["x_sample"])
    pp, psm = f(inp["p_prompt"]), f(inp["p_sample"])
    stp = f(inp["state_pool"])

    def chunks(vv):
        return f(vv).reshape(-1, 128).T
    gvec = np.zeros((128, 80), np.float32)
    for col, vv in ((0, inp["norm_mix"][1]), (8, inp["norm_mlp"][0]), (16, inp["norm_mlp"][1]), (24, inp["norm_ple"][0]), (32, inp["norm_ple"][1]),
                    (40, inp["pool_scale"][0]), (48, inp["norm_kv"]), (56, inp["norm_final"]), (64, inp["q_norm"][0]), (67, inp["kv_norm"])):
        c = chunks(vv)
        gvec[:, col:col + c.shape[1]] = c
    wdkv = f(inp["w_dkv"])
    w_dkv4 = np.concatenate([wdkv[:, :KVR + ROPE], wdkv[:, KVR + 32:KVR + 64], wdkv[:, KVR:KVR + 32]], 1)
    wuq = f(inp["w_uq"][0])
    w_uq_nope = wuq[:, :, :NOPE].reshape(QR, NH * 128)
    rp = wuq[:, :, NOPE:]
    w_uq_rope = np.concatenate([rp, rp[:, :, 32:], rp[:, :, :32]], 2).reshape(QR, NH * 128)
    w_ukT = f(np.transpose(f(inp["w_uk"]), (2, 1, 0))).reshape(128, NH * 256)
    shared = dict(gvec=gvec, gmix0=f(inp["norm_mix"][0])[None, :], pool_w=f(inp["pool_w"][0]), w_up=f(inp["w_up"]), w_down=f(inp["w_down"]),
                  w_gate=f(inp["w_ple_gate"]), w_proj=f(inp["w_ple_proj"]), w_dkv4=f(w_dkv4), w_dq=f(inp["w_dq"][0]),
                  w_uq_nope=f(w_uq_nope), w_uq_rope=f(w_uq_rope), w_ukT=w_ukT, w_uv=f(inp["w_uv"]).reshape(KVR, NH * 128),
                  w_o=f(inp["w_o"][0]).reshape(NH * 128, D))
    if cfg["SAMPLE_ATT"]:
        shared["cache_latent"] = f(inp["cache_latent"])
        shared["cache_krope"] = f(inp["cache_krope"])
    maps = []
    for k in range(NCORE):
        b, v = k // VR, k % VR
        xs_k = xs[k * DPC:(k + 1) * DPC].reshape(128, D)
        xtok_l, xT_l, halo_l, pT0_l = [], [], [], []
        for vv in _pass_vranks(cfg, v):
            blocks = vr_blocks(cfg, vv)
            xt_ = np.concatenate([xp[b, blk * 128:(blk + 1) * 128] for blk in blocks] + [xs_k], 0)
            p0 = np.concatenate([pp[0, b, blk * 128:(blk + 1) * 128] for blk in blocks] + [psm[0, k * DPC:(k + 1) * DPC].reshape(128, PLE)], 0)
            hal = np.zeros((len(blocks), 16, D), np.float32)
            for i_, blk in enumerate(blocks):
                if blk > 0:
                    hal[i_] = xp[b, blk * 128 - 16:blk * 128]
            xtok_l.append(xt_)
            xT_l.append(xt_.T)
            halo_l.append(hal)
            pT0_l.append(p0.T)
        blocks = vr_blocks(cfg, v)
        p1 = np.concatenate([pp[1, b, blk * 128:(blk + 1) * 128] for blk in blocks] + [psm[1, k * DPC:(k + 1) * DPC].reshape(128, PLE)], 0)
        m = dict(shared)
        m.update(_tables(cfg, k))
        m["xtok"] = f(np.stack(xtok_l))
        m["xT"] = f(np.stack(xT_l))
        m["xhalo"] = f(np.stack(halo_l))
        m["pT0"] = f(np.stack(pT0_l))
        m["pT1"] = f(p1.T)
        sp = stp[0, k * DPC:(k + 1) * DPC]
        m["spin"] = f(sp)
        m["uhalo"] = f(sp.reshape(2, 120, D))
        if cfg["SAMPLE_ATT"]:
            m["ptab"] = np.ascontiguousarray(np.asarray(inp["page_table"])[k * DPC:(k + 1) * DPC].astype(np.int32).reshape(1, -1))
        maps.append(m)
    return maps


def _assemble(cfg, res):
    NCORE, BATCH, NBLK = cfg["NCORE"], cfg["BATCH"], cfg["NBLK"]
    VR = NCORE // BATCH
    DPC = cfg["DEC"] // NCORE
    SEQ = NBLK * 128
    y_p = np.zeros((BATCH, SEQ, D), np.float32)
    lat_p = np.zeros((BATCH, SEQ, KVR), np.float32)
    kr_p = np.zeros((BATCH, SEQ, ROPE), np.float32)
    y_s = np.zeros((cfg["DEC"], cfg["DSEQ"], D), np.float32)
    lat_s = np.zeros((cfg["DEC"], cfg["DSEQ"], KVR), np.float32)
    kr_s = np.zeros((cfg["DEC"], cfg["DSEQ"], ROPE), np.float32)
    pool_s = np.zeros((1, cfg["DEC"], 15, D), np.float32)
    pool_p = np.zeros((1, BATCH, 15, D), np.float32)
    for k in range(NCORE):
        r = res[k]
        b, v = k // VR, k % VR
        yT, latT, krT = np.asarray(r["yT"]), np.asarray(r["latT"]), np.asarray(r["krT"])
        l = 0
        for blk in vr_blocks(cfg, v):
            sl = slice(l * 128, (l + 1) * 128)
            y_p[b, blk * 128:(blk + 1) * 128] = yT[:, sl].T
            lat_p[b, blk * 128:(blk + 1) * 128] = latT[:, sl].T
            kr_p[b, blk * 128:(blk + 1) * 128] = krT[:, sl].T
            l += 1
        sl = slice(l * 128, (l + 1) * 128)
        y_s[k * DPC:(k + 1) * DPC] = yT[:, sl].T.reshape(DPC, cfg["DSEQ"], D)
        lat_s[k * DPC:(k + 1) * DPC] = latT[:, sl].T.reshape(DPC, cfg["DSEQ"], KVR)
        kr_s[k * DPC:(k + 1) * DPC] = krT[:, sl].T.reshape(DPC, cfg["DSEQ"], ROPE)
        pool_s[0, k * DPC:(k + 1) * DPC] = np.asarray(r["pools"])
        if v == 0:
            pool_p[0, b] = np.asarray(r["poolp"])
    return (y_p, y_s, pool_p, pool_s, lat_p, kr_p, lat_s, kr_s)


_NC_CACHE = {}


def kernel(**inputs):
    cfg = dict(FULL)
    key = "full"
    if key not in _NC_CACHE:
        _NC_CACHE[key] = build_program(cfg)
    nc = _NC_CACHE[key]
    maps = _prep(cfg, inputs)
    res = run_bass_kernel_spmd(nc, maps, core_ids=list(range(cfg["NCORE"])))
    return _assemble(cfg, res.results)
```

```python
import contextlib
import math
import numpy as np
import concourse.bass as bass
import concourse.mybir as mybir
from concourse.bass_utils import run_bass_kernel_spmd

F32 = mybir.dt.float32
BF16 = mybir.dt.bfloat16
I32 = mybir.dt.int32
AF = mybir.ActivationFunctionType
ALU = mybir.AluOpType

COMPUTE = ("pe", "act", "dve", "pool")
N_DMA_SLOTS = {"sp": 8, "pool": 4, "act": 2, "pe": 1, "dve": 1}


class Sched:
    def __init__(self, nc):
        self.nc = nc
        self.eng_names = ("pe", "act", "dve", "pool", "sp")
        self.prog = {e: [] for e in self.eng_names}
        self.n_comp = {e: 0 for e in self.eng_names}
        self.n_dma = {e: 0 for e in self.eng_names}
        self.slot_cnt = {}
        self.last_w = {}
        self.readers = {}
        self.known = {e: {} for e in self.eng_names}
        self.targets = {e: set() for e in self.eng_names}
        self.final_dma = []
        self.n_cc = 0

    def _add_dep(self, deps, tok, eng):
        if tok is None:
            return
        if tok[0] == "c":
            if tok[1] == eng and eng == "pe":
                return
            lane = ("c", tok[1])
            v = tok[2]
        else:
            lane = (tok[0], tok[1], tok[2])
            v = tok[3]
        if deps.get(lane, 0) < v:
            deps[lane] = v

    def op(self, eng, fn, reads=(), writes=(), dma=False, final=False, cc=False):
        deps = {}
        for k in reads:
            self._add_dep(deps, self.last_w.get(k), eng)
        for k in writes:
            self._add_dep(deps, self.last_w.get(k), eng)
            for t in self.readers.get(k, ()):
                self._add_dep(deps, t, eng)
        if cc:
            self.n_cc += 1
            tok = ("x", "cc", self.n_cc, 1)
        elif dma:
            n = self.n_dma[eng]
            self.n_dma[eng] = n + 1
            slot = n % N_DMA_SLOTS[eng]
            cnt = self.slot_cnt.get((eng, slot), 0) + 1
            self.slot_cnt[(eng, slot)] = cnt
            tok = ("d", eng, slot, cnt)
            if cnt > 1:
                lane = ("d", eng, slot)
                if deps.get(lane, 0) < cnt - 1:
                    deps[lane] = cnt - 1
        else:
            self.n_comp[eng] += 1
            tok = ("c", eng, self.n_comp[eng])
        waits = []
        kn = self.known[eng]
        for lane, v in deps.items():
            if kn.get(lane, 0) >= v:
                continue
            kn[lane] = v
            waits.append((lane, v))
            if lane[0] == "c":
                self.targets[lane[1]].add(v)
        self.prog[eng].append((waits, fn, tok))
        for k in reads:
            self.readers.setdefault(k, []).append(tok)
        for k in writes:
            self.last_w[k] = tok
            self.readers[k] = []
        if final:
            self.final_dma.append(tok)
        return tok

    def barrier(self):
        for eng in self.eng_names:
            deps = {}
            for F in COMPUTE:
                if self.n_comp[F] > 0 and not (F == eng and eng == "pe"):
                    deps[("c", F)] = self.n_comp[F]
            for (e, s_), c in self.slot_cnt.items():
                deps[("d", e, s_)] = c
            for i in range(1, self.n_cc + 1):
                deps[("x", "cc", i)] = 1
            waits = []
            kn = self.known[eng]
            for lane, v in deps.items():
                if kn.get(lane, 0) >= v:
                    continue
                kn[lane] = v
                waits.append((lane, v))
                if lane[0] == "c":
                    self.targets[lane[1]].add(v)
            self.prog[eng].append((waits, None, None))

    def emit(self):
        nc = self.nc
        with contextlib.ExitStack() as st:
            dsem = {}
            for i in range(1, self.n_cc + 1):
                dsem[("x", "cc", i)] = st.enter_context(nc.semaphore("cc_%d" % i))
            csem = {e: st.enter_context(nc.semaphore("c_" + e)) for e in COMPUTE}
            for (e, s) in self.slot_cnt:
                dsem[("d", e, s)] = st.enter_context(nc.semaphore("d_%s_%d" % (e, s)))
            cum = {}
            for e in COMPUTE:
                tg = sorted(self.targets[e])
                cum[e] = {idx: i + 1 for i, idx in enumerate(tg)}
            fin = {}
            for tok in self.final_dma:
                lane = ("d", tok[1], tok[2])
                fin[lane] = max(fin.get(lane, 0), tok[3])
            block = st.enter_context(nc.Block())
            engobj = {"pe": "tensor", "act": "scalar", "dve": "vector", "pool": "gpsimd", "sp": "sync"}

            def build(ename):
                def body(eng):
                    for waits, fn, tok in self.prog[ename]:
                        for lane, v in waits:
                            if lane[0] == "c":
                                eng.wait_ge(csem[lane[1]], cum[lane[1]][v])
                            elif lane[0] == "d":
                                eng.wait_ge(dsem[lane], 16 * v)
                            else:
                                eng.wait_ge(dsem[lane], v)
                        if fn is None:
                            continue
                        ins = fn(eng)
                        if tok[0] == "c":
                            if tok[2] in cum[tok[1]]:
                                ins.then_inc(csem[tok[1]], 1)
                        elif tok[0] == "d":
                            ins.then_inc(dsem[(tok[0], tok[1], tok[2])], 16)
                        else:
                            ins.then_inc(dsem[(tok[0], tok[1], tok[2])], 1)
                    if ename == "sp":
                        for lane, v in fin.items():
                            eng.wait_ge(dsem[lane], 16 * v)
                return body

            for ename in self.eng_names:
                if not self.prog[ename] and ename != "sp":
                    continue
                getattr(block, engobj[ename])(build(ename))


D = 1024
DC = 8
KVR = 256
QR = 384
NH = 8
NOPE = 128
ROPE = 64
PLE = 256
POOL_W = (2, 4, 8, 16)
EPS = 1e-6
SM_SCALE = 1.0 / math.sqrt(NOPE + ROPE)
THETA = 10000.0

FULL = dict(NCORE=8, BATCH=2, NBLK=64, DEC=128, DSEQ=8, NPG=64, NPOOL=10240, FF=4096, USE_CC=False, SAMPLE_ATT=True)


def core_blocks(cfg, k):
    nc_, nb = cfg["NCORE"], cfg["NBLK"]
    out = []
    for i in range(nb // (2 * nc_)):
        out.append(2 * nc_ * i + k)
        out.append(2 * nc_ * i + 2 * nc_ - 1 - k)
    return out


def owner_of(cfg, j):
    nc_ = cfg["NCORE"]
    r = j % (2 * nc_)
    k = r if r < nc_ else 2 * nc_ - 1 - r
    m = (j // (2 * nc_)) * 2 + (0 if r < nc_ else 1)
    return k, m


def build_program(cfg):
    NCORE, BATCH, NBLK, FF = cfg["NCORE"], cfg["BATCH"], cfg["NBLK"], cfg["FF"]
    VR = NCORE // BATCH
    NPASS = VR
    NTB = NBLK // VR
    NT = NTB
    T = (NT + 1) * 128
    TP = NT * 128
    NFB = FF // 512
    DPC = cfg["DEC"] // NCORE
    NPG = cfg["NPG"]
    assert DPC * cfg["DSEQ"] == 128
    groups = []
    l = 0
    while l < NT:
        n = min(4, NT - l)
        groups.append((l * 128, n * 128, list(range(l, l + n))))
        l += n
    groups.append((NT * 128, 128, [NT]))
    NG = len(groups)
    groups_all = groups

    nc = bass.Bass("TRN2", target_bir_lowering=False)

    def din(name, shape, dt=F32):
        return nc.dram_tensor(name, list(shape), dt, kind="ExternalInput").ap()

    def dout(name, shape, dt=F32):
        return nc.dram_tensor(name, list(shape), dt, kind="ExternalOutput").ap()

    xT_d = din("xT", [NPASS, D, T])
    xtok_d = din("xtok", [NPASS, T, D])
    xhalo_d = din("xhalo", [NPASS, NT, 16, D])
    uhalo_d = din("uhalo", [2, 120, D])
    spin_d = din("spin", [DPC, 15, D])
    pT0_d = din("pT0", [NPASS, PLE, T])
    pT1_d = din("pT1", [PLE, T])
    gvec_d = din("gvec", [128, 80])
    gmix0_d = din("gmix0", [1, D])
    cos_d = din("cosT", [NPASS, 64, T])
    sin_d = din("sinT", [NPASS, 64, T])
    amain_d = din("amain", [4, 128, 128])
    afirst_d = din("afirst", [NPASS, 4, 128, 128])
    ahalo_d = din("ahalo", [4, 16, 128])
    amain_s_d = din("amain_s", [4, 128, 128])
    ahalo_s_d = din("ahalo_s", [2, 4, 120, 128])
    bmask_d = din("bmask", [VR * 2, 128, 128])
    identf_d = din("identf", [128, 128])
    poolw_d = din("pool_w", [4, 256, 256])
    wup_d = din("w_up", [2, D, FF])
    wdn_d = din("w_down", [2, FF, D])
    wg_d = din("w_gate", [2, D, D])
    wp_d = din("w_proj", [2, PLE, D])
    wdkv_d = din("w_dkv4", [D, 384])
    wdq_d = din("w_dq", [D, QR])
    wuqn_d = din("w_uq_nope", [QR, NH * 128])
    wuqr_d = din("w_uq_rope", [QR, NH * 128])
    wukT_d = din("w_ukT", [128, NH * 256])
    wuv_d = din("w_uv", [KVR, NH * 128])
    wo_d = din("w_o", [NH * 128, D])
    if cfg["SAMPLE_ATT"]:
        cache_d = din("cache_comb", [cfg["NPOOL"], 128, KVR + ROPE])
        ptab_d = din("ptab", [1, DPC * NPG], I32)
        smask_d = din("smask", [128, 128])
        pidx_d = din("pidx", [128, 1])

    yT_o = dout("yT", [D, T])
    latT_o = dout("latT", [KVR, T])
    krT_o = dout("krT", [ROPE, T])
    poolp_o = dout("poolp", [15, D])
    pools_o = dout("pools", [DPC, 15, D])

    cx_all = nc.dram_tensor("cx_all", [VR * 576, TP], BF16)

    def c_tok_view(t, r):
        return t.ap()[r * 576 + 320:(r + 1) * 576, :].rearrange("a (t c) -> (a t) c", c=256)

    S = Sched(nc)
    st = contextlib.ExitStack()
    with st:
        ARENA = 206 * 1024
        arena = st.enter_context(nc.sbuf_tensor("arena", [128, ARENA], mybir.dt.uint8))
        top = [0]

        def sb(name, shape, dt):
            nb = 2 if dt == BF16 else 4
            per = int(np.prod(shape[1:])) * nb
            off = (top[0] + 63) // 64 * 64
            assert off + per <= ARENA, (name, off, per, ARENA)
            top[0] = off + per
            v = arena[0:shape[0], off:off + per].bitcast(dt)
            if len(shape) == 3:
                v = v.rearrange("p (a b) -> p a b", a=shape[1])
            elif len(shape) == 4:
                v = v.rearrange("p (a b c) -> p a b c", a=shape[1], b=shape[2])
            return v

        ps = [st.enter_context(nc.psum_tensor("ps%d" % i, [128, 512], F32)) for i in range(7)]
        psb = st.enter_context(nc.psum_tensor("psb", [128, 1024], BF16))
        PSK = ["ps%d" % i for i in range(7)]

        hT = sb("hT", [128, DC, T], F32)
        gvec = sb("gvec", [128, 80], F32)
        ones_bf = sb("ones_bf", [128, 128], BF16)
        ident_bf = sb("ident_bf", [128, 128], BF16)
        mask_bf = sb("mask_bf", [128, VR * 2, 128], BF16)
        sqb = [sb("sqb%d" % i, [128, 512], BF16) for i in range(2)]
        rstd = [sb("rstd%d" % i, [128, 512], F32) for i in range(2)]
        tmpf = [sb("tmpf%d" % i, [128, 512], F32) for i in range(2)]
        tmpb = [sb("tmpb%d" % i, [128, 512], BF16) for i in range(3)]
        sKT = sb("sKT", [128, 3, 128], BF16)
        sV = sb("sV", [128, 258], BF16)
        smask_bf = sb("smask_bf", [128, 16, 8], BF16)
        idx_all = sb("idx_all", [128, DPC * NPG], I32)
        pidx = sb("pidx", [128, 1], F32)
        cqT = sb("cqT", [128, 3, T], BF16)
        M_U = top[0]
        uT = sb("uT", [128, DC, T], BF16)
        wA = [sb("wA%d" % i, [128, 8, 512], BF16) for i in range(2)]
        wB = [sb("wB%d" % i, [128, 4, 1024], BF16) for i in range(2)]
        aT = [sb("aT%d" % i, [128, 4, 512], BF16) for i in range(2)]
        M_1 = top[0]

        G_MIX1, G_MLP0, G_MLP1, G_PLE0, G_PLE1, G_PSC, G_KV, G_FIN, G_Q, G_KVN = 0, 8, 16, 24, 32, 40, 48, 56, 64, 67

        cnt = {"ps": 0, "sq": 0, "rs": 0, "tf": 0, "tb": 0, "up4": 0, "dn3": 0}

        def rot(kind, n):
            v = cnt[kind]
            cnt[kind] = (v + 1) % n
            return v

        def dma(eng, out, in_, reads=(), writes=(), final=False, **kw):
            return S.op(eng, lambda e: e.dma_start(out=out, in_=in_, **kw), reads=reads, writes=writes, dma=True, final=final)

        def mm(out, lhsT, rhs, start, stop, reads, writes):
            S.op("pe", lambda e: e.matmul(out, lhsT=lhsT, rhs=rhs, start=start, stop=stop), reads=reads, writes=writes)

        dma("sp", gvec[:], gvec_d[:, :], writes=["gvec"])
        dma("pool", ident_bf[:], identf_d[:, :], writes=["ident_bf"])
        dma("pool", mask_bf[:], bmask_d.rearrange("m p t -> p m t"), writes=["mask_bf"])
        S.op("dve", lambda e: e.memset(ones_bf[:], 1.0), writes=["ones_bf"])
        if cfg["SAMPLE_ATT"]:
            dma("pool", smask_bf[:], smask_d.rearrange("p (s i) -> p s i", i=8), writes=["smask_bf"])
            dma("sp", idx_all[:], ptab_d[0:1, :].broadcast_to([128, DPC * NPG]), writes=["idx_all"])
            dma("sp", pidx[:], pidx_d[:, :], writes=["pidx"])
            S.op("dve", lambda e: e.tensor_scalar(out=idx_all[:], in0=idx_all[:], scalar1=128.0, scalar2=pidx[:, 0:1], op0=ALU.mult, op1=ALU.add),
                 reads=["idx_all", "pidx"], writes=["idx_all"])
            S.op("dve", lambda e: e.memset(sV[:, 0:2], 1.0), writes=["sV"])
        for pj in range(NPASS):
            own = (pj == NPASS - 1)
            groups = groups_all if own else groups_all[:-1]
            NG = len(groups)
            S.barrier()
            top[0] = M_1
            for (c0, ncol, tiles) in groups:
                gi = c0
                dma("sp", hT[:, :, c0:c0 + ncol], xT_d[pj].rearrange("(c p) t -> p c t", p=128)[:, :, c0:c0 + ncol],
                    writes=[("hT", ch, gi) for ch in range(DC)])

            def norm_T(src_fn, src_keys_fn, nch, dim, gcol, out_fn, out_keys_fn, g_id, ncol, eps=EPS, out2_fn=None, out2_keys_fn=None):
                b = PSK[rot("ps", 2)]
                bank = ps[int(b[2:])]
                for ch in range(nch):
                    q = rot("sq", 2)
                    S.op("act", lambda e, ch=ch, q=q: e.activation(out=sqb[q][:, 0:ncol], in_=src_fn(ch), func=AF.Square),
                         reads=src_keys_fn(ch), writes=["sqb%d" % q])
                    mm(bank[:, 0:ncol], ones_bf[:], sqb[q][:, 0:ncol], ch == 0, ch == nch - 1, ["ones_bf", "sqb%d" % q], [b])
                r = rot("rs", 2)
                S.op("act", lambda e: e.activation(out=rstd[r][:, 0:ncol], in_=bank[:, 0:ncol], func=AF.Ln, scale=1.0 / dim, bias=eps),
                     reads=[b], writes=["rstd%d" % r])
                S.op("act", lambda e: e.activation(out=rstd[r][:, 0:ncol], in_=rstd[r][:, 0:ncol], func=AF.Exp, scale=-0.5),
                     reads=["rstd%d" % r], writes=["rstd%d" % r])
                for ch in range(nch):
                    S.op("dve", lambda e, ch=ch: e.scalar_tensor_tensor(out=out_fn(ch), in0=src_fn(ch), scalar=gvec[:, gcol + ch:gcol + ch + 1],
                                                                        in1=rstd[r][:, 0:ncol], op0=ALU.mult, op1=ALU.mult),
                         reads=list(src_keys_fn(ch)) + ["gvec", "rstd%d" % r], writes=out_keys_fn(ch))
                    if out2_fn is not None:
                        S.op("dve", lambda e, ch=ch: e.scalar_tensor_tensor(out=out2_fn(ch), in0=src_fn(ch), scalar=gvec[:, gcol + ch:gcol + ch + 1],
                                                                            in1=rstd[r][:, 0:ncol], op0=ALU.mult, op1=ALU.mult),
                             reads=list(src_keys_fn(ch)) + ["gvec", "rstd%d" % r], writes=out2_keys_fn(ch))

            def norm_h(gcol):
                for (c0, ncol, tiles) in groups:
                    norm_T(lambda ch, c0=c0, ncol=ncol: hT[:, ch, c0:c0 + ncol], lambda ch, c0=c0: [("hT", ch, c0)], DC, D, gcol,
                           lambda ch, c0=c0, ncol=ncol: uT[:, ch, c0:c0 + ncol], lambda ch, c0=c0: [("uT", ch, c0)], c0, ncol)

            def load_w(buf, bufkey, src_ap):
                dma("pool", buf, src_ap, writes=[bufkey])

            if True:
                top[0] = M_U
                sb0 = sb
                xt = [sb0("xt%d" % i, [128, D], F32) for i in range(2)]
                xh = [sb0("xh%d" % i, [16, D], F32) for i in range(2)]
                gbc = sb0("gbc", [128, D], F32)
                ub = [sb0("ub%d" % i, [128, D], BF16) for i in range(2)]
                uhb = [sb0("uhb%d" % i, [16, D], BF16) for i in range(2)]
                uf = sb0("uf", [128, D], F32)
                junk = sb0("junk", [128, D], F32)
                ss = [sb0("ss%d" % i, [128, 2], F32) for i in range(2)]
                am = sb0("am", [128, 4, 128], BF16)
                af = sb0("af", [128, 4, 128], BF16)
                ah = sb0("ah", [16, 4, 128], BF16)
                ams = sb0("ams", [128, 4, 128], BF16)
                ahs = sb0("ahs", [120, 2, 4, 128], BF16)
                uhs = sb0("uhs", [120, 2, D], BF16)
                dT = sb0("dT", [128, DC, 512], BF16)
                pw = sb0("pw", [128, 4, 2, 256], BF16)

                dma("sp", gbc[:], gmix0_d[0:1, :].broadcast_to([128, D]), writes=["gbc"])
                dma("pool", am[:], amain_d.rearrange("g p t -> p g t"), writes=["am"])
                dma("pool", af[:], afirst_d[pj].rearrange("g p t -> p g t"), writes=["af"])
                dma("pool", ah[:], ahalo_d.rearrange("g p t -> p g t"), writes=["ah"])
                dma("pool", ams[:], amain_s_d.rearrange("g p t -> p g t"), writes=["ams"])
                dma("pool", ahs[:], ahalo_s_d.rearrange("a g p t -> p a g t"), writes=["ahs"])
                dma("pool", uhs[:], uhalo_d.rearrange("a p d -> p a d"), writes=["uhs"])
                dma("pool", pw[:], poolw_d.rearrange("g (k p) d -> p g k d", p=128), writes=["pw"])
                if own:
                    dma("sp", pools_o[:, 0:7, :], spin_d[:, 8:15, :], final=True)

                for (c0, ncol, tiles) in groups:
                    for ti, l in enumerate(tiles):
                        i2 = l % 2
                        is_s = (l == NT)
                        dma("sp", xt[i2][:], xtok_d[pj][l * 128:(l + 1) * 128, :], writes=["xt%d" % i2])
                        S.op("act", lambda e, i2=i2: e.activation(out=junk[:], in_=xt[i2][:], func=AF.Square, accum_out=ss[i2][:, 0:1]),
                             reads=["xt%d" % i2], writes=["junk", "ss%d" % i2])
                        S.op("act", lambda e, i2=i2: e.activation(out=ss[i2][:, 0:1], in_=ss[i2][:, 0:1], func=AF.Ln, scale=1.0 / D, bias=EPS),
                             reads=["ss%d" % i2], writes=["ss%d" % i2])
                        S.op("act", lambda e, i2=i2: e.activation(out=ss[i2][:, 0:1], in_=ss[i2][:, 0:1], func=AF.Exp, scale=-0.5),
                             reads=["ss%d" % i2], writes=["ss%d" % i2])
                        S.op("dve", lambda e, i2=i2: e.scalar_tensor_tensor(out=ub[i2][:], in0=xt[i2][:], scalar=ss[i2][:, 0:1], in1=gbc[:],
                                                                            op0=ALU.mult, op1=ALU.mult),
                             reads=["xt%d" % i2, "ss%d" % i2, "gbc"], writes=["ub%d" % i2])
                        last_of_batch = own and (not is_s) and (l == NT - 1)
                        if is_s or last_of_batch:
                            S.op("dve", lambda e, i2=i2: e.scalar_tensor_tensor(out=uf[:], in0=xt[i2][:], scalar=ss[i2][:, 0:1], in1=gbc[:],
                                                                                op0=ALU.mult, op1=ALU.mult),
                                 reads=["xt%d" % i2, "ss%d" % i2, "gbc"], writes=["uf"])
                            if is_s:
                                dma("sp", pools_o[:, 7:15, :].rearrange("s i d -> (s i) d") if False else pools_o[:, 7:15, :],
                                    uf[:].rearrange("(s i) d -> s i d", i=8) if False else uf[:], reads=["uf"], final=True) if False else None
                                for s_ in range(DPC):
                                    dma("sp", pools_o[s_, 7:15, :], uf[s_ * 8:(s_ + 1) * 8, :], reads=["uf"], final=True)
                            else:
                                dma("sp", poolp_o[:, :], uf[113:128, :], reads=["uf"], final=True)
                        if not is_s:
                            dma("sp", xh[i2][:], xhalo_d[pj][l], writes=["xh%d" % i2])
                            S.op("act", lambda e, i2=i2: e.activation(out=junk[0:16, :], in_=xh[i2][:], func=AF.Square, accum_out=ss[i2][0:16, 1:2]),
                                 reads=["xh%d" % i2], writes=["junk", "ssh%d" % i2])
                            S.op("act", lambda e, i2=i2: e.activation(out=ss[i2][0:16, 1:2], in_=ss[i2][0:16, 1:2], func=AF.Ln, scale=1.0 / D, bias=EPS),
                                 reads=["ssh%d" % i2], writes=["ssh%d" % i2])
                            S.op("act", lambda e, i2=i2: e.activation(out=ss[i2][0:16, 1:2], in_=ss[i2][0:16, 1:2], func=AF.Exp, scale=-0.5),
                                 reads=["ssh%d" % i2], writes=["ssh%d" % i2])
                            S.op("dve", lambda e, i2=i2: e.scalar_tensor_tensor(out=uhb[i2][:], in0=xh[i2][:], scalar=ss[i2][0:16, 1:2], in1=gbc[0:16, :],
                                                                                op0=ALU.mult, op1=ALU.mult),
                                 reads=["xh%d" % i2, "ssh%d" % i2, "gbc"], writes=["uhb%d" % i2])
                        first = (not is_s) and (l == 0)
                        for half in range(2):
                            b = PSK[rot("ps", 2)]
                            bank = ps[int(b[2:])]
                            for cc in range(4):
                                ch = half * 4 + cc
                                g = ch // 2
                                o = bank[:, cc * 128:(cc + 1) * 128]
                                if is_s:
                                    mm(o, ub[i2][:, ch * 128:(ch + 1) * 128], ams[:, g, :], True, False, ["ub%d" % i2, "ams"], [b])
                                    mm(o, uhs[:, 0, ch * 128:(ch + 1) * 128], ahs[:, 0, g, :], False, False, ["uhs", "ahs"], [b])
                                    mm(o, uhs[:, 1, ch * 128:(ch + 1) * 128], ahs[:, 1, g, :], False, True, ["uhs", "ahs"], [b])
                                else:
                                    amat = af if first else am
                                    mm(o, ub[i2][:, ch * 128:(ch + 1) * 128], amat[:, g, :], True, False, ["ub%d" % i2, "af", "am"], [b])
                                    mm(o, uhb[i2][:, ch * 128:(ch + 1) * 128], ah[:, g, :], False, True, ["uhb%d" % i2, "ah"], [b])
                            S.op("act", lambda e, half=half, ti=ti, bank=bank: e.activation(
                                out=dT[:, half * 4:half * 4 + 4, ti * 128:(ti + 1) * 128], in_=bank[:, :].rearrange("p (c t) -> p c t", c=4), func=AF.Copy),
                                reads=[b], writes=[("dT", half, ti)])
                    for j in range(DC):
                        g = j // 2
                        b = PSK[2 + rot("ps", 2)] if False else PSK[rot("ps", 2)]
                        bank = ps[int(b[2:])]
                        for kc in range(2):
                            mm(bank[:, 0:ncol], pw[:, g, kc, (j % 2) * 128:(j % 2) * 128 + 128], dT[:, 2 * g + kc, 0:ncol], kc == 0, kc == 1,
                               ["pw"] + [("dT", (2 * g + kc) // 4, ti) for ti in range(len(tiles))], [b])
                        S.op("dve", lambda e, j=j, bank=bank, c0=c0, ncol=ncol: e.scalar_tensor_tensor(
                            out=hT[:, j, c0:c0 + ncol], in0=bank[:, 0:ncol], scalar=gvec[:, G_PSC + j:G_PSC + j + 1], in1=hT[:, j, c0:c0 + ncol],
                            op0=ALU.mult, op1=ALU.add), reads=[b, "gvec", ("hT", j, c0)], writes=[("hT", j, c0)])

            def mlp(layer, gcol):
                norm_h(gcol)
                for fb in range(NFB):
                    i2 = fb % 2
                    load_w(wA[i2][:], "wA%d" % i2, wup_d[layer].rearrange("(k p) f -> p k f", p=128)[:, :, fb * 512:(fb + 1) * 512])
                    load_w(wB[i2][:], "wB%d" % i2, wdn_d[layer][fb * 512:(fb + 1) * 512, :].rearrange("(k p) d -> p k d", p=128))
                    for gi, (c0, ncol, tiles) in enumerate(groups):
                        a2 = gi % 2
                        for fc in range(4):
                            b = PSK[(0, 1, 4, 5)[rot("up4", 4)]]
                            bank = ps[int(b[2:])]
                            for k in range(DC):
                                mm(bank[:, 0:ncol], wA[i2][:, k, fc * 128:(fc + 1) * 128], uT[:, k, c0:c0 + ncol], k == 0, k == DC - 1,
                                   ["wA%d" % i2, ("uT", k, c0)], [b])
                            tb = rot("tb", 3)
                            S.op("act", lambda e, bank=bank, tb=tb, ncol=ncol: e.activation(out=tmpb[tb][:, 0:ncol], in_=bank[:, 0:ncol], func=AF.Relu),
                                 reads=[b], writes=["tmpb%d" % tb])
                            S.op("dve", lambda e, tb=tb, a2=a2, fc=fc, ncol=ncol: e.tensor_tensor(out=aT[a2][:, fc, 0:ncol], in0=tmpb[tb][:, 0:ncol],
                                                                                             in1=tmpb[tb][:, 0:ncol], op=ALU.mult),
                                 reads=["tmpb%d" % tb], writes=[("aT", a2, fc)])
                        for j in range(DC):
                            b = PSK[(2, 3, 6)[rot("dn3", 3)]]
                            bank = ps[int(b[2:])]
                            for fc in range(4):
                                mm(bank[:, 0:ncol], wB[i2][:, fc, j * 128:(j + 1) * 128], aT[a2][:, fc, 0:ncol], fc == 0, fc == 3,
                                   ["wB%d" % i2, ("aT", a2, fc)], [b])
                            S.op("dve", lambda e, j=j, bank=bank, c0=c0, ncol=ncol: e.tensor_tensor(
                                out=hT[:, j, c0:c0 + ncol], in0=bank[:, 0:ncol], in1=hT[:, j, c0:c0 + ncol], op=ALU.add),
                                reads=[b, ("hT", j, c0)], writes=[("hT", j, c0)])

            def ple(layer, gcol):
                norm_h(gcol)
                for hlf in range(2):
                    load_w(wA[hlf][:], "wA%d" % hlf, wg_d[layer].rearrange("(k p) f -> p k f", p=128)[:, :, hlf * 512:(hlf + 1) * 512])
                load_w(wB[0][:, 0:2, :], "wB0", wp_d[layer].rearrange("(k p) d -> p k d", p=128))
                for gi, (c0, ncol, tiles) in enumerate(groups):
                    a2 = gi % 2
                    dma("pool", aT[a2][:, 0:2, 0:ncol], (pT0_d[pj] if layer == 0 else pT1_d).rearrange("(k p) t -> p k t", p=128)[:, :, c0:c0 + ncol],
                        writes=[("aT", a2, 0), ("aT", a2, 1)])
                    for j in range(DC):
                        b = PSK[rot("ps", 2)]
                        bank = ps[int(b[2:])]
                        for k in range(DC):
                            mm(bank[:, 0:ncol], wA[j // 4][:, k, (j % 4) * 128:(j % 4) * 128 + 128], uT[:, k, c0:c0 + ncol], k == 0, k == DC - 1,
                               ["wA%d" % (j // 4), ("uT", k, c0)], [b])
                        tf = rot("tf", 2)
                        S.op("act", lambda e, bank=bank, tf=tf, ncol=ncol: e.activation(out=tmpf[tf][:, 0:ncol], in_=bank[:, 0:ncol], func=AF.Sigmoid),
                             reads=[b], writes=["tmpf%d" % tf])
                        b2 = PSK[2 + rot("rs", 2)]
                        bank2 = ps[int(b2[2:])]
                        for k in range(2):
                            mm(bank2[:, 0:ncol], wB[0][:, k, j * 128:(j + 1) * 128], aT[a2][:, k, 0:ncol], k == 0, k == 1,
                               ["wB0", ("aT", a2, k)], [b2])
                        S.op("dve", lambda e, bank2=bank2, tf=tf, ncol=ncol: e.tensor_tensor(out=tmpf[tf][:, 0:ncol], in0=bank2[:, 0:ncol],
                                                                                         in1=tmpf[tf][:, 0:ncol], op=ALU.mult),
                             reads=[b2, "tmpf%d" % tf], writes=["tmpf%d" % tf])
                        S.op("dve", lambda e, j=j, tf=tf, c0=c0, ncol=ncol: e.tensor_tensor(out=hT[:, j, c0:c0 + ncol], in0=tmpf[tf][:, 0:ncol],
                                                                                         in1=hT[:, j, c0:c0 + ncol], op=ALU.add),
                             reads=["tmpf%d" % tf, ("hT", j, c0)], writes=[("hT", j, c0)])

            S.barrier()
            top[0] = M_1
            mlp(0, G_MLP0)
            ple(0, G_PLE0)

            if True:
                sb1 = sb
                cf = sb1("cf", [128, 2, 512], F32)
                co = sb1("co", [128, 2, 512], F32)
                cb = sb1("cb", [128, 2, 512], BF16)
                ctok = sb1("ctok", [128, 4, 256], BF16)
                krf = sb1("krf", [64, 2, 512], F32)
                kro = sb1("kro", [64, 512], F32)
                krb = sb1("krb", [64, 512], BF16)
                cs = sb1("cs", [64, 512], F32)
                sn = sb1("sn", [64, 512], F32)
                norm_h(G_KV)
                wkv = wA[0]
                load_w(wkv[:, :, 0:384], "wA0", wdkv_d.rearrange("(k p) f -> p k f", p=128))
                for gi, (c0, ncol, tiles) in enumerate(groups):
                    for mchunk in range(4):
                        msz = 128 if mchunk < 2 else 64
                        m0 = mchunk * 128 if mchunk < 2 else 256 + (mchunk - 2) * 64
                        b = PSK[rot("ps", 2)]
                        bank = ps[int(b[2:])]
                        for k in range(DC):
                            mm(bank[0:msz, 0:ncol], wkv[:, k, m0:m0 + msz], uT[:, k, c0:c0 + ncol], k == 0, k == DC - 1, ["wA0", ("uT", k, c0)], [b])
                        if mchunk < 2:
                            S.op("act", lambda e, bank=bank, mchunk=mchunk, ncol=ncol: e.activation(out=cf[:, mchunk, 0:ncol], in_=bank[:, 0:ncol], func=AF.Copy),
                                 reads=[b], writes=[("cf", mchunk)])
                        else:
                            S.op("act", lambda e, bank=bank, mchunk=mchunk, ncol=ncol: e.activation(out=krf[:, mchunk - 2, 0:ncol], in_=bank[0:64, 0:ncol], func=AF.Copy),
                                 reads=[b], writes=[("krf", mchunk - 2)])
                    norm_T(lambda ch, ncol=ncol: cf[:, ch, 0:ncol], lambda ch: [("cf", ch)], 2, KVR, G_KVN,
                           lambda ch, ncol=ncol: co[:, ch, 0:ncol], lambda ch: [("co", ch)], c0, ncol,
                           out2_fn=lambda ch, ncol=ncol: cb[:, ch, 0:ncol], out2_keys_fn=lambda ch: [("cb", ch)])
                    if own:
                        dma("sp", latT_o.rearrange("(k p) t -> p k t", p=128)[:, :, c0:c0 + ncol], co[:, :, 0:ncol], reads=[("co", 0), ("co", 1)], final=True)
                    dma("sp", cs[:, 0:ncol], cos_d[pj][:, c0:c0 + ncol], writes=["cs"])
                    dma("sp", sn[:, 0:ncol], sin_d[pj][:, c0:c0 + ncol], writes=["sn"])
                    S.op("dve", lambda e, ncol=ncol: e.tensor_tensor(out=krf[:, 0, 0:ncol], in0=krf[:, 0, 0:ncol], in1=cs[:, 0:ncol], op=ALU.mult),
                         reads=[("krf", 0), "cs"], writes=[("krf", 0)])
                    S.op("dve", lambda e, ncol=ncol: e.tensor_tensor(out=krf[:, 1, 0:ncol], in0=krf[:, 1, 0:ncol], in1=sn[:, 0:ncol], op=ALU.mult),
                         reads=[("krf", 1), "sn"], writes=[("krf", 1)])
                    S.op("dve", lambda e, ncol=ncol: e.tensor_tensor(out=kro[:, 0:ncol], in0=krf[:, 0, 0:ncol], in1=krf[:, 1, 0:ncol], op=ALU.add),
                         reads=[("krf", 0), ("krf", 1)], writes=["kro"])
                    S.op("dve", lambda e, ncol=ncol: e.tensor_copy(out=krb[:, 0:ncol], in_=kro[:, 0:ncol]), reads=["kro"], writes=["krb"])
                    if own:
                        dma("sp", krT_o[:, c0:c0 + ncol], kro[:, 0:ncol], reads=["kro"], final=True)
                    if not (own and gi == NG - 1):
                        dma("sp", cx_all.ap()[pj * 576:pj * 576 + 256, c0:c0 + ncol].rearrange("(k p) t -> p k t", p=128), cb[:, :, 0:ncol],
                            reads=[("cb", 0), ("cb", 1)], writes=["cx_all"])
                        dma("sp", cx_all.ap()[pj * 576 + 256:pj * 576 + 320, c0:c0 + ncol], krb[:, 0:ncol], reads=["krb"], writes=["cx_all"])
                        for ti, l in enumerate(tiles):
                            for ch in range(2):
                                S.op("pe", lambda e, ti=ti, ch=ch: e.transpose(out=psb[:, (ti * 2 + ch) * 128:(ti * 2 + ch + 1) * 128],
                                                                              in_=cb[:, ch, ti * 128:(ti + 1) * 128], identity=ident_bf[:]),
                                     reads=[("cb", ch), "ident_bf"], writes=["psb"])
                        nt_ = len(tiles)
                        S.op("act", lambda e, nt_=nt_: e.activation(out=ctok[:, 0:nt_, :], in_=psb[:, 0:nt_ * 256].rearrange("p (t c) -> p t c", c=256), func=AF.Copy),
                             reads=["psb"], writes=["ctok"])
                        dma("sp", c_tok_view(cx_all, pj)[c0:c0 + ncol, :].rearrange("(t p) c -> p t c", p=128), ctok[:, 0:nt_, :], reads=["ctok"], writes=["cx_all"])
                    else:
                        S.op("dve", lambda e: e.tensor_copy(out=sKT[:, 0:2, :], in_=cb[:, :, 0:128]), reads=[("cb", 0), ("cb", 1)], writes=["sKT"])
                        S.op("dve", lambda e: e.tensor_copy(out=sKT[0:64, 2, :], in_=krb[:, 0:128]), reads=["krb"], writes=["sKT"])
                        for ch in range(2):
                            S.op("pe", lambda e, ch=ch: e.transpose(out=psb[:, ch * 128:(ch + 1) * 128], in_=cb[:, ch, 0:128], identity=ident_bf[:]),
                                 reads=[("cb", ch), "ident_bf"], writes=["psb"])
                        S.op("act", lambda e: e.activation(out=sV[:, 2:258], in_=psb[:, 0:256], func=AF.Copy), reads=["psb"], writes=["sV"])

        groups = groups_all
        NG = len(groups)
        if True:
            sb2 = sb
            S.barrier()
            top[0] = M_1
            cqf = sb2("cqf", [128, 3, 512], F32)
            norm_h(G_MIX1)
            wq = wA[1]
            load_w(wq[:, :, 0:QR], "wA1", wdq_d.rearrange("(k p) f -> p k f", p=128))
            for gi, (c0, ncol, tiles) in enumerate(groups):
                for mc in range(3):
                    b = PSK[rot("ps", 2)]
                    bank = ps[int(b[2:])]
                    for k in range(DC):
                        mm(bank[:, 0:ncol], wq[:, k, mc * 128:(mc + 1) * 128], uT[:, k, c0:c0 + ncol], k == 0, k == DC - 1, ["wA1", ("uT", k, c0)], [b])
                    S.op("act", lambda e, bank=bank, mc=mc, ncol=ncol: e.activation(out=cqf[:, mc, 0:ncol], in_=bank[:, 0:ncol], func=AF.Copy),
                         reads=[b], writes=[("cqf", mc)])
                norm_T(lambda ch, ncol=ncol: cqf[:, ch, 0:ncol], lambda ch: [("cqf", ch)], 3, QR, G_Q,
                       lambda ch, c0=c0, ncol=ncol: cqT[:, ch, c0:c0 + ncol], lambda ch, c0=c0: [("cqT", ch, c0)], c0, ncol)
            S.barrier()
            top[0] = M_U
            AGW = 128
            w_uqn = sb2("w_uqn", [128, 3, NH * 128], BF16)
            w_uqr = sb2("w_uqr", [128, 3, NH * 128], BF16)
            w_ukT = sb2("w_ukT", [128, NH * 256], BF16)
            w_uv = sb2("w_uv", [128, 2, NH * 128], BF16)
            w_o = sb2("w_o", [128, NH, D], BF16)
            qT = sb2("qT", [128, 3, NH, AGW], BF16)
            qn = [sb2("qn%d" % i, [128, AGW], BF16) for i in range(2)]
            qrf = sb2("qrf", [64, 2, AGW], F32)
            cs2 = sb2("cs2", [64, AGW], F32)
            sn2 = sb2("sn2", [64, AGW], F32)
            KT = [sb2("KT%d" % i, [128, 3, 8 * 128], BF16) for i in range(2)]
            V = [sb2("V%d" % i, [128, 8, 258], BF16) for i in range(2)]
            PT = tmpb
            acc = sb2("acc", [128, NH, 257], F32)
            rden = sb2("rden", [128, NH], F32)
            Ob = sb2("Ob", [128, NH, 256], BF16)
            OT = sb2("OT", [128, 2, NH, AGW], BF16)
            oT = sb2("oT", [128, NH, AGW], BF16)
            NPB_ = 8 if cfg["SAMPLE_ATT"] else 2
            pgV = [sb2("pgV%d" % i, [128, 322], BF16) for i in range(NPB_)]
            qTs = sb2("qTs", [128, 3, NH, 128], BF16)
            OTs = sb2("OTs", [128, 2, NH, 128], BF16)
            acc_s = sb2("acc_s", [64, 257], F32)
            pKT2 = [sb2("pKT2_%d" % i, [128, 2, 3, 128], BF16) for i in range(2)]
            ps6b = ps[6][:, :].bitcast(BF16)
            tcount = [0]
            Obs = sb2("Obs", [64, 256], BF16)
            rdens = sb2("rdens", [64, 1], F32)
            for i in range(NPB_):
                S.op("dve", lambda e, i=i: e.memset(pgV[i][:, 0:2], 1.0), writes=["pgV%d" % i])
            print("arena top (attention phase):", top[0], ARENA)
            dma("pool", w_uqn[:], wuqn_d.rearrange("(k p) f -> p k f", p=128), writes=["w_uqn"])
            dma("pool", w_uqr[:], wuqr_d.rearrange("(k p) f -> p k f", p=128), writes=["w_uqr"])
            dma("pool", w_ukT[:], wukT_d[:, :], writes=["w_ukT"])
            dma("pool", w_uv[:], wuv_d.rearrange("(k p) f -> p k f", p=128), writes=["w_uv"])
            dma("pool", w_o[:], wo_d.rearrange("(h p) d -> p h d", p=128), writes=["w_o"])
            for i in range(2):
                S.op("dve", lambda e, i=i: e.memset(V[i][:, :, 256:258], 1.0), writes=["V%d" % i])
            agroups = []
            l = 0
            while l < NT:
                n = min(1, NT - l)
                agroups.append((l * 128, n * 128, list(range(l, l + n))))
                l += n
            agroups.append((NT * 128, 128, [NT]))
            kv_cnt = [0]
            NPOOL = cfg["NPOOL"]
            NPB = NPB_
            pcount = [0]

            def qproj(c0, ncol, qT, qk):
                dma("sp", cs2[:, 0:ncol], cos_d[NPASS - 1][:, c0:c0 + ncol], writes=["cs2"])
                dma("sp", sn2[:, 0:ncol], sin_d[NPASS - 1][:, c0:c0 + ncol], writes=["sn2"])
                for h in range(NH):
                    b = PSK[6]
                    bank = ps[6]
                    for k in range(3):
                        mm(bank[:, 0:ncol], w_uqn[:, k, h * 128:(h + 1) * 128], cqT[:, k, c0:c0 + ncol], k == 0, k == 2, ["w_uqn", ("cqT", k, c0)], [b])
                    q2 = h % 2
                    S.op("act", lambda e, q2=q2, ncol=ncol: e.activation(out=qn[q2][:, 0:ncol], in_=ps[6][:, 0:ncol], func=AF.Copy),
                         reads=[b], writes=["qn%d" % q2])
                    for rc in range(2):
                        mm(bank[:, 0:ncol], w_ukT[:, h * 256 + rc * 128:h * 256 + (rc + 1) * 128], qn[q2][:, 0:ncol], True, True, ["w_ukT", "qn%d" % q2], [b])
                        S.op("act", lambda e, rc=rc, h=h, ncol=ncol: e.activation(out=qT[:, rc, h, 0:ncol], in_=ps[6][:, 0:ncol], func=AF.Copy),
                             reads=[b], writes=[(qk, h)])
                    for half in range(2):
                        for k in range(3):
                            mm(bank[0:64, 0:ncol], w_uqr[:, k, h * 128 + half * 64:h * 128 + (half + 1) * 64], cqT[:, k, c0:c0 + ncol], k == 0, k == 2,
                               ["w_uqr", ("cqT", k, c0)], [b])
                        tab = cs2 if half == 0 else sn2
                        S.op("dve", lambda e, half=half, tab=tab, ncol=ncol: e.tensor_tensor(out=qrf[:, half, 0:ncol], in0=ps[6][0:64, 0:ncol], in1=tab[:, 0:ncol], op=ALU.mult),
                             reads=[b, "cs2", "sn2"], writes=[("qrf", half)])
                    S.op("dve", lambda e, h=h, ncol=ncol: e.tensor_tensor(out=qT[0:64, 2, h, 0:ncol], in0=qrf[:, 0, 0:ncol], in1=qrf[:, 1, 0:ncol], op=ALU.add),
                         reads=[("qrf", 0), ("qrf", 1)], writes=[(qk, h)])

            def sample_unit(s_, j0, j1):
                accb = PSK[0]
                accbank = ps[0]
                jl = list(range(j0, j1))
                groups_ = []
                while jl:
                    if len(jl) >= 2 and jl[0] != NPG and jl[1] != NPG:
                        groups_.append(jl[:2]); jl = jl[2:]
                    else:
                        groups_.append(jl[:1]); jl = jl[1:]
                for grp in groups_:
                    own_pg = (grp[0] == NPG)
                    ng = len(grp)
                    sb_ = PSK[4 + rot("sq", 2)]
                    sbank = ps[int(sb_[2:])]
                    if not own_pg:
                        tsel = tcount[0] % 2
                        tcount[0] += 1
                        tb_key = "psb" if tsel == 0 else PSK[6]
                        tbank = psb if tsel == 0 else ps6b
                        kbuf = tsel
                        pbs = []
                        for pp, j in enumerate(grp):
                            pb = pcount[0] % NPB
                            pcount[0] += 1
                            pbs.append(pb)
                            idx = s_ * NPG + j
                            S.op("pool", lambda e, idx=idx, pb=pb: e.indirect_dma_start(
                                out=pgV[pb][:, 2:322], out_offset=None, in_=cache_d.rearrange("n p c -> (n p) c"),
                                in_offset=bass.IndirectOffsetOnAxis(ap=idx_all[:, idx:idx + 1], axis=0)),
                                reads=["idx_all"], writes=["pgV%d" % pb], dma=True)
                            for ch in range(2):
                                S.op("pe", lambda e, pb=pb, ch=ch, pp=pp, tbank=tbank: e.transpose(
                                    out=tbank[:, pp * 384 + ch * 128:pp * 384 + (ch + 1) * 128], in_=pgV[pb][:, 2 + ch * 128:2 + (ch + 1) * 128], identity=ident_bf[:]),
                                    reads=["pgV%d" % pb, "ident_bf"], writes=[tb_key])
                            S.op("pe", lambda e, pb=pb, pp=pp, tbank=tbank: e.transpose(out=tbank[0:64, pp * 384 + 256:pp * 384 + 384], in_=pgV[pb][:, 258:322], identity=ident_bf[:]),
                                 reads=["pgV%d" % pb, "ident_bf"], writes=[tb_key])
                        S.op("act", lambda e, kbuf=kbuf, ng=ng, tbank=tbank: e.activation(
                            out=pKT2[kbuf][:, 0:ng, 0:2, :], in_=tbank[:, 0:ng * 384].rearrange("p (g c t) -> p g c t", g=ng, c=3)[:, :, 0:2, :], func=AF.Copy),
                            reads=[tb_key], writes=["pKT%d" % kbuf])
                        S.op("act", lambda e, kbuf=kbuf, ng=ng, tbank=tbank: e.activation(
                            out=pKT2[kbuf][0:64, 0:ng, 2, :], in_=tbank[0:64, 0:ng * 384].rearrange("p (g c t) -> p g c t", g=ng, c=3)[:, :, 2, :], func=AF.Copy),
                            reads=[tb_key], writes=["pKT%d" % kbuf])
                        srcs = [(pKT2[kbuf][:, pp], "pKT%d" % kbuf, pgV[pbs[pp]], "pgV%d" % pbs[pp]) for pp in range(ng)]
                    else:
                        srcs = [(sKT, "sKT", sV, "sV")]
                    for pp, (kt_ap, kt_key, v_ap, v_key) in enumerate(srcs):
                        for k in range(3):
                            kk = 128 if k < 2 else 64
                            mm(sbank[:, pp * 64:(pp + 1) * 64].rearrange("p (h t) -> p h t", h=NH), kt_ap[0:kk, k, :],
                               qTs[0:kk, k, :, s_ * 8:(s_ + 1) * 8], k == 0, k == 2, [kt_key] + [("qTs", hh) for hh in range(NH)], [sb_])
                    p3 = rot("tb", 3)
                    S.op("act", lambda e, sbank=sbank, p3=p3, ng=ng: e.activation(out=PT[p3][:, 0:ng * 64], in_=sbank[:, 0:ng * 64], func=AF.Exp, scale=SM_SCALE),
                         reads=[sb_], writes=["PT%d" % p3])
                    if own_pg:
                        S.op("dve", lambda e, p3=p3, s_=s_: e.tensor_tensor(
                            out=PT[p3][:, 0:64].rearrange("p (h t) -> p h t", h=NH), in0=PT[p3][:, 0:64].rearrange("p (h t) -> p h t", h=NH),
                            in1=smask_bf[:, s_:s_ + 1, :].broadcast_to([128, NH, 8]), op=ALU.mult),
                            reads=["PT%d" % p3, "smask_bf"], writes=["PT%d" % p3])
                    for pp, (kt_ap, kt_key, v_ap, v_key) in enumerate(srcs):
                        jj = grp[pp]
                        mm(accbank[0:64, 0:257], PT[p3][:, pp * 64:(pp + 1) * 64], v_ap[:, 1:258], jj == j0, jj == j1 - 1, ["PT%d" % p3, v_key], [accb])
                if j0 == 0:
                    S.op("dve", lambda e, accbank=accbank: e.tensor_copy(out=acc_s[:], in_=accbank[0:64, 0:257]), reads=[accb], writes=["acc_s"])
                else:
                    S.op("dve", lambda e, accbank=accbank: e.tensor_tensor(out=acc_s[:], in0=accbank[0:64, 0:257], in1=acc_s[:], op=ALU.add),
                         reads=[accb, "acc_s"], writes=["acc_s"])
                if j1 != NPG + 1:
                    return
                S.op("dve", lambda e: e.reciprocal(out=rdens[:], in_=acc_s[:, 0:1]), reads=["acc_s"], writes=["rdens"])
                S.op("dve", lambda e: e.tensor_scalar(out=Obs[:], in0=acc_s[:, 1:257], scalar1=rdens[:, 0:1], scalar2=None, op0=ALU.mult),
                     reads=["acc_s", "rdens"], writes=["Obs"])
                for rc in range(2):
                    S.op("pe", lambda e, rc=rc: e.transpose(out=psb[:, rc * 64:(rc + 1) * 64], in_=Obs[:, rc * 128:(rc + 1) * 128], identity=ident_bf[0:64, 0:64]),
                         reads=["Obs", "ident_bf"], writes=["psb"])
                for rc in range(2):
                    S.op("act", lambda e, rc=rc, s_=s_: e.activation(out=OTs[:, rc, :, s_ * 8:(s_ + 1) * 8],
                                                                    in_=psb[:, rc * 64:(rc + 1) * 64].rearrange("p (h t) -> p h t", h=NH), func=AF.Copy),
                         reads=["psb"], writes=[("OTs", 0)])

            def outproj(c0, ncol, OT, otkeys):
                for h in range(NH):
                    b = PSK[6]
                    for rc in range(2):
                        mm(ps[6][:, 0:ncol], w_uv[:, rc, h * 128:(h + 1) * 128], OT[:, rc, h, 0:ncol], rc == 0, rc == 1,
                           ["w_uv"] + otkeys, [b])
                    S.op("act", lambda e, h=h, ncol=ncol: e.activation(out=oT[:, h, 0:ncol], in_=ps[6][:, 0:ncol], func=AF.Copy), reads=[b], writes=[("oT", h)])
                for j in range(DC):
                    b = PSK[6]
                    for h in range(NH):
                        mm(ps[6][:, 0:ncol], w_o[:, h, j * 128:(j + 1) * 128], oT[:, h, 0:ncol], h == 0, h == NH - 1, ["w_o", ("oT", h)], [b])
                    S.op("dve", lambda e, j=j, c0=c0, ncol=ncol: e.tensor_tensor(out=hT[:, j, c0:c0 + ncol], in0=ps[6][:, 0:ncol], in1=hT[:, j, c0:c0 + ncol], op=ALU.add),
                         reads=[b, ("hT", j, c0)], writes=[("hT", j, c0)])


            sc0 = NT * 128
            if cfg["SAMPLE_ATT"]:
                qproj(sc0, 128, qTs, "qTs")
            units = []
            UP = 8
            for s__ in range(DPC):
                for j0 in range(0, NPG, UP):
                    j1 = min(NPG, j0 + UP)
                    units.append((s__, j0, j1 + 1 if j1 == NPG else j1))
            n_slots = sum(VR * ((l_ + 8) // 8) for l_ in range(NT))
            per_slot = -(-len(units) // max(1, n_slots))
            upos = [0]

            def pop_units(n):
                if not cfg["SAMPLE_ATT"]:
                    return
                for _ in range(n):
                    if upos[0] < len(units):
                        sample_unit(*units[upos[0]])
                        upos[0] += 1
            for gi, (c0, ncol, tiles) in enumerate(agroups[:-1]):
                qproj(c0, ncol, qT, "qT")
                for ti, l in enumerate(tiles):
                    m_ = l
                    first_chunk = True
                    for r in range(VR):
                      for s0 in range(0, m_ + 1, 8):
                        nsl = min(8, m_ + 1 - s0)
                        dg = m_ - s0
                        kb = kv_cnt[0] % 2
                        kv_cnt[0] += 1
                        dma("sp", KT[kb][:, 0:2, 0:nsl * 128],
                            cx_all.ap()[r * 576:r * 576 + 256, s0 * 128:(s0 + nsl) * 128].rearrange("(k p) t -> p k t", p=128),
                            reads=["cx_all"], writes=["KT%d" % kb])
                        dma("sp", KT[kb][0:64, 2, 0:nsl * 128],
                            cx_all.ap()[r * 576 + 256:r * 576 + 320, s0 * 128:(s0 + nsl) * 128],
                            reads=["cx_all"], writes=["KT%d" % kb])
                        dma("sp", V[kb][:, 0:nsl, 0:256],
                            c_tok_view(cx_all, r)[s0 * 128:(s0 + nsl) * 128, :].rearrange("(s p) c -> p s c", p=128),
                            reads=["cx_all"], writes=["V%d" % kb])
                        for hg in range(2):
                            for sl in range(nsl):
                                sb_ = PSK[4 + rot("sq", 2)]
                                sbank = ps[int(sb_[2:])]
                                for k in range(3):
                                    kk = 128 if k < 2 else 64
                                    mm(sbank[:, :].rearrange("p (h t) -> p h t", h=4), KT[kb][0:kk, k, sl * 128:(sl + 1) * 128],
                                       qT[0:kk, k, hg * 4:hg * 4 + 4, ti * 128:(ti + 1) * 128], k == 0, k == 2,
                                       ["KT%d" % kb] + [("qT", hg * 4 + hh) for hh in range(4)], [sb_])
                                p3 = rot("tb", 3)
                                S.op("act", lambda e, sbank=sbank, p3=p3: e.activation(out=PT[p3][:], in_=sbank[:, :], func=AF.Exp, scale=SM_SCALE),
                                     reads=[sb_], writes=["PT%d" % p3])
                                if dg == sl:
                                    mi = r * 2 + (m_ % 2)
                                    S.op("dve", lambda e, p3=p3, mi=mi: e.tensor_tensor(
                                        out=PT[p3][:, :].rearrange("p (h t) -> p h t", h=4), in0=PT[p3][:, :].rearrange("p (h t) -> p h t", h=4),
                                        in1=mask_bf[:, mi:mi + 1, :].broadcast_to([128, 4, 128]), op=ALU.mult),
                                         reads=["PT%d" % p3, "mask_bf"], writes=["PT%d" % p3])
                                for hh in range(4):
                                    mm(ps[hh][:, 0:257], PT[p3][:, hh * 128:(hh + 1) * 128], V[kb][:, sl, 0:257], sl == 0, sl == nsl - 1,
                                       ["PT%d" % p3, "V%d" % kb], [PSK[hh]])
                            for hh in range(4):
                                h = hg * 4 + hh
                                if first_chunk:
                                    S.op("dve", lambda e, hh=hh, h=h: e.tensor_copy(out=acc[:, h, :], in_=ps[hh][:, 0:257]), reads=[PSK[hh]], writes=[("acc", h)])
                                else:
                                    S.op("dve", lambda e, hh=hh, h=h: e.tensor_tensor(out=acc[:, h, :], in0=ps[hh][:, 0:257], in1=acc[:, h, :], op=ALU.add),
                                         reads=[PSK[hh], ("acc", h)], writes=[("acc", h)])
                        first_chunk = False
                        pop_units(per_slot)
                    S.op("dve", lambda e: e.reciprocal(out=rden[:], in_=acc[:, :, 256]), reads=[("acc", h) for h in range(NH)], writes=["rden"])
                    for h in range(NH):
                        S.op("dve", lambda e, h=h: e.tensor_scalar(out=Ob[:, h, :], in0=acc[:, h, 0:256], scalar1=rden[:, h:h + 1], scalar2=None, op0=ALU.mult),
                             reads=[("acc", h), "rden"], writes=[("Ob", h)])
                    for h4 in range(2):
                        for hh in range(4):
                            h = h4 * 4 + hh
                            for rc in range(2):
                                S.op("pe", lambda e, h=h, hh=hh, rc=rc: e.transpose(out=psb[:, (hh * 2 + rc) * 128:(hh * 2 + rc + 1) * 128],
                                                                                   in_=Ob[:, h, rc * 128:(rc + 1) * 128], identity=ident_bf[:]),
                                     reads=[("Ob", h), "ident_bf"], writes=["psb"])
                        for rc in range(2):
                            S.op("act", lambda e, h4=h4, rc=rc, ti=ti: e.activation(
                                out=OT[:, rc, h4 * 4:h4 * 4 + 4, ti * 128:(ti + 1) * 128],
                                in_=psb[:, :].rearrange("p (h r t) -> p r h t", h=4, r=2)[:, rc], func=AF.Copy),
                                reads=["psb"], writes=[("OT", ti)])
                outproj(c0, ncol, OT, [("OT", ti) for ti in range(len(tiles))])
            if cfg["SAMPLE_ATT"]:
                pop_units(len(units))
                outproj(sc0, 128, OTs, [("OTs", 0)])

        S.barrier()
        top[0] = M_1
        mlp(1, G_MLP1)
        ple(1, G_PLE1)
        for (c0, ncol, tiles) in groups:
            yst = tmpf
            norm_T(lambda ch, c0=c0, ncol=ncol: hT[:, ch, c0:c0 + ncol], lambda ch, c0=c0: [("hT", ch, c0)], DC, D, G_FIN,
                   lambda ch, c0=c0, ncol=ncol: hT[:, ch, c0:c0 + ncol], lambda ch, c0=c0: [("hT", ch, c0)], c0, ncol)
            dma("sp", yT_o.rearrange("(c p) t -> p c t", p=128)[:, :, c0:c0 + ncol], hT[:, :, c0:c0 + ncol],
                reads=[("hT", ch, c0) for ch in range(DC)], final=True)
        S.emit()
    return nc


def vr_blocks(cfg, v):
    VR = cfg["NCORE"] // cfg["BATCH"]
    out = []
    for i in range(cfg["NBLK"] // (2 * VR)):
        out.append(2 * VR * i + v)
        out.append(2 * VR * i + 2 * VR - 1 - v)
    return out


def _pass_vranks(cfg, v):
    VR = cfg["NCORE"] // cfg["BATCH"]
    return [(v + 1 + j) % VR for j in range(VR)]


def _tables(cfg, k):
    NCORE, BATCH, NBLK = cfg["NCORE"], cfg["BATCH"], cfg["NBLK"]
    VR = NCORE // BATCH
    v = k % VR
    DPC = cfg["DEC"] // NCORE
    past = cfg["NPG"] * 128
    half = ROPE // 2
    inv_freq = np.power(np.float32(THETA), -np.arange(half, dtype=np.float32) / np.float32(half)).astype(np.float32)
    t = np.arange(128)
    amain = np.zeros((4, 128, 128), np.float32)
    afirst0 = np.zeros((4, 128, 128), np.float32)
    ahalo = np.zeros((4, 16, 128), np.float32)
    amain_s = np.zeros((4, 128, 128), np.float32)
    ahalo_s = np.zeros((2, 4, 120, 128), np.float32)
    for g, w in enumerate(POOL_W):
        diff = t[None, :] - t[:, None]
        inw = (diff >= 0) & (diff < w)
        amain[g] = inw / np.float32(w) - np.eye(128, dtype=np.float32)
        cntv = np.minimum(t + 1, w).astype(np.float32)
        afirst0[g] = inw / cntv[None, :] - np.eye(128, dtype=np.float32)
        i = np.arange(16)
        ahalo[g] = (i[:, None] >= t[None, :] + 17 - w) / np.float32(w)
        for s_ in range(16):
            for ii in range(8):
                for j in range(8):
                    amain_s[g, s_ * 8 + ii, s_ * 8 + j] = (1.0 / w if 0 <= j - ii < w else 0.0) - (1.0 if ii == j else 0.0)
        for a in range(2):
            for sl in range(8):
                for r in range(15):
                    for j in range(8):
                        if r > j + 15 - w:
                            ahalo_s[a, g, sl * 15 + r, (a * 8 + sl) * 8 + j] = 1.0 / w
    cosT, sinT, afirst = [], [], []
    for vv in _pass_vranks(cfg, v):
        blocks = vr_blocks(cfg, vv)
        pos = [np.arange(blk * 128, (blk + 1) * 128) for blk in blocks]
        pos.append(np.tile(past + np.arange(cfg["DSEQ"]), DPC))
        pos = np.concatenate(pos)
        ang = (pos.astype(np.float32)[:, None] * inv_freq[None, :]).astype(np.float32)
        cos = np.cos(ang).astype(np.float32)
        sin = np.sin(ang).astype(np.float32)
        cosT.append(np.concatenate([cos, cos], 1).T)
        sinT.append(np.concatenate([-sin, sin], 1).T)
        afirst.append(afirst0 if blocks[0] == 0 else amain)
    tri = (t[:, None] <= t[None, :]).astype(np.float32)
    bmask = np.zeros((VR * 2, 128, 128), np.float32)
    for rel, vv in enumerate(_pass_vranks(cfg, v)):
        bmask[rel * 2 + 0] = 1.0 if vv < v else (tri if vv == v else 0.0)
        bmask[rel * 2 + 1] = 1.0 if vv > v else (tri if vv == v else 0.0)
    smask = np.zeros((128, 16, 8), np.float32)
    for key in range(128):
        for i in range(8):
            if key % 8 <= i:
                smask[key, key // 8, i] = 1.0
    f = lambda a: np.ascontiguousarray(np.asarray(a, dtype=np.float32))
    return dict(pidx=np.arange(128, dtype=np.float32).reshape(128, 1), smask=smask.reshape(128, 128), cosT=f(np.stack(cosT)), sinT=f(np.stack(sinT)), amain=amain, afirst=f(np.stack(afirst)), ahalo=ahalo, amain_s=amain_s,
                ahalo_s=ahalo_s, bmask=bmask, identf=np.eye(128, dtype=np.float32))


def _prep(cfg, inp):
    NCORE, BATCH, NBLK = cfg["NCORE"], cfg["BATCH"], cfg["NBLK"]
    VR = NCORE // BATCH
    DPC = cfg["DEC"] // NCORE
    f = lambda a: np.ascontiguousarray(np.asarray(a, dtype=np.float32))
    xp, xs = f(inp["x_prompt"]), f(inp["x_sample"])
    pp, psm = f(inp["p_prompt"]), f(inp["p_sample"])
    stp = f(inp["state_pool"])

    def chunks(vv):
        return f(vv).reshape(-1, 128).T
    gvec = np.zeros((128, 80), np.float32)
    for col, vv in ((0, inp["norm_mix"][1]), (8, inp["norm_mlp"][0]), (16, inp["norm_mlp"][1]), (24, inp["norm_ple"][0]), (32, inp["norm_ple"][1]),
                    (40, inp["pool_scale"][0]), (48, inp["norm_kv"]), (56, inp["norm_final"]), (64, inp["q_norm"][0]), (67, inp["kv_norm"])):
        c = chunks(vv)
        gvec[:, col:col + c.shape[1]] = c
    wdkv = f(inp["w_dkv"])
    w_dkv4 = np.concatenate([wdkv[:, :KVR + ROPE], wdkv[:, KVR + 32:KVR + 64], wdkv[:, KVR:KVR + 32]], 1)
    wuq = f(inp["w_uq"][0])
    w_uq_nope = wuq[:, :, :NOPE].reshape(QR, NH * 128)
    rp = wuq[:, :, NOPE:]
    w_uq_rope = np.concatenate([rp, rp[:, :, 32:], rp[:, :, :32]], 2).reshape(QR, NH * 128)
    w_ukT = f(np.transpose(f(inp["w_uk"]), (2, 1, 0))).reshape(128, NH * 256)
    shared = dict(gvec=gvec, gmix0=f(inp["norm_mix"][0])[None, :], pool_w=f(inp["pool_w"][0]), w_up=f(inp["w_up"]), w_down=f(inp["w_down"]),
                  w_gate=f(inp["w_ple_gate"]), w_proj=f(inp["w_ple_proj"]), w_dkv4=f(w_dkv4), w_dq=f(inp["w_dq"][0]),
                  w_uq_nope=f(w_uq_nope), w_uq_rope=f(w_uq_rope), w_ukT=w_ukT, w_uv=f(inp["w_uv"]).reshape(KVR, NH * 128),
                  w_o=f(inp["w_o"][0]).reshape(NH * 128, D))
    if cfg["SAMPLE_ATT"]:
        shared["cache_comb"] = np.ascontiguousarray(np.concatenate([f(inp["cache_latent"]), f(inp["cache_krope"])], -1))
    maps = []
    for k in range(NCORE):
        b, v = k // VR, k % VR
        xs_k = xs[k * DPC:(k + 1) * DPC].reshape(128, D)
        xtok_l, xT_l, halo_l, pT0_l = [], [], [], []
        for vv in _pass_vranks(cfg, v):
            blocks = vr_blocks(cfg, vv)
            xt_ = np.concatenate([xp[b, blk * 128:(blk + 1) * 128] for blk in blocks] + [xs_k], 0)
            p0 = np.concatenate([pp[0, b, blk * 128:(blk + 1) * 128] for blk in blocks] + [psm[0, k * DPC:(k + 1) * DPC].reshape(128, PLE)], 0)
            hal = np.zeros((len(blocks), 16, D), np.float32)
            for i_, blk in enumerate(blocks):
                if blk > 0:
                    hal[i_] = xp[b, blk * 128 - 16:blk * 128]
            xtok_l.append(xt_)
            xT_l.append(xt_.T)
            halo_l.append(hal)
            pT0_l.append(p0.T)
        blocks = vr_blocks(cfg, v)
        p1 = np.concatenate([pp[1, b, blk * 128:(blk + 1) * 128] for blk in blocks] + [psm[1, k * DPC:(k + 1) * DPC].reshape(128, PLE)], 0)
        m = dict(shared)
        m.update(_tables(cfg, k))
        m["xtok"] = f(np.stack(xtok_l))
        m["xT"] = f(np.stack(xT_l))
        m["xhalo"] = f(np.stack(halo_l))
        m["pT0"] = f(np.stack(pT0_l))
        m["pT1"] = f(p1.T)
        sp = stp[0, k * DPC:(k + 1) * DPC]
        m["spin"] = f(sp)
        m["uhalo"] = f(sp.reshape(2, 120, D))
        if cfg["SAMPLE_ATT"]:
            m["ptab"] = np.ascontiguousarray(np.asarray(inp["page_table"])[k * DPC:(k + 1) * DPC].astype(np.int32).reshape(1, -1))
        maps.append(m)
    return maps


def _assemble(cfg, res):
    NCORE, BATCH, NBLK = cfg["NCORE"], cfg["BATCH"], cfg["NBLK"]
    VR = NCORE // BATCH
    DPC = cfg["DEC"] // NCORE
    SEQ = NBLK * 128
    y_p = np.zeros((BATCH, SEQ, D), np.float32)
    lat_p = np.zeros((BATCH, SEQ, KVR), np.float32)
    kr_p = np.zeros((BATCH, SEQ, ROPE), np.float32)
    y_s = np.zeros((cfg["DEC"], cfg["DSEQ"], D), np.float32)
    lat_s = np.zeros((cfg["DEC"], cfg["DSEQ"], KVR), np.float32)
    kr_s = np.zeros((cfg["DEC"], cfg["DSEQ"], ROPE), np.float32)
    pool_s = np.zeros((1, cfg["DEC"], 15, D), np.float32)
    pool_p = np.zeros((1, BATCH, 15, D), np.float32)
    for k in range(NCORE):
        r = res[k]
        b, v = k // VR, k % VR
        yT, latT, krT = np.asarray(r["yT"]), np.asarray(r["latT"]), np.asarray(r["krT"])
        l = 0
        for blk in vr_blocks(cfg, v):
            sl = slice(l * 128, (l + 1) * 128)
            y_p[b, blk * 128:(blk + 1) * 128] = yT[:, sl].T
            lat_p[b, blk * 128:(blk + 1) * 128] = latT[:, sl].T
            kr_p[b, blk * 128:(blk + 1) * 128] = krT[:, sl].T
            l += 1
        sl = slice(l * 128, (l + 1) * 128)
        y_s[k * DPC:(k + 1) * DPC] = yT[:, sl].T.reshape(DPC, cfg["DSEQ"], D)
        lat_s[k * DPC:(k + 1) * DPC] = latT[:, sl].T.reshape(DPC, cfg["DSEQ"], KVR)
        kr_s[k * DPC:(k + 1) * DPC] = krT[:, sl].T.reshape(DPC, cfg["DSEQ"], ROPE)
        pool_s[0, k * DPC:(k + 1) * DPC] = np.asarray(r["pools"])
        if v == 0:
            pool_p[0, b] = np.asarray(r["poolp"])
    return (y_p, y_s, pool_p, pool_s, lat_p, kr_p, lat_s, kr_s)


_NC_CACHE = {}


def kernel(**inputs):
    cfg = dict(FULL)
    key = "full"
    if key not in _NC_CACHE:
        _NC_CACHE[key] = build_program(cfg)
    nc = _NC_CACHE[key]
    maps = _prep(cfg, inputs)
    res = run_bass_kernel_spmd(nc, maps, core_ids=list(range(cfg["NCORE"])))
    return _assemble(cfg, res.results)
```

```python
import contextlib
import math
import numpy as np
import concourse.bass as bass
import concourse.mybir as mybir
from concourse.bass_utils import run_bass_kernel_spmd

F32 = mybir.dt.float32
BF16 = mybir.dt.bfloat16
I32 = mybir.dt.int32
AF = mybir.ActivationFunctionType
ALU = mybir.AluOpType

COMPUTE = ("pe", "act", "dve", "pool")
N_DMA_SLOTS = {"sp": 8, "pool": 4, "act": 2, "pe": 1, "dve": 1}


class Sched:
    def __init__(self, nc):
        self.nc = nc
        self.eng_names = ("pe", "act", "dve", "pool", "sp")
        self.prog = {e: [] for e in self.eng_names}
        self.n_comp = {e: 0 for e in self.eng_names}
        self.n_dma = {e: 0 for e in self.eng_names}
        self.slot_cnt = {}
        self.last_w = {}
        self.readers = {}
        self.known = {e: {} for e in self.eng_names}
        self.targets = {e: set() for e in self.eng_names}
        self.final_dma = []
        self.n_cc = 0

    def _add_dep(self, deps, tok, eng):
        if tok is None:
            return
        if tok[0] == "c":
            if tok[1] == eng and eng == "pe":
                return
            lane = ("c", tok[1])
            v = tok[2]
        else:
            lane = (tok[0], tok[1], tok[2])
            v = tok[3]
        if deps.get(lane, 0) < v:
            deps[lane] = v

    def op(self, eng, fn, reads=(), writes=(), dma=False, final=False, cc=False):
        deps = {}
        for k in reads:
            self._add_dep(deps, self.last_w.get(k), eng)
        for k in writes:
            self._add_dep(deps, self.last_w.get(k), eng)
            for t in self.readers.get(k, ()):
                self._add_dep(deps, t, eng)
        if cc:
            self.n_cc += 1
            tok = ("x", "cc", self.n_cc, 1)
        elif dma:
            n = self.n_dma[eng]
            self.n_dma[eng] = n + 1
            slot = n % N_DMA_SLOTS[eng]
            cnt = self.slot_cnt.get((eng, slot), 0) + 1
            self.slot_cnt[(eng, slot)] = cnt
            tok = ("d", eng, slot, cnt)
            if cnt > 1:
                lane = ("d", eng, slot)
                if deps.get(lane, 0) < cnt - 1:
                    deps[lane] = cnt - 1
        else:
            self.n_comp[eng] += 1
            tok = ("c", eng, self.n_comp[eng])
        waits = []
        kn = self.known[eng]
        for lane, v in deps.items():
            if kn.get(lane, 0) >= v:
                continue
            kn[lane] = v
            waits.append((lane, v))
            if lane[0] == "c":
                self.targets[lane[1]].add(v)
        self.prog[eng].append((waits, fn, tok))
        for k in reads:
            self.readers.setdefault(k, []).append(tok)
        for k in writes:
            self.last_w[k] = tok
            self.readers[k] = []
        if final:
            self.final_dma.append(tok)
        return tok

    def barrier(self):
        for eng in self.eng_names:
            deps = {}
            for F in COMPUTE:
                if self.n_comp[F] > 0 and not (F == eng and eng == "pe"):
                    deps[("c", F)] = self.n_comp[F]
            for (e, s_), c in self.slot_cnt.items():
                deps[("d", e, s_)] = c
            for i in range(1, self.n_cc + 1):
                deps[("x", "cc", i)] = 1
            waits = []
            kn = self.known[eng]
            for lane, v in deps.items():
                if kn.get(lane, 0) >= v:
                    continue
                kn[lane] = v
                waits.append((lane, v))
                if lane[0] == "c":
                    self.targets[lane[1]].add(v)
            self.prog[eng].append((waits, None, None))

    def emit(self):
        nc = self.nc
        with contextlib.ExitStack() as st:
            dsem = {}
            for i in range(1, self.n_cc + 1):
                dsem[("x", "cc", i)] = st.enter_context(nc.semaphore("cc_%d" % i))
            csem = {e: st.enter_context(nc.semaphore("c_" + e)) for e in COMPUTE}
            for (e, s) in self.slot_cnt:
                dsem[("d", e, s)] = st.enter_context(nc.semaphore("d_%s_%d" % (e, s)))
            cum = {}
            for e in COMPUTE:
                tg = sorted(self.targets[e])
                cum[e] = {idx: i + 1 for i, idx in enumerate(tg)}
            fin = {}
            for tok in self.final_dma:
                lane = ("d", tok[1], tok[2])
                fin[lane] = max(fin.get(lane, 0), tok[3])
            block = st.enter_context(nc.Block())
            engobj = {"pe": "tensor", "act": "scalar", "dve": "vector", "pool": "gpsimd", "sp": "sync"}

            def build(ename):
                def body(eng):
                    for waits, fn, tok in self.prog[ename]:
                        for lane, v in waits:
                            if lane[0] == "c":
                                eng.wait_ge(csem[lane[1]], cum[lane[1]][v])
                            elif lane[0] == "d":
                                eng.wait_ge(dsem[lane], 16 * v)
                            else:
                                eng.wait_ge(dsem[lane], v)
                        if fn is None:
                            continue
                        ins = fn(eng)
                        if tok[0] == "c":
                            if tok[2] in cum[tok[1]]:
                                ins.then_inc(csem[tok[1]], 1)
                        elif tok[0] == "d":
                            ins.then_inc(dsem[(tok[0], tok[1], tok[2])], 16)
                        else:
                            ins.then_inc(dsem[(tok[0], tok[1], tok[2])], 1)
                    if ename == "sp":
                        for lane, v in fin.items():
                            eng.wait_ge(dsem[lane], 16 * v)
                return body

            for ename in self.eng_names:
                if not self.prog[ename] and ename != "sp":
                    continue
                getattr(block, engobj[ename])(build(ename))


D = 1024
DC = 8
KVR = 256
QR = 384
NH = 8
NOPE = 128
ROPE = 64
PLE = 256
POOL_W = (2, 4, 8, 16)
EPS = 1e-6
SM_SCALE = 1.0 / math.sqrt(NOPE + ROPE)
THETA = 10000.0

FULL = dict(NCORE=8, BATCH=2, NBLK=64, DEC=128, DSEQ=8, NPG=64, NPOOL=10240, FF=4096, USE_CC=False, SAMPLE_ATT=True)


def core_blocks(cfg, k):
    nc_, nb = cfg["NCORE"], cfg["NBLK"]
    out = []
    for i in range(nb // (2 * nc_)):
        out.append(2 * nc_ * i + k)
        out.append(2 * nc_ * i + 2 * nc_ - 1 - k)
    return out


def owner_of(cfg, j):
    nc_ = cfg["NCORE"]
    r = j % (2 * nc_)
    k = r if r < nc_ else 2 * nc_ - 1 - r
    m = (j // (2 * nc_)) * 2 + (0 if r < nc_ else 1)
    return k, m


def build_program(cfg):
    NCORE, BATCH, NBLK, FF = cfg["NCORE"], cfg["BATCH"], cfg["NBLK"], cfg["FF"]
    VR = NCORE // BATCH
    NPASS = VR
    NTB = NBLK // VR
    NT = NTB
    T = (NT + 1) * 128
    TP = NT * 128
    NFB = FF // 512
    DPC = cfg["DEC"] // NCORE
    NPG = cfg["NPG"]
    assert DPC * cfg["DSEQ"] == 128
    groups = []
    l = 0
    while l < NT:
        n = min(4, NT - l)
        groups.append((l * 128, n * 128, list(range(l, l + n))))
        l += n
    groups.append((NT * 128, 128, [NT]))
    NG = len(groups)
    groups_all = groups

    nc = bass.Bass("TRN2", target_bir_lowering=False)

    def din(name, shape, dt=F32):
        return nc.dram_tensor(name, list(shape), dt, kind="ExternalInput").ap()

    def dout(name, shape, dt=F32):
        return nc.dram_tensor(name, list(shape), dt, kind="ExternalOutput").ap()

    xT_d = din("xT", [NPASS, D, T])
    xtok_d = din("xtok", [NPASS, T, D])
    xhalo_d = din("xhalo", [NPASS, NT, 16, D])
    uhalo_d = din("uhalo", [2, 120, D])
    spin_d = din("spin", [DPC, 15, D])
    pT0_d = din("pT0", [NPASS, PLE, T])
    pT1_d = din("pT1", [PLE, T])
    gvec_d = din("gvec", [128, 80])
    gmix0_d = din("gmix0", [1, D])
    cos_d = din("cosT", [NPASS, 64, T])
    sin_d = din("sinT", [NPASS, 64, T])
    amain_d = din("amain", [4, 128, 128])
    afirst_d = din("afirst", [NPASS, 4, 128, 128])
    ahalo_d = din("ahalo", [4, 16, 128])
    amain_s_d = din("amain_s", [4, 128, 128])
    ahalo_s_d = din("ahalo_s", [2, 4, 120, 128])
    bmask_d = din("bmask", [VR * 2, 128, 128])
    identf_d = din("identf", [128, 128])
    poolw_d = din("pool_w", [4, 256, 256])
    wup_d = din("w_up", [2, D, FF])
    wdn_d = din("w_down", [2, FF, D])
    wg_d = din("w_gate", [2, D, D])
    wp_d = din("w_proj", [2, PLE, D])
    wdkv_d = din("w_dkv4", [D, 384])
    wdq_d = din("w_dq", [D, QR])
    wuqn_d = din("w_uq_nope", [QR, NH * 128])
    wuqr_d = din("w_uq_rope", [QR, NH * 128])
    wukT_d = din("w_ukT", [128, NH * 256])
    wuv_d = din("w_uv", [KVR, NH * 128])
    wo_d = din("w_o", [NH * 128, D])
    if cfg["SAMPLE_ATT"]:
        cache_d = din("cache_comb", [cfg["NPOOL"], 128, KVR + ROPE])
        ptab_d = din("ptab", [1, DPC * NPG], I32)
        smask_d = din("smask", [128, 128])
        pidx_d = din("pidx", [128, 1])

    yT_o = dout("yT", [D, T])
    latT_o = dout("latT", [KVR, T])
    krT_o = dout("krT", [ROPE, T])
    poolp_o = dout("poolp", [15, D])
    pools_o = dout("pools", [DPC, 15, D])

    cx_all = nc.dram_tensor("cx_all", [VR * 576, TP], BF16)

    def c_tok_view(t, r):
        return t.ap()[r * 576 + 320:(r + 1) * 576, :].rearrange("a (t c) -> (a t) c", c=256)

    S = Sched(nc)
    st = contextlib.ExitStack()
    with st:
        ARENA = 206 * 1024
        arena = st.enter_context(nc.sbuf_tensor("arena", [128, ARENA], mybir.dt.uint8))
        top = [0]

        def sb(name, shape, dt):
            nb = 2 if dt == BF16 else 4
            per = int(np.prod(shape[1:])) * nb
            off = (top[0] + 63) // 64 * 64
            assert off + per <= ARENA, (name, off, per, ARENA)
            top[0] = off + per
            v = arena[0:shape[0], off:off + per].bitcast(dt)
            if len(shape) == 3:
                v = v.rearrange("p (a b) -> p a b", a=shape[1])
            elif len(shape) == 4:
                v = v.rearrange("p (a b c) -> p a b c", a=shape[1], b=shape[2])
            return v

        ps = [st.enter_context(nc.psum_tensor("ps%d" % i, [128, 512], F32)) for i in range(7)]
        psb = st.enter_context(nc.psum_tensor("psb", [128, 1024], BF16))
        PSK = ["ps%d" % i for i in range(7)]

        hT = sb("hT", [128, DC, T], F32)
        gvec = sb("gvec", [128, 80], F32)
        ones_bf = sb("ones_bf", [128, 128], BF16)
        ident_bf = sb("ident_bf", [128, 128], BF16)
        mask_bf = sb("mask_bf", [128, VR * 2, 128], BF16)
        sqb = [sb("sqb%d" % i, [128, 512], BF16) for i in range(2)]
        rstd = [sb("rstd%d" % i, [128, 512], F32) for i in range(2)]
        tmpf = [sb("tmpf%d" % i, [128, 512], F32) for i in range(2)]
        tmpb = [sb("tmpb%d" % i, [128, 512], BF16) for i in range(3)]
        sKT = sb("sKT", [128, 3, 128], BF16)
        sV = sb("sV", [128, 258], BF16)
        smask_bf = sb("smask_bf", [128, 16, 8], BF16)
        idx_all = sb("idx_all", [128, DPC * NPG], I32)
        pidx = sb("pidx", [128, 1], F32)
        cqT = sb("cqT", [128, 3, T], BF16)
        M_U = top[0]
        uT = sb("uT", [128, DC, T], BF16)
        wA = [sb("wA%d" % i, [128, 8, 512], BF16) for i in range(2)]
        wB = [sb("wB%d" % i, [128, 4, 1024], BF16) for i in range(2)]
        aT = [sb("aT%d" % i, [128, 4, 512], BF16) for i in range(2)]
        M_1 = top[0]

        G_MIX1, G_MLP0, G_MLP1, G_PLE0, G_PLE1, G_PSC, G_KV, G_FIN, G_Q, G_KVN = 0, 8, 16, 24, 32, 40, 48, 56, 64, 67

        cnt = {"ps": 0, "sq": 0, "rs": 0, "tf": 0, "tb": 0, "up4": 0, "dn3": 0}

        def rot(kind, n):
            v = cnt[kind]
            cnt[kind] = (v + 1) % n
            return v

        def dma(eng, out, in_, reads=(), writes=(), final=False, **kw):
            return S.op(eng, lambda e: e.dma_start(out=out, in_=in_, **kw), reads=reads, writes=writes, dma=True, final=final)

        def mm(out, lhsT, rhs, start, stop, reads, writes):
            S.op("pe", lambda e: e.matmul(out, lhsT=lhsT, rhs=rhs, start=start, stop=stop), reads=reads, writes=writes)

        dma("sp", gvec[:], gvec_d[:, :], writes=["gvec"])
        dma("pool", ident_bf[:], identf_d[:, :], writes=["ident_bf"])
        dma("pool", mask_bf[:], bmask_d.rearrange("m p t -> p m t"), writes=["mask_bf"])
        S.op("dve", lambda e: e.memset(ones_bf[:], 1.0), writes=["ones_bf"])
        if cfg["SAMPLE_ATT"]:
            dma("pool", smask_bf[:], smask_d.rearrange("p (s i) -> p s i", i=8), writes=["smask_bf"])
            dma("sp", idx_all[:], ptab_d[0:1, :].broadcast_to([128, DPC * NPG]), writes=["idx_all"])
            dma("sp", pidx[:], pidx_d[:, :], writes=["pidx"])
            S.op("dve", lambda e: e.tensor_scalar(out=idx_all[:], in0=idx_all[:], scalar1=128.0, scalar2=pidx[:, 0:1], op0=ALU.mult, op1=ALU.add),
                 reads=["idx_all", "pidx"], writes=["idx_all"])
            S.op("dve", lambda e: e.memset(sV[:, 0:2], 1.0), writes=["sV"])
        for pj in range(NPASS):
            own = (pj == NPASS - 1)
            groups = groups_all if own else groups_all[:-1]
            NG = len(groups)
            S.barrier()
            top[0] = M_1
            for (c0, ncol, tiles) in groups:
                gi = c0
                dma("sp", hT[:, :, c0:c0 + ncol], xT_d[pj].rearrange("(c p) t -> p c t", p=128)[:, :, c0:c0 + ncol],
                    writes=[("hT", ch, gi) for ch in range(DC)])

            def norm_T(src_fn, src_keys_fn, nch, dim, gcol, out_fn, out_keys_fn, g_id, ncol, eps=EPS, out2_fn=None, out2_keys_fn=None):
                b = PSK[rot("ps", 2)]
                bank = ps[int(b[2:])]
                for ch in range(nch):
                    q = rot("sq", 2)
                    S.op("act", lambda e, ch=ch, q=q: e.activation(out=sqb[q][:, 0:ncol], in_=src_fn(ch), func=AF.Square),
                         reads=src_keys_fn(ch), writes=["sqb%d" % q])
                    mm(bank[:, 0:ncol], ones_bf[:], sqb[q][:, 0:ncol], ch == 0, ch == nch - 1, ["ones_bf", "sqb%d" % q], [b])
                r = rot("rs", 2)
                S.op("act", lambda e: e.activation(out=rstd[r][:, 0:ncol], in_=bank[:, 0:ncol], func=AF.Ln, scale=1.0 / dim, bias=eps),
                     reads=[b], writes=["rstd%d" % r])
                S.op("act", lambda e: e.activation(out=rstd[r][:, 0:ncol], in_=rstd[r][:, 0:ncol], func=AF.Exp, scale=-0.5),
                     reads=["rstd%d" % r], writes=["rstd%d" % r])
                for ch in range(nch):
                    S.op("dve", lambda e, ch=ch: e.scalar_tensor_tensor(out=out_fn(ch), in0=src_fn(ch), scalar=gvec[:, gcol + ch:gcol + ch + 1],
                                                                        in1=rstd[r][:, 0:ncol], op0=ALU.mult, op1=ALU.mult),
                         reads=list(src_keys_fn(ch)) + ["gvec", "rstd%d" % r], writes=out_keys_fn(ch))
                    if out2_fn is not None:
                        S.op("dve", lambda e, ch=ch: e.scalar_tensor_tensor(out=out2_fn(ch), in0=src_fn(ch), scalar=gvec[:, gcol + ch:gcol + ch + 1],
                                                                            in1=rstd[r][:, 0:ncol], op0=ALU.mult, op1=ALU.mult),
                             reads=list(src_keys_fn(ch)) + ["gvec", "rstd%d" % r], writes=out2_keys_fn(ch))

            def norm_h(gcol):
                for (c0, ncol, tiles) in groups:
                    norm_T(lambda ch, c0=c0, ncol=ncol: hT[:, ch, c0:c0 + ncol], lambda ch, c0=c0: [("hT", ch, c0)], DC, D, gcol,
                           lambda ch, c0=c0, ncol=ncol: uT[:, ch, c0:c0 + ncol], lambda ch, c0=c0: [("uT", ch, c0)], c0, ncol)

            def load_w(buf, bufkey, src_ap):
                dma("pool", buf, src_ap, writes=[bufkey])

            if True:
                top[0] = M_U
                sb0 = sb
                xt = [sb0("xt%d" % i, [128, D], F32) for i in range(2)]
                xh = [sb0("xh%d" % i, [16, D], F32) for i in range(2)]
                gbc = sb0("gbc", [128, D], F32)
                ub = [sb0("ub%d" % i, [128, D], BF16) for i in range(2)]
                uhb = [sb0("uhb%d" % i, [16, D], BF16) for i in range(2)]
                uf = sb0("uf", [128, D], F32)
                junk = sb0("junk", [128, D], F32)
                ss = [sb0("ss%d" % i, [128, 2], F32) for i in range(2)]
                am = sb0("am", [128, 4, 128], BF16)
                af = sb0("af", [128, 4, 128], BF16)
                ah = sb0("ah", [16, 4, 128], BF16)
                ams = sb0("ams", [128, 4, 128], BF16)
                ahs = sb0("ahs", [120, 2, 4, 128], BF16)
                uhs = sb0("uhs", [120, 2, D], BF16)
                dT = sb0("dT", [128, DC, 512], BF16)
                pw = sb0("pw", [128, 4, 2, 256], BF16)

                dma("sp", gbc[:], gmix0_d[0:1, :].broadcast_to([128, D]), writes=["gbc"])
                dma("pool", am[:], amain_d.rearrange("g p t -> p g t"), writes=["am"])
                dma("pool", af[:], afirst_d[pj].rearrange("g p t -> p g t"), writes=["af"])
                dma("pool", ah[:], ahalo_d.rearrange("g p t -> p g t"), writes=["ah"])
                dma("pool", ams[:], amain_s_d.rearrange("g p t -> p g t"), writes=["ams"])
                dma("pool", ahs[:], ahalo_s_d.rearrange("a g p t -> p a g t"), writes=["ahs"])
                dma("pool", uhs[:], uhalo_d.rearrange("a p d -> p a d"), writes=["uhs"])
                dma("pool", pw[:], poolw_d.rearrange("g (k p) d -> p g k d", p=128), writes=["pw"])
                if own:
                    dma("sp", pools_o[:, 0:7, :], spin_d[:, 8:15, :], final=True)

                for (c0, ncol, tiles) in groups:
                    for ti, l in enumerate(tiles):
                        i2 = l % 2
                        is_s = (l == NT)
                        dma("sp", xt[i2][:], xtok_d[pj][l * 128:(l + 1) * 128, :], writes=["xt%d" % i2])
                        S.op("act", lambda e, i2=i2: e.activation(out=junk[:], in_=xt[i2][:], func=AF.Square, accum_out=ss[i2][:, 0:1]),
                             reads=["xt%d" % i2], writes=["junk", "ss%d" % i2])
                        S.op("act", lambda e, i2=i2: e.activation(out=ss[i2][:, 0:1], in_=ss[i2][:, 0:1], func=AF.Ln, scale=1.0 / D, bias=EPS),
                             reads=["ss%d" % i2], writes=["ss%d" % i2])
                        S.op("act", lambda e, i2=i2: e.activation(out=ss[i2][:, 0:1], in_=ss[i2][:, 0:1], func=AF.Exp, scale=-0.5),
                             reads=["ss%d" % i2], writes=["ss%d" % i2])
                        S.op("dve", lambda e, i2=i2: e.scalar_tensor_tensor(out=ub[i2][:], in0=xt[i2][:], scalar=ss[i2][:, 0:1], in1=gbc[:],
                                                                            op0=ALU.mult, op1=ALU.mult),
                             reads=["xt%d" % i2, "ss%d" % i2, "gbc"], writes=["ub%d" % i2])
                        last_of_batch = own and (not is_s) and (l == NT - 1)
                        if is_s or last_of_batch:
                            S.op("dve", lambda e, i2=i2: e.scalar_tensor_tensor(out=uf[:], in0=xt[i2][:], scalar=ss[i2][:, 0:1], in1=gbc[:],
                                                                                op0=ALU.mult, op1=ALU.mult),
                                 reads=["xt%d" % i2, "ss%d" % i2, "gbc"], writes=["uf"])
                            if is_s:
                                dma("sp", pools_o[:, 7:15, :].rearrange("s i d -> (s i) d") if False else pools_o[:, 7:15, :],
                                    uf[:].rearrange("(s i) d -> s i d", i=8) if False else uf[:], reads=["uf"], final=True) if False else None
                                for s_ in range(DPC):
                                    dma("sp", pools_o[s_, 7:15, :], uf[s_ * 8:(s_ + 1) * 8, :], reads=["uf"], final=True)
                            else:
                                dma("sp", poolp_o[:, :], uf[113:128, :], reads=["uf"], final=True)
                        if not is_s:
                            dma("sp", xh[i2][:], xhalo_d[pj][l], writes=["xh%d" % i2])
                            S.op("act", lambda e, i2=i2: e.activation(out=junk[0:16, :], in_=xh[i2][:], func=AF.Square, accum_out=ss[i2][0:16, 1:2]),
                                 reads=["xh%d" % i2], writes=["junk", "ssh%d" % i2])
                            S.op("act", lambda e, i2=i2: e.activation(out=ss[i2][0:16, 1:2], in_=ss[i2][0:16, 1:2], func=AF.Ln, scale=1.0 / D, bias=EPS),
                                 reads=["ssh%d" % i2], writes=["ssh%d" % i2])
                            S.op("act", lambda e, i2=i2: e.activation(out=ss[i2][0:16, 1:2], in_=ss[i2][0:16, 1:2], func=AF.Exp, scale=-0.5),
                                 reads=["ssh%d" % i2], writes=["ssh%d" % i2])
                            S.op("dve", lambda e, i2=i2: e.scalar_tensor_tensor(out=uhb[i2][:], in0=xh[i2][:], scalar=ss[i2][0:16, 1:2], in1=gbc[0:16, :],
                                                                                op0=ALU.mult, op1=ALU.mult),
                                 reads=["xh%d" % i2, "ssh%d" % i2, "gbc"], writes=["uhb%d" % i2])
                        first = (not is_s) and (l == 0)
                        for half in range(2):
                            b = PSK[rot("ps", 2)]
                            bank = ps[int(b[2:])]
                            for cc in range(4):
                                ch = half * 4 + cc
                                g = ch // 2
                                o = bank[:, cc * 128:(cc + 1) * 128]
                                if is_s:
                                    mm(o, ub[i2][:, ch * 128:(ch + 1) * 128], ams[:, g, :], True, False, ["ub%d" % i2, "ams"], [b])
                                    mm(o, uhs[:, 0, ch * 128:(ch + 1) * 128], ahs[:, 0, g, :], False, False, ["uhs", "ahs"], [b])
                                    mm(o, uhs[:, 1, ch * 128:(ch + 1) * 128], ahs[:, 1, g, :], False, True, ["uhs", "ahs"], [b])
                                else:
                                    amat = af if first else am
                                    mm(o, ub[i2][:, ch * 128:(ch + 1) * 128], amat[:, g, :], True, False, ["ub%d" % i2, "af", "am"], [b])
                                    mm(o, uhb[i2][:, ch * 128:(ch + 1) * 128], ah[:, g, :], False, True, ["uhb%d" % i2, "ah"], [b])
                            S.op("act", lambda e, half=half, ti=ti, bank=bank: e.activation(
                                out=dT[:, half * 4:half * 4 + 4, ti * 128:(ti + 1) * 128], in_=bank[:, :].rearrange("p (c t) -> p c t", c=4), func=AF.Copy),
                                reads=[b], writes=[("dT", half, ti)])
                    for j in range(DC):
                        g = j // 2
                        b = PSK[2 + rot("ps", 2)] if False else PSK[rot("ps", 2)]
                        bank = ps[int(b[2:])]
                        for kc in range(2):
                            mm(bank[:, 0:ncol], pw[:, g, kc, (j % 2) * 128:(j % 2) * 128 + 128], dT[:, 2 * g + kc, 0:ncol], kc == 0, kc == 1,
                               ["pw"] + [("dT", (2 * g + kc) // 4, ti) for ti in range(len(tiles))], [b])
                        S.op("dve", lambda e, j=j, bank=bank, c0=c0, ncol=ncol: e.scalar_tensor_tensor(
                            out=hT[:, j, c0:c0 + ncol], in0=bank[:, 0:ncol], scalar=gvec[:, G_PSC + j:G_PSC + j + 1], in1=hT[:, j, c0:c0 + ncol],
                            op0=ALU.mult, op1=ALU.add), reads=[b, "gvec", ("hT", j, c0)], writes=[("hT", j, c0)])

            def mlp(layer, gcol):
                norm_h(gcol)
                for fb in range(NFB):
                    i2 = fb % 2
                    load_w(wA[i2][:], "wA%d" % i2, wup_d[layer].rearrange("(k p) f -> p k f", p=128)[:, :, fb * 512:(fb + 1) * 512])
                    load_w(wB[i2][:], "wB%d" % i2, wdn_d[layer][fb * 512:(fb + 1) * 512, :].rearrange("(k p) d -> p k d", p=128))
                    for gi, (c0, ncol, tiles) in enumerate(groups):
                        a2 = gi % 2
                        for fc in range(4):
                            b = PSK[(0, 1, 4, 5)[rot("up4", 4)]]
                            bank = ps[int(b[2:])]
                            for k in range(DC):
                                mm(bank[:, 0:ncol], wA[i2][:, k, fc * 128:(fc + 1) * 128], uT[:, k, c0:c0 + ncol], k == 0, k == DC - 1,
                                   ["wA%d" % i2, ("uT", k, c0)], [b])
                            tb = rot("tb", 3)
                            S.op("act", lambda e, bank=bank, tb=tb, ncol=ncol: e.activation(out=tmpb[tb][:, 0:ncol], in_=bank[:, 0:ncol], func=AF.Relu),
                                 reads=[b], writes=["tmpb%d" % tb])
                            S.op("dve", lambda e, tb=tb, a2=a2, fc=fc, ncol=ncol: e.tensor_tensor(out=aT[a2][:, fc, 0:ncol], in0=tmpb[tb][:, 0:ncol],
                                                                                             in1=tmpb[tb][:, 0:ncol], op=ALU.mult),
                                 reads=["tmpb%d" % tb], writes=[("aT", a2, fc)])
                        for j in range(DC):
                            b = PSK[(2, 3, 6)[rot("dn3", 3)]]
                            bank = ps[int(b[2:])]
                            for fc in range(4):
                                mm(bank[:, 0:ncol], wB[i2][:, fc, j * 128:(j + 1) * 128], aT[a2][:, fc, 0:ncol], fc == 0, fc == 3,
                                   ["wB%d" % i2, ("aT", a2, fc)], [b])
                            S.op("dve", lambda e, j=j, bank=bank, c0=c0, ncol=ncol: e.tensor_tensor(
                                out=hT[:, j, c0:c0 + ncol], in0=bank[:, 0:ncol], in1=hT[:, j, c0:c0 + ncol], op=ALU.add),
                                reads=[b, ("hT", j, c0)], writes=[("hT", j, c0)])

            def ple(layer, gcol):
                norm_h(gcol)
                for hlf in range(2):
                    load_w(wA[hlf][:], "wA%d" % hlf, wg_d[layer].rearrange("(k p) f -> p k f", p=128)[:, :, hlf * 512:(hlf + 1) * 512])
                load_w(wB[0][:, 0:2, :], "wB0", wp_d[layer].rearrange("(k p) d -> p k d", p=128))
                for gi, (c0, ncol, tiles) in enumerate(groups):
                    a2 = gi % 2
                    dma("pool", aT[a2][:, 0:2, 0:ncol], (pT0_d[pj] if layer == 0 else pT1_d).rearrange("(k p) t -> p k t", p=128)[:, :, c0:c0 + ncol],
                        writes=[("aT", a2, 0), ("aT", a2, 1)])
                    for j in range(DC):
                        b = PSK[rot("ps", 2)]
                        bank = ps[int(b[2:])]
                        for k in range(DC):
                            mm(bank[:, 0:ncol], wA[j // 4][:, k, (j % 4) * 128:(j % 4) * 128 + 128], uT[:, k, c0:c0 + ncol], k == 0, k == DC - 1,
                               ["wA%d" % (j // 4), ("uT", k, c0)], [b])
                        tf = rot("tf", 2)
                        S.op("act", lambda e, bank=bank, tf=tf, ncol=ncol: e.activation(out=tmpf[tf][:, 0:ncol], in_=bank[:, 0:ncol], func=AF.Sigmoid),
                             reads=[b], writes=["tmpf%d" % tf])
                        b2 = PSK[2 + rot("rs", 2)]
                        bank2 = ps[int(b2[2:])]
                        for k in range(2):
                            mm(bank2[:, 0:ncol], wB[0][:, k, j * 128:(j + 1) * 128], aT[a2][:, k, 0:ncol], k == 0, k == 1,
                               ["wB0", ("aT", a2, k)], [b2])
                        S.op("dve", lambda e, bank2=bank2, tf=tf, ncol=ncol: e.tensor_tensor(out=tmpf[tf][:, 0:ncol], in0=bank2[:, 0:ncol],
                                                                                         in1=tmpf[tf][:, 0:ncol], op=ALU.mult),
                             reads=[b2, "tmpf%d" % tf], writes=["tmpf%d" % tf])
                        S.op("dve", lambda e, j=j, tf=tf, c0=c0, ncol=ncol: e.tensor_tensor(out=hT[:, j, c0:c0 + ncol], in0=tmpf[tf][:, 0:ncol],
                                                                                         in1=hT[:, j, c0:c0 + ncol], op=ALU.add),
                             reads=["tmpf%d" % tf, ("hT", j, c0)], writes=[("hT", j, c0)])

            S.barrier()
            top[0] = M_1
            mlp(0, G_MLP0)
            ple(0, G_PLE0)

            if True:
                sb1 = sb
                cf = sb1("cf", [128, 2, 512], F32)
                co = sb1("co", [128, 2, 512], F32)
                cb = sb1("cb", [128, 2, 512], BF16)
                ctok = sb1("ctok", [128, 4, 256], BF16)
                krf = sb1("krf", [64, 2, 512], F32)
                kro = sb1("kro", [64, 512], F32)
                krb = sb1("krb", [64, 512], BF16)
                cs = sb1("cs", [64, 512], F32)
                sn = sb1("sn", [64, 512], F32)
                norm_h(G_KV)
                wkv = wA[0]
                load_w(wkv[:, :, 0:384], "wA0", wdkv_d.rearrange("(k p) f -> p k f", p=128))
                for gi, (c0, ncol, tiles) in enumerate(groups):
                    for mchunk in range(4):
                        msz = 128 if mchunk < 2 else 64
                        m0 = mchunk * 128 if mchunk < 2 else 256 + (mchunk - 2) * 64
                        b = PSK[rot("ps", 2)]
                        bank = ps[int(b[2:])]
                        for k in range(DC):
                            mm(bank[0:msz, 0:ncol], wkv[:, k, m0:m0 + msz], uT[:, k, c0:c0 + ncol], k == 0, k == DC - 1, ["wA0", ("uT", k, c0)], [b])
                        if mchunk < 2:
                            S.op("act", lambda e, bank=bank, mchunk=mchunk, ncol=ncol: e.activation(out=cf[:, mchunk, 0:ncol], in_=bank[:, 0:ncol], func=AF.Copy),
                                 reads=[b], writes=[("cf", mchunk)])
                        else:
                            S.op("act", lambda e, bank=bank, mchunk=mchunk, ncol=ncol: e.activation(out=krf[:, mchunk - 2, 0:ncol], in_=bank[0:64, 0:ncol], func=AF.Copy),
                                 reads=[b], writes=[("krf", mchunk - 2)])
                    norm_T(lambda ch, ncol=ncol: cf[:, ch, 0:ncol], lambda ch: [("cf", ch)], 2, KVR, G_KVN,
                           lambda ch, ncol=ncol: co[:, ch, 0:ncol], lambda ch: [("co", ch)], c0, ncol,
                           out2_fn=lambda ch, ncol=ncol: cb[:, ch, 0:ncol], out2_keys_fn=lambda ch: [("cb", ch)])
                    if own:
                        dma("sp", latT_o.rearrange("(k p) t -> p k t", p=128)[:, :, c0:c0 + ncol], co[:, :, 0:ncol], reads=[("co", 0), ("co", 1)], final=True)
                    dma("sp", cs[:, 0:ncol], cos_d[pj][:, c0:c0 + ncol], writes=["cs"])
                    dma("sp", sn[:, 0:ncol], sin_d[pj][:, c0:c0 + ncol], writes=["sn"])
                    S.op("dve", lambda e, ncol=ncol: e.tensor_tensor(out=krf[:, 0, 0:ncol], in0=krf[:, 0, 0:ncol], in1=cs[:, 0:ncol], op=ALU.mult),
                         reads=[("krf", 0), "cs"], writes=[("krf", 0)])
                    S.op("dve", lambda e, ncol=ncol: e.tensor_tensor(out=krf[:, 1, 0:ncol], in0=krf[:, 1, 0:ncol], in1=sn[:, 0:ncol], op=ALU.mult),
                         reads=[("krf", 1), "sn"], writes=[("krf", 1)])
                    S.op("dve", lambda e, ncol=ncol: e.tensor_tensor(out=kro[:, 0:ncol], in0=krf[:, 0, 0:ncol], in1=krf[:, 1, 0:ncol], op=ALU.add),
                         reads=[("krf", 0), ("krf", 1)], writes=["kro"])
                    S.op("dve", lambda e, ncol=ncol: e.tensor_copy(out=krb[:, 0:ncol], in_=kro[:, 0:ncol]), reads=["kro"], writes=["krb"])
                    if own:
                        dma("sp", krT_o[:, c0:c0 + ncol], kro[:, 0:ncol], reads=["kro"], final=True)
                    if not (own and gi == NG - 1):
                        dma("sp", cx_all.ap()[pj * 576:pj * 576 + 256, c0:c0 + ncol].rearrange("(k p) t -> p k t", p=128), cb[:, :, 0:ncol],
                            reads=[("cb", 0), ("cb", 1)], writes=["cx_all"])
                        dma("sp", cx_all.ap()[pj * 576 + 256:pj * 576 + 320, c0:c0 + ncol], krb[:, 0:ncol], reads=["krb"], writes=["cx_all"])
                        for ti, l in enumerate(tiles):
                            for ch in range(2):
                                S.op("pe", lambda e, ti=ti, ch=ch: e.transpose(out=psb[:, (ti * 2 + ch) * 128:(ti * 2 + ch + 1) * 128],
                                                                              in_=cb[:, ch, ti * 128:(ti + 1) * 128], identity=ident_bf[:]),
                                     reads=[("cb", ch), "ident_bf"], writes=["psb"])
                        nt_ = len(tiles)
                        S.op("act", lambda e, nt_=nt_: e.activation(out=ctok[:, 0:nt_, :], in_=psb[:, 0:nt_ * 256].rearrange("p (t c) -> p t c", c=256), func=AF.Copy),
                             reads=["psb"], writes=["ctok"])
                        dma("sp", c_tok_view(cx_all, pj)[c0:c0 + ncol, :].rearrange("(t p) c -> p t c", p=128), ctok[:, 0:nt_, :], reads=["ctok"], writes=["cx_all"])
                    else:
                        S.op("dve", lambda e: e.tensor_copy(out=sKT[:, 0:2, :], in_=cb[:, :, 0:128]), reads=[("cb", 0), ("cb", 1)], writes=["sKT"])
                        S.op("dve", lambda e: e.tensor_copy(out=sKT[0:64, 2, :], in_=krb[:, 0:128]), reads=["krb"], writes=["sKT"])
                        for ch in range(2):
                            S.op("pe", lambda e, ch=ch: e.transpose(out=psb[:, ch * 128:(ch + 1) * 128], in_=cb[:, ch, 0:128], identity=ident_bf[:]),
                                 reads=[("cb", ch), "ident_bf"], writes=["psb"])
                        S.op("act", lambda e: e.activation(out=sV[:, 2:258], in_=psb[:, 0:256], func=AF.Copy), reads=["psb"], writes=["sV"])

        groups = groups_all
        NG = len(groups)
        if True:
            sb2 = sb
            S.barrier()
            top[0] = M_1
            cqf = sb2("cqf", [128, 3, 512], F32)
            norm_h(G_MIX1)
            wq = wA[1]
            load_w(wq[:, :, 0:QR], "wA1", wdq_d.rearrange("(k p) f -> p k f", p=128))
            for gi, (c0, ncol, tiles) in enumerate(groups):
                for mc in range(3):
                    b = PSK[rot("ps", 2)]
                    bank = ps[int(b[2:])]
                    for k in range(DC):
                        mm(bank[:, 0:ncol], wq[:, k, mc * 128:(mc + 1) * 128], uT[:, k, c0:c0 + ncol], k == 0, k == DC - 1, ["wA1", ("uT", k, c0)], [b])
                    S.op("act", lambda e, bank=bank, mc=mc, ncol=ncol: e.activation(out=cqf[:, mc, 0:ncol], in_=bank[:, 0:ncol], func=AF.Copy),
                         reads=[b], writes=[("cqf", mc)])
                norm_T(lambda ch, ncol=ncol: cqf[:, ch, 0:ncol], lambda ch: [("cqf", ch)], 3, QR, G_Q,
                       lambda ch, c0=c0, ncol=ncol: cqT[:, ch, c0:c0 + ncol], lambda ch, c0=c0: [("cqT", ch, c0)], c0, ncol)
            S.barrier()
            top[0] = M_U
            AGW = 128
            w_uqn = sb2("w_uqn", [128, 3, NH * 128], BF16)
            w_uqr = sb2("w_uqr", [128, 3, NH * 128], BF16)
            w_ukT = sb2("w_ukT", [128, NH * 256], BF16)
            w_uv = sb2("w_uv", [128, 2, NH * 128], BF16)
            w_o = sb2("w_o", [128, NH, D], BF16)
            qT = sb2("qT", [128, 3, NH, AGW], BF16)
            qn = [sb2("qn%d" % i, [128, AGW], BF16) for i in range(2)]
            qrf = sb2("qrf", [64, 2, AGW], F32)
            cs2 = sb2("cs2", [64, AGW], F32)
            sn2 = sb2("sn2", [64, AGW], F32)
            KT = [sb2("KT%d" % i, [128, 3, 8 * 128], BF16) for i in range(2)]
            V = [sb2("V%d" % i, [128, 8, 258], BF16) for i in range(2)]
            PT = tmpb
            acc = sb2("acc", [128, NH, 257], F32)
            rden = sb2("rden", [128, NH], F32)
            Ob = sb2("Ob", [128, NH, 256], BF16)
            OT = sb2("OT", [128, 2, NH, AGW], BF16)
            oT = sb2("oT", [128, NH, AGW], BF16)
            NPB_ = 8 if cfg["SAMPLE_ATT"] else 2
            pgV = [sb2("pgV%d" % i, [128, 322], BF16) for i in range(NPB_)]
            qTs = sb2("qTs", [128, 3, NH, 128], BF16)
            OTs = sb2("OTs", [128, 2, NH, 128], BF16)
            acc_s = sb2("acc_s", [64, 257], F32)
            pKT2 = [sb2("pKT2_%d" % i, [128, 2, 3, 128], BF16) for i in range(2)]
            ps6b = ps[6][:, :].bitcast(BF16)
            tcount = [0]
            Obs = sb2("Obs", [64, 256], BF16)
            rdens = sb2("rdens", [64, 1], F32)
            for i in range(NPB_):
                S.op("dve", lambda e, i=i: e.memset(pgV[i][:, 0:2], 1.0), writes=["pgV%d" % i])
            print("arena top (attention phase):", top[0], ARENA)
            dma("pool", w_uqn[:], wuqn_d.rearrange("(k p) f -> p k f", p=128), writes=["w_uqn"])
            dma("pool", w_uqr[:], wuqr_d.rearrange("(k p) f -> p k f", p=128), writes=["w_uqr"])
            dma("pool", w_ukT[:], wukT_d[:, :], writes=["w_ukT"])
            dma("pool", w_uv[:], wuv_d.rearrange("(k p) f -> p k f", p=128), writes=["w_uv"])
            dma("pool", w_o[:], wo_d.rearrange("(h p) d -> p h d", p=128), writes=["w_o"])
            for i in range(2):
                S.op("dve", lambda e, i=i: e.memset(V[i][:, :, 256:258], 1.0), writes=["V%d" % i])
            agroups = []
            l = 0
            while l < NT:
                n = min(1, NT - l)
                agroups.append((l * 128, n * 128, list(range(l, l + n))))
                l += n
            agroups.append((NT * 128, 128, [NT]))
            kv_cnt = [0]
            NPOOL = cfg["NPOOL"]
            NPB = NPB_
            pcount = [0]

            def qproj(c0, ncol, qT, qk):
                dma("sp", cs2[:, 0:ncol], cos_d[NPASS - 1][:, c0:c0 + ncol], writes=["cs2"])
                dma("sp", sn2[:, 0:ncol], sin_d[NPASS - 1][:, c0:c0 + ncol], writes=["sn2"])
                for h in range(NH):
                    b = PSK[6]
                    bank = ps[6]
                    for k in range(3):
                        mm(bank[:, 0:ncol], w_uqn[:, k, h * 128:(h + 1) * 128], cqT[:, k, c0:c0 + ncol], k == 0, k == 2, ["w_uqn", ("cqT", k, c0)], [b])
                    q2 = h % 2
                    S.op("act", lambda e, q2=q2, ncol=ncol: e.activation(out=qn[q2][:, 0:ncol], in_=ps[6][:, 0:ncol], func=AF.Copy),
                         reads=[b], writes=["qn%d" % q2])
                    for rc in range(2):
                        mm(bank[:, 0:ncol], w_ukT[:, h * 256 + rc * 128:h * 256 + (rc + 1) * 128], qn[q2][:, 0:ncol], True, True, ["w_ukT", "qn%d" % q2], [b])
                        S.op("act", lambda e, rc=rc, h=h, ncol=ncol: e.activation(out=qT[:, rc, h, 0:ncol], in_=ps[6][:, 0:ncol], func=AF.Copy),
                             reads=[b], writes=[(qk, h)])
                    for half in range(2):
                        for k in range(3):
                            mm(bank[0:64, 0:ncol], w_uqr[:, k, h * 128 + half * 64:h * 128 + (half + 1) * 64], cqT[:, k, c0:c0 + ncol], k == 0, k == 2,
                               ["w_uqr", ("cqT", k, c0)], [b])
                        tab = cs2 if half == 0 else sn2
                        S.op("dve", lambda e, half=half, tab=tab, ncol=ncol: e.tensor_tensor(out=qrf[:, half, 0:ncol], in0=ps[6][0:64, 0:ncol], in1=tab[:, 0:ncol], op=ALU.mult),
                             reads=[b, "cs2", "sn2"], writes=[("qrf", half)])
                    S.op("dve", lambda e, h=h, ncol=ncol: e.tensor_tensor(out=qT[0:64, 2, h, 0:ncol], in0=qrf[:, 0, 0:ncol], in1=qrf[:, 1, 0:ncol], op=ALU.add),
                         reads=[("qrf", 0), ("qrf", 1)], writes=[(qk, h)])

            def sample_unit(s_, j0, j1):
                accb = PSK[0]
                accbank = ps[0]
                jl = list(range(j0, j1))
                groups_ = []
                while jl:
                    if len(jl) >= 2 and jl[0] != NPG and jl[1] != NPG:
                        groups_.append(jl[:2]); jl = jl[2:]
                    else:
                        groups_.append(jl[:1]); jl = jl[1:]
                st_ = {}

                def stageA(gi_):
                    grp = groups_[gi_]
                    own_pg = (grp[0] == NPG)
                    ng = len(grp)
                    sb_ = PSK[4 + rot("sq", 2)]
                    sbank = ps[int(sb_[2:])]
                    if not own_pg:
                        tsel = tcount[0] % 2
                        tcount[0] += 1
                        tb_key = "psb" if tsel == 0 else PSK[6]
                        tbank = psb if tsel == 0 else ps6b
                        kbuf = tsel
                        pbs = []
                        for pp, j in enumerate(grp):
                            pb = pcount[0] % NPB
                            pcount[0] += 1
                            pbs.append(pb)
                            idx = s_ * NPG + j
                            S.op("pool", lambda e, idx=idx, pb=pb: e.indirect_dma_start(
                                out=pgV[pb][:, 2:322], out_offset=None, in_=cache_d.rearrange("n p c -> (n p) c"),
                                in_offset=bass.IndirectOffsetOnAxis(ap=idx_all[:, idx:idx + 1], axis=0)),
                                reads=["idx_all"], writes=["pgV%d" % pb], dma=True)
                            for ch in range(2):
                                S.op("pe", lambda e, pb=pb, ch=ch, pp=pp, tbank=tbank: e.transpose(
                                    out=tbank[:, pp * 384 + ch * 128:pp * 384 + (ch + 1) * 128], in_=pgV[pb][:, 2 + ch * 128:2 + (ch + 1) * 128], identity=ident_bf[:]),
                                    reads=["pgV%d" % pb, "ident_bf"], writes=[tb_key])
                            S.op("pe", lambda e, pb=pb, pp=pp, tbank=tbank: e.transpose(out=tbank[0:64, pp * 384 + 256:pp * 384 + 384], in_=pgV[pb][:, 258:322], identity=ident_bf[:]),
                                 reads=["pgV%d" % pb, "ident_bf"], writes=[tb_key])
                        S.op("act", lambda e, kbuf=kbuf, ng=ng, tbank=tbank: e.activation(
                            out=pKT2[kbuf][:, 0:ng, 0:2, :], in_=tbank[:, 0:ng * 384].rearrange("p (g c t) -> p g c t", g=ng, c=3)[:, :, 0:2, :], func=AF.Copy),
                            reads=[tb_key], writes=["pKT%d" % kbuf])
                        S.op("act", lambda e, kbuf=kbuf, ng=ng, tbank=tbank: e.activation(
                            out=pKT2[kbuf][0:64, 0:ng, 2, :], in_=tbank[0:64, 0:ng * 384].rearrange("p (g c t) -> p g c t", g=ng, c=3)[:, :, 2, :], func=AF.Copy),
                            reads=[tb_key], writes=["pKT%d" % kbuf])
                        srcs = [(pKT2[kbuf][:, pp], "pKT%d" % kbuf, pgV[pbs[pp]], "pgV%d" % pbs[pp]) for pp in range(ng)]
                    else:
                        srcs = [(sKT, "sKT", sV, "sV")]
                    st_[gi_] = dict(grp=grp, own_pg=own_pg, ng=ng, sb_=sb_, sbank=sbank, srcs=srcs)

                def stageB(gi_):
                    d_ = st_[gi_]
                    grp, own_pg, ng, sb_, sbank, srcs = d_["grp"], d_["own_pg"], d_["ng"], d_["sb_"], d_["sbank"], d_["srcs"]
                    for pp, (kt_ap, kt_key, v_ap, v_key) in enumerate(srcs):
                        for k in range(3):
                            kk = 128 if k < 2 else 64
                            mm(sbank[:, pp * 64:(pp + 1) * 64].rearrange("p (h t) -> p h t", h=NH), kt_ap[0:kk, k, :],
                               qTs[0:kk, k, :, s_ * 8:(s_ + 1) * 8], k == 0, k == 2, [kt_key] + [("qTs", hh) for hh in range(NH)], [sb_])
                    p3 = rot("tb", 3)
                    S.op("act", lambda e, sbank=sbank, p3=p3, ng=ng: e.activation(out=PT[p3][:, 0:ng * 64], in_=sbank[:, 0:ng * 64], func=AF.Exp, scale=SM_SCALE),
                         reads=[sb_], writes=["PT%d" % p3])
                    if own_pg:
                        S.op("dve", lambda e, p3=p3, s_=s_: e.tensor_tensor(
                            out=PT[p3][:, 0:64].rearrange("p (h t) -> p h t", h=NH), in0=PT[p3][:, 0:64].rearrange("p (h t) -> p h t", h=NH),
                            in1=smask_bf[:, s_:s_ + 1, :].broadcast_to([128, NH, 8]), op=ALU.mult),
                            reads=["PT%d" % p3, "smask_bf"], writes=["PT%d" % p3])
                    d_["p3"] = p3

                def stageC(gi_):
                    d_ = st_[gi_]
                    grp, srcs, p3 = d_["grp"], d_["srcs"], d_["p3"]
                    for pp, (kt_ap, kt_key, v_ap, v_key) in enumerate(srcs):
                        jj = grp[pp]
                        mm(accbank[0:64, 0:257], PT[p3][:, pp * 64:(pp + 1) * 64], v_ap[:, 1:258], jj == j0, jj == j1 - 1, ["PT%d" % p3, v_key], [accb])
                G_ = len(groups_)
                for i_ in range(G_ + 2):
                    if i_ < G_:
                        stageA(i_)
                    if 0 <= i_ - 1 < G_:
                        stageB(i_ - 1)
                    if 0 <= i_ - 2 < G_:
                        stageC(i_ - 2)
                if j0 == 0:
                    S.op("dve", lambda e, accbank=accbank: e.tensor_copy(out=acc_s[:], in_=accbank[0:64, 0:257]), reads=[accb], writes=["acc_s"])
                else:
                    S.op("dve", lambda e, accbank=accbank: e.tensor_tensor(out=acc_s[:], in0=accbank[0:64, 0:257], in1=acc_s[:], op=ALU.add),
                         reads=[accb, "acc_s"], writes=["acc_s"])
                if j1 != NPG + 1:
                    return
                S.op("dve", lambda e: e.reciprocal(out=rdens[:], in_=acc_s[:, 0:1]), reads=["acc_s"], writes=["rdens"])
                S.op("dve", lambda e: e.tensor_scalar(out=Obs[:], in0=acc_s[:, 1:257], scalar1=rdens[:, 0:1], scalar2=None, op0=ALU.mult),
                     reads=["acc_s", "rdens"], writes=["Obs"])
                for rc in range(2):
                    S.op("pe", lambda e, rc=rc: e.transpose(out=psb[:, rc * 64:(rc + 1) * 64], in_=Obs[:, rc * 128:(rc + 1) * 128], identity=ident_bf[0:64, 0:64]),
                         reads=["Obs", "ident_bf"], writes=["psb"])
                for rc in range(2):
                    S.op("act", lambda e, rc=rc, s_=s_: e.activation(out=OTs[:, rc, :, s_ * 8:(s_ + 1) * 8],
                                                                    in_=psb[:, rc * 64:(rc + 1) * 64].rearrange("p (h t) -> p h t", h=NH), func=AF.Copy),
                         reads=["psb"], writes=[("OTs", 0)])

            def outproj(c0, ncol, OT, otkeys):
                for h in range(NH):
                    b = PSK[6]
                    for rc in range(2):
                        mm(ps[6][:, 0:ncol], w_uv[:, rc, h * 128:(h + 1) * 128], OT[:, rc, h, 0:ncol], rc == 0, rc == 1,
                           ["w_uv"] + otkeys, [b])
                    S.op("act", lambda e, h=h, ncol=ncol: e.activation(out=oT[:, h, 0:ncol], in_=ps[6][:, 0:ncol], func=AF.Copy), reads=[b], writes=[("oT", h)])
                for j in range(DC):
                    b = PSK[6]
                    for h in range(NH):
                        mm(ps[6][:, 0:ncol], w_o[:, h, j * 128:(j + 1) * 128], oT[:, h, 0:ncol], h == 0, h == NH - 1, ["w_o", ("oT", h)], [b])
                    S.op("dve", lambda e, j=j, c0=c0, ncol=ncol: e.tensor_tensor(out=hT[:, j, c0:c0 + ncol], in0=ps[6][:, 0:ncol], in1=hT[:, j, c0:c0 + ncol], op=ALU.add),
                         reads=[b, ("hT", j, c0)], writes=[("hT", j, c0)])


            sc0 = NT * 128
            if cfg["SAMPLE_ATT"]:
                qproj(sc0, 128, qTs, "qTs")
            units = []
            UP = 8
            for s__ in range(DPC):
                for j0 in range(0, NPG, UP):
                    j1 = min(NPG, j0 + UP)
                    units.append((s__, j0, j1 + 1 if j1 == NPG else j1))
            n_slots = sum(VR * ((l_ + 8) // 8) for l_ in range(NT))
            per_slot = -(-len(units) // max(1, n_slots))
            upos = [0]

            def pop_units(n):
                if not cfg["SAMPLE_ATT"]:
                    return
                for _ in range(n):
                    if upos[0] < len(units):
                        sample_unit(*units[upos[0]])
                        upos[0] += 1
            for gi, (c0, ncol, tiles) in enumerate(agroups[:-1]):
                qproj(c0, ncol, qT, "qT")
                for ti, l in enumerate(tiles):
                    m_ = l
                    first_chunk = True
                    for r in range(VR):
                      for s0 in range(0, m_ + 1, 8):
                        nsl = min(8, m_ + 1 - s0)
                        dg = m_ - s0
                        kb = kv_cnt[0] % 2
                        kv_cnt[0] += 1
                        dma("sp", KT[kb][:, 0:2, 0:nsl * 128],
                            cx_all.ap()[r * 576:r * 576 + 256, s0 * 128:(s0 + nsl) * 128].rearrange("(k p) t -> p k t", p=128),
                            reads=["cx_all"], writes=["KT%d" % kb])
                        dma("sp", KT[kb][0:64, 2, 0:nsl * 128],
                            cx_all.ap()[r * 576 + 256:r * 576 + 320, s0 * 128:(s0 + nsl) * 128],
                            reads=["cx_all"], writes=["KT%d" % kb])
                        dma("sp", V[kb][:, 0:nsl, 0:256],
                            c_tok_view(cx_all, r)[s0 * 128:(s0 + nsl) * 128, :].rearrange("(s p) c -> p s c", p=128),
                            reads=["cx_all"], writes=["V%d" % kb])
                        for hg in range(2):
                            for sl in range(nsl):
                                sb_ = PSK[4 + rot("sq", 2)]
                                sbank = ps[int(sb_[2:])]
                                for k in range(3):
                                    kk = 128 if k < 2 else 64
                                    mm(sbank[:, :].rearrange("p (h t) -> p h t", h=4), KT[kb][0:kk, k, sl * 128:(sl + 1) * 128],
                                       qT[0:kk, k, hg * 4:hg * 4 + 4, ti * 128:(ti + 1) * 128], k == 0, k == 2,
                                       ["KT%d" % kb] + [("qT", hg * 4 + hh) for hh in range(4)], [sb_])
                                p3 = rot("tb", 3)
                                S.op("act", lambda e, sbank=sbank, p3=p3: e.activation(out=PT[p3][:], in_=sbank[:, :], func=AF.Exp, scale=SM_SCALE),
                                     reads=[sb_], writes=["PT%d" % p3])
                                if dg == sl:
                                    mi = r * 2 + (m_ % 2)
                                    S.op("dve", lambda e, p3=p3, mi=mi: e.tensor_tensor(
                                        out=PT[p3][:, :].rearrange("p (h t) -> p h t", h=4), in0=PT[p3][:, :].rearrange("p (h t) -> p h t", h=4),
                                        in1=mask_bf[:, mi:mi + 1, :].broadcast_to([128, 4, 128]), op=ALU.mult),
                                         reads=["PT%d" % p3, "mask_bf"], writes=["PT%d" % p3])
                                for hh in range(4):
                                    mm(ps[hh][:, 0:257], PT[p3][:, hh * 128:(hh + 1) * 128], V[kb][:, sl, 0:257], sl == 0, sl == nsl - 1,
                                       ["PT%d" % p3, "V%d" % kb], [PSK[hh]])
                            for hh in range(4):
                                h = hg * 4 + hh
                                if first_chunk:
                                    S.op("dve", lambda e, hh=hh, h=h: e.tensor_copy(out=acc[:, h, :], in_=ps[hh][:, 0:257]), reads=[PSK[hh]], writes=[("acc", h)])
                                else:
                                    S.op("dve", lambda e, hh=hh, h=h: e.tensor_tensor(out=acc[:, h, :], in0=ps[hh][:, 0:257], in1=acc[:, h, :], op=ALU.add),
                                         reads=[PSK[hh], ("acc", h)], writes=[("acc", h)])
                        first_chunk = False
                        pop_units(per_slot)
                    S.op("dve", lambda e: e.reciprocal(out=rden[:], in_=acc[:, :, 256]), reads=[("acc", h) for h in range(NH)], writes=["rden"])
                    for h in range(NH):
                        S.op("dve", lambda e, h=h: e.tensor_scalar(out=Ob[:, h, :], in0=acc[:, h, 0:256], scalar1=rden[:, h:h + 1], scalar2=None, op0=ALU.mult),
                             reads=[("acc", h), "rden"], writes=[("Ob", h)])
                    for h4 in range(2):
                        for hh in range(4):
                            h = h4 * 4 + hh
                            for rc in range(2):
                                S.op("pe", lambda e, h=h, hh=hh, rc=rc: e.transpose(out=psb[:, (hh * 2 + rc) * 128:(hh * 2 + rc + 1) * 128],
                                                                                   in_=Ob[:, h, rc * 128:(rc + 1) * 128], identity=ident_bf[:]),
                                     reads=[("Ob", h), "ident_bf"], writes=["psb"])
                        for rc in range(2):
                            S.op("act", lambda e, h4=h4, rc=rc, ti=ti: e.activation(
                                out=OT[:, rc, h4 * 4:h4 * 4 + 4, ti * 128:(ti + 1) * 128],
                                in_=psb[:, :].rearrange("p (h r t) -> p r h t", h=4, r=2)[:, rc], func=AF.Copy),
                                reads=["psb"], writes=[("OT", ti)])
                outproj(c0, ncol, OT, [("OT", ti) for ti in range(len(tiles))])
            if cfg["SAMPLE_ATT"]:
                pop_units(len(units))
                outproj(sc0, 128, OTs, [("OTs", 0)])

        S.barrier()
        top[0] = M_1
        mlp(1, G_MLP1)
        ple(1, G_PLE1)
        for (c0, ncol, tiles) in groups:
            yst = tmpf
            norm_T(lambda ch, c0=c0, ncol=ncol: hT[:, ch, c0:c0 + ncol], lambda ch, c0=c0: [("hT", ch, c0)], DC, D, G_FIN,
                   lambda ch, c0=c0, ncol=ncol: hT[:, ch, c0:c0 + ncol], lambda ch, c0=c0: [("hT", ch, c0)], c0, ncol)
            dma("sp", yT_o.rearrange("(c p) t -> p c t", p=128)[:, :, c0:c0 + ncol], hT[:, :, c0:c0 + ncol],
                reads=[("hT", ch, c0) for ch in range(DC)], final=True)
        S.emit()
    return nc


def vr_blocks(cfg, v):
    VR = cfg["NCORE"] // cfg["BATCH"]
    out = []
    for i in range(cfg["NBLK"] // (2 * VR)):
        out.append(2 * VR * i + v)
        out.append(2 * VR * i + 2 * VR - 1 - v)
    return out


def _pass_vranks(cfg, v):
    VR = cfg["NCORE"] // cfg["BATCH"]
    return [(v + 1 + j) % VR for j in range(VR)]


def _tables(cfg, k):
    NCORE, BATCH, NBLK = cfg["NCORE"], cfg["BATCH"], cfg["NBLK"]
    VR = NCORE // BATCH
    v = k % VR
    DPC = cfg["DEC"] // NCORE
    past = cfg["NPG"] * 128
    half = ROPE // 2
    inv_freq = np.power(np.float32(THETA), -np.arange(half, dtype=np.float32) / np.float32(half)).astype(np.float32)
    t = np.arange(128)
    amain = np.zeros((4, 128, 128), np.float32)
    afirst0 = np.zeros((4, 128, 128), np.float32)
    ahalo = np.zeros((4, 16, 128), np.float32)
    amain_s = np.zeros((4, 128, 128), np.float32)
    ahalo_s = np.zeros((2, 4, 120, 128), np.float32)
    for g, w in enumerate(POOL_W):
        diff = t[None, :] - t[:, None]
        inw = (diff >= 0) & (diff < w)
        amain[g] = inw / np.float32(w) - np.eye(128, dtype=np.float32)
        cntv = np.minimum(t + 1, w).astype(np.float32)
        afirst0[g] = inw / cntv[None, :] - np.eye(128, dtype=np.float32)
        i = np.arange(16)
        ahalo[g] = (i[:, None] >= t[None, :] + 17 - w) / np.float32(w)
        for s_ in range(16):
            for ii in range(8):
                for j in range(8):
                    amain_s[g, s_ * 8 + ii, s_ * 8 + j] = (1.0 / w if 0 <= j - ii < w else 0.0) - (1.0 if ii == j else 0.0)
        for a in range(2):
            for sl in range(8):
                for r in range(15):
                    for j in range(8):
                        if r > j + 15 - w:
                            ahalo_s[a, g, sl * 15 + r, (a * 8 + sl) * 8 + j] = 1.0 / w
    cosT, sinT, afirst = [], [], []
    for vv in _pass_vranks(cfg, v):
        blocks = vr_blocks(cfg, vv)
        pos = [np.arange(blk * 128, (blk + 1) * 128) for blk in blocks]
        pos.append(np.tile(past + np.arange(cfg["DSEQ"]), DPC))
        pos = np.concatenate(pos)
        ang = (pos.astype(np.float32)[:, None] * inv_freq[None, :]).astype(np.float32)
        cos = np.cos(ang).astype(np.float32)
        sin = np.sin(ang).astype(np.float32)
        cosT.append(np.concatenate([cos, cos], 1).T)
        sinT.append(np.concatenate([-sin, sin], 1).T)
        afirst.append(afirst0 if blocks[0] == 0 else amain)
    tri = (t[:, None] <= t[None, :]).astype(np.float32)
    bmask = np.zeros((VR * 2, 128, 128), np.float32)
    for rel, vv in enumerate(_pass_vranks(cfg, v)):
        bmask[rel * 2 + 0] = 1.0 if vv < v else (tri if vv == v else 0.0)
        bmask[rel * 2 + 1] = 1.0 if vv > v else (tri if vv == v else 0.0)
    smask = np.zeros((128, 16, 8), np.float32)
    for key in range(128):
        for i in range(8):
            if key % 8 <= i:
                smask[key, key // 8, i] = 1.0
    f = lambda a: np.ascontiguousarray(np.asarray(a, dtype=np.float32))
    return dict(pidx=np.arange(128, dtype=np.float32).reshape(128, 1), smask=smask.reshape(128, 128), cosT=f(np.stack(cosT)), sinT=f(np.stack(sinT)), amain=amain, afirst=f(np.stack(afirst)), ahalo=ahalo, amain_s=amain_s,
                ahalo_s=ahalo_s, bmask=bmask, identf=np.eye(128, dtype=np.float32))


def _prep(cfg, inp):
    NCORE, BATCH, NBLK = cfg["NCORE"], cfg["BATCH"], cfg["NBLK"]
    VR = NCORE // BATCH
    DPC = cfg["DEC"] // NCORE
    f = lambda a: np.ascontiguousarray(np.asarray(a, dtype=np.float32))
    xp, xs = f(inp["x_prompt"]), f(inp["x_sample"])
    pp, psm = f(inp["p_prompt"]), f(inp["p_sample"])
    stp = f(inp["state_pool"])

    def chunks(vv):
        return f(vv).reshape(-1, 128).T
    gvec = np.zeros((128, 80), np.float32)
    for col, vv in ((0, inp["norm_mix"][1]), (8, inp["norm_mlp"][0]), (16, inp["norm_mlp"][1]), (24, inp["norm_ple"][0]), (32, inp["norm_ple"][1]),
                    (40, inp["pool_scale"][0]), (48, inp["norm_kv"]), (56, inp["norm_final"]), (64, inp["q_norm"][0]), (67, inp["kv_norm"])):
        c = chunks(vv)
        gvec[:, col:col + c.shape[1]] = c
    wdkv = f(inp["w_dkv"])
    w_dkv4 = np.concatenate([wdkv[:, :KVR + ROPE], wdkv[:, KVR + 32:KVR + 64], wdkv[:, KVR:KVR + 32]], 1)
    wuq = f(inp["w_uq"][0])
    w_uq_nope = wuq[:, :, :NOPE].reshape(QR, NH * 128)
    rp = wuq[:, :, NOPE:]
    w_uq_rope = np.concatenate([rp, rp[:, :, 32:], rp[:, :, :32]], 2).reshape(QR, NH * 128)
    w_ukT = f(np.transpose(f(inp["w_uk"]), (2, 1, 0))).reshape(128, NH * 256)
    shared = dict(gvec=gvec, gmix0=f(inp["norm_mix"][0])[None, :], pool_w=f(inp["pool_w"][0]), w_up=f(inp["w_up"]), w_down=f(inp["w_down"]),
                  w_gate=f(inp["w_ple_gate"]), w_proj=f(inp["w_ple_proj"]), w_dkv4=f(w_dkv4), w_dq=f(inp["w_dq"][0]),
                  w_uq_nope=f(w_uq_nope), w_uq_rope=f(w_uq_rope), w_ukT=w_ukT, w_uv=f(inp["w_uv"]).reshape(KVR, NH * 128),
                  w_o=f(inp["w_o"][0]).reshape(NH * 128, D))
    if cfg["SAMPLE_ATT"]:
        shared["cache_comb"] = np.ascontiguousarray(np.concatenate([f(inp["cache_latent"]), f(inp["cache_krope"])], -1))
    maps = []
    for k in range(NCORE):
        b, v = k // VR, k % VR
        xs_k = xs[k * DPC:(k + 1) * DPC].reshape(128, D)
        xtok_l, xT_l, halo_l, pT0_l = [], [], [], []
        for vv in _pass_vranks(cfg, v):
            blocks = vr_blocks(cfg, vv)
            xt_ = np.concatenate([xp[b, blk * 128:(blk + 1) * 128] for blk in blocks] + [xs_k], 0)
            p0 = np.concatenate([pp[0, b, blk * 128:(blk + 1) * 128] for blk in blocks] + [psm[0, k * DPC:(k + 1) * DPC].reshape(128, PLE)], 0)
            hal = np.zeros((len(blocks), 16, D), np.float32)
            for i_, blk in enumerate(blocks):
                if blk > 0:
                    hal[i_] = xp[b, blk * 128 - 16:blk * 128]
            xtok_l.append(xt_)
            xT_l.append(xt_.T)
            halo_l.append(hal)
            pT0_l.append(p0.T)
        blocks = vr_blocks(cfg, v)
        p1 = np.concatenate([pp[1, b, blk * 128:(blk + 1) * 128] for blk in blocks] + [psm[1, k * DPC:(k + 1) * DPC].reshape(128, PLE)], 0)
        m = dict(shared)
        m.update(_tables(cfg, k))
        m["xtok"] = f(np.stack(xtok_l))
        m["xT"] = f(np.stack(xT_l))
        m["xhalo"] = f(np.stack(halo_l))
        m["pT0"] = f(np.stack(pT0_l))
        m["pT1"] = f(p1.T)
        sp = stp[0, k * DPC:(k + 1) * DPC]
        m["spin"] = f(sp)
        m["uhalo"] = f(sp.reshape(2, 120, D))
        if cfg["SAMPLE_ATT"]:
            m["ptab"] = np.ascontiguousarray(np.asarray(inp["page_table"])[k * DPC:(k + 1) * DPC].astype(np.int32).reshape(1, -1))
        maps.append(m)
    return maps


def _assemble(cfg, res):
    NCORE, BATCH, NBLK = cfg["NCORE"], cfg["BATCH"], cfg["NBLK"]
    VR = NCORE // BATCH
    DPC = cfg["DEC"] // NCORE
    SEQ = NBLK * 128
    y_p = np.zeros((BATCH, SEQ, D), np.float32)
    lat_p = np.zeros((BATCH, SEQ, KVR), np.float32)
    kr_p = np.zeros((BATCH, SEQ, ROPE), np.float32)
    y_s = np.zeros((cfg["DEC"], cfg["DSEQ"], D), np.float32)
    lat_s = np.zeros((cfg["DEC"], cfg["DSEQ"], KVR), np.float32)
    kr_s = np.zeros((cfg["DEC"], cfg["DSEQ"], ROPE), np.float32)
    pool_s = np.zeros((1, cfg["DEC"], 15, D), np.float32)
    pool_p = np.zeros((1, BATCH, 15, D), np.float32)
    for k in range(NCORE):
        r = res[k]
        b, v = k // VR, k % VR
        yT, latT, krT = np.asarray(r["yT"]), np.asarray(r["latT"]), np.asarray(r["krT"])
        l = 0
        for blk in vr_blocks(cfg, v):
            sl = slice(l * 128, (l + 1) * 128)
            y_p[b, blk * 128:(blk + 1) * 128] = yT[:, sl].T
            lat_p[b, blk * 128:(blk + 1) * 128] = latT[:, sl].T
            kr_p[b, blk * 128:(blk + 1) * 128] = krT[:, sl].T
            l += 1
        sl = slice(l * 128, (l + 1) * 128)
        y_s[k * DPC:(k + 1) * DPC] = yT[:, sl].T.reshape(DPC, cfg["DSEQ"], D)
        lat_s[k * DPC:(k + 1) * DPC] = latT[:, sl].T.reshape(DPC, cfg["DSEQ"], KVR)
        kr_s[k * DPC:(k + 1) * DPC] = krT[:, sl].T.reshape(DPC, cfg["DSEQ"], ROPE)
        pool_s[0, k * DPC:(k + 1) * DPC] = np.asarray(r["pools"])
        if v == 0:
            pool_p[0, b] = np.asarray(r["poolp"])
    return (y_p, y_s, pool_p, pool_s, lat_p, kr_p, lat_s, kr_s)


_NC_CACHE = {}


def kernel(**inputs):
    cfg = dict(FULL)
    key = "full"
    if key not in _NC_CACHE:
        _NC_CACHE[key] = build_program(cfg)
    nc = _NC_CACHE[key]
    maps = _prep(cfg, inputs)
    res = run_bass_kernel_spmd(nc, maps, core_ids=list(range(cfg["NCORE"])))
    return _assemble(cfg, res.results)
```
